# Optimizing a Trainium2 kernel written in Bass

```python
import jax, jax.numpy as jnp
from jax import lax
import numpy as np

D_MODEL = 1024
BATCH = 8
SEQ = 4096
DEPTH = 2

N_MEM = 256
HEAD_DIM = 64
N_GMLP_GROUPS = 6
N_FOX_HEADS = 6
N_MEM_HEADS = 4
D_GMLP = N_GMLP_GROUPS * HEAD_DIM
D_FOX = N_FOX_HEADS * HEAD_DIM
D_MEMQ = N_MEM_HEADS * HEAD_DIM
D_MIX = D_GMLP + D_FOX + D_MEMQ
D_IN = 2 * D_GMLP + 3 * D_FOX + N_FOX_HEADS + D_MEMQ
CHUNK = 128
Q_BLOCK = 128
D_FF = 4 * D_MODEL
RMS_EPS = 1e-6
NEG_INF = -1e30

kernel_name = "hybrid_gmlp_fox_memxattn_block"


def rmsnorm(x, g):
    xf = x.astype(jnp.float32)
    y = xf * lax.rsqrt(jnp.mean(jnp.square(xf), axis=-1, keepdims=True) + RMS_EPS)
    return (y * g.astype(jnp.float32)).astype(x.dtype)


def gmlp_spatial_gating(u, v, w_s, b_s, v_norm_g):
    B, S, _ = u.shape
    n_chunks = S // CHUNK
    u = jax.nn.gelu(u)
    v = jax.nn.gelu(v).reshape(B, S, N_GMLP_GROUPS, HEAD_DIM)
    v = rmsnorm(v, v_norm_g.reshape(N_GMLP_GROUPS, HEAD_DIM))
    v = v.reshape(B, n_chunks, CHUNK, N_GMLP_GROUPS, HEAD_DIM)
    w_causal = jnp.tril(w_s)
    mixed = jnp.einsum('gts,bcsgd->bctgd', w_causal, v)
    mixed = mixed + b_s.T[None, None, :, :, None].astype(mixed.dtype)
    return u * mixed.reshape(B, S, D_GMLP)


def forgetting_attention(q, k, v, log_f):
    B, S, H, dh = q.shape
    scale = dh ** -0.5
    c = jnp.cumsum(log_f, axis=1).transpose(0, 2, 1)
    q = q.transpose(0, 2, 1, 3)
    k = k.transpose(0, 2, 1, 3)
    v = v.transpose(0, 2, 1, 3)
    outs = []
    for i in range(S // Q_BLOCK):
        start, end = i * Q_BLOCK, (i + 1) * Q_BLOCK
        qb = q[:, :, start:end]
        kb = k[:, :, :end]
        vb = v[:, :, :end]
        s = jnp.einsum('bhtd,bhsd->bhts', qb, kb).astype(jnp.float32) * scale
        s = s + c[:, :, start:end, None] - c[:, :, None, :end]
        mask = jnp.arange(end)[None, :] <= jnp.arange(start, end)[:, None]
        s = jnp.where(mask, s, NEG_INF)
        p = jax.nn.softmax(s, axis=-1).astype(vb.dtype)
        outs.append(jnp.einsum('bhts,bhsd->bthd', p, vb))
    o = jnp.concatenate(outs, axis=1)
    return o.reshape(B, S, H * dh)


def memory_cross_attention(qm, mem_n, w_kv):
    B, S, _ = qm.shape
    M = mem_n.shape[1]
    km, vm = jnp.split(mem_n @ w_kv, 2, axis=-1)
    q = qm.reshape(B, S, N_MEM_HEADS, HEAD_DIM)
    km = km.reshape(B, M, N_MEM_HEADS, HEAD_DIM)
    vm = vm.reshape(B, M, N_MEM_HEADS, HEAD_DIM)
    s = jnp.einsum('bthd,bmhd->bhtm', q, km).astype(jnp.float32) * (HEAD_DIM ** -0.5)
    p = jax.nn.softmax(s, axis=-1).astype(vm.dtype)
    o = jnp.einsum('bhtm,bmhd->bthd', p, vm)
    return o.reshape(B, S, D_MEMQ)


def setup_inputs(seed: int = 0) -> dict:
    key = jax.random.key(seed)
    ks = jax.random.split(key, 17)
    f32 = jnp.float32

    def nrm(k, shape, scale):
        return jax.random.normal(k, shape, f32) * scale

    def gain(k, shape):
        return 1.0 + 0.05 * jax.random.normal(k, shape, f32)

    return {
        "x": jax.random.normal(ks[0], (BATCH, SEQ, D_MODEL), f32),
        "mem": jax.random.normal(ks[1], (BATCH, N_MEM, D_MODEL), f32),
        "norm_pre_mix": gain(ks[2], (DEPTH, D_MODEL)),
        "norm_post_mix": gain(ks[3], (DEPTH, D_MODEL)),
        "norm_pre_ffn": gain(ks[4], (DEPTH, D_MODEL)),
        "norm_post_ffn": gain(ks[5], (DEPTH, D_MODEL)),
        "norm_mem": gain(ks[6], (DEPTH, D_MODEL)),
        "w_in": nrm(ks[7], (DEPTH, D_MODEL, D_IN), D_MODEL ** -0.5),
        "b_forget": 3.0 + 0.5 * jax.random.normal(ks[8], (DEPTH, N_FOX_HEADS), f32),
        "gmlp_v_norm": gain(ks[9], (DEPTH, D_GMLP)),
        "gmlp_w_s": nrm(ks[10], (DEPTH, N_GMLP_GROUPS, CHUNK, CHUNK), CHUNK ** -0.5),
        "gmlp_b_s": 1.0 + 0.1 * jax.random.normal(ks[11], (DEPTH, N_GMLP_GROUPS, CHUNK), f32),
        "w_mem_kv": nrm(ks[12], (DEPTH, D_MODEL, 2 * D_MEMQ), D_MODEL ** -0.5),
        "w_out": nrm(ks[13], (DEPTH, D_MIX, D_MODEL), D_MIX ** -0.5),
        "w_ff1": nrm(ks[14], (DEPTH, D_MODEL, D_FF), D_MODEL ** -0.5),
        "w_ff2": nrm(ks[15], (DEPTH, D_FF, D_MODEL), D_FF ** -0.5),
    }


def reference(x, mem, norm_pre_mix, norm_post_mix, norm_pre_ffn, norm_post_ffn, norm_mem,
              w_in, b_forget, gmlp_v_norm, gmlp_w_s, gmlp_b_s, w_mem_kv, w_out, w_ff1, w_ff2):
    splits = np.cumsum([D_GMLP, D_GMLP, D_FOX, D_FOX, D_FOX, N_FOX_HEADS]).tolist()
    B, S, _ = x.shape
    for l in range(DEPTH):
        h = rmsnorm(x, norm_pre_mix[l])
        z = h @ w_in[l]
        g_u, g_v, f_q, f_k, f_v, f_gate, m_q = jnp.split(z, splits, axis=-1)

        out_a = gmlp_spatial_gating(g_u, g_v, gmlp_w_s[l], gmlp_b_s[l], gmlp_v_norm[l])

        log_f = jax.nn.log_sigmoid((f_gate + b_forget[l]).astype(jnp.float32))
        out_b = forgetting_attention(
            f_q.reshape(B, S, N_FOX_HEADS, HEAD_DIM),
            f_k.reshape(B, S, N_FOX_HEADS, HEAD_DIM),
            f_v.reshape(B, S, N_FOX_HEADS, HEAD_DIM),
            log_f)

        mem_n = rmsnorm(mem, norm_mem[l])
        out_c = memory_cross_attention(m_q, mem_n, w_mem_kv[l])

        y = jnp.concatenate([out_a, out_b, out_c], axis=-1) @ w_out[l]
        x = x + rmsnorm(y, norm_post_mix[l])

        h = rmsnorm(x, norm_pre_ffn[l])
        f = jnp.square(jax.nn.relu(h @ w_ff1[l])) @ w_ff2[l]
        x = x + rmsnorm(f, norm_post_ffn[l])
    return x
```

```python
from contextlib import ExitStack
import numpy as np
import ml_dtypes
import concourse.bass as bass
import concourse.mybir as mybir
from concourse.bass_utils import run_bass_kernel_spmd

F32 = mybir.dt.float32
BF16 = mybir.dt.bfloat16
AF = mybir.ActivationFunctionType
ALU = mybir.AluOpType
AX = mybir.AxisListType

COMPUTE = ("pe", "act", "dve", "pool")
EPS = 1e-6


class Res:
    __slots__ = ("name", "lw", "rd", "group", "excl")

    def __init__(self, name, group=None):
        self.name = name
        self.lw = None
        self.rd = []
        self.group = group
        self.excl = name.startswith("bank")


class Group:
    def __init__(self):
        self.since = []
        self.fdeps = []


class DmaSem:
    __slots__ = ("name", "total", "sem")

    def __init__(self, name):
        self.name = name
        self.total = 0
        self.sem = None


class Op:
    __slots__ = ("eng", "fn", "deps", "idx", "dma", "dma_total", "needs_inc", "cnt", "tag")

    def __init__(self, eng, fn):
        self.eng = eng
        self.fn = fn
        self.deps = []
        self.dma = None
        self.dma_total = 0
        self.needs_inc = False
        self.cnt = 0


class Prog:
    def __init__(self, nc):
        self.nc = nc
        self.ops = []
        self.dmasems = []
        self.resd = {}
        self.trace = None

    def R(self, name, group=None):
        r = self.resd.get(name)
        if r is None:
            r = Res(name, group)
            self.resd[name] = r
        return r

    def dmasem(self, name):
        d = DmaSem(name)
        self.dmasems.append(d)
        return d

    def fence(self, group):
        last = {}
        keep = []
        for o in group.since + group.fdeps:
            if o.dma is not None:
                keep.append(o)
            else:
                if o.eng not in last or last[o.eng].idx < o.idx:
                    last[o.eng] = o
        group.fdeps = keep + list(last.values())
        group.since = []

    def _track(self, op, reads, writes):
        reads = list(reads)
        writes = list(writes)
        for r in list(reads):
            if r.excl:
                reads.remove(r)
                if r not in writes:
                    writes.append(r)
        op.tag = "R:" + ",".join(r.name for r in reads) + " W:" + ",".join(w.name for w in writes)
        deps = set()
        for r in reads:
            if r.lw is not None:
                deps.add(r.lw)
        for w in writes:
            if w.lw is not None:
                deps.add(w.lw)
            for o in w.rd:
                deps.add(o)
        groups = set()
        for r in list(reads) + list(writes):
            if r.group is not None:
                groups.add(r.group)
        for g in groups:
            for o in g.fdeps:
                deps.add(o)
            g.since.append(op)
        deps.discard(op)
        best = {}
        red = []
        for d in deps:
            if d.dma is not None:
                red.append(d)
            elif d.eng not in best or best[d.eng].idx < d.idx:
                best[d.eng] = d
        deps = red + list(best.values())
        for r in reads:
            r.rd.append(op)
        for w in writes:
            w.lw = op
            w.rd = []
        op.deps = list(deps)

    def op(self, eng, fn, reads=(), writes=()):
        o = Op(eng, fn)
        o.idx = len(self.ops)
        self.ops.append(o)
        self._track(o, reads, writes)
        return o

    def dma(self, queue, fn, sem, reads=(), writes=()):
        o = Op(queue, fn)
        o.idx = len(self.ops)
        o.dma = sem
        sem.total += 16
        o.dma_total = sem.total
        self.ops.append(o)
        self._track(o, reads, writes)
        return o

    def final(self):
        o = Op("sp", lambda e: e.nop())
        o.idx = len(self.ops)
        o.tag = "final"
        last = {}
        for p in self.ops:
            if p.dma is not None:
                last[("d", p.dma.name)] = p
            elif p.eng in COMPUTE:
                last[("e", p.eng)] = p
        o.deps = list(last.values())
        self.ops.append(o)

    def emit(self, stack):
        nc = self.nc
        for o in self.ops:
            for d in o.deps:
                if d.dma is None:
                    if d.eng == "pe" and o.eng == "pe" and o.dma is None:
                        continue
                    d.needs_inc = True
        sems = {}
        for e in COMPUTE:
            sems[e] = stack.enter_context(nc.semaphore("s_" + e))
        for d in self.dmasems:
            if d.total > 0:
                d.sem = stack.enter_context(nc.semaphore("d_" + d.name))
        cnt = {e: 0 for e in COMPUTE}
        for o in self.ops:
            if o.dma is None and o.needs_inc:
                cnt[o.eng] += 1
                o.cnt = cnt[o.eng]
        per_eng = {e: [] for e in ("pe", "act", "dve", "pool", "sp")}
        for o in self.ops:
            per_eng[o.eng].append(o)
        self.stats = {e: len(v) for e, v in per_eng.items()}
        self.stats["incs"] = dict(cnt)

        def run_engine(ename, eng):
            seen = {}
            nw = 0
            for o in per_eng[ename]:
                need = {}
                for d in o.deps:
                    if d.dma is not None:
                        key = ("d", d.dma.name)
                        val = d.dma_total
                        semh = d.dma.sem
                    else:
                        if d.eng == "pe" and ename == "pe" and o.dma is None:
                            continue
                        key = ("e", d.eng)
                        val = d.cnt
                        semh = sems[d.eng]
                    if seen.get(key, 0) >= val:
                        continue
                    if key not in need or need[key][1] < val:
                        need[key] = (semh, val)
                for key, (semh, val) in need.items():
                    eng.wait_ge(semh, val)
                    seen[key] = val
                    nw += 1
                if self.trace is not None:
                    self.trace.append((ename, o.idx, [(k, v[1]) for k, v in need.items()], o.cnt if o.needs_inc else None, o.dma_total if o.dma else None, o.tag))
                ins = o.fn(eng)
                if o.dma is not None:
                    ins.then_inc(o.dma.sem, 16)
                elif o.needs_inc:
                    ins.then_inc(sems[ename], 1)
            self.stats["waits_" + ename] = nw

        with nc.Block() as block:
            @block.tensor
            def _(e):
                run_engine("pe", e)

            @block.scalar
            def _(e):
                run_engine("act", e)

            @block.vector
            def _(e):
                run_engine("dve", e)

            @block.gpsimd
            def _(e):
                run_engine("pool", e)

            @block.sync
            def _(e):
                run_engine("sp", e)


D = 1024
KC = 8
DIN = 2182
DFF = 4096
NMEM = 256
T = 512
NSLOT = 4

WIN_BLOCKS = [(0, 512), (512, 512), (1024, 512), (1536, 390), (1926, 256)]


class StopBuild(Exception):
    pass


def build(S, L, dbg=None):
    NT = S // 128
    NG = S // T
    nc = bass.Bass("TRN2", target_bir_lowering=False)

    def din(name, shape, dt=F32):
        return nc.dram_tensor(name, shape, dt, kind="ExternalInput")

    x_t = din("x", [S, D])
    mem_t = din("mem", [NMEM, D])
    g_premix_t = din("norm_pre_mix", [L, D])
    g_postmix_t = din("norm_post_mix", [L, D])
    g_preffn_t = din("norm_pre_ffn", [L, D])
    g_postffn_t = din("norm_post_ffn", [L, D])
    g_mem_t = din("norm_mem", [L, D])
    w_in_t = din("w_in", [L, D, DIN])
    b_forget_t = din("b_forget", [L, 6])
    gv_t = din("gmlp_v_norm", [L, 384])
    ws_t = din("gmlp_w_s", [L, 6, 128, 128])
    bs_t = din("gmlp_b_s", [L, 6, 128])
    wkv_t = din("w_mem_kv", [L, D, 512])
    wout_t = din("w_out", [L, D, D])
    w1_t = din("w_ff1", [L, D, DFF])
    w2_t = din("w_ff2", [L, DFF, D])
    ident_t = din("c_ident", [128, 128], BF16)
    triu_t = din("c_triu", [128, 128], BF16)
    triuf_t = din("c_triuf", [128, 128], F32)
    onesf_t = din("c_onesf", [128, 128], F32)
    out_t = nc.dram_tensor("out", [S, D], F32, kind="ExternalOutput")

    x_d, mem_d, out_d = x_t.ap(), mem_t.ap(), out_t.ap()
    w_in_d, wkv_d, wout_d, w1_d, w2_d = w_in_t.ap(), wkv_t.ap(), wout_t.ap(), w1_t.ap(), w2_t.ap()
    ws_d = ws_t.ap()

    with ExitStack() as st:
        P = Prog(nc)
        R = P.R

        def sb(name, shape, dt):
            return st.enter_context(nc.sbuf_tensor(name, shape, dt))

        KT = [sb(f"KT{l}", [128, 3, S], BF16) for l in range(L)]
        VP = [sb(f"VP{l}", [128, NT, 6, 65], BF16) for l in range(L)]
        CALL = [sb(f"CALL{l}", [128, NT, 6], F32) for l in range(L)]
        EALL = [sb(f"EALL{l}", [128, NT, 6], F32) for l in range(L)]
        KMT = [sb(f"KMT{l}", [128, 2, NMEM], BF16) for l in range(L)]
        VMP = [sb(f"VMP{l}", [128, 2, 4, 65], BF16) for l in range(L)]
        WST = [sb(f"WST{l}", [128, 6, 128], BF16) for l in range(L)]
        GCOL = [sb(f"GCOL{l}", [128, 3, 8], F32) for l in range(L)]
        BS = [sb(f"BS{l}", [128, 6], F32) for l in range(L)]
        BFG = [sb(f"BFG{l}", [128, 6], F32) for l in range(L)]
        gpm = sb("gpm", [128, D], F32)
        gpf = gpm
        xg = sb("xg", [128, 4, D], F32)
        hT = sb("hT", [128, KC, T], BF16)
        xnb = [sb("xnb0", [128, D], BF16)]
        stat = sb("stat", [128, 64], F32)
        ident = sb("ident", [128, 128], BF16)
        triu = sb("triu", [128, 128], BF16)
        triuf = sb("triuf", [128, 128], F32)
        onesf = sb("onesf", [128, 128], F32)
        slots = [sb(f"slot{i}", [128, 4096], BF16) for i in range(NSLOT)]
        ARENA_BF = 18432
        arena = sb("arena", [128, ARENA_BF], BF16)
        AG = Group()

        class Carver:
            def __init__(self):
                self.off = 0

            def take(self, nbf):
                o = self.off
                self.off += nbf
                assert self.off <= ARENA_BF, self.off
                return o

        cv = Carver()

        def a_bf(n):
            o = cv.take(n)
            return arena[:, o:o + n]

        def a_f32(n):
            o = cv.take(2 * n)
            return arena[:, o:o + 2 * n].bitcast(F32)

        QT = a_bf(3 * T).rearrange("p (c t) -> p c t", t=T)
        QMT = a_bf(2 * T).rearrange("p (c t) -> p c t", t=T)
        gmT = a_bf(3 * T).rearrange("p (c t) -> p c t", t=T)
        attT = a_bf(10 * T).rearrange("p (c t) -> p c t", t=T)
        NPT = 3
        PT = [a_bf(T) for _ in range(NPT)]
        zA = a_f32(768)
        tmpv = a_f32(384)
        tmpv2 = a_f32(384)
        vn = a_bf(384)
        outa = a_bf(384)
        beta = a_f32(2 * 4 * 32).rearrange("p (a b c) -> p a b c", a=2, b=4)
        rs = a_f32(T)
        bcsb = a_f32(T)
        GVS = bcsb[:, 0:384]
        fg = a_f32(24).rearrange("p (a b) -> p a b", b=6)
        spb = a_f32(24).rearrange("p (a b) -> p a b", b=6)
        att_end = cv.off
        cv.off = 0
        hidT = a_bf(32 * T).rearrange("p (c t) -> p c t", t=T)
        rtmp = [a_f32(T)]
        cv.off = max(cv.off, att_end)
        _yo = cv.take(2 * T)
        ytmp_bf = arena[:, _yo:_yo + 2 * T]
        ytmp = ytmp_bf.bitcast(F32)

        def AR(name):
            return R(name, AG)

        banks = [st.enter_context(nc.psum_tensor(f"bank{i}", [128, 512], F32)) for i in range(8)]
        banks_bf = [b[:, :].bitcast(BF16) for b in banks]

        def BK(i):
            return R(f"bank{i}")

        d_setup = P.dmasem("setup")
        d_x = [P.dmasem(f"x{i}") for i in range(4)]
        d_o = [P.dmasem(f"o{i}") for i in range(4)]
        d_slot = [P.dmasem(f"sl{i}") for i in range(NSLOT)]
        d_ws = P.dmasem("ws")
        d_gpm = P.dmasem("gpm")
        d_gpf = d_gpm
        d_gv = P.dmasem("gv")

        slot_ctr = [0]

        def load_slot(dst_fn, src, reads=()):
            i = slot_ctr[0] % NSLOT
            slot_ctr[0] += 1
            dst = dst_fn(slots[i])
            P.dma("pool", lambda e, dst=dst, src=src: e.dma_start(out=dst, in_=src), d_slot[i],
                  reads=list(reads), writes=[R(f"slot{i}")])
            return i

        def bcast_rows(t, row_off, n):
            return bass.AP(t, row_off, [[0, 128], [1, n]])

        setup_res = []

        def setup_dma(dst, src, resname, **kw):
            P.dma("sp", lambda e, dst=dst, src=src, kw=kw: e.dma_start(out=dst, in_=src, **kw), d_setup, writes=[R(resname)])
            setup_res.append(R(resname))

        setup_dma(ident[:], ident_t.ap(), "ident")
        setup_dma(triu[:], triu_t.ap(), "triu")
        setup_dma(triuf[:], triuf_t.ap(), "triuf")
        setup_dma(onesf[:], onesf_t.ap(), "onesf")
        for l in range(L):
            for k, gt in enumerate((g_premix_t, g_preffn_t, g_mem_t)):
                setup_dma(GCOL[l][:, k, :], bass.AP(gt, l * D, [[1, 128], [128, 8]]), f"GCOL{l}", allow_slow_non_contiguous=True)
            setup_dma(BFG[l][:], bcast_rows(b_forget_t, l * 6, 6), f"BFG{l}")
            setup_dma(BS[l][:], bass.AP(bs_t, l * 768, [[1, 128], [128, 6]]), f"BS{l}", allow_slow_non_contiguous=True)
        last_setup = P.ops[-1]
        for r in setup_res:
            r.lw = last_setup
            r.rd = []
        P.op("dve", lambda e: e.memset(stat[:], 1.0e6), writes=[R(f"stat{i}") for i in range(4)] + [R(f"statv{i}") for i in range(4)] + [R(f"statp{i}") for i in range(4)])
        for l in range(L):
            P.op("dve", lambda e, l=l: e.memset(VP[l][:, :, :, 64:65], 1.0), writes=[R(f"VPones{l}")])
            P.op("dve", lambda e, l=l: e.memset(VMP[l][:, :, :, 64:65], 1.0), writes=[R(f"VMPones{l}")])

        def rstd_from_ss(ss_ap, out_ap, n, rd, wr):
            P.op("act", lambda e: e.activation(out=out_ap, in_=ss_ap, func=AF.Ln, scale=1.0 / n, bias=EPS), reads=rd, writes=wr)
            P.op("act", lambda e: e.activation(out=out_ap, in_=out_ap, func=AF.Exp, scale=-0.5), reads=wr, writes=wr)

        tbank_ctr = [0]

        def norm_transpose(src_ap, src_res, gcol_ap, gres, dst_ap, dst_res, slot):
            c0 = slot * 4
            ssr = R(f"stat{slot}")
            xb = 0
            P.op("act", lambda e: e.activation(out=xnb[xb][:], in_=src_ap, func=AF.Square, accum_out=stat[:, c0:c0 + 1]),
                 reads=[src_res], writes=[ssr, R(f"xnb{xb}")])
            rstd_from_ss(stat[:, c0:c0 + 1], stat[:, c0 + 1:c0 + 2], D, [ssr], [ssr])
            P.op("dve", lambda e: e.tensor_scalar(out=xnb[xb][:], in0=src_ap, scalar1=stat[:, c0 + 1:c0 + 2], scalar2=None, op0=ALU.mult),
                 reads=[src_res, ssr], writes=[R(f"xnb{xb}")])
            tb = 6 + (tbank_ctr[0] % 2)
            tbank_ctr[0] += 1
            for kc in range(KC):
                P.op("pe", lambda e, kc=kc: e.transpose(out=banks_bf[tb][:, kc * 128:(kc + 1) * 128], in_=xnb[xb][:, kc * 128:(kc + 1) * 128], identity=ident[:]),
                     reads=[R(f"xnb{xb}"), R("ident")], writes=[BK(tb)])
            P.op("dve", lambda e: e.tensor_tensor(out=dst_ap, in0=banks_bf[tb][:, 0:1024].rearrange("p (k t) -> p k t", t=128),
                                                  in1=gcol_ap.unsqueeze(2).to_broadcast([128, KC, 128]), op=ALU.mult),
                 reads=[BK(tb), gres], writes=[dst_res])

        bank_rr = [0]

        def next_bank(lo=0, hi=6):
            b = lo + (bank_rr[0] % (hi - lo))
            bank_rr[0] += 1
            return b

        def layer_setup(l):
            for mt in range(2):
                P.dma("sp", lambda e, mt=mt: e.dma_start(out=xg[:, mt, :], in_=mem_d[mt * 128:(mt + 1) * 128, :]), d_x[mt], writes=[R(f"xg{mt}")])
            for mt in range(2):
                norm_transpose(xg[:, mt, :], R(f"xg{mt}"), GCOL[l][:, 2, :], R(f"GCOL{l}"),
                               hT[:, :, mt * 128:(mt + 1) * 128], R(f"hT{mt}"), slot=mt)
            si = load_slot(lambda s: s[:, :].rearrange("p (k n) -> p k n", n=512), wkv_d[l].rearrange("(k p) n -> p k n", p=128))
            wkv = slots[si][:, :].rearrange("p (k n) -> p k n", n=512)
            hres = [R("hT0"), R("hT1")]
            for pm in range(2):
                b = next_bank()
                for kc in range(KC):
                    P.op("pe", lambda e, kc=kc, pm=pm, b=b: e.matmul(banks[b][:, 0:NMEM], lhsT=wkv[:, kc, pm * 128:(pm + 1) * 128], rhs=hT[:, kc, 0:NMEM],
                                                                      start=(kc == 0), stop=(kc == KC - 1)),
                         reads=[R(f"slot{si}")] + hres, writes=[BK(b)])
                P.op("act", lambda e, pm=pm, b=b: e.activation(out=KMT[l][:, pm, :], in_=banks[b][:, 0:NMEM], func=AF.Copy),
                     reads=[BK(b)], writes=[R(f"KMT{l}")])
            for mt in range(2):
                b = next_bank()
                for kc in range(KC):
                    P.op("pe", lambda e, kc=kc, mt=mt, b=b: e.matmul(banks[b][:, 0:256], lhsT=hT[:, kc, mt * 128:(mt + 1) * 128], rhs=wkv[:, kc, 256:512],
                                                                      start=(kc == 0), stop=(kc == KC - 1)),
                         reads=[R(f"slot{si}"), hres[mt]], writes=[BK(b)])
                P.op("act", lambda e, mt=mt, b=b: e.activation(out=VMP[l][:, mt, :, 0:64], in_=banks[b][:, 0:256].rearrange("p (h d) -> p h d", d=64), func=AF.Copy),
                     reads=[BK(b)], writes=[R(f"VMP{l}")])
            P.dma("sp", lambda e: e.dma_start(out=zA.rearrange("p (g s) -> p g s", s=128), in_=ws_d[l].rearrange("g t s -> t g s")), d_ws,
                  writes=[AR("zA")])
            wsb = xnb[0][:, 0:768]
            P.op("dve", lambda e: e.tensor_copy(out=wsb, in_=zA), reads=[AR("zA")], writes=[R("xnb0")])
            tb = 6 + (tbank_ctr[0] % 2)
            tbank_ctr[0] += 1
            for gg in range(6):
                P.op("pe", lambda e, gg=gg: e.transpose(out=banks_bf[tb][:, gg * 128:(gg + 1) * 128], in_=wsb[:, gg * 128:(gg + 1) * 128], identity=ident[:]),
                     reads=[R("xnb0"), R("ident")], writes=[BK(tb)])
            P.op("dve", lambda e: e.tensor_tensor(out=WST[l][:], in0=banks_bf[tb][:, 0:768].rearrange("p (g t) -> p g t", t=128),
                                                  in1=triu[:].unsqueeze(1).to_broadcast([128, 6, 128]), op=ALU.mult),
                 reads=[BK(tb), R("triu")], writes=[R(f"WST{l}")])

        def mixer(g, l):
            xres = [R(f"xg{tt}") for tt in range(4)]
            hres = [R(f"hT{tt}") for tt in range(4)]
            P.dma("sp", lambda e: e.dma_start(out=gpm[:], in_=bcast_rows(g_postmix_t, l * D, D)), d_gpm, writes=[R("gpm")])
            P.dma("sp", lambda e: e.dma_start(out=GVS, in_=bcast_rows(gv_t, l * 384, 384)), d_gv, writes=[AR("bcsb")])
            for tt in range(4):
                norm_transpose(xg[:, tt, :], xres[tt], GCOL[l][:, 0, :], R(f"GCOL{l}"),
                               hT[:, :, tt * 128:(tt + 1) * 128], hres[tt], slot=tt)
            if dbg == "norm":
                raise StopBuild()
            blk = [None] * 5

            def load_blk(bi):
                c0, ncol = WIN_BLOCKS[bi]
                si = load_slot(lambda s, ncol=ncol: s[:, 0:KC * ncol].rearrange("p (k n) -> p k n", n=ncol),
                               w_in_d[l][:, c0:c0 + ncol].rearrange("(k p) n -> p k n", p=128))
                blk[bi] = (si, slots[si][:, 0:KC * ncol].rearrange("p (k n) -> p k n", n=ncol))

            load_blk(0)
            load_blk(1)
            load_blk(2)
            zbanks = []
            for tt in range(4):
                ba, bb = next_bank(), next_bank()
                for kc in range(KC):
                    P.op("pe", lambda e, kc=kc, tt=tt, ba=ba: e.matmul(banks[ba][:, :], lhsT=hT[:, kc, tt * 128:(tt + 1) * 128], rhs=blk[0][1][:, kc, :],
                                                                        start=(kc == 0), stop=(kc == KC - 1)),
                         reads=[hres[tt], R(f"slot{blk[0][0]}")], writes=[BK(ba)])
                for kc in range(KC):
                    P.op("pe", lambda e, kc=kc, tt=tt, bb=bb: e.matmul(banks[bb][:, 0:256], lhsT=hT[:, kc, tt * 128:(tt + 1) * 128], rhs=blk[1][1][:, kc, 0:256],
                                                                        start=(kc == 0), stop=(kc == KC - 1)),
                         reads=[hres[tt], R(f"slot{blk[1][0]}")], writes=[BK(bb)])
                P.op("act", lambda e, ba=ba: e.activation(out=zA[:, 0:512], in_=banks[ba][:, :], func=AF.Gelu_apprx_tanh), reads=[BK(ba)], writes=[AR("zA")])
                P.op("act", lambda e, bb=bb: e.activation(out=zA[:, 512:768], in_=banks[bb][:, 0:256], func=AF.Gelu_apprx_tanh), reads=[BK(bb)], writes=[AR("zA")])
                v3 = zA[:, 384:768].rearrange("p (g d) -> p g d", d=64)
                P.op("dve", lambda e: e.tensor_tensor(out=tmpv, in0=zA[:, 384:768], in1=zA[:, 384:768], op=ALU.mult), reads=[AR("zA")], writes=[AR("tmpv")])
                c0 = 16 + tt * 8
                sr = R(f"statv{tt}")
                P.op("dve", lambda e, c0=c0: e.reduce_sum(out=stat[:, c0:c0 + 6], in_=tmpv.rearrange("p (g d) -> p g d", d=64), axis=AX.X),
                     reads=[AR("tmpv")], writes=[sr])
                rstd_from_ss(stat[:, c0:c0 + 6], stat[:, c0:c0 + 6], 64, [sr], [sr])
                P.op("dve", lambda e, c0=c0, v3=v3: e.tensor_tensor(out=tmpv.rearrange("p (g d) -> p g d", d=64), in0=v3,
                                                                     in1=stat[:, c0:c0 + 6].unsqueeze(2).to_broadcast([128, 6, 64]), op=ALU.mult),
                     reads=[AR("zA"), sr], writes=[AR("tmpv")])
                P.op("dve", lambda e: e.tensor_tensor(out=vn, in0=tmpv, in1=GVS, op=ALU.mult), reads=[AR("tmpv"), AR("bcsb")], writes=[AR("vn")])
                bm = next_bank()
                for gg in range(6):
                    P.op("pe", lambda e, gg=gg, bm=bm: e.matmul(banks[bm][:, gg * 64:(gg + 1) * 64], lhsT=WST[l][:, gg, :], rhs=vn[:, gg * 64:(gg + 1) * 64],
                                                                 start=True, stop=True),
                         reads=[AR("vn"), R(f"WST{l}")], writes=[BK(bm)])
                P.op("dve", lambda e, bm=bm: e.tensor_tensor(out=tmpv2.rearrange("p (g d) -> p g d", d=64), in0=banks[bm][:, 0:384].rearrange("p (g d) -> p g d", d=64),
                                                              in1=BS[l][:].unsqueeze(2).to_broadcast([128, 6, 64]), op=ALU.add),
                     reads=[BK(bm), R(f"BS{l}")], writes=[AR("tmpv2")])
                P.op("dve", lambda e: e.tensor_tensor(out=outa, in0=tmpv2, in1=zA[:, 0:384], op=ALU.mult), reads=[AR("tmpv2"), AR("zA")], writes=[AR("outa")])
                tb = 6 + (tbank_ctr[0] % 2)
                tbank_ctr[0] += 1
                for c in range(3):
                    P.op("pe", lambda e, c=c, tb=tb: e.transpose(out=banks_bf[tb][:, c * 128:(c + 1) * 128], in_=outa[:, c * 128:(c + 1) * 128], identity=ident[:]),
                         reads=[AR("outa"), R("ident")], writes=[BK(tb)])
                P.op("act", lambda e, tb=tb, tt=tt: e.activation(out=gmT[:, :, tt * 128:(tt + 1) * 128], in_=banks_bf[tb][:, 0:384].rearrange("p (c t) -> p c t", t=128), func=AF.Copy),
                     reads=[BK(tb)], writes=[AR(f"gmT{tt}")])
            load_blk(4)
            load_blk(3)
            fm = [(1, 256, "q", 0), (1, 384, "q", 1), (2, 0, "q", 2),
                  (2, 128, "k", 0), (2, 256, "k", 1), (2, 384, "k", 2),
                  (4, 0, "m", 0), (4, 128, "m", 1)]
            for (bi, lc, kind, c) in fm:
                b = next_bank()
                for kc in range(KC):
                    P.op("pe", lambda e, kc=kc, bi=bi, lc=lc, b=b: e.matmul(banks[b][:, :], lhsT=blk[bi][1][:, kc, lc:lc + 128], rhs=hT[:, kc, :],
                                                                             start=(kc == 0), stop=(kc == KC - 1)),
                         reads=hres + [R(f"slot{blk[bi][0]}")], writes=[BK(b)])
                if kind == "q":
                    P.op("act", lambda e, b=b, c=c: e.activation(out=QT[:, c, :], in_=banks[b][:, :], func=AF.Copy, scale=0.125), reads=[BK(b)], writes=[AR(f"QT{c}")])
                elif kind == "m":
                    P.op("act", lambda e, b=b, c=c: e.activation(out=QMT[:, c, :], in_=banks[b][:, :], func=AF.Copy, scale=0.125), reads=[BK(b)], writes=[AR(f"QMT{c}")])
                else:
                    P.op("dve", lambda e, b=b, c=c: e.tensor_copy(out=KT[l][:, c, g * T:(g + 1) * T], in_=banks[b][:, :]), reads=[BK(b)], writes=[R(f"KT{l}_{c}")])
            for tt in range(4):
                b = next_bank()
                for kc in range(KC):
                    P.op("pe", lambda e, kc=kc, tt=tt, b=b: e.matmul(banks[b][:, 0:390], lhsT=hT[:, kc, tt * 128:(tt + 1) * 128], rhs=blk[3][1][:, kc, :],
                                                                      start=(kc == 0), stop=(kc == KC - 1)),
                         reads=[hres[tt], R(f"slot{blk[3][0]}")], writes=[BK(b)])
                P.op("act", lambda e, b=b, tt=tt: e.activation(out=VP[l][:, 4 * g + tt, :, 0:64], in_=banks[b][:, 0:384].rearrange("p (h d) -> p h d", d=64), func=AF.Copy),
                     reads=[BK(b)], writes=[R(f"VP{l}")])
                P.op("dve", lambda e, b=b, tt=tt: e.tensor_tensor(out=fg[:, tt, :], in0=banks[b][:, 384:390], in1=BFG[l][:], op=ALU.add),
                     reads=[BK(b), R(f"BFG{l}")], writes=[AR("fg")])
            P.op("act", lambda e: e.activation(out=spb, in_=fg, func=AF.Exp, scale=-1.0), reads=[AR("fg")], writes=[AR("spb")])
            P.op("act", lambda e: e.activation(out=spb, in_=spb, func=AF.Ln, bias=1.0), reads=[AR("spb")], writes=[AR("spb")])
            for tt in range(4):
                ti = 4 * g + tt
                b = next_bank()
                P.op("pe", lambda e, b=b, tt=tt: e.matmul(banks[b][:, 0:6], lhsT=triuf[:], rhs=spb[:, tt, :], start=True, stop=True),
                     reads=[AR("spb"), R("triuf")], writes=[BK(b)])
                P.op("pe", lambda e, b=b, tt=tt: e.matmul(banks[b][:, 8:14], lhsT=onesf[:], rhs=spb[:, tt, :], start=True, stop=True),
                     reads=[AR("spb"), R("onesf")], writes=[BK(b)])
                cr = R(f"CE{l}")
                if ti == 0:
                    P.op("dve", lambda e, b=b, ti=ti: e.tensor_copy(out=CALL[l][:, ti, :], in_=banks[b][:, 0:6]), reads=[BK(b)], writes=[cr])
                    P.op("dve", lambda e, b=b, ti=ti: e.tensor_copy(out=EALL[l][:, ti, :], in_=banks[b][:, 8:14]), reads=[BK(b)], writes=[cr])
                else:
                    P.op("dve", lambda e, b=b, ti=ti: e.tensor_tensor(out=CALL[l][:, ti, :], in0=banks[b][:, 0:6], in1=EALL[l][:, ti - 1, :], op=ALU.add),
                         reads=[BK(b), cr], writes=[cr])
                    P.op("dve", lambda e, b=b, ti=ti: e.tensor_tensor(out=EALL[l][:, ti, :], in0=banks[b][:, 8:14], in1=EALL[l][:, ti - 1, :], op=ALU.add),
                         reads=[BK(b), cr], writes=[cr])

            if dbg in ("win", "gm", "gm2"):
                raise StopBuild()
            nj = 4 * g + 4
            sb_rr = [0]
            pt_rr = [0]
            ktres = [R(f"KT{l}_{c}") for c in range(3)]

            def normalize(ob, hidx):
                P.op("dve", lambda e: e.reciprocal(out=rs[64:65, :], in_=banks[ob][64:65, :]), reads=[BK(ob)], writes=[AR("rs")])
                P.op("pe", lambda e: e.matmul(banks[5][0:64, :], lhsT=onesf[64:65, 0:64], rhs=rs[64:65, :], start=True, stop=True),
                     reads=[AR("rs"), R("onesf")], writes=[BK(5)])
                P.op("act", lambda e: e.activation(out=bcsb[0:64, :], in_=banks[5][0:64, :], func=AF.Copy), reads=[BK(5)], writes=[AR("bcsb")])
                P.op("dve", lambda e: e.tensor_tensor(out=attT[0:64, hidx, :], in0=banks[ob][0:64, :], in1=bcsb[0:64, :], op=ALU.mult),
                     reads=[BK(ob), AR("bcsb")], writes=[AR(f"attT{hidx}")])

            LOOK = 2
            pend = []
            deferred = []

            def tick():
                for d in deferred:
                    d[0] -= 1
                while deferred and deferred[0][0] <= 0:
                    deferred.pop(0)[1]()

            def norm_part1(ob):
                P.op("dve", lambda e: e.reciprocal(out=rs[64:65, :], in_=banks[ob][64:65, :]), reads=[BK(ob)], writes=[AR("rs")])

            def norm_part2(ob, hidx):
                P.op("pe", lambda e: e.matmul(banks[5][0:64, :], lhsT=onesf[64:65, 0:64], rhs=rs[64:65, :], start=True, stop=True),
                     reads=[AR("rs"), R("onesf")], writes=[BK(5)])
                P.op("act", lambda e: e.activation(out=bcsb[0:64, :], in_=banks[5][0:64, :], func=AF.Copy), reads=[BK(5)], writes=[AR("bcsb")])
                P.op("dve", lambda e: e.tensor_tensor(out=attT[0:64, hidx, :], in0=banks[ob][0:64, :], in1=bcsb[0:64, :], op=ALU.mult),
                     reads=[BK(ob), AR("bcsb")], writes=[AR(f"attT{hidx}")])

            def emit_pv(blk_):
                (kind, hidx, j, col0, pk, ob, first, last) = blk_
                if kind == "f":
                    P.op("pe", lambda e: e.matmul(banks[ob][0:65, col0:T], lhsT=VP[l][:, j, hidx, :], rhs=PT[pk][:, col0:T], start=first, stop=last),
                         reads=[R(f"VP{l}"), R(f"VPones{l}"), AR(f"PT{pk}")], writes=[BK(ob)])
                else:
                    P.op("pe", lambda e: e.matmul(banks[ob][0:65, :], lhsT=VMP[l][:, j, hidx - 6, :], rhs=PT[pk][:, :], start=first, stop=last),
                         reads=[R(f"VMP{l}"), R(f"VMPones{l}"), AR(f"PT{pk}")], writes=[BK(ob)])
                if last:
                    norm_part1(ob)
                    deferred.append([2, lambda ob=ob, hidx=hidx: norm_part2(ob, hidx)])

            def push(blk_):
                pend.append(blk_)
                if len(pend) > LOOK:
                    emit_pv(pend.pop(0))
                tick()

            for h in range(6):
                p, r0 = h // 2, 64 * (h % 2)
                par = h % 2
                br = AR(f"beta{par}")
                for il in range(4):
                    P.op("dve", lambda e, il=il, par=par, h=h: e.tensor_scalar(out=beta[:, par, il, 0:nj], in0=CALL[l][:, 0:nj, h],
                                                                                scalar1=EALL[l][:, 4 * g + il, h:h + 1], scalar2=None, op0=ALU.subtract),
                         reads=[R(f"CE{l}")], writes=[br])
                ob = 3 + (h % 2)
                for j in range(nj):
                    il0 = max(0, j - 4 * g)
                    col0 = il0 * 128
                    sbk = sb_rr[0] % 3
                    sb_rr[0] += 1
                    pk = pt_rr[0] % NPT
                    pt_rr[0] += 1
                    P.op("pe", lambda e, j=j, col0=col0, sbk=sbk, p=p, r0=r0: e.matmul(banks[sbk][:, col0:T], lhsT=KT[l][r0:r0 + 64, p, j * 128:(j + 1) * 128],
                                                                                        rhs=QT[r0:r0 + 64, p, col0:T], start=True, stop=True),
                         reads=[ktres[p], AR(f"QT{p}")], writes=[BK(sbk)])
                    for il in range(il0, 4):
                        P.op("act", lambda e, il=il, j=j, sbk=sbk, pk=pk, par=par: e.activation(out=PT[pk][:, il * 128:(il + 1) * 128], in_=banks[sbk][:, il * 128:(il + 1) * 128],
                                                                                                  func=AF.Exp, bias=beta[:, par, il, j:j + 1]),
                             reads=[BK(sbk), br], writes=[AR(f"PT{pk}")])
                    if j >= 4 * g:
                        P.op("dve", lambda e, il0=il0, pk=pk: e.tensor_tensor(out=PT[pk][:, il0 * 128:(il0 + 1) * 128], in0=PT[pk][:, il0 * 128:(il0 + 1) * 128],
                                                                                in1=triu[:], op=ALU.mult),
                             reads=[AR(f"PT{pk}"), R("triu")], writes=[AR(f"PT{pk}")])
                    push(("f", h, j, col0, pk, ob, j == 0, j == nj - 1))
            for hm in range(4):
                p, r0 = hm // 2, 64 * (hm % 2)
                ob = 3 + (hm % 2)
                for jm in range(2):
                    sbk = sb_rr[0] % 3
                    sb_rr[0] += 1
                    pk = pt_rr[0] % NPT
                    pt_rr[0] += 1
                    P.op("pe", lambda e, jm=jm, sbk=sbk, p=p, r0=r0: e.matmul(banks[sbk][:, :], lhsT=KMT[l][r0:r0 + 64, p, jm * 128:(jm + 1) * 128],
                                                                               rhs=QMT[r0:r0 + 64, p, :], start=True, stop=True),
                         reads=[R(f"KMT{l}"), AR(f"QMT{p}")], writes=[BK(sbk)])
                    P.op("act", lambda e, sbk=sbk, pk=pk: e.activation(out=PT[pk][:, :], in_=banks[sbk][:, :], func=AF.Exp),
                         reads=[BK(sbk)], writes=[AR(f"PT{pk}")])
                    push(("m", 6 + hm, jm, 0, pk, ob, jm == 0, jm == 1))
            while pend:
                emit_pv(pend.pop(0))
                tick()
            while deferred:
                deferred.pop(0)[1]()
            if dbg == "att":
                raise StopBuild()
            sg = load_slot(lambda s: s[:, 0:3 * D].rearrange("p (c n) -> p c n", n=D), wout_d[l][0:384, :].rearrange("(c p) n -> p c n", p=128))
            wo_g = slots[sg][:, 0:3 * D].rearrange("p (c n) -> p c n", n=D)
            wo_h = []
            for (h0, nh) in ((0, 4), (4, 4), (8, 2)):
                si = load_slot(lambda s, nh=nh: s[0:64, 0:nh * D].rearrange("p (c n) -> p c n", n=D),
                               wout_d[l][384 + 64 * h0:384 + 64 * (h0 + nh), :].rearrange("(c p) n -> p c n", p=64))
                v = slots[si][0:64, 0:nh * D].rearrange("p (c n) -> p c n", n=D)
                for k in range(nh):
                    wo_h.append((si, v, k))
            attres = [AR(f"attT{i}") for i in range(10)]
            for tt in range(4):
                yb = [next_bank(0, 4), next_bank(0, 4)]
                for half in range(2):
                    b = yb[half]
                    for c in range(3):
                        P.op("pe", lambda e, c=c, tt=tt, half=half, b=b: e.matmul(banks[b][:, :], lhsT=gmT[:, c, tt * 128:(tt + 1) * 128],
                                                                                   rhs=wo_g[:, c, half * 512:(half + 1) * 512], start=(c == 0), stop=False),
                             reads=[AR(f"gmT{tt}"), R(f"slot{sg}")], writes=[BK(b)])
                    for hh in range(10):
                        si, v, k = wo_h[hh]
                        P.op("pe", lambda e, hh=hh, tt=tt, half=half, b=b, v=v, k=k: e.matmul(banks[b][:, :], lhsT=attT[0:64, hh, tt * 128:(tt + 1) * 128],
                                                                                               rhs=v[:, k, half * 512:(half + 1) * 512], start=False, stop=(hh == 9)),
                             reads=[attres[hh], R(f"slot{si}")], writes=[BK(b)])
                post_norm(yb, tt, gpm, R("gpm"), xres[tt])

        def post_norm(yb, tt, gbuf, gres, xr):
            c0 = 48 + tt * 4
            sr = R(f"statp{tt}")
            for half in range(2):
                P.op("act", lambda e, half=half: e.activation(out=ytmp_bf[:, half * 512:(half + 1) * 512], in_=banks[yb[half]][:, :], func=AF.Square,
                                                               accum_out=stat[:, c0 + half:c0 + half + 1]),
                     reads=[BK(yb[half])], writes=[sr, R("ytmp")])
            P.op("dve", lambda e: e.tensor_tensor(out=stat[:, c0 + 2:c0 + 3], in0=stat[:, c0:c0 + 1], in1=stat[:, c0 + 1:c0 + 2], op=ALU.add), reads=[sr], writes=[sr])
            rstd_from_ss(stat[:, c0 + 2:c0 + 3], stat[:, c0 + 3:c0 + 4], D, [sr], [sr])
            for half in range(2):
                P.op("dve", lambda e, half=half: e.scalar_tensor_tensor(out=ytmp, in0=banks[yb[half]][:, :], scalar=stat[:, c0 + 3:c0 + 4],
                                                                         in1=gbuf[:, half * 512:(half + 1) * 512], op0=ALU.mult, op1=ALU.mult),
                     reads=[BK(yb[half]), sr, gres], writes=[R("ytmp")])
                P.op("dve", lambda e, half=half: e.tensor_tensor(out=xg[:, tt, half * 512:(half + 1) * 512], in0=xg[:, tt, half * 512:(half + 1) * 512], in1=ytmp, op=ALU.add),
                     reads=[R("ytmp"), xr], writes=[xr])

        def ffn(g, l, last):
            xres = [R(f"xg{tt}") for tt in range(4)]
            hres = [R(f"hT{tt}") for tt in range(4)]
            P.dma("sp", lambda e: e.dma_start(out=gpf[:], in_=bcast_rows(g_postffn_t, l * D, D)), d_gpf, writes=[R("gpm")])
            for tt in range(4):
                norm_transpose(xg[:, tt, :], xres[tt], GCOL[l][:, 1, :], R(f"GCOL{l}"),
                               hT[:, :, tt * 128:(tt + 1) * 128], hres[tt], slot=tt)
            P.fence(AG)
            for blk_i in range(8):
                si = load_slot(lambda s: s[:, :].rearrange("p (k n) -> p k n", n=512),
                               w1_d[l][:, blk_i * 512:(blk_i + 1) * 512].rearrange("(k p) n -> p k n", p=128))
                wb = slots[si][:, :].rearrange("p (k n) -> p k n", n=512)
                for fcl in range(4):
                    fc = blk_i * 4 + fcl
                    b = next_bank()
                    for kc in range(KC):
                        P.op("pe", lambda e, kc=kc, fcl=fcl, b=b, wb=wb: e.matmul(banks[b][:, :], lhsT=wb[:, kc, fcl * 128:(fcl + 1) * 128], rhs=hT[:, kc, :],
                                                                                   start=(kc == 0), stop=(kc == KC - 1)),
                             reads=hres + [R(f"slot{si}")], writes=[BK(b)])
                    rk = 0
                    P.op("act", lambda e, b=b, rk=rk: e.activation(out=rtmp[rk], in_=banks[b][:, :], func=AF.Relu), reads=[BK(b)], writes=[AR(f"rtmp{rk}")])
                    P.op("dve", lambda e, fc=fc, rk=rk: e.tensor_tensor(out=hidT[:, fc, :], in0=rtmp[rk], in1=rtmp[rk], op=ALU.mult),
                         reads=[AR(f"rtmp{rk}")], writes=[AR(f"hidT{fc}")])
            if dbg == "ffn1":
                raise StopBuild()
            for blk_i in range(8):
                si = load_slot(lambda s: s[:, :].rearrange("p (c n) -> p c n", n=D),
                               w2_d[l][blk_i * 512:(blk_i + 1) * 512, :].rearrange("(c p) n -> p c n", p=128))
                wb = slots[si][:, :].rearrange("p (c n) -> p c n", n=D)
                for fcl in range(4):
                    fc = blk_i * 4 + fcl
                    for tt in range(4):
                        for half in range(2):
                            b = tt * 2 + half
                            P.op("pe", lambda e, fc=fc, fcl=fcl, tt=tt, half=half, b=b, wb=wb: e.matmul(banks[b][:, :], lhsT=hidT[:, fc, tt * 128:(tt + 1) * 128],
                                                                                                         rhs=wb[:, fcl, half * 512:(half + 1) * 512],
                                                                                                         start=(fc == 0), stop=(fc == 31)),
                                 reads=[AR(f"hidT{fc}"), R(f"slot{si}")], writes=[BK(b)])
            for tt in range(4):
                post_norm([tt * 2, tt * 2 + 1], tt, gpf, R("gpm"), xres[tt])
                if last:
                    r0 = (4 * g + tt) * 128
                    P.dma("sp", lambda e, tt=tt, r0=r0: e.dma_start(out=out_d[r0:r0 + 128, :], in_=xg[:, tt, :]), d_o[tt],
                          reads=[xres[tt]], writes=[R(f"outd{tt}")])
            P.fence(AG)

        try:
            for l in range(L):
                layer_setup(l)
            P.fence(AG)
            if dbg == "setup":
                raise StopBuild()
            for g in range(NG):
                for tt in range(4):
                    r0 = (4 * g + tt) * 128
                    P.dma("sp", lambda e, tt=tt, r0=r0: e.dma_start(out=xg[:, tt, :], in_=x_d[r0:r0 + 128, :]), d_x[tt], writes=[R(f"xg{tt}")])
                for l in range(L):
                    mixer(g, l)
                    if dbg == "mix" or dbg == f"mix:{g}:{l}":
                        raise StopBuild()
                    ffn(g, l, last=(l == L - 1))
                    if dbg == f"ffn:{g}:{l}":
                        raise StopBuild()
        except StopBuild:
            if dbg == "win":
                P.op("dve", lambda e: e.tensor_copy(out=xg[:, 0, :].rearrange("p (c t) -> p c t", t=T), in_=QMT[:, :, :]),
                     reads=[AR("QMT0"), AR("QMT1"), R("xg0")], writes=[R("xg0")])
                P.op("dve", lambda e: e.tensor_copy(out=xg[:, 1, 0:512].rearrange("p (c t) -> p c t", t=256), in_=KMT[0][:, :, :]),
                     reads=[R("KMT0"), R("xg1")], writes=[R("xg1")])
                P.op("dve", lambda e: e.tensor_copy(out=xg[:, 2, 0:520].rearrange("p (c t) -> p c t", t=260), in_=VMP[0][:, :, :, :].rearrange("p a b c -> p a (b c)")),
                     reads=[R("VMP0"), R("VMPones0"), R("xg2")], writes=[R("xg2")])
            if dbg == "gm2":
                ar = [AR("zA"), AR("vn"), AR("tmpv2"), AR("outa"), R("WST0")] + [R(f"xg{i}") for i in range(4)]
                wr = [R(f"xg{i}") for i in range(4)]
                P.op("dve", lambda e: e.tensor_copy(out=xg[:, 0, 0:768], in_=zA), reads=ar, writes=wr)
                P.op("dve", lambda e: e.tensor_copy(out=xg[:, 1, 0:384], in_=vn), reads=ar, writes=wr)
                P.op("dve", lambda e: e.tensor_copy(out=xg[:, 1, 384:768], in_=tmpv2), reads=ar, writes=wr)
                P.op("dve", lambda e: e.tensor_copy(out=xg[:, 2, 0:384], in_=outa), reads=ar, writes=wr)
                P.op("dve", lambda e: e.tensor_copy(out=xg[:, 3, 0:768], in_=WST[0][:, :, :].rearrange("p g t -> p (g t)")), reads=ar, writes=wr)
            if dbg == "gm":
                P.op("dve", lambda e: e.tensor_copy(out=xg[:, 0, :].rearrange("p (c t) -> p c t", t=T), in_=gmT[:, 0:2, :]),
                     reads=[AR(f"gmT{i}") for i in range(4)] + [R("xg0")], writes=[R("xg0")])
                P.op("dve", lambda e: e.tensor_copy(out=xg[:, 1, 0:512], in_=gmT[:, 2, :]),
                     reads=[AR(f"gmT{i}") for i in range(4)] + [R("xg1")], writes=[R("xg1")])
            if dbg in ("att3", "att", "att5", "att4"):
                for q in range(4):
                    P.op("dve", lambda e, q=q: e.tensor_copy(out=xg[:, q, :].rearrange("p (c t) -> p c t", t=T), in_=attT[:, 2 * q:2 * q + 2, :]),
                         reads=[AR(f"attT{i}") for i in range(10)] + [R(f"xg{q}")], writes=[R(f"xg{q}")])
            for tt in range(4):
                P.dma("sp", lambda e, tt=tt: e.dma_start(out=out_d[tt * 128:(tt + 1) * 128, :], in_=xg[:, tt, :]), d_o[tt],
                      reads=[R(f"xg{tt}")], writes=[R(f"outd{tt}")])
        P.final()
        if build.want_trace:
            P.trace = []
        P.emit(st)
        build.stats = P.stats
        build.trace = P.trace
    return nc


build.want_trace = False

WNAMES = ["norm_pre_mix", "norm_post_mix", "norm_pre_ffn", "norm_post_ffn", "norm_mem", "w_in", "b_forget",
          "gmlp_v_norm", "gmlp_w_s", "gmlp_b_s", "w_mem_kv", "w_out", "w_ff1", "w_ff2"]

_cache = {}


def _consts():
    iu = np.triu(np.ones((128, 128), np.float32))
    return {
        "c_ident": np.eye(128, dtype=np.float32).astype(ml_dtypes.bfloat16),
        "c_triu": iu.astype(ml_dtypes.bfloat16),
        "c_triuf": iu.copy(),
        "c_onesf": np.ones((128, 128), np.float32),
    }


DBG = None


def run_layers(x, mem, weights, L):
    B, S, _ = x.shape
    key = (S, L)
    if key not in _cache:
        _cache[key] = build(S, L, dbg=DBG)
    nc = _cache[key]
    consts = _consts()
    in_maps = []
    for b in range(B):
        m = {"x": np.ascontiguousarray(x[b]), "mem": np.ascontiguousarray(mem[b])}
        for k in WNAMES:
            m[k] = np.ascontiguousarray(weights[k])
        m.update(consts)
        in_maps.append(m)
    res = run_bass_kernel_spmd(nc, in_maps, core_ids=list(range(B)))
    return np.stack([res.results[b]["out"] for b in range(B)], axis=0)


FUSED = True


def kernel(**inputs):
    x = np.asarray(inputs["x"], dtype=np.float32)
    mem = np.asarray(inputs["mem"], dtype=np.float32)
    W = {k: np.asarray(inputs[k], dtype=np.float32) for k in WNAMES}
    depth = W["w_in"].shape[0]
    if FUSED:
        return run_layers(x, mem, W, depth)
    for l in range(depth):
        x = run_layers(x, mem, {k: v[l:l + 1] for k, v in W.items()}, 1)
    return x
```

```python
from contextlib import ExitStack
import numpy as np
import ml_dtypes
import concourse.bass as bass
import concourse.mybir as mybir
from concourse.bass_utils import run_bass_kernel_spmd

F32 = mybir.dt.float32
BF16 = mybir.dt.bfloat16
AF = mybir.ActivationFunctionType
ALU = mybir.AluOpType
AX = mybir.AxisListType

COMPUTE = ("pe", "act", "dve", "pool")
EPS = 1e-6


class Res:
    __slots__ = ("name", "lw", "rd", "group", "excl")

    def __init__(self, name, group=None):
        self.name = name
        self.lw = None
        self.rd = []
        self.group = group
        self.excl = name.startswith("bank")


class Group:
    def __init__(self):
        self.since = []
        self.fdeps = []


class DmaSem:
    __slots__ = ("name", "total", "sem")

    def __init__(self, name):
        self.name = name
        self.total = 0
        self.sem = None


class Op:
    __slots__ = ("eng", "fn", "deps", "idx", "dma", "dma_total", "needs_inc", "cnt", "tag")

    def __init__(self, eng, fn):
        self.eng = eng
        self.fn = fn
        self.deps = []
        self.dma = None
        self.dma_total = 0
        self.needs_inc = False
        self.cnt = 0


class Prog:
    def __init__(self, nc):
        self.nc = nc
        self.ops = []
        self.dmasems = []
        self.resd = {}
        self.trace = None

    def R(self, name, group=None):
        r = self.resd.get(name)
        if r is None:
            r = Res(name, group)
            self.resd[name] = r
        return r

    def dmasem(self, name):
        d = DmaSem(name)
        self.dmasems.append(d)
        return d

    def fence(self, group):
        last = {}
        keep = []
        for o in group.since + group.fdeps:
            if o.dma is not None:
                keep.append(o)
            else:
                if o.eng not in last or last[o.eng].idx < o.idx:
                    last[o.eng] = o
        group.fdeps = keep + list(last.values())
        group.since = []

    def _track(self, op, reads, writes):
        reads = list(reads)
        writes = list(writes)
        for r in list(reads):
            if r.excl:
                reads.remove(r)
                if r not in writes:
                    writes.append(r)
        op.tag = "R:" + ",".join(r.name for r in reads) + " W:" + ",".join(w.name for w in writes)
        deps = set()
        for r in reads:
            if r.lw is not None:
                deps.add(r.lw)
        for w in writes:
            if w.lw is not None:
                deps.add(w.lw)
            for o in w.rd:
                deps.add(o)
        groups = set()
        for r in list(reads) + list(writes):
            if r.group is not None:
                groups.add(r.group)
        for g in groups:
            for o in g.fdeps:
                deps.add(o)
            g.since.append(op)
        deps.discard(op)
        best = {}
        red = []
        for d in deps:
            if d.dma is not None:
                red.append(d)
            elif d.eng not in best or best[d.eng].idx < d.idx:
                best[d.eng] = d
        deps = red + list(best.values())
        for r in reads:
            r.rd.append(op)
        for w in writes:
            w.lw = op
            w.rd = []
        op.deps = list(deps)

    def op(self, eng, fn, reads=(), writes=()):
        o = Op(eng, fn)
        o.idx = len(self.ops)
        self.ops.append(o)
        self._track(o, reads, writes)
        return o

    def dma(self, queue, fn, sem, reads=(), writes=()):
        o = Op(queue, fn)
        o.idx = len(self.ops)
        o.dma = sem
        sem.total += 16
        o.dma_total = sem.total
        self.ops.append(o)
        self._track(o, reads, writes)
        return o

    def final(self):
        o = Op("sp", lambda e: e.nop())
        o.idx = len(self.ops)
        o.tag = "final"
        last = {}
        for p in self.ops:
            if p.dma is not None:
                last[("d", p.dma.name)] = p
            elif p.eng in COMPUTE:
                last[("e", p.eng)] = p
        o.deps = list(last.values())
        self.ops.append(o)

    def emit(self, stack):
        nc = self.nc
        for o in self.ops:
            for d in o.deps:
                if d.dma is None:
                    if d.eng == "pe" and o.eng == "pe" and o.dma is None:
                        continue
                    d.needs_inc = True
        sems = {}
        for e in COMPUTE:
            sems[e] = stack.enter_context(nc.semaphore("s_" + e))
        for d in self.dmasems:
            if d.total > 0:
                d.sem = stack.enter_context(nc.semaphore("d_" + d.name))
        cnt = {e: 0 for e in COMPUTE}
        for o in self.ops:
            if o.dma is None and o.needs_inc:
                cnt[o.eng] += 1
                o.cnt = cnt[o.eng]
        per_eng = {e: [] for e in ("pe", "act", "dve", "pool", "sp")}
        for o in self.ops:
            per_eng[o.eng].append(o)
        self.stats = {e: len(v) for e, v in per_eng.items()}
        self.stats["incs"] = dict(cnt)

        def run_engine(ename, eng):
            seen = {}
            nw = 0
            for o in per_eng[ename]:
                need = {}
                for d in o.deps:
                    if d.dma is not None:
                        key = ("d", d.dma.name)
                        val = d.dma_total
                        semh = d.dma.sem
                    else:
                        if d.eng == "pe" and ename == "pe" and o.dma is None:
                            continue
                        key = ("e", d.eng)
                        val = d.cnt
                        semh = sems[d.eng]
                    if seen.get(key, 0) >= val:
                        continue
                    if key not in need or need[key][1] < val:
                        need[key] = (semh, val)
                for key, (semh, val) in need.items():
                    eng.wait_ge(semh, val)
                    seen[key] = val
                    nw += 1
                if self.trace is not None:
                    self.trace.append((ename, o.idx, [(k, v[1]) for k, v in need.items()], o.cnt if o.needs_inc else None, o.dma_total if o.dma else None, o.tag))
                ins = o.fn(eng)
                if o.dma is not None:
                    ins.then_inc(o.dma.sem, 16)
                elif o.needs_inc:
                    ins.then_inc(sems[ename], 1)
            self.stats["waits_" + ename] = nw

        with nc.Block() as block:
            @block.tensor
            def _(e):
                run_engine("pe", e)

            @block.scalar
            def _(e):
                run_engine("act", e)

            @block.vector
            def _(e):
                run_engine("dve", e)

            @block.gpsimd
            def _(e):
                run_engine("pool", e)

            @block.sync
            def _(e):
                run_engine("sp", e)


D = 1024
KC = 8
DIN = 2182
DFF = 4096
NMEM = 256
T = 512
NSLOT = 4

WIN_BLOCKS = [(0, 512), (512, 512), (1024, 512), (1536, 390), (1926, 256)]


class StopBuild(Exception):
    pass


def build(S, L, dbg=None):
    NT = S // 128
    NG = S // T
    nc = bass.Bass("TRN2", target_bir_lowering=False)

    def din(name, shape, dt=F32):
        return nc.dram_tensor(name, shape, dt, kind="ExternalInput")

    x_t = din("x", [S, D])
    mem_t = din("mem", [NMEM, D])
    g_premix_t = din("norm_pre_mix", [L, D])
    g_postmix_t = din("norm_post_mix", [L, D])
    g_preffn_t = din("norm_pre_ffn", [L, D])
    g_postffn_t = din("norm_post_ffn", [L, D])
    g_mem_t = din("norm_mem", [L, D])
    w_in_t = din("w_in", [L, D, DIN])
    b_forget_t = din("b_forget", [L, 6])
    gv_t = din("gmlp_v_norm", [L, 384])
    ws_t = din("gmlp_w_s", [L, 6, 128, 128])
    bs_t = din("gmlp_b_s", [L, 6, 128])
    wkv_t = din("w_mem_kv", [L, D, 512])
    wout_t = din("w_out", [L, D, D])
    w1_t = din("w_ff1", [L, D, DFF])
    w2_t = din("w_ff2", [L, DFF, D])
    ident_t = din("c_ident", [128, 128], BF16)
    triu_t = din("c_triu", [128, 128], BF16)
    triuf_t = din("c_triuf", [128, 128], F32)
    onesf_t = din("c_onesf", [128, 128], F32)
    out_t = nc.dram_tensor("out", [S, D], F32, kind="ExternalOutput")

    x_d, mem_d, out_d = x_t.ap(), mem_t.ap(), out_t.ap()
    w_in_d, wkv_d, wout_d, w1_d, w2_d = w_in_t.ap(), wkv_t.ap(), wout_t.ap(), w1_t.ap(), w2_t.ap()
    ws_d = ws_t.ap()

    with ExitStack() as st:
        P = Prog(nc)
        R = P.R

        def sb(name, shape, dt):
            return st.enter_context(nc.sbuf_tensor(name, shape, dt))

        KT = [sb(f"KT{l}", [128, 3, S], BF16) for l in range(L)]
        VP = [sb(f"VP{l}", [128, NT, 6, 65], BF16) for l in range(L)]
        CALL = [sb(f"CALL{l}", [128, NT, 6], F32) for l in range(L)]
        EALL = [sb(f"EALL{l}", [128, NT, 6], F32) for l in range(L)]
        KMT = [sb(f"KMT{l}", [128, 2, NMEM], BF16) for l in range(L)]
        VMP = [sb(f"VMP{l}", [128, 2, 4, 65], BF16) for l in range(L)]
        WST = [sb(f"WST{l}", [128, 6, 128], BF16) for l in range(L)]
        GCOL = [sb(f"GCOL{l}", [128, 3, 8], F32) for l in range(L)]
        BS = [sb(f"BS{l}", [128, 6], F32) for l in range(L)]
        BFG = [sb(f"BFG{l}", [128, 6], F32) for l in range(L)]
        gpm = sb("gpm", [128, D], F32)
        gpf = gpm
        xg = sb("xg", [128, 4, D], F32)
        hT = sb("hT", [128, KC, T], BF16)
        xnb = [sb("xnb0", [128, D], BF16)]
        stat = sb("stat", [128, 64], F32)
        ident = sb("ident", [128, 128], BF16)
        triu = sb("triu", [128, 128], BF16)
        triuf = sb("triuf", [128, 128], F32)
        onesf = sb("onesf", [128, 128], F32)
        slots = [sb(f"slot{i}", [128, 4096], BF16) for i in range(NSLOT)]
        ARENA_BF = 18432
        arena = sb("arena", [128, ARENA_BF], BF16)
        AG = Group()

        class Carver:
            def __init__(self):
                self.off = 0

            def take(self, nbf):
                o = self.off
                self.off += nbf
                assert self.off <= ARENA_BF, self.off
                return o

        cv = Carver()

        def a_bf(n):
            o = cv.take(n)
            return arena[:, o:o + n]

        def a_f32(n):
            o = cv.take(2 * n)
            return arena[:, o:o + 2 * n].bitcast(F32)

        QT = a_bf(3 * T).rearrange("p (c t) -> p c t", t=T)
        QMT = a_bf(2 * T).rearrange("p (c t) -> p c t", t=T)
        gmT = a_bf(3 * T).rearrange("p (c t) -> p c t", t=T)
        attT = a_bf(10 * T).rearrange("p (c t) -> p c t", t=T)
        NPT = 3
        PT_OFF = cv.off
        PT = [a_bf(T) for _ in range(NPT)]
        zA = a_f32(768)
        tmpv = a_f32(384)
        tmpv2 = a_f32(384)
        vn = a_bf(384)
        outa = a_bf(384)
        beta = a_f32(2 * 4 * 32).rearrange("p (a b c) -> p a b c", a=2, b=4)
        rs = a_f32(T)
        bcsb = a_f32(T)
        GVS = bcsb[:, 0:384]
        fg = a_f32(24).rearrange("p (a b) -> p a b", b=6)
        spb = a_f32(24).rearrange("p (a b) -> p a b", b=6)
        att_end = cv.off
        cv.off = 0
        hidT = a_bf(32 * T).rearrange("p (c t) -> p c t", t=T)
        rtmp = [a_f32(T)]
        cv.off = max(cv.off, att_end)
        _yo = cv.take(2 * T)
        ytmp_bf = arena[:, _yo:_yo + 2 * T]
        ytmp = ytmp_bf.bitcast(F32)

        def AR(name):
            return R(name, AG)

        banks = [st.enter_context(nc.psum_tensor(f"bank{i}", [128, 512], F32)) for i in range(8)]
        banks_bf = [b[:, :].bitcast(BF16) for b in banks]

        def BK(i):
            return R(f"bank{i}")

        d_setup = P.dmasem("setup")
        d_x = [P.dmasem(f"x{i}") for i in range(4)]
        d_o = [P.dmasem(f"o{i}") for i in range(4)]
        d_slot = [P.dmasem(f"sl{i}") for i in range(NSLOT)]
        d_ws = P.dmasem("ws")
        d_gpm = P.dmasem("gpm")
        d_gpf = d_gpm
        d_gv = P.dmasem("gv")

        slot_ctr = [0]

        def load_slot(dst_fn, src, reads=()):
            i = slot_ctr[0] % NSLOT
            slot_ctr[0] += 1
            dst = dst_fn(slots[i])
            P.dma("pool", lambda e, dst=dst, src=src: e.dma_start(out=dst, in_=src), d_slot[i],
                  reads=list(reads), writes=[R(f"slot{i}")])
            return i

        def bcast_rows(t, row_off, n):
            return bass.AP(t, row_off, [[0, 128], [1, n]])

        setup_res = []

        def setup_dma(dst, src, resname, **kw):
            P.dma("sp", lambda e, dst=dst, src=src, kw=kw: e.dma_start(out=dst, in_=src, **kw), d_setup, writes=[R(resname)])
            setup_res.append(R(resname))

        setup_dma(ident[:], ident_t.ap(), "ident")
        setup_dma(triu[:], triu_t.ap(), "triu")
        setup_dma(triuf[:], triuf_t.ap(), "triuf")
        setup_dma(onesf[:], onesf_t.ap(), "onesf")
        for l in range(L):
            for k, gt in enumerate((g_premix_t, g_preffn_t, g_mem_t)):
                setup_dma(GCOL[l][:, k, :], bass.AP(gt, l * D, [[1, 128], [128, 8]]), f"GCOL{l}", allow_slow_non_contiguous=True)
            setup_dma(BFG[l][:], bcast_rows(b_forget_t, l * 6, 6), f"BFG{l}")
            setup_dma(BS[l][:], bass.AP(bs_t, l * 768, [[1, 128], [128, 6]]), f"BS{l}", allow_slow_non_contiguous=True)
        last_setup = P.ops[-1]
        for r in setup_res:
            r.lw = last_setup
            r.rd = []
        P.op("dve", lambda e: e.memset(stat[:], 1.0e6), writes=[R("statn")] + [R(f"statv{i}") for i in range(4)] + [R(f"statp{i}") for i in range(4)])
        for l in range(L):
            P.op("dve", lambda e, l=l: e.memset(VP[l][:, :, :, 64:65], 1.0), writes=[R(f"VPones{l}")])
            P.op("dve", lambda e, l=l: e.memset(VMP[l][:, :, :, 64:65], 1.0), writes=[R(f"VMPones{l}")])

        def rstd_from_ss(ss_ap, out_ap, n, rd, wr):
            P.op("act", lambda e: e.activation(out=out_ap, in_=ss_ap, func=AF.Ln, scale=1.0 / n, bias=EPS), reads=rd, writes=wr)
            P.op("act", lambda e: e.activation(out=out_ap, in_=out_ap, func=AF.Exp, scale=-0.5), reads=wr, writes=wr)

        tbank_ctr = [0]

        def norm_transpose_n(srcs, gcol_ap, gres, dsts):
            n = len(srcs)
            ssr = R("statn")
            for i, (src_ap, src_res) in enumerate(srcs):
                P.op("act", lambda e, i=i, src_ap=src_ap: e.activation(out=xnb[0][:], in_=src_ap, func=AF.Square, accum_out=stat[:, i:i + 1]),
                     reads=[src_res], writes=[ssr] + ([R("xnb0")] if i == 0 else []))
            rstd_from_ss(stat[:, 0:n], stat[:, 4:4 + n], D, [ssr], [ssr])
            for i, ((src_ap, src_res), (dst_ap, dst_res)) in enumerate(zip(srcs, dsts)):
                P.op("dve", lambda e, i=i, src_ap=src_ap: e.tensor_scalar(out=xnb[0][:], in0=src_ap, scalar1=stat[:, 4 + i:5 + i], scalar2=None, op0=ALU.mult),
                     reads=[src_res, ssr], writes=[R("xnb0")])
                tb = 6 + (tbank_ctr[0] % 2)
                tbank_ctr[0] += 1
                for kc in range(KC):
                    P.op("pe", lambda e, kc=kc, tb=tb: e.transpose(out=banks_bf[tb][:, kc * 128:(kc + 1) * 128], in_=xnb[0][:, kc * 128:(kc + 1) * 128], identity=ident[:]),
                         reads=[R("xnb0"), R("ident")], writes=[BK(tb)])
                P.op("dve", lambda e, tb=tb, dst_ap=dst_ap: e.tensor_tensor(out=dst_ap, in0=banks_bf[tb][:, 0:1024].rearrange("p (k t) -> p k t", t=128),
                                                                             in1=gcol_ap.unsqueeze(2).to_broadcast([128, KC, 128]), op=ALU.mult),
                     reads=[BK(tb), gres], writes=[dst_res])

        bank_rr = [0]

        def next_bank(lo=0, hi=6):
            b = lo + (bank_rr[0] % (hi - lo))
            bank_rr[0] += 1
            return b

        def layer_setup(l):
            for mt in range(2):
                P.dma("sp", lambda e, mt=mt: e.dma_start(out=xg[:, mt, :], in_=mem_d[mt * 128:(mt + 1) * 128, :]), d_x[mt], writes=[R(f"xg{mt}")])
            norm_transpose_n([(xg[:, mt, :], R(f"xg{mt}")) for mt in range(2)], GCOL[l][:, 2, :], R(f"GCOL{l}"),
                             [(hT[:, :, mt * 128:(mt + 1) * 128], R(f"hT{mt}")) for mt in range(2)])
            si = load_slot(lambda s: s[:, :].rearrange("p (k n) -> p k n", n=512), wkv_d[l].rearrange("(k p) n -> p k n", p=128))
            wkv = slots[si][:, :].rearrange("p (k n) -> p k n", n=512)
            hres = [R("hT0"), R("hT1")]
            for pm in range(2):
                b = next_bank()
                for kc in range(KC):
                    P.op("pe", lambda e, kc=kc, pm=pm, b=b: e.matmul(banks[b][:, 0:NMEM], lhsT=wkv[:, kc, pm * 128:(pm + 1) * 128], rhs=hT[:, kc, 0:NMEM],
                                                                      start=(kc == 0), stop=(kc == KC - 1)),
                         reads=[R(f"slot{si}")] + hres, writes=[BK(b)])
                P.op("act", lambda e, pm=pm, b=b: e.activation(out=KMT[l][:, pm, :], in_=banks[b][:, 0:NMEM], func=AF.Copy),
                     reads=[BK(b)], writes=[R(f"KMT{l}")])
            for mt in range(2):
                b = next_bank()
                for kc in range(KC):
                    P.op("pe", lambda e, kc=kc, mt=mt, b=b: e.matmul(banks[b][:, 0:256], lhsT=hT[:, kc, mt * 128:(mt + 1) * 128], rhs=wkv[:, kc, 256:512],
                                                                      start=(kc == 0), stop=(kc == KC - 1)),
                         reads=[R(f"slot{si}"), hres[mt]], writes=[BK(b)])
                P.op("act", lambda e, mt=mt, b=b: e.activation(out=VMP[l][:, mt, :, 0:64], in_=banks[b][:, 0:256].rearrange("p (h d) -> p h d", d=64), func=AF.Copy),
                     reads=[BK(b)], writes=[R(f"VMP{l}")])
            P.dma("sp", lambda e: e.dma_start(out=zA.rearrange("p (g s) -> p g s", s=128), in_=ws_d[l].rearrange("g t s -> t g s")), d_ws,
                  writes=[AR("zA")])
            wsb = xnb[0][:, 0:768]
            P.op("dve", lambda e: e.tensor_copy(out=wsb, in_=zA), reads=[AR("zA")], writes=[R("xnb0")])
            tb = 6 + (tbank_ctr[0] % 2)
            tbank_ctr[0] += 1
            for gg in range(6):
                P.op("pe", lambda e, gg=gg: e.transpose(out=banks_bf[tb][:, gg * 128:(gg + 1) * 128], in_=wsb[:, gg * 128:(gg + 1) * 128], identity=ident[:]),
                     reads=[R("xnb0"), R("ident")], writes=[BK(tb)])
            P.op("dve", lambda e: e.tensor_tensor(out=WST[l][:], in0=banks_bf[tb][:, 0:768].rearrange("p (g t) -> p g t", t=128),
                                                  in1=triu[:].unsqueeze(1).to_broadcast([128, 6, 128]), op=ALU.mult),
                 reads=[BK(tb), R("triu")], writes=[R(f"WST{l}")])

        def mixer(g, l):
            xres = [R(f"xg{tt}") for tt in range(4)]
            hres = [R(f"hT{tt}") for tt in range(4)]
            P.dma("sp", lambda e: e.dma_start(out=gpm[:], in_=bcast_rows(g_postmix_t, l * D, D)), d_gpm, writes=[R("gpm")])
            P.dma("sp", lambda e: e.dma_start(out=GVS, in_=bcast_rows(gv_t, l * 384, 384)), d_gv, writes=[AR("bcsb")])
            norm_transpose_n([(xg[:, tt, :], xres[tt]) for tt in range(4)], GCOL[l][:, 0, :], R(f"GCOL{l}"),
                             [(hT[:, :, tt * 128:(tt + 1) * 128], hres[tt]) for tt in range(4)])
            if dbg == "norm":
                raise StopBuild()
            blk = [None] * 5

            def load_blk(bi):
                c0, ncol = WIN_BLOCKS[bi]
                si = load_slot(lambda s, ncol=ncol: s[:, 0:KC * ncol].rearrange("p (k n) -> p k n", n=ncol),
                               w_in_d[l][:, c0:c0 + ncol].rearrange("(k p) n -> p k n", p=128))
                blk[bi] = (si, slots[si][:, 0:KC * ncol].rearrange("p (k n) -> p k n", n=ncol))

            load_blk(0)
            load_blk(1)
            load_blk(2)
            load_blk(4)
            zAb = [zA, arena[:, PT_OFF:PT_OFF + 1536].bitcast(F32)]
            zAr = [[AR("zA")], [AR("PT0"), AR("PT1"), AR("PT2")]]

            def a_mm(tt):
                zi = tt % 2
                ba, bb = next_bank(), next_bank()
                for kc in range(KC):
                    P.op("pe", lambda e, kc=kc: e.matmul(banks[ba][:, :], lhsT=hT[:, kc, tt * 128:(tt + 1) * 128], rhs=blk[0][1][:, kc, :],
                                                          start=(kc == 0), stop=(kc == KC - 1)),
                         reads=[hres[tt], R(f"slot{blk[0][0]}")], writes=[BK(ba)])
                for kc in range(KC):
                    P.op("pe", lambda e, kc=kc: e.matmul(banks[bb][:, 0:256], lhsT=hT[:, kc, tt * 128:(tt + 1) * 128], rhs=blk[1][1][:, kc, 0:256],
                                                          start=(kc == 0), stop=(kc == KC - 1)),
                         reads=[hres[tt], R(f"slot{blk[1][0]}")], writes=[BK(bb)])
                P.op("act", lambda e: e.activation(out=zAb[zi][:, 0:512], in_=banks[ba][:, :], func=AF.Gelu_apprx_tanh), reads=[BK(ba)], writes=zAr[zi])
                P.op("act", lambda e: e.activation(out=zAb[zi][:, 512:768], in_=banks[bb][:, 0:256], func=AF.Gelu_apprx_tanh), reads=[BK(bb)], writes=zAr[zi])

            def g_chain(tt):
                zi = tt % 2
                zz, zr = zAb[zi], zAr[zi]
                v3 = zz[:, 384:768].rearrange("p (g d) -> p g d", d=64)
                P.op("dve", lambda e: e.tensor_tensor(out=tmpv, in0=zz[:, 384:768], in1=zz[:, 384:768], op=ALU.mult), reads=zr, writes=[AR("tmpv")])
                c0 = 16 + tt * 8
                sr = R(f"statv{tt}")
                P.op("dve", lambda e: e.reduce_sum(out=stat[:, c0:c0 + 6], in_=tmpv.rearrange("p (g d) -> p g d", d=64), axis=AX.X),
                     reads=[AR("tmpv")], writes=[sr])
                rstd_from_ss(stat[:, c0:c0 + 6], stat[:, c0:c0 + 6], 64, [sr], [sr])
                P.op("dve", lambda e: e.tensor_tensor(out=tmpv.rearrange("p (g d) -> p g d", d=64), in0=v3,
                                                      in1=stat[:, c0:c0 + 6].unsqueeze(2).to_broadcast([128, 6, 64]), op=ALU.mult),
                     reads=zr + [sr], writes=[AR("tmpv")])
                P.op("dve", lambda e: e.tensor_tensor(out=vn, in0=tmpv, in1=GVS, op=ALU.mult), reads=[AR("tmpv"), AR("bcsb")], writes=[AR("vn")])
                bm = next_bank()
                for gg in range(6):
                    P.op("pe", lambda e, gg=gg: e.matmul(banks[bm][:, gg * 64:(gg + 1) * 64], lhsT=WST[l][:, gg, :], rhs=vn[:, gg * 64:(gg + 1) * 64],
                                                          start=True, stop=True),
                         reads=[AR("vn"), R(f"WST{l}")], writes=[BK(bm)])
                P.op("dve", lambda e: e.tensor_tensor(out=tmpv2.rearrange("p (g d) -> p g d", d=64), in0=banks[bm][:, 0:384].rearrange("p (g d) -> p g d", d=64),
                                                      in1=BS[l][:].unsqueeze(2).to_broadcast([128, 6, 64]), op=ALU.add),
                     reads=[BK(bm), R(f"BS{l}")], writes=[AR("tmpv2")])
                P.op("dve", lambda e: e.tensor_tensor(out=outa, in0=tmpv2, in1=zz[:, 0:384], op=ALU.mult), reads=[AR("tmpv2")] + zr, writes=[AR("outa")])
                tb = 6 + (tbank_ctr[0] % 2)
                tbank_ctr[0] += 1
                for c in range(3):
                    P.op("pe", lambda e, c=c: e.transpose(out=banks_bf[tb][:, c * 128:(c + 1) * 128], in_=outa[:, c * 128:(c + 1) * 128], identity=ident[:]),
                         reads=[AR("outa"), R("ident")], writes=[BK(tb)])
                P.op("act", lambda e: e.activation(out=gmT[:, :, tt * 128:(tt + 1) * 128], in_=banks_bf[tb][:, 0:384].rearrange("p (c t) -> p c t", t=128), func=AF.Copy),
                     reads=[BK(tb)], writes=[AR(f"gmT{tt}")])

            fm = [(1, 256, "q", 0), (1, 384, "q", 1), (2, 0, "q", 2),
                  (2, 128, "k", 0), (2, 256, "k", 1), (2, 384, "k", 2),
                  (4, 0, "m", 0), (4, 128, "m", 1)]

            def fm_chunk(bi, lc, kind, c):
                b = next_bank()
                for kc in range(KC):
                    P.op("pe", lambda e, kc=kc: e.matmul(banks[b][:, :], lhsT=blk[bi][1][:, kc, lc:lc + 128], rhs=hT[:, kc, :],
                                                          start=(kc == 0), stop=(kc == KC - 1)),
                         reads=hres + [R(f"slot{blk[bi][0]}")], writes=[BK(b)])
                if kind == "q":
                    P.op("act", lambda e: e.activation(out=QT[:, c, :], in_=banks[b][:, :], func=AF.Copy, scale=0.125), reads=[BK(b)], writes=[AR(f"QT{c}")])
                elif kind == "m":
                    P.op("act", lambda e: e.activation(out=QMT[:, c, :], in_=banks[b][:, :], func=AF.Copy, scale=0.125), reads=[BK(b)], writes=[AR(f"QMT{c}")])
                else:
                    P.op("dve", lambda e: e.tensor_copy(out=KT[l][:, c, g * T:(g + 1) * T], in_=banks[b][:, :]), reads=[BK(b)], writes=[R(f"KT{l}_{c}")])

            def c_tile(tt):
                b = next_bank()
                for kc in range(KC):
                    P.op("pe", lambda e, kc=kc: e.matmul(banks[b][:, 0:390], lhsT=hT[:, kc, tt * 128:(tt + 1) * 128], rhs=blk[3][1][:, kc, :],
                                                          start=(kc == 0), stop=(kc == KC - 1)),
                         reads=[hres[tt], R(f"slot{blk[3][0]}")], writes=[BK(b)])
                P.op("act", lambda e: e.activation(out=VP[l][:, 4 * g + tt, :, 0:64], in_=banks[b][:, 0:384].rearrange("p (h d) -> p h d", d=64), func=AF.Copy),
                     reads=[BK(b)], writes=[R(f"VP{l}")])
                P.op("dve", lambda e: e.tensor_tensor(out=fg[:, tt, :], in0=banks[b][:, 384:390], in1=BFG[l][:], op=ALU.add),
                     reads=[BK(b), R(f"BFG{l}")], writes=[AR("fg")])

            a_mm(0)
            a_mm(1)
            for q in fm[0:4]:
                fm_chunk(*q)
            g_chain(0)
            a_mm(2)
            for q in fm[4:8]:
                fm_chunk(*q)
            g_chain(1)
            a_mm(3)
            load_blk(3)
            g_chain(2)
            for tt in range(4):
                c_tile(tt)
            g_chain(3)
            P.op("act", lambda e: e.activation(out=spb, in_=fg, func=AF.Exp, scale=-1.0), reads=[AR("fg")], writes=[AR("spb")])
            P.op("act", lambda e: e.activation(out=spb, in_=spb, func=AF.Ln, bias=1.0), reads=[AR("spb")], writes=[AR("spb")])
            for tt in range(4):
                ti = 4 * g + tt
                b = next_bank()
                P.op("pe", lambda e, b=b, tt=tt: e.matmul(banks[b][:, 0:6], lhsT=triuf[:], rhs=spb[:, tt, :], start=True, stop=True),
                     reads=[AR("spb"), R("triuf")], writes=[BK(b)])
                P.op("pe", lambda e, b=b, tt=tt: e.matmul(banks[b][:, 8:14], lhsT=onesf[:], rhs=spb[:, tt, :], start=True, stop=True),
                     reads=[AR("spb"), R("onesf")], writes=[BK(b)])
                cr = R(f"CE{l}")
                if ti == 0:
                    P.op("dve", lambda e, b=b, ti=ti: e.tensor_copy(out=CALL[l][:, ti, :], in_=banks[b][:, 0:6]), reads=[BK(b)], writes=[cr])
                    P.op("dve", lambda e, b=b, ti=ti: e.tensor_copy(out=EALL[l][:, ti, :], in_=banks[b][:, 8:14]), reads=[BK(b)], writes=[cr])
                else:
                    P.op("dve", lambda e, b=b, ti=ti: e.tensor_tensor(out=CALL[l][:, ti, :], in0=banks[b][:, 0:6], in1=EALL[l][:, ti - 1, :], op=ALU.add),
                         reads=[BK(b), cr], writes=[cr])
                    P.op("dve", lambda e, b=b, ti=ti: e.tensor_tensor(out=EALL[l][:, ti, :], in0=banks[b][:, 8:14], in1=EALL[l][:, ti - 1, :], op=ALU.add),
                         reads=[BK(b), cr], writes=[cr])

            if dbg in ("win", "gm", "gm2"):
                raise StopBuild()
            nj = 4 * g + 4
            sb_rr = [0]
            pt_rr = [0]
            ktres = [R(f"KT{l}_{c}") for c in range(3)]

            def normalize(ob, hidx):
                P.op("dve", lambda e: e.reciprocal(out=rs[64:65, :], in_=banks[ob][64:65, :]), reads=[BK(ob)], writes=[AR("rs")])
                P.op("pe", lambda e: e.matmul(banks[5][0:64, :], lhsT=onesf[64:65, 0:64], rhs=rs[64:65, :], start=True, stop=True),
                     reads=[AR("rs"), R("onesf")], writes=[BK(5)])
                P.op("act", lambda e: e.activation(out=bcsb[0:64, :], in_=banks[5][0:64, :], func=AF.Copy), reads=[BK(5)], writes=[AR("bcsb")])
                P.op("dve", lambda e: e.tensor_tensor(out=attT[0:64, hidx, :], in0=banks[ob][0:64, :], in1=bcsb[0:64, :], op=ALU.mult),
                     reads=[BK(ob), AR("bcsb")], writes=[AR(f"attT{hidx}")])

            LOOK = 2
            pend = []
            deferred = []

            def tick():
                for d in deferred:
                    d[0] -= 1
                while deferred and deferred[0][0] <= 0:
                    deferred.pop(0)[1]()

            def norm_part1(ob):
                P.op("dve", lambda e: e.reciprocal(out=rs[64:65, :], in_=banks[ob][64:65, :]), reads=[BK(ob)], writes=[AR("rs")])

            def norm_part2(ob, hidx):
                P.op("pe", lambda e: e.matmul(banks[5][0:64, :], lhsT=onesf[64:65, 0:64], rhs=rs[64:65, :], start=True, stop=True),
                     reads=[AR("rs"), R("onesf")], writes=[BK(5)])
                P.op("act", lambda e: e.activation(out=bcsb[0:64, :], in_=banks[5][0:64, :], func=AF.Copy), reads=[BK(5)], writes=[AR("bcsb")])
                P.op("dve", lambda e: e.tensor_tensor(out=attT[0:64, hidx, :], in0=banks[ob][0:64, :], in1=bcsb[0:64, :], op=ALU.mult),
                     reads=[BK(ob), AR("bcsb")], writes=[AR(f"attT{hidx}")])

            def emit_pv(blk_):
                (kind, hidx, j, col0, pk, ob, first, last) = blk_
                if kind == "f":
                    P.op("pe", lambda e: e.matmul(banks[ob][0:65, col0:T], lhsT=VP[l][:, j, hidx, :], rhs=PT[pk][:, col0:T], start=first, stop=last),
                         reads=[R(f"VP{l}"), R(f"VPones{l}"), AR(f"PT{pk}")], writes=[BK(ob)])
                else:
                    P.op("pe", lambda e: e.matmul(banks[ob][0:65, :], lhsT=VMP[l][:, j, hidx - 6, :], rhs=PT[pk][:, :], start=first, stop=last),
                         reads=[R(f"VMP{l}"), R(f"VMPones{l}"), AR(f"PT{pk}")], writes=[BK(ob)])
                if last:
                    while deferred:
                        deferred.pop(0)[1]()
                    norm_part1(ob)
                    deferred.append([4, lambda ob=ob, hidx=hidx: norm_part2(ob, hidx)])

            def push(blk_):
                pend.append(blk_)
                if len(pend) > LOOK:
                    emit_pv(pend.pop(0))
                tick()

            for h in range(6):
                p, r0 = h // 2, 64 * (h % 2)
                par = h % 2
                br = AR(f"beta{par}")
                for il in range(4):
                    P.op("dve", lambda e, il=il, par=par, h=h: e.tensor_scalar(out=beta[:, par, il, 0:nj], in0=CALL[l][:, 0:nj, h],
                                                                                scalar1=EALL[l][:, 4 * g + il, h:h + 1], scalar2=None, op0=ALU.subtract),
                         reads=[R(f"CE{l}")], writes=[br])
                ob = 3 + (h % 2)
                for j in range(nj):
                    il0 = max(0, j - 4 * g)
                    col0 = il0 * 128
                    sbk = sb_rr[0] % 3
                    sb_rr[0] += 1
                    pk = pt_rr[0] % NPT
                    pt_rr[0] += 1
                    P.op("pe", lambda e, j=j, col0=col0, sbk=sbk, p=p, r0=r0: e.matmul(banks[sbk][:, col0:T], lhsT=KT[l][r0:r0 + 64, p, j * 128:(j + 1) * 128],
                                                                                        rhs=QT[r0:r0 + 64, p, col0:T], start=True, stop=True),
                         reads=[ktres[p], AR(f"QT{p}")], writes=[BK(sbk)])
                    for il in range(il0, 4):
                        P.op("act", lambda e, il=il, j=j, sbk=sbk, pk=pk, par=par: e.activation(out=PT[pk][:, il * 128:(il + 1) * 128], in_=banks[sbk][:, il * 128:(il + 1) * 128],
                                                                                                  func=AF.Exp, bias=beta[:, par, il, j:j + 1]),
                             reads=[BK(sbk), br], writes=[AR(f"PT{pk}")])
                    if j >= 4 * g:
                        P.op("dve", lambda e, il0=il0, pk=pk: e.tensor_tensor(out=PT[pk][:, il0 * 128:(il0 + 1) * 128], in0=PT[pk][:, il0 * 128:(il0 + 1) * 128],
                                                                                in1=triu[:], op=ALU.mult),
                             reads=[AR(f"PT{pk}"), R("triu")], writes=[AR(f"PT{pk}")])
                    push(("f", h, j, col0, pk, ob, j == 0, j == nj - 1))
            for hm in range(4):
                p, r0 = hm // 2, 64 * (hm % 2)
                ob = 3 + (hm % 2)
                for jm in range(2):
                    sbk = sb_rr[0] % 3
                    sb_rr[0] += 1
                    pk = pt_rr[0] % NPT
                    pt_rr[0] += 1
                    P.op("pe", lambda e, jm=jm, sbk=sbk, p=p, r0=r0: e.matmul(banks[sbk][:, :], lhsT=KMT[l][r0:r0 + 64, p, jm * 128:(jm + 1) * 128],
                                                                               rhs=QMT[r0:r0 + 64, p, :], start=True, stop=True),
                         reads=[R(f"KMT{l}"), AR(f"QMT{p}")], writes=[BK(sbk)])
                    P.op("act", lambda e, sbk=sbk, pk=pk: e.activation(out=PT[pk][:, :], in_=banks[sbk][:, :], func=AF.Exp),
                         reads=[BK(sbk)], writes=[AR(f"PT{pk}")])
                    push(("m", 6 + hm, jm, 0, pk, ob, jm == 0, jm == 1))
            while pend:
                emit_pv(pend.pop(0))
                tick()
            while deferred:
                deferred.pop(0)[1]()
            if dbg == "att":
                raise StopBuild()
            sg = load_slot(lambda s: s[:, 0:3 * D].rearrange("p (c n) -> p c n", n=D), wout_d[l][0:384, :].rearrange("(c p) n -> p c n", p=128))
            wo_g = slots[sg][:, 0:3 * D].rearrange("p (c n) -> p c n", n=D)
            wo_h = []
            for (h0, nh) in ((0, 4), (4, 4), (8, 2)):
                si = load_slot(lambda s, nh=nh: s[0:64, 0:nh * D].rearrange("p (c n) -> p c n", n=D),
                               wout_d[l][384 + 64 * h0:384 + 64 * (h0 + nh), :].rearrange("(c p) n -> p c n", p=64))
                v = slots[si][0:64, 0:nh * D].rearrange("p (c n) -> p c n", n=D)
                for k in range(nh):
                    wo_h.append((si, v, k))
            attres = [AR(f"attT{i}") for i in range(10)]
            for tt in range(4):
                yb = [next_bank(0, 4), next_bank(0, 4)]
                for half in range(2):
                    b = yb[half]
                    for c in range(3):
                        P.op("pe", lambda e, c=c, tt=tt, half=half, b=b: e.matmul(banks[b][:, :], lhsT=gmT[:, c, tt * 128:(tt + 1) * 128],
                                                                                   rhs=wo_g[:, c, half * 512:(half + 1) * 512], start=(c == 0), stop=False),
                             reads=[AR(f"gmT{tt}"), R(f"slot{sg}")], writes=[BK(b)])
                    for hh in range(10):
                        si, v, k = wo_h[hh]
                        P.op("pe", lambda e, hh=hh, tt=tt, half=half, b=b, v=v, k=k: e.matmul(banks[b][:, :], lhsT=attT[0:64, hh, tt * 128:(tt + 1) * 128],
                                                                                               rhs=v[:, k, half * 512:(half + 1) * 512], start=False, stop=(hh == 9)),
                             reads=[attres[hh], R(f"slot{si}")], writes=[BK(b)])
                post_norm(yb, tt, gpm, R("gpm"), xres[tt])

        def post_norm(yb, tt, gbuf, gres, xr):
            c0 = 48 + tt * 4
            sr = R(f"statp{tt}")
            for half in range(2):
                P.op("act", lambda e, half=half: e.activation(out=ytmp_bf[:, half * 512:(half + 1) * 512], in_=banks[yb[half]][:, :], func=AF.Square,
                                                               accum_out=stat[:, c0 + half:c0 + half + 1]),
                     reads=[BK(yb[half])], writes=[sr, R("ytmp")])
            P.op("dve", lambda e: e.tensor_tensor(out=stat[:, c0 + 2:c0 + 3], in0=stat[:, c0:c0 + 1], in1=stat[:, c0 + 1:c0 + 2], op=ALU.add), reads=[sr], writes=[sr])
            rstd_from_ss(stat[:, c0 + 2:c0 + 3], stat[:, c0 + 3:c0 + 4], D, [sr], [sr])
            for half in range(2):
                P.op("dve", lambda e, half=half: e.scalar_tensor_tensor(out=ytmp, in0=banks[yb[half]][:, :], scalar=stat[:, c0 + 3:c0 + 4],
                                                                         in1=gbuf[:, half * 512:(half + 1) * 512], op0=ALU.mult, op1=ALU.mult),
                     reads=[BK(yb[half]), sr, gres], writes=[R("ytmp")])
                P.op("dve", lambda e, half=half: e.tensor_tensor(out=xg[:, tt, half * 512:(half + 1) * 512], in0=xg[:, tt, half * 512:(half + 1) * 512], in1=ytmp, op=ALU.add),
                     reads=[R("ytmp"), xr], writes=[xr])

        def ffn(g, l, last):
            xres = [R(f"xg{tt}") for tt in range(4)]
            hres = [R(f"hT{tt}") for tt in range(4)]
            P.dma("sp", lambda e: e.dma_start(out=gpf[:], in_=bcast_rows(g_postffn_t, l * D, D)), d_gpf, writes=[R("gpm")])
            norm_transpose_n([(xg[:, tt, :], xres[tt]) for tt in range(4)], GCOL[l][:, 1, :], R(f"GCOL{l}"),
                             [(hT[:, :, tt * 128:(tt + 1) * 128], hres[tt]) for tt in range(4)])
            P.fence(AG)
            for blk_i in range(8):
                si = load_slot(lambda s: s[:, :].rearrange("p (k n) -> p k n", n=512),
                               w1_d[l][:, blk_i * 512:(blk_i + 1) * 512].rearrange("(k p) n -> p k n", p=128))
                wb = slots[si][:, :].rearrange("p (k n) -> p k n", n=512)
                for fcl in range(4):
                    fc = blk_i * 4 + fcl
                    b = next_bank()
                    for kc in range(KC):
                        P.op("pe", lambda e, kc=kc, fcl=fcl, b=b, wb=wb: e.matmul(banks[b][:, :], lhsT=wb[:, kc, fcl * 128:(fcl + 1) * 128], rhs=hT[:, kc, :],
                                                                                   start=(kc == 0), stop=(kc == KC - 1)),
                             reads=hres + [R(f"slot{si}")], writes=[BK(b)])
                    rk = 0
                    P.op("act", lambda e, b=b, rk=rk: e.activation(out=rtmp[rk], in_=banks[b][:, :], func=AF.Relu), reads=[BK(b)], writes=[AR(f"rtmp{rk}")])
                    P.op("dve", lambda e, fc=fc, rk=rk: e.tensor_tensor(out=hidT[:, fc, :], in0=rtmp[rk], in1=rtmp[rk], op=ALU.mult),
                         reads=[AR(f"rtmp{rk}")], writes=[AR(f"hidT{fc}")])
            if dbg == "ffn1":
                raise StopBuild()
            for blk_i in range(8):
                si = load_slot(lambda s: s[:, :].rearrange("p (c n) -> p c n", n=D),
                               w2_d[l][blk_i * 512:(blk_i + 1) * 512, :].rearrange("(c p) n -> p c n", p=128))
                wb = slots[si][:, :].rearrange("p (c n) -> p c n", n=D)
                for fcl in range(4):
                    fc = blk_i * 4 + fcl
                    for tt in range(4):
                        for half in range(2):
                            b = tt * 2 + half
                            P.op("pe", lambda e, fc=fc, fcl=fcl, tt=tt, half=half, b=b, wb=wb: e.matmul(banks[b][:, :], lhsT=hidT[:, fc, tt * 128:(tt + 1) * 128],
                                                                                                         rhs=wb[:, fcl, half * 512:(half + 1) * 512],
                                                                                                         start=(fc == 0), stop=(fc == 31)),
                                 reads=[AR(f"hidT{fc}"), R(f"slot{si}")], writes=[BK(b)])
            for tt in range(4):
                post_norm([tt * 2, tt * 2 + 1], tt, gpf, R("gpm"), xres[tt])
                if last:
                    r0 = (4 * g + tt) * 128
                    P.dma("sp", lambda e, tt=tt, r0=r0: e.dma_start(out=out_d[r0:r0 + 128, :], in_=xg[:, tt, :]), d_o[tt],
                          reads=[xres[tt]], writes=[R(f"outd{tt}")])
            P.fence(AG)

        try:
            for l in range(L):
                layer_setup(l)
            P.fence(AG)
            if dbg == "setup":
                raise StopBuild()
            for g in range(NG):
                for tt in range(4):
                    r0 = (4 * g + tt) * 128
                    P.dma("sp", lambda e, tt=tt, r0=r0: e.dma_start(out=xg[:, tt, :], in_=x_d[r0:r0 + 128, :]), d_x[tt], writes=[R(f"xg{tt}")])
                for l in range(L):
                    mixer(g, l)
                    if dbg == "mix" or dbg == f"mix:{g}:{l}":
                        raise StopBuild()
                    ffn(g, l, last=(l == L - 1))
                    if dbg == f"ffn:{g}:{l}":
                        raise StopBuild()
        except StopBuild:
            if dbg == "win":
                P.op("dve", lambda e: e.tensor_copy(out=xg[:, 0, :].rearrange("p (c t) -> p c t", t=T), in_=QMT[:, :, :]),
                     reads=[AR("QMT0"), AR("QMT1"), R("xg0")], writes=[R("xg0")])
                P.op("dve", lambda e: e.tensor_copy(out=xg[:, 1, 0:512].rearrange("p (c t) -> p c t", t=256), in_=KMT[0][:, :, :]),
                     reads=[R("KMT0"), R("xg1")], writes=[R("xg1")])
                P.op("dve", lambda e: e.tensor_copy(out=xg[:, 2, 0:520].rearrange("p (c t) -> p c t", t=260), in_=VMP[0][:, :, :, :].rearrange("p a b c -> p a (b c)")),
                     reads=[R("VMP0"), R("VMPones0"), R("xg2")], writes=[R("xg2")])
            if dbg == "gm2":
                ar = [AR("zA"), AR("vn"), AR("tmpv2"), AR("outa"), R("WST0")] + [R(f"xg{i}") for i in range(4)]
                wr = [R(f"xg{i}") for i in range(4)]
                P.op("dve", lambda e: e.tensor_copy(out=xg[:, 0, 0:768], in_=zA), reads=ar, writes=wr)
                P.op("dve", lambda e: e.tensor_copy(out=xg[:, 1, 0:384], in_=vn), reads=ar, writes=wr)
                P.op("dve", lambda e: e.tensor_copy(out=xg[:, 1, 384:768], in_=tmpv2), reads=ar, writes=wr)
                P.op("dve", lambda e: e.tensor_copy(out=xg[:, 2, 0:384], in_=outa), reads=ar, writes=wr)
                P.op("dve", lambda e: e.tensor_copy(out=xg[:, 3, 0:768], in_=WST[0][:, :, :].rearrange("p g t -> p (g t)")), reads=ar, writes=wr)
            if dbg == "gm":
                P.op("dve", lambda e: e.tensor_copy(out=xg[:, 0, :].rearrange("p (c t) -> p c t", t=T), in_=gmT[:, 0:2, :]),
                     reads=[AR(f"gmT{i}") for i in range(4)] + [R("xg0")], writes=[R("xg0")])
                P.op("dve", lambda e: e.tensor_copy(out=xg[:, 1, 0:512], in_=gmT[:, 2, :]),
                     reads=[AR(f"gmT{i}") for i in range(4)] + [R("xg1")], writes=[R("xg1")])
            if dbg in ("att3", "att", "att5", "att4"):
                for q in range(4):
                    P.op("dve", lambda e, q=q: e.tensor_copy(out=xg[:, q, :].rearrange("p (c t) -> p c t", t=T), in_=attT[:, 2 * q:2 * q + 2, :]),
                         reads=[AR(f"attT{i}") for i in range(10)] + [R(f"xg{q}")], writes=[R(f"xg{q}")])
            for tt in range(4):
                P.dma("sp", lambda e, tt=tt: e.dma_start(out=out_d[tt * 128:(tt + 1) * 128, :], in_=xg[:, tt, :]), d_o[tt],
                      reads=[R(f"xg{tt}")], writes=[R(f"outd{tt}")])
        P.final()
        if build.want_trace:
            P.trace = []
        P.emit(st)
        build.stats = P.stats
        build.trace = P.trace
    return nc


build.want_trace = False

WNAMES = ["norm_pre_mix", "norm_post_mix", "norm_pre_ffn", "norm_post_ffn", "norm_mem", "w_in", "b_forget",
          "gmlp_v_norm", "gmlp_w_s", "gmlp_b_s", "w_mem_kv", "w_out", "w_ff1", "w_ff2"]

_cache = {}


def _consts():
    iu = np.triu(np.ones((128, 128), np.float32))
    return {
        "c_ident": np.eye(128, dtype=np.float32).astype(ml_dtypes.bfloat16),
        "c_triu": iu.astype(ml_dtypes.bfloat16),
        "c_triuf": iu.copy(),
        "c_onesf": np.ones((128, 128), np.float32),
    }


DBG = None


def run_layers(x, mem, weights, L):
    B, S, _ = x.shape
    key = (S, L)
    if key not in _cache:
        _cache[key] = build(S, L, dbg=DBG)
    nc = _cache[key]
    consts = _consts()
    in_maps = []
    for b in range(B):
        m = {"x": np.ascontiguousarray(x[b]), "mem": np.ascontiguousarray(mem[b])}
        for k in WNAMES:
            m[k] = np.ascontiguousarray(weights[k])
        m.update(consts)
        in_maps.append(m)
    res = run_bass_kernel_spmd(nc, in_maps, core_ids=list(range(B)))
    return np.stack([res.results[b]["out"] for b in range(B)], axis=0)


FUSED = True


def kernel(**inputs):
    x = np.asarray(inputs["x"], dtype=np.float32)
    mem = np.asarray(inputs["mem"], dtype=np.float32)
    W = {k: np.asarray(inputs[k], dtype=np.float32) for k in WNAMES}
    depth = W["w_in"].shape[0]
    if FUSED:
        return run_layers(x, mem, W, depth)
    for l in range(depth):
        x = run_layers(x, mem, {k: v[l:l + 1] for k, v in W.items()}, 1)
    return x
```

```python
from contextlib import ExitStack
import numpy as np
import ml_dtypes
import concourse.bass as bass
import concourse.mybir as mybir
from concourse.bass_utils import run_bass_kernel_spmd

F32 = mybir.dt.float32
BF16 = mybir.dt.bfloat16
AF = mybir.ActivationFunctionType
ALU = mybir.AluOpType
AX = mybir.AxisListType

COMPUTE = ("pe", "act", "dve", "pool")
EPS = 1e-6


class Res:
    __slots__ = ("name", "lw", "rd", "group", "excl")

    def __init__(self, name, group=None):
        self.name = name
        self.lw = None
        self.rd = []
        self.group = group
        self.excl = name.startswith("bank")


class Group:
    def __init__(self):
        self.since = []
        self.fdeps = []


class DmaSem:
    __slots__ = ("name", "total", "sem")

    def __init__(self, name):
        self.name = name
        self.total = 0
        self.sem = None


class Op:
    __slots__ = ("eng", "fn", "deps", "idx", "dma", "dma_total", "needs_inc", "cnt", "tag")

    def __init__(self, eng, fn):
        self.eng = eng
        self.fn = fn
        self.deps = []
        self.dma = None
        self.dma_total = 0
        self.needs_inc = False
        self.cnt = 0


class Prog:
    def __init__(self, nc):
        self.nc = nc
        self.ops = []
        self.dmasems = []
        self.resd = {}
        self.trace = None

    def R(self, name, group=None):
        r = self.resd.get(name)
        if r is None:
            r = Res(name, group)
            self.resd[name] = r
        return r

    def dmasem(self, name):
        d = DmaSem(name)
        self.dmasems.append(d)
        return d

    def fence(self, group):
        last = {}
        keep = []
        for o in group.since + group.fdeps:
            if o.dma is not None:
                keep.append(o)
            else:
                if o.eng not in last or last[o.eng].idx < o.idx:
                    last[o.eng] = o
        group.fdeps = keep + list(last.values())
        group.since = []

    def _track(self, op, reads, writes):
        reads = list(reads)
        writes = list(writes)
        for r in list(reads):
            if r.excl:
                reads.remove(r)
                if r not in writes:
                    writes.append(r)
        op.tag = "R:" + ",".join(r.name for r in reads) + " W:" + ",".join(w.name for w in writes)
        deps = set()
        for r in reads:
            if r.lw is not None:
                deps.add(r.lw)
        for w in writes:
            if w.lw is not None:
                deps.add(w.lw)
            for o in w.rd:
                deps.add(o)
        groups = set()
        for r in list(reads) + list(writes):
            if r.group is not None:
                groups.add(r.group)
        for g in groups:
            for o in g.fdeps:
                deps.add(o)
            g.since.append(op)
        deps.discard(op)
        best = {}
        red = []
        for d in deps:
            if d.dma is not None:
                red.append(d)
            elif d.eng not in best or best[d.eng].idx < d.idx:
                best[d.eng] = d
        deps = red + list(best.values())
        for r in reads:
            r.rd.append(op)
        for w in writes:
            w.lw = op
            w.rd = []
        op.deps = list(deps)

    def op(self, eng, fn, reads=(), writes=()):
        o = Op(eng, fn)
        o.idx = len(self.ops)
        self.ops.append(o)
        self._track(o, reads, writes)
        return o

    def dma(self, queue, fn, sem, reads=(), writes=()):
        o = Op(queue, fn)
        o.idx = len(self.ops)
        o.dma = sem
        sem.total += 16
        o.dma_total = sem.total
        self.ops.append(o)
        self._track(o, reads, writes)
        return o

    def final(self):
        o = Op("sp", lambda e: e.nop())
        o.idx = len(self.ops)
        o.tag = "final"
        last = {}
        for p in self.ops:
            if p.dma is not None:
                last[("d", p.dma.name)] = p
            elif p.eng in COMPUTE:
                last[("e", p.eng)] = p
        o.deps = list(last.values())
        self.ops.append(o)

    def emit(self, stack):
        nc = self.nc
        for o in self.ops:
            for d in o.deps:
                if d.dma is None:
                    if d.eng == "pe" and o.eng == "pe" and o.dma is None:
                        continue
                    d.needs_inc = True
        sems = {}
        for e in COMPUTE:
            sems[e] = stack.enter_context(nc.semaphore("s_" + e))
        for d in self.dmasems:
            if d.total > 0:
                d.sem = stack.enter_context(nc.semaphore("d_" + d.name))
        cnt = {e: 0 for e in COMPUTE}
        for o in self.ops:
            if o.dma is None and o.needs_inc:
                cnt[o.eng] += 1
                o.cnt = cnt[o.eng]
        per_eng = {e: [] for e in ("pe", "act", "dve", "pool", "sp")}
        for o in self.ops:
            per_eng[o.eng].append(o)
        self.stats = {e: len(v) for e, v in per_eng.items()}
        self.stats["incs"] = dict(cnt)

        def run_engine(ename, eng):
            seen = {}
            nw = 0
            for o in per_eng[ename]:
                need = {}
                for d in o.deps:
                    if d.dma is not None:
                        key = ("d", d.dma.name)
                        val = d.dma_total
                        semh = d.dma.sem
                    else:
                        if d.eng == "pe" and ename == "pe" and o.dma is None:
                            continue
                        key = ("e", d.eng)
                        val = d.cnt
                        semh = sems[d.eng]
                    if seen.get(key, 0) >= val:
                        continue
                    if key not in need or need[key][1] < val:
                        need[key] = (semh, val)
                for key, (semh, val) in need.items():
                    eng.wait_ge(semh, val)
                    seen[key] = val
                    nw += 1
                if self.trace is not None:
                    self.trace.append((ename, o.idx, [(k, v[1]) for k, v in need.items()], o.cnt if o.needs_inc else None, o.dma_total if o.dma else None, o.tag))
                ins = o.fn(eng)
                if o.dma is not None:
                    ins.then_inc(o.dma.sem, 16)
                elif o.needs_inc:
                    ins.then_inc(sems[ename], 1)
            self.stats["waits_" + ename] = nw

        with nc.Block() as block:
            @block.tensor
            def _(e):
                run_engine("pe", e)

            @block.scalar
            def _(e):
                run_engine("act", e)

            @block.vector
            def _(e):
                run_engine("dve", e)

            @block.gpsimd
            def _(e):
                run_engine("pool", e)

            @block.sync
            def _(e):
                run_engine("sp", e)


D = 1024
KC = 8
DIN = 2182
DFF = 4096
NMEM = 256
T = 512
NSLOT = 4

WIN_BLOCKS = [(0, 512), (512, 512), (1024, 512), (1536, 390), (1926, 256)]


class StopBuild(Exception):
    pass


def build(S, L, dbg=None):
    NT = S // 128
    NG = S // T
    nc = bass.Bass("TRN2", target_bir_lowering=False)

    def din(name, shape, dt=F32):
        return nc.dram_tensor(name, shape, dt, kind="ExternalInput")

    x_t = din("x", [S, D])
    mem_t = din("mem", [NMEM, D])
    g_premix_t = din("norm_pre_mix", [L, D])
    g_postmix_t = din("norm_post_mix", [L, D])
    g_preffn_t = din("norm_pre_ffn", [L, D])
    g_postffn_t = din("norm_post_ffn", [L, D])
    g_mem_t = din("norm_mem", [L, D])
    w_in_t = din("w_in", [L, D, DIN])
    b_forget_t = din("b_forget", [L, 6])
    gv_t = din("gmlp_v_norm", [L, 384])
    ws_t = din("gmlp_w_s", [L, 6, 128, 128])
    bs_t = din("gmlp_b_s", [L, 6, 128])
    wkv_t = din("w_mem_kv", [L, D, 512])
    wout_t = din("w_out", [L, D, D])
    w1_t = din("w_ff1", [L, D, DFF])
    w2_t = din("w_ff2", [L, DFF, D])
    ident_t = din("c_ident", [128, 128], BF16)
    triu_t = din("c_triu", [128, 128], BF16)
    triuf_t = din("c_triuf", [128, 128], F32)
    onesf_t = din("c_onesf", [128, 128], F32)
    out_t = nc.dram_tensor("out", [S, D], F32, kind="ExternalOutput")

    x_d, mem_d, out_d = x_t.ap(), mem_t.ap(), out_t.ap()
    w_in_d, wkv_d, wout_d, w1_d, w2_d = w_in_t.ap(), wkv_t.ap(), wout_t.ap(), w1_t.ap(), w2_t.ap()
    ws_d = ws_t.ap()

    with ExitStack() as st:
        P = Prog(nc)
        R = P.R

        def sb(name, shape, dt):
            return st.enter_context(nc.sbuf_tensor(name, shape, dt))

        KT = [sb(f"KT{l}", [128, 3, S], BF16) for l in range(L)]
        VP = [sb(f"VP{l}", [128, NT, 6, 65], BF16) for l in range(L)]
        CALL = [sb(f"CALL{l}", [128, NT, 6], F32) for l in range(L)]
        EALL = [sb(f"EALL{l}", [128, NT, 6], F32) for l in range(L)]
        KMT = [sb(f"KMT{l}", [128, 2, NMEM], BF16) for l in range(L)]
        VMP = [sb(f"VMP{l}", [128, 2, 4, 65], BF16) for l in range(L)]
        WST = [sb(f"WST{l}", [128, 6, 128], BF16) for l in range(L)]
        GCOL = [sb(f"GCOL{l}", [128, 3, 8], F32) for l in range(L)]
        BS = [sb(f"BS{l}", [128, 6], F32) for l in range(L)]
        BFG = [sb(f"BFG{l}", [128, 6], F32) for l in range(L)]
        gpm = sb("gpm", [128, D], F32)
        gpf = gpm
        xg = sb("xg", [128, 4, D], F32)
        hT = sb("hT", [128, KC, T], BF16)
        xnb = [sb("xnb0", [128, D], BF16)]
        stat = sb("stat", [128, 64], F32)
        ident = sb("ident", [128, 128], BF16)
        triu = sb("triu", [128, 128], BF16)
        triuf = sb("triuf", [128, 128], F32)
        onesf = sb("onesf", [128, 128], F32)
        slots = [sb(f"slot{i}", [128, 4096], BF16) for i in range(NSLOT)]
        ARENA_BF = 18432
        arena = sb("arena", [128, ARENA_BF], BF16)
        AG = Group()

        class Carver:
            def __init__(self):
                self.off = 0

            def take(self, nbf):
                o = self.off
                self.off += nbf
                assert self.off <= ARENA_BF, self.off
                return o

        cv = Carver()

        def a_bf(n):
            o = cv.take(n)
            return arena[:, o:o + n]

        def a_f32(n):
            o = cv.take(2 * n)
            return arena[:, o:o + 2 * n].bitcast(F32)

        QT = a_bf(3 * T).rearrange("p (c t) -> p c t", t=T)
        QMT = a_bf(2 * T).rearrange("p (c t) -> p c t", t=T)
        gmT = a_bf(3 * T).rearrange("p (c t) -> p c t", t=T)
        attT = a_bf(10 * T).rearrange("p (c t) -> p c t", t=T)
        NPT = 3
        PT_OFF = cv.off
        PT = [a_bf(T) for _ in range(NPT)]
        zA = a_f32(768)
        tmpv2 = a_f32(384)
        vn = a_bf(384)
        outa = a_bf(384)
        beta = a_f32(2 * 32).rearrange("p (a c) -> p a c", a=2)
        daug = a_bf(2 * T).rearrange("p (a t) -> p a t", a=2)
        rs = a_f32(T)
        tmpv = rs[:, 0:384]
        bcsb = a_f32(T)
        GVS = bcsb[:, 0:384]
        fg = a_f32(24).rearrange("p (a b) -> p a b", b=6)
        spb = a_f32(24).rearrange("p (a b) -> p a b", b=6)
        att_end = cv.off
        cv.off = 0
        hidT = a_bf(32 * T).rearrange("p (c t) -> p c t", t=T)
        rtmp = [a_f32(T)]
        cv.off = max(cv.off, att_end)
        _yo = cv.take(2 * T)
        ytmp_bf = arena[:, _yo:_yo + 2 * T]
        ytmp = ytmp_bf.bitcast(F32)

        def AR(name):
            return R(name, AG)

        banks = [st.enter_context(nc.psum_tensor(f"bank{i}", [128, 512], F32)) for i in range(8)]
        banks_bf = [b[:, :].bitcast(BF16) for b in banks]

        def BK(i):
            return R(f"bank{i}")

        d_setup = P.dmasem("setup")
        d_x = [P.dmasem(f"x{i}") for i in range(4)]
        d_o = [P.dmasem(f"o{i}") for i in range(4)]
        d_slot = [P.dmasem(f"sl{i}") for i in range(NSLOT)]
        d_ws = P.dmasem("ws")
        d_gpm = P.dmasem("gpm")
        d_gpf = d_gpm
        d_gv = P.dmasem("gv")

        slot_ctr = [0]

        def load_slot(dst_fn, src, reads=()):
            i = slot_ctr[0] % NSLOT
            slot_ctr[0] += 1
            dst = dst_fn(slots[i])
            P.dma("pool", lambda e, dst=dst, src=src: e.dma_start(out=dst, in_=src), d_slot[i],
                  reads=list(reads), writes=[R(f"slot{i}")])
            return i

        def bcast_rows(t, row_off, n):
            return bass.AP(t, row_off, [[0, 128], [1, n]])

        setup_res = []

        def setup_dma(dst, src, resname, **kw):
            P.dma("sp", lambda e, dst=dst, src=src, kw=kw: e.dma_start(out=dst, in_=src, **kw), d_setup, writes=[R(resname)])
            setup_res.append(R(resname))

        setup_dma(ident[:], ident_t.ap(), "ident")
        setup_dma(triu[:], triu_t.ap(), "triu")
        setup_dma(triuf[:], triuf_t.ap(), "triuf")
        setup_dma(onesf[:], onesf_t.ap(), "onesf")
        for l in range(L):
            for k, gt in enumerate((g_premix_t, g_preffn_t, g_mem_t)):
                setup_dma(GCOL[l][:, k, :], bass.AP(gt, l * D, [[1, 128], [128, 8]]), f"GCOL{l}", allow_slow_non_contiguous=True)
            setup_dma(BFG[l][:], bcast_rows(b_forget_t, l * 6, 6), f"BFG{l}")
            setup_dma(BS[l][:], bass.AP(bs_t, l * 768, [[1, 128], [128, 6]]), f"BS{l}", allow_slow_non_contiguous=True)
        last_setup = P.ops[-1]
        for r in setup_res:
            r.lw = last_setup
            r.rd = []
        P.op("dve", lambda e: e.memset(stat[:], 1.0e6), writes=[R("statn")] + [R(f"statv{i}") for i in range(4)] + [R(f"statp{i}") for i in range(4)])
        for l in range(L):
            P.op("dve", lambda e, l=l: e.memset(VP[l][:, :, :, 64:65], 1.0), writes=[R(f"VPones{l}")])
            P.op("dve", lambda e, l=l: e.memset(VMP[l][:, :, :, 64:65], 1.0), writes=[R(f"VMPones{l}")])

        def rstd_from_ss(ss_ap, out_ap, n, rd, wr):
            P.op("act", lambda e: e.activation(out=out_ap, in_=ss_ap, func=AF.Ln, scale=1.0 / n, bias=EPS), reads=rd, writes=wr)
            P.op("act", lambda e: e.activation(out=out_ap, in_=out_ap, func=AF.Exp, scale=-0.5), reads=wr, writes=wr)

        tbank_ctr = [0]

        def norm_transpose_n(srcs, gcol_ap, gres, dsts):
            n = len(srcs)
            ssr = R("statn")
            for i, (src_ap, src_res) in enumerate(srcs):
                P.op("act", lambda e, i=i, src_ap=src_ap: e.activation(out=xnb[0][:], in_=src_ap, func=AF.Square, accum_out=stat[:, i:i + 1]),
                     reads=[src_res], writes=[ssr] + ([R("xnb0")] if i == 0 else []))
            rstd_from_ss(stat[:, 0:n], stat[:, 4:4 + n], D, [ssr], [ssr])
            for i, ((src_ap, src_res), (dst_ap, dst_res)) in enumerate(zip(srcs, dsts)):
                P.op("dve", lambda e, i=i, src_ap=src_ap: e.tensor_scalar(out=xnb[0][:], in0=src_ap, scalar1=stat[:, 4 + i:5 + i], scalar2=None, op0=ALU.mult),
                     reads=[src_res, ssr], writes=[R("xnb0")])
                tb = 6 + (tbank_ctr[0] % 2)
                tbank_ctr[0] += 1
                for kc in range(KC):
                    P.op("pe", lambda e, kc=kc, tb=tb: e.transpose(out=banks_bf[tb][:, kc * 128:(kc + 1) * 128], in_=xnb[0][:, kc * 128:(kc + 1) * 128], identity=ident[:]),
                         reads=[R("xnb0"), R("ident")], writes=[BK(tb)])
                P.op("dve", lambda e, tb=tb, dst_ap=dst_ap: e.tensor_tensor(out=dst_ap, in0=banks_bf[tb][:, 0:1024].rearrange("p (k t) -> p k t", t=128),
                                                                             in1=gcol_ap.unsqueeze(2).to_broadcast([128, KC, 128]), op=ALU.mult),
                     reads=[BK(tb), gres], writes=[dst_res])

        bank_rr = [0]

        def next_bank(lo=0, hi=6):
            b = lo + (bank_rr[0] % (hi - lo))
            bank_rr[0] += 1
            return b

        def layer_setup(l):
            for mt in range(2):
                P.dma("sp", lambda e, mt=mt: e.dma_start(out=xg[:, mt, :], in_=mem_d[mt * 128:(mt + 1) * 128, :]), d_x[mt], writes=[R(f"xg{mt}")])
            norm_transpose_n([(xg[:, mt, :], R(f"xg{mt}")) for mt in range(2)], GCOL[l][:, 2, :], R(f"GCOL{l}"),
                             [(hT[:, :, mt * 128:(mt + 1) * 128], R(f"hT{mt}")) for mt in range(2)])
            si = load_slot(lambda s: s[:, :].rearrange("p (k n) -> p k n", n=512), wkv_d[l].rearrange("(k p) n -> p k n", p=128))
            wkv = slots[si][:, :].rearrange("p (k n) -> p k n", n=512)
            hres = [R("hT0"), R("hT1")]
            for pm in range(2):
                b = next_bank()
                for kc in range(KC):
                    P.op("pe", lambda e, kc=kc, pm=pm, b=b: e.matmul(banks[b][:, 0:NMEM], lhsT=wkv[:, kc, pm * 128:(pm + 1) * 128], rhs=hT[:, kc, 0:NMEM],
                                                                      start=(kc == 0), stop=(kc == KC - 1)),
                         reads=[R(f"slot{si}")] + hres, writes=[BK(b)])
                P.op("act", lambda e, pm=pm, b=b: e.activation(out=KMT[l][:, pm, :], in_=banks[b][:, 0:NMEM], func=AF.Copy),
                     reads=[BK(b)], writes=[R(f"KMT{l}")])
            for mt in range(2):
                b = next_bank()
                for kc in range(KC):
                    P.op("pe", lambda e, kc=kc, mt=mt, b=b: e.matmul(banks[b][:, 0:256], lhsT=hT[:, kc, mt * 128:(mt + 1) * 128], rhs=wkv[:, kc, 256:512],
                                                                      start=(kc == 0), stop=(kc == KC - 1)),
                         reads=[R(f"slot{si}"), hres[mt]], writes=[BK(b)])
                P.op("act", lambda e, mt=mt, b=b: e.activation(out=VMP[l][:, mt, :, 0:64], in_=banks[b][:, 0:256].rearrange("p (h d) -> p h d", d=64), func=AF.Copy),
                     reads=[BK(b)], writes=[R(f"VMP{l}")])
            P.dma("sp", lambda e: e.dma_start(out=zA.rearrange("p (g s) -> p g s", s=128), in_=ws_d[l].rearrange("g t s -> t g s")), d_ws,
                  writes=[AR("zA")])
            wsb = xnb[0][:, 0:768]
            P.op("dve", lambda e: e.tensor_copy(out=wsb, in_=zA), reads=[AR("zA")], writes=[R("xnb0")])
            tb = 6 + (tbank_ctr[0] % 2)
            tbank_ctr[0] += 1
            for gg in range(6):
                P.op("pe", lambda e, gg=gg: e.transpose(out=banks_bf[tb][:, gg * 128:(gg + 1) * 128], in_=wsb[:, gg * 128:(gg + 1) * 128], identity=ident[:]),
                     reads=[R("xnb0"), R("ident")], writes=[BK(tb)])
            P.op("dve", lambda e: e.tensor_tensor(out=WST[l][:], in0=banks_bf[tb][:, 0:768].rearrange("p (g t) -> p g t", t=128),
                                                  in1=triu[:].unsqueeze(1).to_broadcast([128, 6, 128]), op=ALU.mult),
                 reads=[BK(tb), R("triu")], writes=[R(f"WST{l}")])

        def mixer(g, l):
            xres = [R(f"xg{tt}") for tt in range(4)]
            hres = [R(f"hT{tt}") for tt in range(4)]
            P.dma("sp", lambda e: e.dma_start(out=gpm[:], in_=bcast_rows(g_postmix_t, l * D, D)), d_gpm, writes=[R("gpm")])
            P.dma("sp", lambda e: e.dma_start(out=GVS, in_=bcast_rows(gv_t, l * 384, 384)), d_gv, writes=[AR("bcsb")])
            norm_transpose_n([(xg[:, tt, :], xres[tt]) for tt in range(4)], GCOL[l][:, 0, :], R(f"GCOL{l}"),
                             [(hT[:, :, tt * 128:(tt + 1) * 128], hres[tt]) for tt in range(4)])
            if dbg == "norm":
                raise StopBuild()
            blk = [None] * 5

            def load_blk(bi):
                c0, ncol = WIN_BLOCKS[bi]
                si = load_slot(lambda s, ncol=ncol: s[:, 0:KC * ncol].rearrange("p (k n) -> p k n", n=ncol),
                               w_in_d[l][:, c0:c0 + ncol].rearrange("(k p) n -> p k n", p=128))
                blk[bi] = (si, slots[si][:, 0:KC * ncol].rearrange("p (k n) -> p k n", n=ncol))

            load_blk(0)
            load_blk(1)
            load_blk(2)
            load_blk(4)
            zAb = [zA, arena[:, PT_OFF:PT_OFF + 1536].bitcast(F32)]
            zAr = [[AR("zA")], [AR("PT0"), AR("PT1"), AR("PT2")]]

            def a_mm(tt):
                zi = tt % 2
                ba, bb = next_bank(), next_bank()
                for kc in range(KC):
                    P.op("pe", lambda e, kc=kc: e.matmul(banks[ba][:, :], lhsT=hT[:, kc, tt * 128:(tt + 1) * 128], rhs=blk[0][1][:, kc, :],
                                                          start=(kc == 0), stop=(kc == KC - 1)),
                         reads=[hres[tt], R(f"slot{blk[0][0]}")], writes=[BK(ba)])
                for kc in range(KC):
                    P.op("pe", lambda e, kc=kc: e.matmul(banks[bb][:, 0:256], lhsT=hT[:, kc, tt * 128:(tt + 1) * 128], rhs=blk[1][1][:, kc, 0:256],
                                                          start=(kc == 0), stop=(kc == KC - 1)),
                         reads=[hres[tt], R(f"slot{blk[1][0]}")], writes=[BK(bb)])
                P.op("act", lambda e: e.activation(out=zAb[zi][:, 0:512], in_=banks[ba][:, :], func=AF.Gelu_apprx_tanh), reads=[BK(ba)], writes=zAr[zi])
                P.op("act", lambda e: e.activation(out=zAb[zi][:, 512:768], in_=banks[bb][:, 0:256], func=AF.Gelu_apprx_tanh), reads=[BK(bb)], writes=zAr[zi])

            def g_chain(tt):
                zi = tt % 2
                zz, zr = zAb[zi], zAr[zi]
                v3 = zz[:, 384:768].rearrange("p (g d) -> p g d", d=64)
                P.op("dve", lambda e: e.tensor_tensor(out=tmpv, in0=zz[:, 384:768], in1=zz[:, 384:768], op=ALU.mult), reads=zr, writes=[AR("rs")])
                c0 = 16 + tt * 8
                sr = R(f"statv{tt}")
                P.op("dve", lambda e: e.reduce_sum(out=stat[:, c0:c0 + 6], in_=tmpv.rearrange("p (g d) -> p g d", d=64), axis=AX.X),
                     reads=[AR("rs")], writes=[sr])
                rstd_from_ss(stat[:, c0:c0 + 6], stat[:, c0:c0 + 6], 64, [sr], [sr])
                P.op("dve", lambda e: e.tensor_tensor(out=tmpv.rearrange("p (g d) -> p g d", d=64), in0=v3,
                                                      in1=stat[:, c0:c0 + 6].unsqueeze(2).to_broadcast([128, 6, 64]), op=ALU.mult),
                     reads=zr + [sr], writes=[AR("rs")])
                P.op("dve", lambda e: e.tensor_tensor(out=vn, in0=tmpv, in1=GVS, op=ALU.mult), reads=[AR("rs"), AR("bcsb")], writes=[AR("vn")])
                bm = next_bank()
                for gg in range(6):
                    P.op("pe", lambda e, gg=gg: e.matmul(banks[bm][:, gg * 64:(gg + 1) * 64], lhsT=WST[l][:, gg, :], rhs=vn[:, gg * 64:(gg + 1) * 64],
                                                          start=True, stop=True),
                         reads=[AR("vn"), R(f"WST{l}")], writes=[BK(bm)])
                P.op("dve", lambda e: e.tensor_tensor(out=tmpv2.rearrange("p (g d) -> p g d", d=64), in0=banks[bm][:, 0:384].rearrange("p (g d) -> p g d", d=64),
                                                      in1=BS[l][:].unsqueeze(2).to_broadcast([128, 6, 64]), op=ALU.add),
                     reads=[BK(bm), R(f"BS{l}")], writes=[AR("tmpv2")])
                P.op("dve", lambda e: e.tensor_tensor(out=outa, in0=tmpv2, in1=zz[:, 0:384], op=ALU.mult), reads=[AR("tmpv2")] + zr, writes=[AR("outa")])
                tb = 6 + (tbank_ctr[0] % 2)
                tbank_ctr[0] += 1
                for c in range(3):
                    P.op("pe", lambda e, c=c: e.transpose(out=banks_bf[tb][:, c * 128:(c + 1) * 128], in_=outa[:, c * 128:(c + 1) * 128], identity=ident[:]),
                         reads=[AR("outa"), R("ident")], writes=[BK(tb)])
                P.op("act", lambda e: e.activation(out=gmT[:, :, tt * 128:(tt + 1) * 128], in_=banks_bf[tb][:, 0:384].rearrange("p (c t) -> p c t", t=128), func=AF.Copy),
                     reads=[BK(tb)], writes=[AR(f"gmT{tt}")])

            fm = [(1, 256, "q", 0), (1, 384, "q", 1), (2, 0, "q", 2),
                  (2, 128, "k", 0), (2, 256, "k", 1), (2, 384, "k", 2),
                  (4, 0, "m", 0), (4, 128, "m", 1)]

            def fm_chunk(bi, lc, kind, c):
                b = next_bank()
                for kc in range(KC):
                    P.op("pe", lambda e, kc=kc: e.matmul(banks[b][:, :], lhsT=blk[bi][1][:, kc, lc:lc + 128], rhs=hT[:, kc, :],
                                                          start=(kc == 0), stop=(kc == KC - 1)),
                         reads=hres + [R(f"slot{blk[bi][0]}")], writes=[BK(b)])
                if kind == "q":
                    P.op("act", lambda e: e.activation(out=QT[:, c, :], in_=banks[b][:, :], func=AF.Copy, scale=0.125), reads=[BK(b)], writes=[AR(f"QT{c}")])
                elif kind == "m":
                    P.op("act", lambda e: e.activation(out=QMT[:, c, :], in_=banks[b][:, :], func=AF.Copy, scale=0.125), reads=[BK(b)], writes=[AR(f"QMT{c}")])
                else:
                    P.op("dve", lambda e: e.tensor_copy(out=KT[l][:, c, g * T:(g + 1) * T], in_=banks[b][:, :]), reads=[BK(b)], writes=[R(f"KT{l}_{c}")])

            def c_tile(tt):
                b = next_bank()
                for kc in range(KC):
                    P.op("pe", lambda e, kc=kc: e.matmul(banks[b][:, 0:390], lhsT=hT[:, kc, tt * 128:(tt + 1) * 128], rhs=blk[3][1][:, kc, :],
                                                          start=(kc == 0), stop=(kc == KC - 1)),
                         reads=[hres[tt], R(f"slot{blk[3][0]}")], writes=[BK(b)])
                P.op("act", lambda e: e.activation(out=VP[l][:, 4 * g + tt, :, 0:64], in_=banks[b][:, 0:384].rearrange("p (h d) -> p h d", d=64), func=AF.Copy),
                     reads=[BK(b)], writes=[R(f"VP{l}")])
                P.op("dve", lambda e: e.tensor_tensor(out=fg[:, tt, :], in0=banks[b][:, 384:390], in1=BFG[l][:], op=ALU.add),
                     reads=[BK(b), R(f"BFG{l}")], writes=[AR("fg")])

            a_mm(0)
            a_mm(1)
            for q in fm[0:4]:
                fm_chunk(*q)
            g_chain(0)
            a_mm(2)
            for q in fm[4:8]:
                fm_chunk(*q)
            g_chain(1)
            a_mm(3)
            load_blk(3)
            g_chain(2)
            for tt in range(4):
                c_tile(tt)
            g_chain(3)
            P.op("act", lambda e: e.activation(out=spb, in_=fg, func=AF.Exp, scale=-1.0), reads=[AR("fg")], writes=[AR("spb")])
            P.op("act", lambda e: e.activation(out=spb, in_=spb, func=AF.Ln, bias=1.0), reads=[AR("spb")], writes=[AR("spb")])
            for tt in range(4):
                ti = 4 * g + tt
                b = next_bank()
                P.op("pe", lambda e, b=b, tt=tt: e.matmul(banks[b][:, 0:6], lhsT=triuf[:], rhs=spb[:, tt, :], start=True, stop=True),
                     reads=[AR("spb"), R("triuf")], writes=[BK(b)])
                P.op("pe", lambda e, b=b, tt=tt: e.matmul(banks[b][:, 8:14], lhsT=onesf[:], rhs=spb[:, tt, :], start=True, stop=True),
                     reads=[AR("spb"), R("onesf")], writes=[BK(b)])
                cr = R(f"CE{l}")
                if ti == 0:
                    P.op("dve", lambda e, b=b, ti=ti: e.tensor_copy(out=CALL[l][:, ti, :], in_=banks[b][:, 0:6]), reads=[BK(b)], writes=[cr])
                    P.op("dve", lambda e, b=b, ti=ti: e.tensor_copy(out=EALL[l][:, ti, :], in_=banks[b][:, 8:14]), reads=[BK(b)], writes=[cr])
                else:
                    P.op("dve", lambda e, b=b, ti=ti: e.tensor_tensor(out=CALL[l][:, ti, :], in0=banks[b][:, 0:6], in1=EALL[l][:, ti - 1, :], op=ALU.add),
                         reads=[BK(b), cr], writes=[cr])
                    P.op("dve", lambda e, b=b, ti=ti: e.tensor_tensor(out=EALL[l][:, ti, :], in0=banks[b][:, 8:14], in1=EALL[l][:, ti - 1, :], op=ALU.add),
                         reads=[BK(b), cr], writes=[cr])

            if dbg in ("win", "gm", "gm2"):
                raise StopBuild()
            nj = 4 * g + 4
            sb_rr = [0]
            pt_rr = [0]
            ktres = [R(f"KT{l}_{c}") for c in range(3)]

            def normalize(ob, hidx):
                P.op("dve", lambda e: e.reciprocal(out=rs[64:65, :], in_=banks[ob][64:65, :]), reads=[BK(ob)], writes=[AR("rs")])
                P.op("pe", lambda e: e.matmul(banks[5][0:64, :], lhsT=onesf[64:65, 0:64], rhs=rs[64:65, :], start=True, stop=True),
                     reads=[AR("rs"), R("onesf")], writes=[BK(5)])
                P.op("act", lambda e: e.activation(out=bcsb[0:64, :], in_=banks[5][0:64, :], func=AF.Copy), reads=[BK(5)], writes=[AR("bcsb")])
                P.op("dve", lambda e: e.tensor_tensor(out=attT[0:64, hidx, :], in0=banks[ob][0:64, :], in1=bcsb[0:64, :], op=ALU.mult),
                     reads=[BK(ob), AR("bcsb")], writes=[AR(f"attT{hidx}")])

            LOOK = 2
            pend = []
            deferred = []

            def tick():
                for d in deferred:
                    d[0] -= 1
                while deferred and deferred[0][0] <= 0:
                    deferred.pop(0)[1]()

            def norm_part1(ob):
                P.op("dve", lambda e: e.reciprocal(out=rs[64:65, :], in_=banks[ob][64:65, :]), reads=[BK(ob)], writes=[AR("rs")])

            def norm_part2(ob, hidx):
                P.op("pe", lambda e: e.matmul(banks[5][0:64, :], lhsT=onesf[64:65, 0:64], rhs=rs[64:65, :], start=True, stop=True),
                     reads=[AR("rs"), R("onesf")], writes=[BK(5)])
                P.op("act", lambda e: e.activation(out=bcsb[0:64, :], in_=banks[5][0:64, :], func=AF.Copy), reads=[BK(5)], writes=[AR("bcsb")])
                P.op("dve", lambda e: e.tensor_tensor(out=attT[0:64, hidx, :], in0=banks[ob][0:64, :], in1=bcsb[0:64, :], op=ALU.mult),
                     reads=[BK(ob), AR("bcsb")], writes=[AR(f"attT{hidx}")])

            def emit_pv(blk_):
                (kind, hidx, j, col0, pk, ob, first, last) = blk_
                if kind == "f":
                    P.op("pe", lambda e: e.matmul(banks[ob][0:65, col0:T], lhsT=VP[l][:, j, hidx, :], rhs=PT[pk][:, col0:T], start=first, stop=last),
                         reads=[R(f"VP{l}"), R(f"VPones{l}"), AR(f"PT{pk}")], writes=[BK(ob)])
                else:
                    P.op("pe", lambda e: e.matmul(banks[ob][0:65, :], lhsT=VMP[l][:, j, hidx - 6, :], rhs=PT[pk][:, :], start=first, stop=last),
                         reads=[R(f"VMP{l}"), R(f"VMPones{l}"), AR(f"PT{pk}")], writes=[BK(ob)])
                if last:
                    while deferred:
                        deferred.pop(0)[1]()
                    norm_part1(ob)
                    deferred.append([4, lambda ob=ob, hidx=hidx: norm_part2(ob, hidx)])

            def push(blk_):
                pend.append(blk_)
                if len(pend) > LOOK:
                    emit_pv(pend.pop(0))
                tick()

            for h in range(6):
                p, r0 = h // 2, 64 * (h % 2)
                par = h % 2
                br = AR(f"beta{par}")
                P.op("dve", lambda e, par=par, h=h: e.tensor_scalar(out=beta[:, par, 0:nj], in0=CALL[l][:, 0:nj, h],
                                                                     scalar1=EALL[l][:, 4 * g + 3, h:h + 1], scalar2=None, op0=ALU.subtract),
                     reads=[R(f"CE{l}")], writes=[br])
                dr = AR(f"daug{par}")
                P.op("dve", lambda e, par=par, h=h: e.tensor_scalar(out=stat[0:1, 8 + 4 * par:12 + 4 * par], in0=EALL[l][0:1, 4 * g:4 * g + 4, h],
                                                                     scalar1=EALL[l][0:1, 4 * g + 3, h:h + 1], scalar2=-1.0, op0=ALU.subtract, op1=ALU.mult),
                     reads=[R(f"CE{l}")], writes=[R(f"dtmp{par}")])
                P.op("dve", lambda e, par=par: e.tensor_copy(out=daug[0:1, par, :].rearrange("p (a b) -> p a b", b=128),
                                                              in_=stat[0:1, 8 + 4 * par:12 + 4 * par].unsqueeze(2).to_broadcast([1, 4, 128])),
                     reads=[R(f"dtmp{par}")], writes=[dr])
                ob = 3 + (h % 2)
                for j in range(nj):
                    il0 = max(0, j - 4 * g)
                    col0 = il0 * 128
                    sbk = sb_rr[0] % 3
                    sb_rr[0] += 1
                    pk = pt_rr[0] % NPT
                    pt_rr[0] += 1
                    P.op("pe", lambda e, j=j, col0=col0, sbk=sbk, p=p, r0=r0: e.matmul(banks[sbk][:, col0:T], lhsT=KT[l][r0:r0 + 64, p, j * 128:(j + 1) * 128],
                                                                                        rhs=QT[r0:r0 + 64, p, col0:T], start=True, stop=False),
                         reads=[ktres[p], AR(f"QT{p}")], writes=[BK(sbk)])
                    P.op("pe", lambda e, col0=col0, sbk=sbk, par=par: e.matmul(banks[sbk][:, col0:T], lhsT=triu[0:1, :], rhs=daug[0:1, par, col0:T], start=False, stop=True),
                         reads=[R("triu"), dr], writes=[BK(sbk)])
                    P.op("act", lambda e, j=j, col0=col0, sbk=sbk, pk=pk, par=par: e.activation(out=PT[pk][:, col0:T], in_=banks[sbk][:, col0:T],
                                                                                                func=AF.Exp, bias=beta[:, par, j:j + 1]),
                         reads=[BK(sbk), br], writes=[AR(f"PT{pk}")])
                    if j >= 4 * g:
                        P.op("dve", lambda e, il0=il0, pk=pk: e.tensor_tensor(out=PT[pk][:, il0 * 128:(il0 + 1) * 128], in0=PT[pk][:, il0 * 128:(il0 + 1) * 128],
                                                                                in1=triu[:], op=ALU.mult),
                             reads=[AR(f"PT{pk}"), R("triu")], writes=[AR(f"PT{pk}")])
                    push(("f", h, j, col0, pk, ob, j == 0, j == nj - 1))
            for hm in range(4):
                p, r0 = hm // 2, 64 * (hm % 2)
                ob = 3 + (hm % 2)
                for jm in range(2):
                    sbk = sb_rr[0] % 3
                    sb_rr[0] += 1
                    pk = pt_rr[0] % NPT
                    pt_rr[0] += 1
                    P.op("pe", lambda e, jm=jm, sbk=sbk, p=p, r0=r0: e.matmul(banks[sbk][:, :], lhsT=KMT[l][r0:r0 + 64, p, jm * 128:(jm + 1) * 128],
                                                                               rhs=QMT[r0:r0 + 64, p, :], start=True, stop=True),
                         reads=[R(f"KMT{l}"), AR(f"QMT{p}")], writes=[BK(sbk)])
                    P.op("act", lambda e, sbk=sbk, pk=pk: e.activation(out=PT[pk][:, :], in_=banks[sbk][:, :], func=AF.Exp),
                         reads=[BK(sbk)], writes=[AR(f"PT{pk}")])
                    push(("m", 6 + hm, jm, 0, pk, ob, jm == 0, jm == 1))
            while pend:
                emit_pv(pend.pop(0))
                tick()
            while deferred:
                deferred.pop(0)[1]()
            if dbg == "att":
                raise StopBuild()
            sg = load_slot(lambda s: s[:, 0:3 * D].rearrange("p (c n) -> p c n", n=D), wout_d[l][0:384, :].rearrange("(c p) n -> p c n", p=128))
            wo_g = slots[sg][:, 0:3 * D].rearrange("p (c n) -> p c n", n=D)
            wo_h = []
            for (h0, nh) in ((0, 4), (4, 4), (8, 2)):
                si = load_slot(lambda s, nh=nh: s[0:64, 0:nh * D].rearrange("p (c n) -> p c n", n=D),
                               wout_d[l][384 + 64 * h0:384 + 64 * (h0 + nh), :].rearrange("(c p) n -> p c n", p=64))
                v = slots[si][0:64, 0:nh * D].rearrange("p (c n) -> p c n", n=D)
                for k in range(nh):
                    wo_h.append((si, v, k))
            attres = [AR(f"attT{i}") for i in range(10)]
            for tt in range(4):
                yb = [next_bank(0, 4), next_bank(0, 4)]
                for half in range(2):
                    b = yb[half]
                    for c in range(3):
                        P.op("pe", lambda e, c=c, tt=tt, half=half, b=b: e.matmul(banks[b][:, :], lhsT=gmT[:, c, tt * 128:(tt + 1) * 128],
                                                                                   rhs=wo_g[:, c, half * 512:(half + 1) * 512], start=(c == 0), stop=False),
                             reads=[AR(f"gmT{tt}"), R(f"slot{sg}")], writes=[BK(b)])
                    for hh in range(10):
                        si, v, k = wo_h[hh]
                        P.op("pe", lambda e, hh=hh, tt=tt, half=half, b=b, v=v, k=k: e.matmul(banks[b][:, :], lhsT=attT[0:64, hh, tt * 128:(tt + 1) * 128],
                                                                                               rhs=v[:, k, half * 512:(half + 1) * 512], start=False, stop=(hh == 9)),
                             reads=[attres[hh], R(f"slot{si}")], writes=[BK(b)])
                post_norm(yb, tt, gpm, R("gpm"), xres[tt])

        def post_norm(yb, tt, gbuf, gres, xr):
            c0 = 48 + tt * 4
            sr = R(f"statp{tt}")
            for half in range(2):
                P.op("act", lambda e, half=half: e.activation(out=ytmp_bf[:, half * 512:(half + 1) * 512], in_=banks[yb[half]][:, :], func=AF.Square,
                                                               accum_out=stat[:, c0 + half:c0 + half + 1]),
                     reads=[BK(yb[half])], writes=[sr, R("ytmp")])
            P.op("dve", lambda e: e.tensor_tensor(out=stat[:, c0 + 2:c0 + 3], in0=stat[:, c0:c0 + 1], in1=stat[:, c0 + 1:c0 + 2], op=ALU.add), reads=[sr], writes=[sr])
            rstd_from_ss(stat[:, c0 + 2:c0 + 3], stat[:, c0 + 3:c0 + 4], D, [sr], [sr])
            for half in range(2):
                P.op("dve", lambda e, half=half: e.scalar_tensor_tensor(out=ytmp, in0=banks[yb[half]][:, :], scalar=stat[:, c0 + 3:c0 + 4],
                                                                         in1=gbuf[:, half * 512:(half + 1) * 512], op0=ALU.mult, op1=ALU.mult),
                     reads=[BK(yb[half]), sr, gres], writes=[R("ytmp")])
                P.op("dve", lambda e, half=half: e.tensor_tensor(out=xg[:, tt, half * 512:(half + 1) * 512], in0=xg[:, tt, half * 512:(half + 1) * 512], in1=ytmp, op=ALU.add),
                     reads=[R("ytmp"), xr], writes=[xr])

        def ffn(g, l, last):
            xres = [R(f"xg{tt}") for tt in range(4)]
            hres = [R(f"hT{tt}") for tt in range(4)]
            P.dma("sp", lambda e: e.dma_start(out=gpf[:], in_=bcast_rows(g_postffn_t, l * D, D)), d_gpf, writes=[R("gpm")])
            norm_transpose_n([(xg[:, tt, :], xres[tt]) for tt in range(4)], GCOL[l][:, 1, :], R(f"GCOL{l}"),
                             [(hT[:, :, tt * 128:(tt + 1) * 128], hres[tt]) for tt in range(4)])
            P.fence(AG)
            for blk_i in range(8):
                si = load_slot(lambda s: s[:, :].rearrange("p (k n) -> p k n", n=512),
                               w1_d[l][:, blk_i * 512:(blk_i + 1) * 512].rearrange("(k p) n -> p k n", p=128))
                wb = slots[si][:, :].rearrange("p (k n) -> p k n", n=512)
                for fcl in range(4):
                    fc = blk_i * 4 + fcl
                    b = next_bank()
                    for kc in range(KC):
                        P.op("pe", lambda e, kc=kc, fcl=fcl, b=b, wb=wb: e.matmul(banks[b][:, :], lhsT=wb[:, kc, fcl * 128:(fcl + 1) * 128], rhs=hT[:, kc, :],
                                                                                   start=(kc == 0), stop=(kc == KC - 1)),
                             reads=hres + [R(f"slot{si}")], writes=[BK(b)])
                    rk = 0
                    P.op("act", lambda e, b=b, rk=rk: e.activation(out=rtmp[rk], in_=banks[b][:, :], func=AF.Relu), reads=[BK(b)], writes=[AR(f"rtmp{rk}")])
                    P.op("dve", lambda e, fc=fc, rk=rk: e.tensor_tensor(out=hidT[:, fc, :], in0=rtmp[rk], in1=rtmp[rk], op=ALU.mult),
                         reads=[AR(f"rtmp{rk}")], writes=[AR(f"hidT{fc}")])
            if dbg == "ffn1":
                raise StopBuild()
            for blk_i in range(8):
                si = load_slot(lambda s: s[:, :].rearrange("p (c n) -> p c n", n=D),
                               w2_d[l][blk_i * 512:(blk_i + 1) * 512, :].rearrange("(c p) n -> p c n", p=128))
                wb = slots[si][:, :].rearrange("p (c n) -> p c n", n=D)
                for fcl in range(4):
                    fc = blk_i * 4 + fcl
                    for tt in range(4):
                        for half in range(2):
                            b = tt * 2 + half
                            P.op("pe", lambda e, fc=fc, fcl=fcl, tt=tt, half=half, b=b, wb=wb: e.matmul(banks[b][:, :], lhsT=hidT[:, fc, tt * 128:(tt + 1) * 128],
                                                                                                         rhs=wb[:, fcl, half * 512:(half + 1) * 512],
                                                                                                         start=(fc == 0), stop=(fc == 31)),
                                 reads=[AR(f"hidT{fc}"), R(f"slot{si}")], writes=[BK(b)])
            for tt in range(4):
                post_norm([tt * 2, tt * 2 + 1], tt, gpf, R("gpm"), xres[tt])
                if last:
                    r0 = (4 * g + tt) * 128
                    P.dma("sp", lambda e, tt=tt, r0=r0: e.dma_start(out=out_d[r0:r0 + 128, :], in_=xg[:, tt, :]), d_o[tt],
                          reads=[xres[tt]], writes=[R(f"outd{tt}")])
            P.fence(AG)

        try:
            for l in range(L):
                layer_setup(l)
            P.fence(AG)
            if dbg == "setup":
                raise StopBuild()
            for g in range(NG):
                for tt in range(4):
                    r0 = (4 * g + tt) * 128
                    P.dma("sp", lambda e, tt=tt, r0=r0: e.dma_start(out=xg[:, tt, :], in_=x_d[r0:r0 + 128, :]), d_x[tt], writes=[R(f"xg{tt}")])
                for l in range(L):
                    mixer(g, l)
                    if dbg == "mix" or dbg == f"mix:{g}:{l}":
                        raise StopBuild()
                    ffn(g, l, last=(l == L - 1))
                    if dbg == f"ffn:{g}:{l}":
                        raise StopBuild()
        except StopBuild:
            if dbg == "win":
                P.op("dve", lambda e: e.tensor_copy(out=xg[:, 0, :].rearrange("p (c t) -> p c t", t=T), in_=QMT[:, :, :]),
                     reads=[AR("QMT0"), AR("QMT1"), R("xg0")], writes=[R("xg0")])
                P.op("dve", lambda e: e.tensor_copy(out=xg[:, 1, 0:512].rearrange("p (c t) -> p c t", t=256), in_=KMT[0][:, :, :]),
                     reads=[R("KMT0"), R("xg1")], writes=[R("xg1")])
                P.op("dve", lambda e: e.tensor_copy(out=xg[:, 2, 0:520].rearrange("p (c t) -> p c t", t=260), in_=VMP[0][:, :, :, :].rearrange("p a b c -> p a (b c)")),
                     reads=[R("VMP0"), R("VMPones0"), R("xg2")], writes=[R("xg2")])
            if dbg == "gm2":
                ar = [AR("zA"), AR("vn"), AR("tmpv2"), AR("outa"), R("WST0")] + [R(f"xg{i}") for i in range(4)]
                wr = [R(f"xg{i}") for i in range(4)]
                P.op("dve", lambda e: e.tensor_copy(out=xg[:, 0, 0:768], in_=zA), reads=ar, writes=wr)
                P.op("dve", lambda e: e.tensor_copy(out=xg[:, 1, 0:384], in_=vn), reads=ar, writes=wr)
                P.op("dve", lambda e: e.tensor_copy(out=xg[:, 1, 384:768], in_=tmpv2), reads=ar, writes=wr)
                P.op("dve", lambda e: e.tensor_copy(out=xg[:, 2, 0:384], in_=outa), reads=ar, writes=wr)
                P.op("dve", lambda e: e.tensor_copy(out=xg[:, 3, 0:768], in_=WST[0][:, :, :].rearrange("p g t -> p (g t)")), reads=ar, writes=wr)
            if dbg == "gm":
                P.op("dve", lambda e: e.tensor_copy(out=xg[:, 0, :].rearrange("p (c t) -> p c t", t=T), in_=gmT[:, 0:2, :]),
                     reads=[AR(f"gmT{i}") for i in range(4)] + [R("xg0")], writes=[R("xg0")])
                P.op("dve", lambda e: e.tensor_copy(out=xg[:, 1, 0:512], in_=gmT[:, 2, :]),
                     reads=[AR(f"gmT{i}") for i in range(4)] + [R("xg1")], writes=[R("xg1")])
            if dbg in ("att3", "att", "att5", "att4"):
                for q in range(4):
                    P.op("dve", lambda e, q=q: e.tensor_copy(out=xg[:, q, :].rearrange("p (c t) -> p c t", t=T), in_=attT[:, 2 * q:2 * q + 2, :]),
                         reads=[AR(f"attT{i}") for i in range(10)] + [R(f"xg{q}")], writes=[R(f"xg{q}")])
            for tt in range(4):
                P.dma("sp", lambda e, tt=tt: e.dma_start(out=out_d[tt * 128:(tt + 1) * 128, :], in_=xg[:, tt, :]), d_o[tt],
                      reads=[R(f"xg{tt}")], writes=[R(f"outd{tt}")])
        P.final()
        if build.want_trace:
            P.trace = []
        P.emit(st)
        build.stats = P.stats
        build.trace = P.trace
    return nc


build.want_trace = False

WNAMES = ["norm_pre_mix", "norm_post_mix", "norm_pre_ffn", "norm_post_ffn", "norm_mem", "w_in", "b_forget",
          "gmlp_v_norm", "gmlp_w_s", "gmlp_b_s", "w_mem_kv", "w_out", "w_ff1", "w_ff2"]

_cache = {}


def _consts():
    iu = np.triu(np.ones((128, 128), np.float32))
    return {
        "c_ident": np.eye(128, dtype=np.float32).astype(ml_dtypes.bfloat16),
        "c_triu": iu.astype(ml_dtypes.bfloat16),
        "c_triuf": iu.copy(),
        "c_onesf": np.ones((128, 128), np.float32),
    }


DBG = None


def run_layers(x, mem, weights, L):
    B, S, _ = x.shape
    key = (S, L)
    if key not in _cache:
        _cache[key] = build(S, L, dbg=DBG)
    nc = _cache[key]
    consts = _consts()
    in_maps = []
    for b in range(B):
        m = {"x": np.ascontiguousarray(x[b]), "mem": np.ascontiguousarray(mem[b])}
        for k in WNAMES:
            m[k] = np.ascontiguousarray(weights[k])
        m.update(consts)
        in_maps.append(m)
    res = run_bass_kernel_spmd(nc, in_maps, core_ids=list(range(B)))
    return np.stack([res.results[b]["out"] for b in range(B)], axis=0)


FUSED = True


def kernel(**inputs):
    x = np.asarray(inputs["x"], dtype=np.float32)
    mem = np.asarray(inputs["mem"], dtype=np.float32)
    W = {k: np.asarray(inputs[k], dtype=np.float32) for k in WNAMES}
    depth = W["w_in"].shape[0]
    if FUSED:
        return run_layers(x, mem, W, depth)
    for l in range(depth):
        x = run_layers(x, mem, {k: v[l:l + 1] for k, v in W.items()}, 1)
    return x
```

```python
from contextlib import ExitStack
import numpy as np
import ml_dtypes
import concourse.bass as bass
import concourse.mybir as mybir
from concourse.bass_utils import run_bass_kernel_spmd

F32 = mybir.dt.float32
BF16 = mybir.dt.bfloat16
AF = mybir.ActivationFunctionType
ALU = mybir.AluOpType
AX = mybir.AxisListType

COMPUTE = ("pe", "act", "dve", "pool")
EPS = 1e-6


class Res:
    __slots__ = ("name", "lw", "rd", "group", "excl")

    def __init__(self, name, group=None):
        self.name = name
        self.lw = None
        self.rd = []
        self.group = group
        self.excl = name.startswith("bank")


class Group:
    def __init__(self):
        self.since = []
        self.fdeps = []


class DmaSem:
    __slots__ = ("name", "total", "sem")

    def __init__(self, name):
        self.name = name
        self.total = 0
        self.sem = None


class Op:
    __slots__ = ("eng", "fn", "deps", "idx", "dma", "dma_total", "needs_inc", "cnt", "tag")

    def __init__(self, eng, fn):
        self.eng = eng
        self.fn = fn
        self.deps = []
        self.dma = None
        self.dma_total = 0
        self.needs_inc = False
        self.cnt = 0


class Prog:
    def __init__(self, nc):
        self.nc = nc
        self.ops = []
        self.dmasems = []
        self.resd = {}
        self.trace = None

    def R(self, name, group=None):
        r = self.resd.get(name)
        if r is None:
            r = Res(name, group)
            self.resd[name] = r
        return r

    def dmasem(self, name):
        d = DmaSem(name)
        self.dmasems.append(d)
        return d

    def fence(self, group):
        last = {}
        keep = []
        for o in group.since + group.fdeps:
            if o.dma is not None:
                keep.append(o)
            else:
                if o.eng not in last or last[o.eng].idx < o.idx:
                    last[o.eng] = o
        group.fdeps = keep + list(last.values())
        group.since = []

    def _track(self, op, reads, writes):
        reads = list(reads)
        writes = list(writes)
        for r in list(reads):
            if r.excl:
                reads.remove(r)
                if r not in writes:
                    writes.append(r)
        op.tag = "R:" + ",".join(r.name for r in reads) + " W:" + ",".join(w.name for w in writes)
        deps = set()
        for r in reads:
            if r.lw is not None:
                deps.add(r.lw)
        for w in writes:
            if w.lw is not None:
                deps.add(w.lw)
            for o in w.rd:
                deps.add(o)
        groups = set()
        for r in list(reads) + list(writes):
            if r.group is not None:
                groups.add(r.group)
        for g in groups:
            for o in g.fdeps:
                deps.add(o)
            g.since.append(op)
        deps.discard(op)
        best = {}
        red = []
        for d in deps:
            if d.dma is not None:
                red.append(d)
            elif d.eng not in best or best[d.eng].idx < d.idx:
                best[d.eng] = d
        deps = red + list(best.values())
        for r in reads:
            r.rd.append(op)
        for w in writes:
            w.lw = op
            w.rd = []
        op.deps = list(deps)

    def op(self, eng, fn, reads=(), writes=()):
        o = Op(eng, fn)
        o.idx = len(self.ops)
        self.ops.append(o)
        self._track(o, reads, writes)
        return o

    def dma(self, queue, fn, sem, reads=(), writes=()):
        o = Op(queue, fn)
        o.idx = len(self.ops)
        o.dma = sem
        sem.total += 16
        o.dma_total = sem.total
        self.ops.append(o)
        self._track(o, reads, writes)
        return o

    def final(self):
        o = Op("sp", lambda e: e.nop())
        o.idx = len(self.ops)
        o.tag = "final"
        last = {}
        for p in self.ops:
            if p.dma is not None:
                last[("d", p.dma.name)] = p
            elif p.eng in COMPUTE:
                last[("e", p.eng)] = p
        o.deps = list(last.values())
        self.ops.append(o)

    def emit(self, stack):
        nc = self.nc
        for o in self.ops:
            for d in o.deps:
                if d.dma is None:
                    if d.eng == "pe" and o.eng == "pe" and o.dma is None:
                        continue
                    d.needs_inc = True
        sems = {}
        for e in COMPUTE:
            sems[e] = stack.enter_context(nc.semaphore("s_" + e))
        for d in self.dmasems:
            if d.total > 0:
                d.sem = stack.enter_context(nc.semaphore("d_" + d.name))
        cnt = {e: 0 for e in COMPUTE}
        for o in self.ops:
            if o.dma is None and o.needs_inc:
                cnt[o.eng] += 1
                o.cnt = cnt[o.eng]
        per_eng = {e: [] for e in ("pe", "act", "dve", "pool", "sp")}
        for o in self.ops:
            per_eng[o.eng].append(o)
        self.stats = {e: len(v) for e, v in per_eng.items()}
        self.stats["incs"] = dict(cnt)

        def run_engine(ename, eng):
            seen = {}
            nw = 0
            for o in per_eng[ename]:
                need = {}
                for d in o.deps:
                    if d.dma is not None:
                        key = ("d", d.dma.name)
                        val = d.dma_total
                        semh = d.dma.sem
                    else:
                        if d.eng == "pe" and ename == "pe" and o.dma is None:
                            continue
                        key = ("e", d.eng)
                        val = d.cnt
                        semh = sems[d.eng]
                    if seen.get(key, 0) >= val:
                        continue
                    if key not in need or need[key][1] < val:
                        need[key] = (semh, val)
                for key, (semh, val) in need.items():
                    eng.wait_ge(semh, val)
                    seen[key] = val
                    nw += 1
                if self.trace is not None:
                    self.trace.append((ename, o.idx, [(k, v[1]) for k, v in need.items()], o.cnt if o.needs_inc else None, o.dma_total if o.dma else None, o.tag))
                ins = o.fn(eng)
                if o.dma is not None:
                    ins.then_inc(o.dma.sem, 16)
                elif o.needs_inc:
                    ins.then_inc(sems[ename], 1)
            self.stats["waits_" + ename] = nw

        with nc.Block() as block:
            @block.tensor
            def _(e):
                run_engine("pe", e)

            @block.scalar
            def _(e):
                run_engine("act", e)

            @block.vector
            def _(e):
                run_engine("dve", e)

            @block.gpsimd
            def _(e):
                run_engine("pool", e)

            @block.sync
            def _(e):
                run_engine("sp", e)


D = 1024
KC = 8
DIN = 2182
DFF = 4096
NMEM = 256
T = 512
NSLOT = 4

WIN_BLOCKS = [(0, 512), (512, 512), (1024, 512), (1536, 390), (1926, 256)]


class StopBuild(Exception):
    pass


def build(S, L, dbg=None):
    NT = S // 128
    NG = S // T
    nc = bass.Bass("TRN2", target_bir_lowering=False)

    def din(name, shape, dt=F32):
        return nc.dram_tensor(name, shape, dt, kind="ExternalInput")

    x_t = din("x", [S, D])
    mem_t = din("mem", [NMEM, D])
    g_premix_t = din("norm_pre_mix", [L, D])
    g_postmix_t = din("norm_post_mix", [L, D])
    g_preffn_t = din("norm_pre_ffn", [L, D])
    g_postffn_t = din("norm_post_ffn", [L, D])
    g_mem_t = din("norm_mem", [L, D])
    w_in_t = din("w_in", [L, D, DIN])
    b_forget_t = din("b_forget", [L, 6])
    gv_t = din("gmlp_v_norm", [L, 384])
    ws_t = din("gmlp_w_s", [L, 6, 128, 128])
    bs_t = din("gmlp_b_s", [L, 6, 128])
    wkv_t = din("w_mem_kv", [L, D, 512])
    wout_t = din("w_out", [L, D, D])
    w1_t = din("w_ff1", [L, D, DFF])
    w2_t = din("w_ff2", [L, DFF, D])
    ident_t = din("c_ident", [128, 128], BF16)
    triu_t = din("c_triu", [128, 128], BF16)
    triuf_t = din("c_triuf", [128, 128], F32)
    onesf_t = din("c_onesf", [128, 128], F32)
    out_t = nc.dram_tensor("out", [S, D], F32, kind="ExternalOutput")

    x_d, mem_d, out_d = x_t.ap(), mem_t.ap(), out_t.ap()
    w_in_d, wkv_d, wout_d, w1_d, w2_d = w_in_t.ap(), wkv_t.ap(), wout_t.ap(), w1_t.ap(), w2_t.ap()
    ws_d = ws_t.ap()

    with ExitStack() as st:
        P = Prog(nc)
        R = P.R

        def sb(name, shape, dt):
            return st.enter_context(nc.sbuf_tensor(name, shape, dt))

        KT = [sb(f"KT{l}", [128, 3, S], BF16) for l in range(L)]
        VP = [sb(f"VP{l}", [128, NT, 6, 65], BF16) for l in range(L)]
        CALL = [sb(f"CALL{l}", [128, NT, 6], F32) for l in range(L)]
        EALL = [sb(f"EALL{l}", [128, NT, 6], F32) for l in range(L)]
        KMT = [sb(f"KMT{l}", [128, 2, NMEM], BF16) for l in range(L)]
        VMP = [sb(f"VMP{l}", [128, 2, 4, 65], BF16) for l in range(L)]
        WST = [sb(f"WST{l}", [128, 6, 128], BF16) for l in range(L)]
        GCOL = [sb(f"GCOL{l}", [128, 3, 8], F32) for l in range(L)]
        BS = [sb(f"BS{l}", [128, 6], F32) for l in range(L)]
        BFG = [sb(f"BFG{l}", [128, 6], F32) for l in range(L)]
        gpm = sb("gpm", [128, D], F32)
        gpf = gpm
        xg = sb("xg", [128, 4, D], F32)
        hT = sb("hT", [128, KC, T], BF16)
        onesb = sb("onesb", [128, 128], BF16)
        stat = sb("stat", [128, 64], F32)
        ident = sb("ident", [128, 128], BF16)
        triu = sb("triu", [128, 128], BF16)
        triuf = sb("triuf", [128, 128], F32)
        onesf = sb("onesf", [128, 128], F32)
        slots = [sb(f"slot{i}", [128, 4096], BF16) for i in range(NSLOT)]
        ARENA_BF = 19680
        arena = sb("arena", [128, ARENA_BF], BF16)
        AG = Group()

        class Carver:
            def __init__(self):
                self.off = 0

            def take(self, nbf):
                o = self.off
                self.off += nbf
                assert self.off <= ARENA_BF, self.off
                return o

        cv = Carver()

        def a_bf(n):
            o = cv.take(n)
            return arena[:, o:o + n]

        def a_f32(n):
            o = cv.take(2 * n)
            return arena[:, o:o + 2 * n].bitcast(F32)

        QT = a_bf(6 * T).rearrange("p (c t) -> p c t", t=T)
        QMT = a_bf(2 * T).rearrange("p (c t) -> p c t", t=T)
        gmT = a_bf(3 * T).rearrange("p (c t) -> p c t", t=T)
        attT = a_bf(10 * T).rearrange("p (c t) -> p c t", t=T)
        NPT = 3
        PT_OFF = cv.off
        PT = [a_bf(T) for _ in range(NPT)]
        xnb = [arena[:, PT_OFF:PT_OFF + D]]
        zA = a_f32(768)
        tmpv2 = a_f32(384)
        vn = a_bf(384)
        outa = a_bf(384)
        beta = a_f32(2 * 32).rearrange("p (a c) -> p a c", a=2)
        daug = a_bf(2 * T).rearrange("p (a t) -> p a t", a=2)
        rs = a_f32(T)
        tmpv = rs[:, 0:384]
        bcsb = a_f32(T)
        GVS = bcsb[:, 0:384]
        fg = a_f32(24).rearrange("p (a b) -> p a b", b=6)
        spb = a_f32(24).rearrange("p (a b) -> p a b", b=6)
        att_end = cv.off
        cv.off = 0
        hidT = a_bf(32 * T).rearrange("p (c t) -> p c t", t=T)
        rtmp = [a_f32(T)]
        cv.off = max(cv.off, att_end)
        _yo = cv.take(2 * T)
        ytmp_bf = arena[:, _yo:_yo + 2 * T]
        ytmp = ytmp_bf.bitcast(F32)

        def AR(name):
            return R(name, AG)

        banks = [st.enter_context(nc.psum_tensor(f"bank{i}", [128, 512], F32)) for i in range(8)]
        banks_bf = [b[:, :].bitcast(BF16) for b in banks]

        def BK(i):
            return R(f"bank{i}")

        d_setup = P.dmasem("setup")
        d_x = [P.dmasem(f"x{i}") for i in range(4)]
        d_o = [P.dmasem(f"o{i}") for i in range(4)]
        d_slot = [P.dmasem(f"sl{i}") for i in range(NSLOT)]
        d_ws = P.dmasem("ws")
        d_gpm = P.dmasem("gpm")
        d_gpf = d_gpm
        d_gv = P.dmasem("gv")

        slot_ctr = [0]

        def load_slot(dst_fn, src, reads=()):
            i = slot_ctr[0] % NSLOT
            slot_ctr[0] += 1
            dst = dst_fn(slots[i])
            P.dma("pool", lambda e, dst=dst, src=src: e.dma_start(out=dst, in_=src), d_slot[i],
                  reads=list(reads), writes=[R(f"slot{i}")])
            return i

        def bcast_rows(t, row_off, n):
            return bass.AP(t, row_off, [[0, 128], [1, n]])

        setup_res = []

        def setup_dma(dst, src, resname, **kw):
            P.dma("sp", lambda e, dst=dst, src=src, kw=kw: e.dma_start(out=dst, in_=src, **kw), d_setup, writes=[R(resname)])
            setup_res.append(R(resname))

        setup_dma(ident[:], ident_t.ap(), "ident")
        setup_dma(triu[:], triu_t.ap(), "triu")
        setup_dma(triuf[:], triuf_t.ap(), "triuf")
        setup_dma(onesf[:], onesf_t.ap(), "onesf")
        for l in range(L):
            for k, gt in enumerate((g_premix_t, g_preffn_t, g_mem_t)):
                setup_dma(GCOL[l][:, k, :], bass.AP(gt, l * D, [[1, 128], [128, 8]]), f"GCOL{l}", allow_slow_non_contiguous=True)
            setup_dma(BFG[l][:], bcast_rows(b_forget_t, l * 6, 6), f"BFG{l}")
            setup_dma(BS[l][:], bass.AP(bs_t, l * 768, [[1, 128], [128, 6]]), f"BS{l}", allow_slow_non_contiguous=True)
        last_setup = P.ops[-1]
        for r in setup_res:
            r.lw = last_setup
            r.rd = []
        P.op("dve", lambda e: e.memset(onesb[:], 1.0), writes=[R("onesb")])
        P.op("dve", lambda e: e.memset(stat[:], 1.0e6), writes=[R("statn"), R("dtmp0"), R("dtmp1")] + [R(f"statv{i}") for i in range(4)] + [R(f"statp{i}") for i in range(4)])
        for l in range(L):
            P.op("dve", lambda e, l=l: e.memset(VP[l][:, :, :, 64:65], 1.0), writes=[R(f"VPones{l}")])
            P.op("dve", lambda e, l=l: e.memset(VMP[l][:, :, :, 64:65], 1.0), writes=[R(f"VMPones{l}")])

        def rstd_from_ss(ss_ap, out_ap, n, rd, wr):
            P.op("act", lambda e: e.activation(out=out_ap, in_=ss_ap, func=AF.Ln, scale=1.0 / n, bias=EPS), reads=rd, writes=wr)
            P.op("act", lambda e: e.activation(out=out_ap, in_=out_ap, func=AF.Exp, scale=-0.5), reads=wr, writes=wr)

        tbank_ctr = [0]

        def norm_transpose_n(srcs, gcol_ap, gres, dsts):
            n = len(srcs)
            ssr = R("statn")
            for i, (src_ap, src_res) in enumerate(srcs):
                P.op("act", lambda e, i=i, src_ap=src_ap: e.activation(out=xnb[0], in_=src_ap, func=AF.Square, accum_out=stat[:, i:i + 1]),
                     reads=[src_res], writes=[ssr] + ([AR("PT0"), AR("PT1")] if i == 0 else []))
            rstd_from_ss(stat[:, 0:n], stat[:, 4:4 + n], D, [ssr], [ssr])
            for i, ((src_ap, src_res), (dst_ap, dst_res)) in enumerate(zip(srcs, dsts)):
                P.op("dve", lambda e, i=i, src_ap=src_ap: e.tensor_scalar(out=xnb[0], in0=src_ap, scalar1=stat[:, 4 + i:5 + i], scalar2=None, op0=ALU.mult),
                     reads=[src_res, ssr], writes=[AR("PT0"), AR("PT1")])
                tb = 6 + (tbank_ctr[0] % 2)
                tbank_ctr[0] += 1
                for kc in range(KC):
                    P.op("pe", lambda e, kc=kc, tb=tb: e.transpose(out=banks_bf[tb][:, kc * 128:(kc + 1) * 128], in_=xnb[0][:, kc * 128:(kc + 1) * 128], identity=ident[:]),
                         reads=[AR("PT0"), AR("PT1"), R("ident")], writes=[BK(tb)])
                P.op("dve", lambda e, tb=tb, dst_ap=dst_ap: e.tensor_tensor(out=dst_ap, in0=banks_bf[tb][:, 0:1024].rearrange("p (k t) -> p k t", t=128),
                                                                             in1=gcol_ap.unsqueeze(2).to_broadcast([128, KC, 128]), op=ALU.mult),
                     reads=[BK(tb), gres], writes=[dst_res])

        bank_rr = [0]

        def next_bank(lo=0, hi=6):
            b = lo + (bank_rr[0] % (hi - lo))
            bank_rr[0] += 1
            return b

        def layer_setup(l):
            for mt in range(2):
                P.dma("sp", lambda e, mt=mt: e.dma_start(out=xg[:, mt, :], in_=mem_d[mt * 128:(mt + 1) * 128, :]), d_x[mt], writes=[R(f"xg{mt}")])
            norm_transpose_n([(xg[:, mt, :], R(f"xg{mt}")) for mt in range(2)], GCOL[l][:, 2, :], R(f"GCOL{l}"),
                             [(hT[:, :, mt * 128:(mt + 1) * 128], R(f"hT{mt}")) for mt in range(2)])
            si = load_slot(lambda s: s[:, :].rearrange("p (k n) -> p k n", n=512), wkv_d[l].rearrange("(k p) n -> p k n", p=128))
            wkv = slots[si][:, :].rearrange("p (k n) -> p k n", n=512)
            hres = [R("hT0"), R("hT1")]
            for pm in range(2):
                b = next_bank()
                for kc in range(KC):
                    P.op("pe", lambda e, kc=kc, pm=pm, b=b: e.matmul(banks[b][:, 0:NMEM], lhsT=wkv[:, kc, pm * 128:(pm + 1) * 128], rhs=hT[:, kc, 0:NMEM],
                                                                      start=(kc == 0), stop=(kc == KC - 1)),
                         reads=[R(f"slot{si}")] + hres, writes=[BK(b)])
                P.op("act", lambda e, pm=pm, b=b: e.activation(out=KMT[l][:, pm, :], in_=banks[b][:, 0:NMEM], func=AF.Copy),
                     reads=[BK(b)], writes=[R(f"KMT{l}")])
            for mt in range(2):
                b = next_bank()
                for kc in range(KC):
                    P.op("pe", lambda e, kc=kc, mt=mt, b=b: e.matmul(banks[b][:, 0:256], lhsT=hT[:, kc, mt * 128:(mt + 1) * 128], rhs=wkv[:, kc, 256:512],
                                                                      start=(kc == 0), stop=(kc == KC - 1)),
                         reads=[R(f"slot{si}"), hres[mt]], writes=[BK(b)])
                P.op("act", lambda e, mt=mt, b=b: e.activation(out=VMP[l][:, mt, :, 0:64], in_=banks[b][:, 0:256].rearrange("p (h d) -> p h d", d=64), func=AF.Copy),
                     reads=[BK(b)], writes=[R(f"VMP{l}")])
            P.dma("sp", lambda e: e.dma_start(out=zA.rearrange("p (g s) -> p g s", s=128), in_=ws_d[l].rearrange("g t s -> t g s")), d_ws,
                  writes=[AR("zA")])
            wsb = xnb[0][:, 0:768]
            P.op("dve", lambda e: e.tensor_copy(out=wsb, in_=zA), reads=[AR("zA")], writes=[AR("PT0"), AR("PT1")])
            tb = 6 + (tbank_ctr[0] % 2)
            tbank_ctr[0] += 1
            for gg in range(6):
                P.op("pe", lambda e, gg=gg: e.transpose(out=banks_bf[tb][:, gg * 128:(gg + 1) * 128], in_=wsb[:, gg * 128:(gg + 1) * 128], identity=ident[:]),
                     reads=[AR("PT0"), AR("PT1"), R("ident")], writes=[BK(tb)])
            P.op("dve", lambda e: e.tensor_tensor(out=WST[l][:], in0=banks_bf[tb][:, 0:768].rearrange("p (g t) -> p g t", t=128),
                                                  in1=triu[:].unsqueeze(1).to_broadcast([128, 6, 128]), op=ALU.mult),
                 reads=[BK(tb), R("triu")], writes=[R(f"WST{l}")])

        def mixer(g, l):
            xres = [R(f"xg{tt}") for tt in range(4)]
            hres = [R(f"hT{tt}") for tt in range(4)]
            P.dma("sp", lambda e: e.dma_start(out=gpm[:], in_=bcast_rows(g_postmix_t, l * D, D)), d_gpm, writes=[R("gpm")])
            P.dma("sp", lambda e: e.dma_start(out=GVS, in_=bcast_rows(gv_t, l * 384, 384)), d_gv, writes=[AR("bcsb")])
            norm_transpose_n([(xg[:, tt, :], xres[tt]) for tt in range(4)], GCOL[l][:, 0, :], R(f"GCOL{l}"),
                             [(hT[:, :, tt * 128:(tt + 1) * 128], hres[tt]) for tt in range(4)])
            if dbg == "norm":
                raise StopBuild()
            blk = [None] * 5

            def load_blk(bi):
                c0, ncol = WIN_BLOCKS[bi]
                si = load_slot(lambda s, ncol=ncol: s[:, 0:KC * ncol].rearrange("p (k n) -> p k n", n=ncol),
                               w_in_d[l][:, c0:c0 + ncol].rearrange("(k p) n -> p k n", p=128))
                blk[bi] = (si, slots[si][:, 0:KC * ncol].rearrange("p (k n) -> p k n", n=ncol))

            load_blk(0)
            load_blk(1)
            load_blk(2)
            load_blk(4)
            P.op("dve", lambda e: e.memset(QT[:, :, :], 0.0), writes=[AR("QTz")] + [AR(f"QT{c}") for c in range(3)])
            zAb = [zA, arena[:, PT_OFF:PT_OFF + 1536].bitcast(F32)]
            zAr = [[AR("zA")], [AR("PT0"), AR("PT1"), AR("PT2")]]

            def a_mm(tt):
                zi = tt % 2
                ba, bb = next_bank(), next_bank()
                for kc in range(KC):
                    P.op("pe", lambda e, kc=kc: e.matmul(banks[ba][:, :], lhsT=hT[:, kc, tt * 128:(tt + 1) * 128], rhs=blk[0][1][:, kc, :],
                                                          start=(kc == 0), stop=(kc == KC - 1)),
                         reads=[hres[tt], R(f"slot{blk[0][0]}")], writes=[BK(ba)])
                for kc in range(KC):
                    P.op("pe", lambda e, kc=kc: e.matmul(banks[bb][:, 0:256], lhsT=hT[:, kc, tt * 128:(tt + 1) * 128], rhs=blk[1][1][:, kc, 0:256],
                                                          start=(kc == 0), stop=(kc == KC - 1)),
                         reads=[hres[tt], R(f"slot{blk[1][0]}")], writes=[BK(bb)])
                P.op("act", lambda e: e.activation(out=zAb[zi][:, 0:512], in_=banks[ba][:, :], func=AF.Gelu_apprx_tanh), reads=[BK(ba)], writes=zAr[zi])
                P.op("act", lambda e: e.activation(out=zAb[zi][:, 512:768], in_=banks[bb][:, 0:256], func=AF.Gelu_apprx_tanh), reads=[BK(bb)], writes=zAr[zi])

            def g_chain(tt):
                zi = tt % 2
                zz, zr = zAb[zi], zAr[zi]
                v3 = zz[:, 384:768].rearrange("p (g d) -> p g d", d=64)
                P.op("dve", lambda e: e.tensor_tensor(out=tmpv, in0=zz[:, 384:768], in1=zz[:, 384:768], op=ALU.mult), reads=zr, writes=[AR("rs")])
                c0 = 16 + tt * 8
                sr = R(f"statv{tt}")
                P.op("dve", lambda e: e.reduce_sum(out=stat[:, c0:c0 + 6], in_=tmpv.rearrange("p (g d) -> p g d", d=64), axis=AX.X),
                     reads=[AR("rs")], writes=[sr])
                rstd_from_ss(stat[:, c0:c0 + 6], stat[:, c0:c0 + 6], 64, [sr], [sr])
                P.op("dve", lambda e: e.tensor_tensor(out=tmpv.rearrange("p (g d) -> p g d", d=64), in0=v3,
                                                      in1=stat[:, c0:c0 + 6].unsqueeze(2).to_broadcast([128, 6, 64]), op=ALU.mult),
                     reads=zr + [sr], writes=[AR("rs")])
                P.op("dve", lambda e: e.tensor_tensor(out=vn, in0=tmpv, in1=GVS, op=ALU.mult), reads=[AR("rs"), AR("bcsb")], writes=[AR("vn")])
                bm = next_bank()
                for gg in range(6):
                    P.op("pe", lambda e, gg=gg: e.matmul(banks[bm][:, gg * 64:(gg + 1) * 64], lhsT=WST[l][:, gg, :], rhs=vn[:, gg * 64:(gg + 1) * 64],
                                                          start=True, stop=True),
                         reads=[AR("vn"), R(f"WST{l}")], writes=[BK(bm)])
                P.op("dve", lambda e: e.tensor_tensor(out=tmpv2.rearrange("p (g d) -> p g d", d=64), in0=banks[bm][:, 0:384].rearrange("p (g d) -> p g d", d=64),
                                                      in1=BS[l][:].unsqueeze(2).to_broadcast([128, 6, 64]), op=ALU.add),
                     reads=[BK(bm), R(f"BS{l}")], writes=[AR("tmpv2")])
                P.op("dve", lambda e: e.tensor_tensor(out=outa, in0=tmpv2, in1=zz[:, 0:384], op=ALU.mult), reads=[AR("tmpv2")] + zr, writes=[AR("outa")])
                tb = 6 + (tbank_ctr[0] % 2)
                tbank_ctr[0] += 1
                for c in range(3):
                    P.op("pe", lambda e, c=c: e.transpose(out=banks_bf[tb][:, c * 128:(c + 1) * 128], in_=outa[:, c * 128:(c + 1) * 128], identity=ident[:]),
                         reads=[AR("outa"), R("ident")], writes=[BK(tb)])
                P.op("act", lambda e: e.activation(out=gmT[:, :, tt * 128:(tt + 1) * 128], in_=banks_bf[tb][:, 0:384].rearrange("p (c t) -> p c t", t=128), func=AF.Copy),
                     reads=[BK(tb)], writes=[AR(f"gmT{tt}")])

            fm = [(1, 256, "q", 0), (1, 384, "q", 1), (2, 0, "q", 2),
                  (2, 128, "k", 0), (2, 256, "k", 1), (2, 384, "k", 2),
                  (4, 0, "m", 0), (4, 128, "m", 1)]

            def fm_chunk(bi, lc, kind, c):
                b = next_bank()
                for kc in range(KC):
                    P.op("pe", lambda e, kc=kc: e.matmul(banks[b][:, :], lhsT=blk[bi][1][:, kc, lc:lc + 128], rhs=hT[:, kc, :],
                                                          start=(kc == 0), stop=(kc == KC - 1)),
                         reads=hres + [R(f"slot{blk[bi][0]}")], writes=[BK(b)])
                if kind == "q":
                    P.op("act", lambda e: e.activation(out=QT[0:64, 2 * c, :], in_=banks[b][0:64, :], func=AF.Copy, scale=0.125), reads=[BK(b), AR("QTz")], writes=[AR(f"QT{c}")])
                    P.op("act", lambda e: e.activation(out=QT[64:128, 2 * c + 1, :], in_=banks[b][64:128, :], func=AF.Copy, scale=0.125), reads=[BK(b), AR("QTz")], writes=[AR(f"QT{c}")])
                elif kind == "m":
                    P.op("act", lambda e: e.activation(out=QMT[:, c, :], in_=banks[b][:, :], func=AF.Copy, scale=0.125), reads=[BK(b)], writes=[AR(f"QMT{c}")])
                else:
                    P.op("dve", lambda e: e.tensor_copy(out=KT[l][:, c, g * T:(g + 1) * T], in_=banks[b][:, :]), reads=[BK(b)], writes=[R(f"KT{l}_{c}")])

            def c_tile(tt):
                b = next_bank()
                for kc in range(KC):
                    P.op("pe", lambda e, kc=kc: e.matmul(banks[b][:, 0:390], lhsT=hT[:, kc, tt * 128:(tt + 1) * 128], rhs=blk[3][1][:, kc, :],
                                                          start=(kc == 0), stop=(kc == KC - 1)),
                         reads=[hres[tt], R(f"slot{blk[3][0]}")], writes=[BK(b)])
                P.op("act", lambda e: e.activation(out=VP[l][:, 4 * g + tt, :, 0:64], in_=banks[b][:, 0:384].rearrange("p (h d) -> p h d", d=64), func=AF.Copy),
                     reads=[BK(b)], writes=[R(f"VP{l}")])
                P.op("dve", lambda e: e.tensor_tensor(out=fg[:, tt, :], in0=banks[b][:, 384:390], in1=BFG[l][:], op=ALU.add),
                     reads=[BK(b), R(f"BFG{l}")], writes=[AR("fg")])

            a_mm(0)
            a_mm(1)
            for q in fm[0:4]:
                fm_chunk(*q)
            g_chain(0)
            a_mm(2)
            for q in fm[4:8]:
                fm_chunk(*q)
            g_chain(1)
            a_mm(3)
            load_blk(3)
            g_chain(2)
            for tt in range(4):
                c_tile(tt)
            g_chain(3)
            P.op("act", lambda e: e.activation(out=spb, in_=fg, func=AF.Exp, scale=-1.0), reads=[AR("fg")], writes=[AR("spb")])
            P.op("act", lambda e: e.activation(out=spb, in_=spb, func=AF.Ln, bias=1.0), reads=[AR("spb")], writes=[AR("spb")])
            for tt in range(4):
                ti = 4 * g + tt
                b = next_bank()
                P.op("pe", lambda e, b=b, tt=tt: e.matmul(banks[b][:, 0:6], lhsT=triuf[:], rhs=spb[:, tt, :], start=True, stop=True),
                     reads=[AR("spb"), R("triuf")], writes=[BK(b)])
                P.op("pe", lambda e, b=b, tt=tt: e.matmul(banks[b][:, 8:14], lhsT=onesf[:], rhs=spb[:, tt, :], start=True, stop=True),
                     reads=[AR("spb"), R("onesf")], writes=[BK(b)])
                cr = R(f"CE{l}")
                if ti == 0:
                    P.op("dve", lambda e, b=b, ti=ti: e.tensor_copy(out=CALL[l][:, ti, :], in_=banks[b][:, 0:6]), reads=[BK(b)], writes=[cr])
                    P.op("dve", lambda e, b=b, ti=ti: e.tensor_copy(out=EALL[l][:, ti, :], in_=banks[b][:, 8:14]), reads=[BK(b)], writes=[cr])
                else:
                    P.op("dve", lambda e, b=b, ti=ti: e.tensor_tensor(out=CALL[l][:, ti, :], in0=banks[b][:, 0:6], in1=EALL[l][:, ti - 1, :], op=ALU.add),
                         reads=[BK(b), cr], writes=[cr])
                    P.op("dve", lambda e, b=b, ti=ti: e.tensor_tensor(out=EALL[l][:, ti, :], in0=banks[b][:, 8:14], in1=EALL[l][:, ti - 1, :], op=ALU.add),
                         reads=[BK(b), cr], writes=[cr])

            if dbg in ("win", "gm", "gm2"):
                raise StopBuild()
            nj = 4 * g + 4
            sb_rr = [0]
            pt_rr = [0]
            ktres = [R(f"KT{l}_{c}") for c in range(3)]

            def normalize(ob, hidx):
                P.op("dve", lambda e: e.reciprocal(out=rs[64:65, :], in_=banks[ob][64:65, :]), reads=[BK(ob)], writes=[AR("rs")])
                P.op("pe", lambda e: e.matmul(banks[5][0:64, :], lhsT=onesf[64:65, 0:64], rhs=rs[64:65, :], start=True, stop=True),
                     reads=[AR("rs"), R("onesf")], writes=[BK(5)])
                P.op("act", lambda e: e.activation(out=bcsb[0:64, :], in_=banks[5][0:64, :], func=AF.Copy), reads=[BK(5)], writes=[AR("bcsb")])
                P.op("dve", lambda e: e.tensor_tensor(out=attT[0:64, hidx, :], in0=banks[ob][0:64, :], in1=bcsb[0:64, :], op=ALU.mult),
                     reads=[BK(ob), AR("bcsb")], writes=[AR(f"attT{hidx}")])

            LOOK = 2
            pend = []
            deferred = []

            def tick():
                for d in deferred:
                    d[0] -= 1
                while deferred and deferred[0][0] <= 0:
                    deferred.pop(0)[1]()

            def norm_part1(ob):
                P.op("dve", lambda e: e.reciprocal(out=rs[64:65, :], in_=banks[ob][64:65, :]), reads=[BK(ob)], writes=[AR("rs")])

            def norm_part2(ob, hidx):
                P.op("pe", lambda e: e.matmul(banks[5][0:64, :], lhsT=onesf[64:65, 0:64], rhs=rs[64:65, :], start=True, stop=True),
                     reads=[AR("rs"), R("onesf")], writes=[BK(5)])
                P.op("act", lambda e: e.activation(out=bcsb[0:64, :], in_=banks[5][0:64, :], func=AF.Copy), reads=[BK(5)], writes=[AR("bcsb")])
                P.op("dve", lambda e: e.tensor_tensor(out=attT[0:64, hidx, :], in0=banks[ob][0:64, :], in1=bcsb[0:64, :], op=ALU.mult),
                     reads=[BK(ob), AR("bcsb")], writes=[AR(f"attT{hidx}")])

            def emit_pv(blk_):
                (kind, hidx, j, col0, pk, ob, first, last) = blk_
                if kind == "f":
                    P.op("pe", lambda e: e.matmul(banks[ob][0:65, col0:T], lhsT=VP[l][:, j, hidx, :], rhs=PT[pk][:, col0:T], start=first, stop=last),
                         reads=[R(f"VP{l}"), R(f"VPones{l}"), AR(f"PT{pk}")], writes=[BK(ob)])
                else:
                    P.op("pe", lambda e: e.matmul(banks[ob][0:65, :], lhsT=VMP[l][:, j, hidx - 6, :], rhs=PT[pk][:, :], start=first, stop=last),
                         reads=[R(f"VMP{l}"), R(f"VMPones{l}"), AR(f"PT{pk}")], writes=[BK(ob)])
                if last:
                    while deferred:
                        deferred.pop(0)[1]()
                    norm_part1(ob)
                    deferred.append([4, lambda ob=ob, hidx=hidx: norm_part2(ob, hidx)])

            def push(blk_):
                pend.append(blk_)
                if len(pend) > LOOK:
                    emit_pv(pend.pop(0))
                tick()

            for h in range(6):
                p, r0 = h // 2, 64 * (h % 2)
                par = h % 2
                br = AR(f"beta{par}")
                P.op("dve", lambda e, par=par, h=h: e.tensor_scalar(out=beta[:, par, 0:nj], in0=CALL[l][:, 0:nj, h],
                                                                     scalar1=EALL[l][:, 4 * g + 3, h:h + 1], scalar2=None, op0=ALU.subtract),
                     reads=[R(f"CE{l}")], writes=[br])
                dr = AR(f"daug{par}")
                P.op("dve", lambda e, par=par, h=h: e.tensor_scalar(out=stat[:, 8 + 4 * par:12 + 4 * par], in0=EALL[l][:, 4 * g:4 * g + 4, h],
                                                                     scalar1=EALL[l][:, 4 * g + 3, h:h + 1], scalar2=-1.0 / 128, op0=ALU.subtract, op1=ALU.mult),
                     reads=[R(f"CE{l}")], writes=[R(f"dtmp{par}")])
                P.op("dve", lambda e, par=par: e.tensor_copy(out=daug[:, par, :].rearrange("p (a b) -> p a b", b=128),
                                                              in_=stat[:, 8 + 4 * par:12 + 4 * par].unsqueeze(2).to_broadcast([128, 4, 128])),
                     reads=[R(f"dtmp{par}")], writes=[dr])
                ob = 3 + (h % 2)
                for j in range(nj):
                    il0 = max(0, j - 4 * g)
                    col0 = il0 * 128
                    sbk = sb_rr[0] % 3
                    sb_rr[0] += 1
                    pk = pt_rr[0] % NPT
                    pt_rr[0] += 1
                    P.op("pe", lambda e, j=j, col0=col0, sbk=sbk, p=p, h=h: e.matmul(banks[sbk][:, col0:T], lhsT=KT[l][:, p, j * 128:(j + 1) * 128],
                                                                                        rhs=QT[:, h, col0:T], start=True, stop=False),
                         reads=[ktres[p], AR(f"QT{p}")], writes=[BK(sbk)])
                    P.op("pe", lambda e, col0=col0, sbk=sbk, par=par: e.matmul(banks[sbk][:, col0:T], lhsT=onesb[:], rhs=daug[:, par, col0:T], start=False, stop=True),
                         reads=[R("onesb"), dr], writes=[BK(sbk)])
                    P.op("act", lambda e, j=j, col0=col0, sbk=sbk, pk=pk, par=par: e.activation(out=PT[pk][:, col0:T], in_=banks[sbk][:, col0:T],
                                                                                                func=AF.Exp, bias=beta[:, par, j:j + 1]),
                         reads=[BK(sbk), br], writes=[AR(f"PT{pk}")])
                    if j >= 4 * g:
                        P.op("dve", lambda e, il0=il0, pk=pk: e.tensor_tensor(out=PT[pk][:, il0 * 128:(il0 + 1) * 128], in0=PT[pk][:, il0 * 128:(il0 + 1) * 128],
                                                                                in1=triu[:], op=ALU.mult),
                             reads=[AR(f"PT{pk}"), R("triu")], writes=[AR(f"PT{pk}")])
                    push(("f", h, j, col0, pk, ob, j == 0, j == nj - 1))
            for hm in range(4):
                p, r0 = hm // 2, 64 * (hm % 2)
                ob = 3 + (hm % 2)
                for jm in range(2):
                    sbk = sb_rr[0] % 3
                    sb_rr[0] += 1
                    pk = pt_rr[0] % NPT
                    pt_rr[0] += 1
                    P.op("pe", lambda e, jm=jm, sbk=sbk, p=p, r0=r0: e.matmul(banks[sbk][:, :], lhsT=KMT[l][r0:r0 + 64, p, jm * 128:(jm + 1) * 128],
                                                                               rhs=QMT[r0:r0 + 64, p, :], start=True, stop=True),
                         reads=[R(f"KMT{l}"), AR(f"QMT{p}")], writes=[BK(sbk)])
                    P.op("act", lambda e, sbk=sbk, pk=pk: e.activation(out=PT[pk][:, :], in_=banks[sbk][:, :], func=AF.Exp),
                         reads=[BK(sbk)], writes=[AR(f"PT{pk}")])
                    push(("m", 6 + hm, jm, 0, pk, ob, jm == 0, jm == 1))
            while pend:
                emit_pv(pend.pop(0))
                tick()
            while deferred:
                deferred.pop(0)[1]()
            if dbg == "att":
                raise StopBuild()
            sg = load_slot(lambda s: s[:, 0:3 * D].rearrange("p (c n) -> p c n", n=D), wout_d[l][0:384, :].rearrange("(c p) n -> p c n", p=128))
            wo_g = slots[sg][:, 0:3 * D].rearrange("p (c n) -> p c n", n=D)
            wo_h = []
            for (h0, nh) in ((0, 4), (4, 4), (8, 2)):
                si = load_slot(lambda s, nh=nh: s[0:64, 0:nh * D].rearrange("p (c n) -> p c n", n=D),
                               wout_d[l][384 + 64 * h0:384 + 64 * (h0 + nh), :].rearrange("(c p) n -> p c n", p=64))
                v = slots[si][0:64, 0:nh * D].rearrange("p (c n) -> p c n", n=D)
                for k in range(nh):
                    wo_h.append((si, v, k))
            attres = [AR(f"attT{i}") for i in range(10)]
            for tt in range(4):
                yb = [next_bank(0, 4), next_bank(0, 4)]
                for half in range(2):
                    b = yb[half]
                    for c in range(3):
                        P.op("pe", lambda e, c=c, tt=tt, half=half, b=b: e.matmul(banks[b][:, :], lhsT=gmT[:, c, tt * 128:(tt + 1) * 128],
                                                                                   rhs=wo_g[:, c, half * 512:(half + 1) * 512], start=(c == 0), stop=False),
                             reads=[AR(f"gmT{tt}"), R(f"slot{sg}")], writes=[BK(b)])
                    for hh in range(10):
                        si, v, k = wo_h[hh]
                        P.op("pe", lambda e, hh=hh, tt=tt, half=half, b=b, v=v, k=k: e.matmul(banks[b][:, :], lhsT=attT[0:64, hh, tt * 128:(tt + 1) * 128],
                                                                                               rhs=v[:, k, half * 512:(half + 1) * 512], start=False, stop=(hh == 9)),
                             reads=[attres[hh], R(f"slot{si}")], writes=[BK(b)])
                post_norm(yb, tt, gpm, R("gpm"), xres[tt])

        def post_norm(yb, tt, gbuf, gres, xr):
            c0 = 48 + tt * 4
            sr = R(f"statp{tt}")
            for half in range(2):
                P.op("act", lambda e, half=half: e.activation(out=ytmp_bf[:, half * 512:(half + 1) * 512], in_=banks[yb[half]][:, :], func=AF.Square,
                                                               accum_out=stat[:, c0 + half:c0 + half + 1]),
                     reads=[BK(yb[half])], writes=[sr, R("ytmp")])
            P.op("dve", lambda e: e.tensor_tensor(out=stat[:, c0 + 2:c0 + 3], in0=stat[:, c0:c0 + 1], in1=stat[:, c0 + 1:c0 + 2], op=ALU.add), reads=[sr], writes=[sr])
            rstd_from_ss(stat[:, c0 + 2:c0 + 3], stat[:, c0 + 3:c0 + 4], D, [sr], [sr])
            for half in range(2):
                P.op("dve", lambda e, half=half: e.scalar_tensor_tensor(out=ytmp, in0=banks[yb[half]][:, :], scalar=stat[:, c0 + 3:c0 + 4],
                                                                         in1=gbuf[:, half * 512:(half + 1) * 512], op0=ALU.mult, op1=ALU.mult),
                     reads=[BK(yb[half]), sr, gres], writes=[R("ytmp")])
                P.op("dve", lambda e, half=half: e.tensor_tensor(out=xg[:, tt, half * 512:(half + 1) * 512], in0=xg[:, tt, half * 512:(half + 1) * 512], in1=ytmp, op=ALU.add),
                     reads=[R("ytmp"), xr], writes=[xr])

        def ffn(g, l, last):
            xres = [R(f"xg{tt}") for tt in range(4)]
            hres = [R(f"hT{tt}") for tt in range(4)]
            P.dma("sp", lambda e: e.dma_start(out=gpf[:], in_=bcast_rows(g_postffn_t, l * D, D)), d_gpf, writes=[R("gpm")])
            norm_transpose_n([(xg[:, tt, :], xres[tt]) for tt in range(4)], GCOL[l][:, 1, :], R(f"GCOL{l}"),
                             [(hT[:, :, tt * 128:(tt + 1) * 128], hres[tt]) for tt in range(4)])
            P.fence(AG)
            for blk_i in range(8):
                si = load_slot(lambda s: s[:, :].rearrange("p (k n) -> p k n", n=512),
                               w1_d[l][:, blk_i * 512:(blk_i + 1) * 512].rearrange("(k p) n -> p k n", p=128))
                wb = slots[si][:, :].rearrange("p (k n) -> p k n", n=512)
                for fcl in range(4):
                    fc = blk_i * 4 + fcl
                    b = next_bank()
                    for kc in range(KC):
                        P.op("pe", lambda e, kc=kc, fcl=fcl, b=b, wb=wb: e.matmul(banks[b][:, :], lhsT=wb[:, kc, fcl * 128:(fcl + 1) * 128], rhs=hT[:, kc, :],
                                                                                   start=(kc == 0), stop=(kc == KC - 1)),
                             reads=hres + [R(f"slot{si}")], writes=[BK(b)])
                    rk = 0
                    P.op("act", lambda e, b=b, rk=rk: e.activation(out=rtmp[rk], in_=banks[b][:, :], func=AF.Relu), reads=[BK(b)], writes=[AR(f"rtmp{rk}")])
                    P.op("dve", lambda e, fc=fc, rk=rk: e.tensor_tensor(out=hidT[:, fc, :], in0=rtmp[rk], in1=rtmp[rk], op=ALU.mult),
                         reads=[AR(f"rtmp{rk}")], writes=[AR(f"hidT{fc}")])
            if dbg == "ffn1":
                raise StopBuild()
            for blk_i in range(8):
                si = load_slot(lambda s: s[:, :].rearrange("p (c n) -> p c n", n=D),
                               w2_d[l][blk_i * 512:(blk_i + 1) * 512, :].rearrange("(c p) n -> p c n", p=128))
                wb = slots[si][:, :].rearrange("p (c n) -> p c n", n=D)
                for fcl in range(4):
                    fc = blk_i * 4 + fcl
                    for tt in range(4):
                        for half in range(2):
                            b = tt * 2 + half
                            P.op("pe", lambda e, fc=fc, fcl=fcl, tt=tt, half=half, b=b, wb=wb: e.matmul(banks[b][:, :], lhsT=hidT[:, fc, tt * 128:(tt + 1) * 128],
                                                                                                         rhs=wb[:, fcl, half * 512:(half + 1) * 512],
                                                                                                         start=(fc == 0), stop=(fc == 31)),
                                 reads=[AR(f"hidT{fc}"), R(f"slot{si}")], writes=[BK(b)])
            for tt in range(4):
                post_norm([tt * 2, tt * 2 + 1], tt, gpf, R("gpm"), xres[tt])
                if last:
                    r0 = (4 * g + tt) * 128
                    P.dma("sp", lambda e, tt=tt, r0=r0: e.dma_start(out=out_d[r0:r0 + 128, :], in_=xg[:, tt, :]), d_o[tt],
                          reads=[xres[tt]], writes=[R(f"outd{tt}")])
            P.fence(AG)

        try:
            for l in range(L):
                layer_setup(l)
            P.fence(AG)
            if dbg == "setup":
                raise StopBuild()
            for g in range(NG):
                for tt in range(4):
                    r0 = (4 * g + tt) * 128
                    P.dma("sp", lambda e, tt=tt, r0=r0: e.dma_start(out=xg[:, tt, :], in_=x_d[r0:r0 + 128, :]), d_x[tt], writes=[R(f"xg{tt}")])
                for l in range(L):
                    mixer(g, l)
                    if dbg == "mix" or dbg == f"mix:{g}:{l}":
                        raise StopBuild()
                    ffn(g, l, last=(l == L - 1))
                    if dbg == f"ffn:{g}:{l}":
                        raise StopBuild()
        except StopBuild:
            if dbg == "win":
                P.op("dve", lambda e: e.tensor_copy(out=xg[:, 0, :].rearrange("p (c t) -> p c t", t=T), in_=QMT[:, :, :]),
                     reads=[AR("QMT0"), AR("QMT1"), R("xg0")], writes=[R("xg0")])
                P.op("dve", lambda e: e.tensor_copy(out=xg[:, 1, 0:512].rearrange("p (c t) -> p c t", t=256), in_=KMT[0][:, :, :]),
                     reads=[R("KMT0"), R("xg1")], writes=[R("xg1")])
                P.op("dve", lambda e: e.tensor_copy(out=xg[:, 2, 0:520].rearrange("p (c t) -> p c t", t=260), in_=VMP[0][:, :, :, :].rearrange("p a b c -> p a (b c)")),
                     reads=[R("VMP0"), R("VMPones0"), R("xg2")], writes=[R("xg2")])
            if dbg == "gm2":
                ar = [AR("zA"), AR("vn"), AR("tmpv2"), AR("outa"), R("WST0")] + [R(f"xg{i}") for i in range(4)]
                wr = [R(f"xg{i}") for i in range(4)]
                P.op("dve", lambda e: e.tensor_copy(out=xg[:, 0, 0:768], in_=zA), reads=ar, writes=wr)
                P.op("dve", lambda e: e.tensor_copy(out=xg[:, 1, 0:384], in_=vn), reads=ar, writes=wr)
                P.op("dve", lambda e: e.tensor_copy(out=xg[:, 1, 384:768], in_=tmpv2), reads=ar, writes=wr)
                P.op("dve", lambda e: e.tensor_copy(out=xg[:, 2, 0:384], in_=outa), reads=ar, writes=wr)
                P.op("dve", lambda e: e.tensor_copy(out=xg[:, 3, 0:768], in_=WST[0][:, :, :].rearrange("p g t -> p (g t)")), reads=ar, writes=wr)
            if dbg == "gm":
                P.op("dve", lambda e: e.tensor_copy(out=xg[:, 0, :].rearrange("p (c t) -> p c t", t=T), in_=gmT[:, 0:2, :]),
                     reads=[AR(f"gmT{i}") for i in range(4)] + [R("xg0")], writes=[R("xg0")])
                P.op("dve", lambda e: e.tensor_copy(out=xg[:, 1, 0:512], in_=gmT[:, 2, :]),
                     reads=[AR(f"gmT{i}") for i in range(4)] + [R("xg1")], writes=[R("xg1")])
            if dbg in ("att3", "att", "att5", "att4"):
                for q in range(4):
                    P.op("dve", lambda e, q=q: e.tensor_copy(out=xg[:, q, :].rearrange("p (c t) -> p c t", t=T), in_=attT[:, 2 * q:2 * q + 2, :]),
                         reads=[AR(f"attT{i}") for i in range(10)] + [R(f"xg{q}")], writes=[R(f"xg{q}")])
            for tt in range(4):
                P.dma("sp", lambda e, tt=tt: e.dma_start(out=out_d[tt * 128:(tt + 1) * 128, :], in_=xg[:, tt, :]), d_o[tt],
                      reads=[R(f"xg{tt}")], writes=[R(f"outd{tt}")])
        P.final()
        if build.want_trace:
            P.trace = []
        P.emit(st)
        build.stats = P.stats
        build.trace = P.trace
    return nc


build.want_trace = False

WNAMES = ["norm_pre_mix", "norm_post_mix", "norm_pre_ffn", "norm_post_ffn", "norm_mem", "w_in", "b_forget",
          "gmlp_v_norm", "gmlp_w_s", "gmlp_b_s", "w_mem_kv", "w_out", "w_ff1", "w_ff2"]

_cache = {}


def _consts():
    iu = np.triu(np.ones((128, 128), np.float32))
    return {
        "c_ident": np.eye(128, dtype=np.float32).astype(ml_dtypes.bfloat16),
        "c_triu": iu.astype(ml_dtypes.bfloat16),
        "c_triuf": iu.copy(),
        "c_onesf": np.ones((128, 128), np.float32),
    }


DBG = None


def run_layers(x, mem, weights, L):
    B, S, _ = x.shape
    key = (S, L)
    if key not in _cache:
        _cache[key] = build(S, L, dbg=DBG)
    nc = _cache[key]
    consts = _consts()
    in_maps = []
    for b in range(B):
        m = {"x": np.ascontiguousarray(x[b]), "mem": np.ascontiguousarray(mem[b])}
        for k in WNAMES:
            m[k] = np.ascontiguousarray(weights[k])
        m.update(consts)
        in_maps.append(m)
    res = run_bass_kernel_spmd(nc, in_maps, core_ids=list(range(B)))
    return np.stack([res.results[b]["out"] for b in range(B)], axis=0)


FUSED = True


def kernel(**inputs):
    x = np.asarray(inputs["x"], dtype=np.float32)
    mem = np.asarray(inputs["mem"], dtype=np.float32)
    W = {k: np.asarray(inputs[k], dtype=np.float32) for k in WNAMES}
    depth = W["w_in"].shape[0]
    if FUSED:
        return run_layers(x, mem, W, depth)
    for l in range(depth):
        x = run_layers(x, mem, {k: v[l:l + 1] for k, v in W.items()}, 1)
    return x
```

```python
from contextlib import ExitStack
import numpy as np
import ml_dtypes
import concourse.bass as bass
import concourse.mybir as mybir
from concourse.bass_utils import run_bass_kernel_spmd

F32 = mybir.dt.float32
BF16 = mybir.dt.bfloat16
AF = mybir.ActivationFunctionType
ALU = mybir.AluOpType
AX = mybir.AxisListType

COMPUTE = ("pe", "act", "dve", "pool")
EPS = 1e-6


class Res:
    __slots__ = ("name", "lw", "rd", "group", "excl")

    def __init__(self, name, group=None):
        self.name = name
        self.lw = None
        self.rd = []
        self.group = group
        self.excl = name.startswith("bank")


class Group:
    def __init__(self):
        self.since = []
        self.fdeps = []


class DmaSem:
    __slots__ = ("name", "total", "sem")

    def __init__(self, name):
        self.name = name
        self.total = 0
        self.sem = None


class Op:
    __slots__ = ("eng", "fn", "deps", "idx", "dma", "dma_total", "needs_inc", "cnt", "tag")

    def __init__(self, eng, fn):
        self.eng = eng
        self.fn = fn
        self.deps = []
        self.dma = None
        self.dma_total = 0
        self.needs_inc = False
        self.cnt = 0


class Prog:
    def __init__(self, nc):
        self.nc = nc
        self.ops = []
        self.dmasems = []
        self.resd = {}
        self.trace = None

    def R(self, name, group=None):
        r = self.resd.get(name)
        if r is None:
            r = Res(name, group)
            self.resd[name] = r
        return r

    def dmasem(self, name):
        d = DmaSem(name)
        self.dmasems.append(d)
        return d

    def fence(self, group):
        last = {}
        keep = []
        for o in group.since + group.fdeps:
            if o.dma is not None:
                keep.append(o)
            else:
                if o.eng not in last or last[o.eng].idx < o.idx:
                    last[o.eng] = o
        group.fdeps = keep + list(last.values())
        group.since = []

    def _track(self, op, reads, writes):
        reads = list(reads)
        writes = list(writes)
        for r in list(reads):
            if r.excl:
                reads.remove(r)
                if r not in writes:
                    writes.append(r)
        op.tag = "R:" + ",".join(r.name for r in reads) + " W:" + ",".join(w.name for w in writes)
        deps = set()
        for r in reads:
            if r.lw is not None:
                deps.add(r.lw)
        for w in writes:
            if w.lw is not None:
                deps.add(w.lw)
            for o in w.rd:
                deps.add(o)
        groups = set()
        for r in list(reads) + list(writes):
            if r.group is not None:
                groups.add(r.group)
        for g in groups:
            for o in g.fdeps:
                deps.add(o)
            g.since.append(op)
        deps.discard(op)
        best = {}
        red = []
        for d in deps:
            if d.dma is not None:
                red.append(d)
            elif d.eng not in best or best[d.eng].idx < d.idx:
                best[d.eng] = d
        deps = red + list(best.values())
        for r in reads:
            r.rd.append(op)
        for w in writes:
            w.lw = op
            w.rd = []
        op.deps = list(deps)

    def op(self, eng, fn, reads=(), writes=()):
        o = Op(eng, fn)
        o.idx = len(self.ops)
        self.ops.append(o)
        self._track(o, reads, writes)
        return o

    def dma(self, queue, fn, sem, reads=(), writes=()):
        o = Op(queue, fn)
        o.idx = len(self.ops)
        o.dma = sem
        sem.total += 16
        o.dma_total = sem.total
        self.ops.append(o)
        self._track(o, reads, writes)
        return o

    def final(self):
        o = Op("sp", lambda e: e.nop())
        o.idx = len(self.ops)
        o.tag = "final"
        last = {}
        for p in self.ops:
            if p.dma is not None:
                last[("d", p.dma.name)] = p
            elif p.eng in COMPUTE:
                last[("e", p.eng)] = p
        o.deps = list(last.values())
        self.ops.append(o)

    def emit(self, stack):
        nc = self.nc
        for o in self.ops:
            for d in o.deps:
                if d.dma is None:
                    if d.eng == "pe" and o.eng == "pe" and o.dma is None:
                        continue
                    d.needs_inc = True
        sems = {}
        for e in COMPUTE:
            sems[e] = stack.enter_context(nc.semaphore("s_" + e))
        for d in self.dmasems:
            if d.total > 0:
                d.sem = stack.enter_context(nc.semaphore("d_" + d.name))
        cnt = {e: 0 for e in COMPUTE}
        for o in self.ops:
            if o.dma is None and o.needs_inc:
                cnt[o.eng] += 1
                o.cnt = cnt[o.eng]
        per_eng = {e: [] for e in ("pe", "act", "dve", "pool", "sp")}
        for o in self.ops:
            per_eng[o.eng].append(o)
        self.stats = {e: len(v) for e, v in per_eng.items()}
        self.stats["incs"] = dict(cnt)

        def run_engine(ename, eng):
            seen = {}
            nw = 0
            for o in per_eng[ename]:
                need = {}
                for d in o.deps:
                    if d.dma is not None:
                        key = ("d", d.dma.name)
                        val = d.dma_total
                        semh = d.dma.sem
                    else:
                        if d.eng == "pe" and ename == "pe" and o.dma is None:
                            continue
                        key = ("e", d.eng)
                        val = d.cnt
                        semh = sems[d.eng]
                    if seen.get(key, 0) >= val:
                        continue
                    if key not in need or need[key][1] < val:
                        need[key] = (semh, val)
                for key, (semh, val) in need.items():
                    eng.wait_ge(semh, val)
                    seen[key] = val
                    nw += 1
                if self.trace is not None:
                    self.trace.append((ename, o.idx, [(k, v[1]) for k, v in need.items()], o.cnt if o.needs_inc else None, o.dma_total if o.dma else None, o.tag))
                ins = o.fn(eng)
                if o.dma is not None:
                    ins.then_inc(o.dma.sem, 16)
                elif o.needs_inc:
                    ins.then_inc(sems[ename], 1)
            self.stats["waits_" + ename] = nw

        with nc.Block() as block:
            @block.tensor
            def _(e):
                run_engine("pe", e)

            @block.scalar
            def _(e):
                run_engine("act", e)

            @block.vector
            def _(e):
                run_engine("dve", e)

            @block.gpsimd
            def _(e):
                run_engine("pool", e)

            @block.sync
            def _(e):
                run_engine("sp", e)


D = 1024
KC = 8
DIN = 2182
DFF = 4096
NMEM = 256
T = 512
NSLOT = 4

WIN_BLOCKS = [(0, 512), (512, 512), (1024, 512), (1536, 390), (1926, 256)]


class StopBuild(Exception):
    pass


def build(S, L, dbg=None):
    NT = S // 128
    NG = S // T
    nc = bass.Bass("TRN2", target_bir_lowering=False)

    def din(name, shape, dt=F32):
        return nc.dram_tensor(name, shape, dt, kind="ExternalInput")

    x_t = din("x", [S, D])
    mem_t = din("mem", [NMEM, D])
    g_premix_t = din("norm_pre_mix", [L, D])
    g_postmix_t = din("norm_post_mix", [L, D])
    g_preffn_t = din("norm_pre_ffn", [L, D])
    g_postffn_t = din("norm_post_ffn", [L, D])
    g_mem_t = din("norm_mem", [L, D])
    w_in_t = din("w_in", [L, D, DIN])
    b_forget_t = din("b_forget", [L, 6])
    gv_t = din("gmlp_v_norm", [L, 384])
    ws_t = din("gmlp_w_s", [L, 6, 128, 128])
    bs_t = din("gmlp_b_s", [L, 6, 128])
    wkv_t = din("w_mem_kv", [L, D, 512])
    wout_t = din("w_out", [L, D, D])
    w1_t = din("w_ff1", [L, D, DFF])
    w2_t = din("w_ff2", [L, DFF, D])
    ident_t = din("c_ident", [128, 128], BF16)
    triu_t = din("c_triu", [128, 128], BF16)
    triuf_t = din("c_triuf", [128, 128], F32)
    onesf_t = din("c_onesf", [128, 128], F32)
    out_t = nc.dram_tensor("out", [S, D], F32, kind="ExternalOutput")

    x_d, mem_d, out_d = x_t.ap(), mem_t.ap(), out_t.ap()
    w_in_d, wkv_d, wout_d, w1_d, w2_d = w_in_t.ap(), wkv_t.ap(), wout_t.ap(), w1_t.ap(), w2_t.ap()
    ws_d = ws_t.ap()

    with ExitStack() as st:
        P = Prog(nc)
        R = P.R

        def sb(name, shape, dt):
            return st.enter_context(nc.sbuf_tensor(name, shape, dt))

        KT = [sb(f"KT{l}", [128, 3, S], BF16) for l in range(L)]
        VP = [sb(f"VP{l}", [128, NT, 6, 65], BF16) for l in range(L)]
        CALL = [sb(f"CALL{l}", [128, NT, 6], F32) for l in range(L)]
        EALL = [sb(f"EALL{l}", [128, NT, 6], F32) for l in range(L)]
        KMT = [sb(f"KMT{l}", [128, 2, NMEM], BF16) for l in range(L)]
        VMP = [sb(f"VMP{l}", [128, 2, 4, 65], BF16) for l in range(L)]
        WST = [sb(f"WST{l}", [128, 6, 128], BF16) for l in range(L)]
        GCOL = [sb(f"GCOL{l}", [128, 3, 8], F32) for l in range(L)]
        BS = [sb(f"BS{l}", [128, 6], F32) for l in range(L)]
        BFG = [sb(f"BFG{l}", [128, 6], F32) for l in range(L)]
        gpm = sb("gpm", [128, D], F32)
        gpf = gpm
        xg = sb("xg", [128, 4, D], F32)
        hT = sb("hT", [128, KC, T], BF16)
        onesb = sb("onesb", [128, 128], BF16)
        stat = sb("stat", [128, 64], F32)
        ident = sb("ident", [128, 128], BF16)
        triu = sb("triu", [128, 128], BF16)
        triuf = sb("triuf", [128, 128], F32)
        onesf = sb("onesf", [128, 128], F32)
        slots = [sb(f"slot{i}", [128, 4096], BF16) for i in range(NSLOT)]
        ARENA_BF = 19680
        arena = sb("arena", [128, ARENA_BF], BF16)
        AG = Group()

        class Carver:
            def __init__(self):
                self.off = 0

            def take(self, nbf):
                o = self.off
                self.off += nbf
                assert self.off <= ARENA_BF, self.off
                return o

        cv = Carver()

        def a_bf(n):
            o = cv.take(n)
            return arena[:, o:o + n]

        def a_f32(n):
            o = cv.take(2 * n)
            return arena[:, o:o + 2 * n].bitcast(F32)

        QT = a_bf(6 * T).rearrange("p (c t) -> p c t", t=T)
        QMT = a_bf(2 * T).rearrange("p (c t) -> p c t", t=T)
        gmT = a_bf(3 * T).rearrange("p (c t) -> p c t", t=T)
        attT = a_bf(10 * T).rearrange("p (c t) -> p c t", t=T)
        NPT = 3
        PT_OFF = cv.off
        PT = [a_bf(T) for _ in range(NPT)]
        xnb = [arena[:, PT_OFF:PT_OFF + D]]
        zA = a_f32(768)
        tmpv2 = a_f32(384)
        vn = a_bf(384)
        outa = a_bf(384)
        beta = a_f32(2 * 32).rearrange("p (a c) -> p a c", a=2)
        daug = a_bf(2 * T).rearrange("p (a t) -> p a t", a=2)
        rs = a_f32(T)
        tmpv = rs[:, 0:384]
        bcsb = a_f32(T)
        GVS = bcsb[:, 0:384]
        fg = a_f32(24).rearrange("p (a b) -> p a b", b=6)
        spb = a_f32(24).rearrange("p (a b) -> p a b", b=6)
        att_end = cv.off
        cv.off = 0
        hidT = a_bf(32 * T).rearrange("p (c t) -> p c t", t=T)
        rtmp = [a_f32(T)]
        cv.off = max(cv.off, att_end)
        _yo = cv.take(2 * T)
        ytmp_bf = arena[:, _yo:_yo + 2 * T]
        ytmp = ytmp_bf.bitcast(F32)

        def AR(name):
            return R(name, AG)

        banks = [st.enter_context(nc.psum_tensor(f"bank{i}", [128, 512], F32)) for i in range(8)]
        banks_bf = [b[:, :].bitcast(BF16) for b in banks]

        def BK(i):
            return R(f"bank{i}")

        d_setup = P.dmasem("setup")
        d_x = [P.dmasem(f"x{i}") for i in range(4)]
        d_o = [P.dmasem(f"o{i}") for i in range(4)]
        d_slot = [P.dmasem(f"sl{i}") for i in range(NSLOT)]
        d_ws = P.dmasem("ws")
        d_gpm = P.dmasem("gpm")
        d_gpf = d_gpm
        d_gv = P.dmasem("gv")

        slot_ctr = [0]

        def load_slot(dst_fn, src, reads=()):
            i = slot_ctr[0] % NSLOT
            slot_ctr[0] += 1
            dst = dst_fn(slots[i])
            P.dma("pool", lambda e, dst=dst, src=src: e.dma_start(out=dst, in_=src), d_slot[i],
                  reads=list(reads), writes=[R(f"slot{i}")])
            return i

        def bcast_rows(t, row_off, n):
            return bass.AP(t, row_off, [[0, 128], [1, n]])

        setup_res = []

        def setup_dma(dst, src, resname, **kw):
            P.dma("sp", lambda e, dst=dst, src=src, kw=kw: e.dma_start(out=dst, in_=src, **kw), d_setup, writes=[R(resname)])
            setup_res.append(R(resname))

        setup_dma(ident[:], ident_t.ap(), "ident")
        setup_dma(triu[:], triu_t.ap(), "triu")
        setup_dma(triuf[:], triuf_t.ap(), "triuf")
        setup_dma(onesf[:], onesf_t.ap(), "onesf")
        for l in range(L):
            for k, gt in enumerate((g_premix_t, g_preffn_t, g_mem_t)):
                setup_dma(GCOL[l][:, k, :], bass.AP(gt, l * D, [[1, 128], [128, 8]]), f"GCOL{l}", allow_slow_non_contiguous=True)
            setup_dma(BFG[l][:], bcast_rows(b_forget_t, l * 6, 6), f"BFG{l}")
            setup_dma(BS[l][:], bass.AP(bs_t, l * 768, [[1, 128], [128, 6]]), f"BS{l}", allow_slow_non_contiguous=True)
        last_setup = P.ops[-1]
        for r in setup_res:
            r.lw = last_setup
            r.rd = []
        P.op("dve", lambda e: e.memset(onesb[:], 1.0), writes=[R("onesb")])
        P.op("dve", lambda e: e.memset(stat[:], 1.0e6), writes=[R("statn"), R("dtmp0"), R("dtmp1")] + [R(f"statn{i}") for i in range(4)] + [R(f"statv{i}") for i in range(4)] + [R(f"statp{i}") for i in range(4)])
        for l in range(L):
            P.op("dve", lambda e, l=l: e.memset(VP[l][:, :, :, 64:65], 1.0), writes=[R(f"VPones{l}")])
            P.op("dve", lambda e, l=l: e.memset(VMP[l][:, :, :, 64:65], 1.0), writes=[R(f"VMPones{l}")])

        def rstd_from_ss(ss_ap, out_ap, n, rd, wr):
            P.op("act", lambda e: e.activation(out=out_ap, in_=ss_ap, func=AF.Ln, scale=1.0 / n, bias=EPS), reads=rd, writes=wr)
            P.op("act", lambda e: e.activation(out=out_ap, in_=out_ap, func=AF.Exp, scale=-0.5), reads=wr, writes=wr)

        tbank_ctr = [0]

        def norm_transpose_n(srcs, gcol_ap, gres, dsts, batched=True, base=0):
            if not batched and len(srcs) > 1:
                for i in range(len(srcs)):
                    norm_transpose_n(srcs[i:i + 1], gcol_ap, gres, dsts[i:i + 1], batched=True, base=i)
                return
            n = len(srcs)
            ssr = R("statn" if n > 1 else f"statn{base}")
            for i, (src_ap, src_res) in enumerate(srcs):
                i = i + base
                P.op("act", lambda e, i=i, src_ap=src_ap: e.activation(out=xnb[0], in_=src_ap, func=AF.Square, accum_out=stat[:, i:i + 1]),
                     reads=[src_res], writes=[ssr] + ([AR("PT0"), AR("PT1")] if i == base else []))
            rstd_from_ss(stat[:, base:base + n], stat[:, 4 + base:4 + base + n], D, [ssr], [ssr])
            for i, ((src_ap, src_res), (dst_ap, dst_res)) in enumerate(zip(srcs, dsts)):
                i = i + base
                P.op("dve", lambda e, i=i, src_ap=src_ap: e.tensor_scalar(out=xnb[0], in0=src_ap, scalar1=stat[:, 4 + i:5 + i], scalar2=None, op0=ALU.mult),
                     reads=[src_res, ssr], writes=[AR("PT0"), AR("PT1")])
                tb = 6 + (tbank_ctr[0] % 2)
                tbank_ctr[0] += 1
                for kc in range(KC):
                    P.op("pe", lambda e, kc=kc, tb=tb: e.transpose(out=banks_bf[tb][:, kc * 128:(kc + 1) * 128], in_=xnb[0][:, kc * 128:(kc + 1) * 128], identity=ident[:]),
                         reads=[AR("PT0"), AR("PT1"), R("ident")], writes=[BK(tb)])
                P.op("dve", lambda e, tb=tb, dst_ap=dst_ap: e.tensor_tensor(out=dst_ap, in0=banks_bf[tb][:, 0:1024].rearrange("p (k t) -> p k t", t=128),
                                                                             in1=gcol_ap.unsqueeze(2).to_broadcast([128, KC, 128]), op=ALU.mult),
                     reads=[BK(tb), gres], writes=[dst_res])

        bank_rr = [0]

        def next_bank(lo=0, hi=6):
            b = lo + (bank_rr[0] % (hi - lo))
            bank_rr[0] += 1
            return b

        def layer_setup(l):
            for mt in range(2):
                P.dma("sp", lambda e, mt=mt: e.dma_start(out=xg[:, mt, :], in_=mem_d[mt * 128:(mt + 1) * 128, :]), d_x[mt], writes=[R(f"xg{mt}")])
            norm_transpose_n([(xg[:, mt, :], R(f"xg{mt}")) for mt in range(2)], GCOL[l][:, 2, :], R(f"GCOL{l}"),
                             [(hT[:, :, mt * 128:(mt + 1) * 128], R(f"hT{mt}")) for mt in range(2)])
            si = load_slot(lambda s: s[:, :].rearrange("p (k n) -> p k n", n=512), wkv_d[l].rearrange("(k p) n -> p k n", p=128))
            wkv = slots[si][:, :].rearrange("p (k n) -> p k n", n=512)
            hres = [R("hT0"), R("hT1")]
            for pm in range(2):
                b = next_bank()
                for kc in range(KC):
                    P.op("pe", lambda e, kc=kc, pm=pm, b=b: e.matmul(banks[b][:, 0:NMEM], lhsT=wkv[:, kc, pm * 128:(pm + 1) * 128], rhs=hT[:, kc, 0:NMEM],
                                                                      start=(kc == 0), stop=(kc == KC - 1)),
                         reads=[R(f"slot{si}")] + hres, writes=[BK(b)])
                P.op("act", lambda e, pm=pm, b=b: e.activation(out=KMT[l][:, pm, :], in_=banks[b][:, 0:NMEM], func=AF.Copy),
                     reads=[BK(b)], writes=[R(f"KMT{l}")])
            for mt in range(2):
                b = next_bank()
                for kc in range(KC):
                    P.op("pe", lambda e, kc=kc, mt=mt, b=b: e.matmul(banks[b][:, 0:256], lhsT=hT[:, kc, mt * 128:(mt + 1) * 128], rhs=wkv[:, kc, 256:512],
                                                                      start=(kc == 0), stop=(kc == KC - 1)),
                         reads=[R(f"slot{si}"), hres[mt]], writes=[BK(b)])
                P.op("act", lambda e, mt=mt, b=b: e.activation(out=VMP[l][:, mt, :, 0:64], in_=banks[b][:, 0:256].rearrange("p (h d) -> p h d", d=64), func=AF.Copy),
                     reads=[BK(b)], writes=[R(f"VMP{l}")])
            P.dma("sp", lambda e: e.dma_start(out=zA.rearrange("p (g s) -> p g s", s=128), in_=ws_d[l].rearrange("g t s -> t g s")), d_ws,
                  writes=[AR("zA")])
            wsb = xnb[0][:, 0:768]
            P.op("dve", lambda e: e.tensor_copy(out=wsb, in_=zA), reads=[AR("zA")], writes=[AR("PT0"), AR("PT1")])
            tb = 6 + (tbank_ctr[0] % 2)
            tbank_ctr[0] += 1
            for gg in range(6):
                P.op("pe", lambda e, gg=gg: e.transpose(out=banks_bf[tb][:, gg * 128:(gg + 1) * 128], in_=wsb[:, gg * 128:(gg + 1) * 128], identity=ident[:]),
                     reads=[AR("PT0"), AR("PT1"), R("ident")], writes=[BK(tb)])
            P.op("dve", lambda e: e.tensor_tensor(out=WST[l][:], in0=banks_bf[tb][:, 0:768].rearrange("p (g t) -> p g t", t=128),
                                                  in1=triu[:].unsqueeze(1).to_broadcast([128, 6, 128]), op=ALU.mult),
                 reads=[BK(tb), R("triu")], writes=[R(f"WST{l}")])

        def mixer(g, l):
            xres = [R(f"xg{tt}") for tt in range(4)]
            hres = [R(f"hT{tt}") for tt in range(4)]
            P.dma("sp", lambda e: e.dma_start(out=gpm[:], in_=bcast_rows(g_postmix_t, l * D, D)), d_gpm, writes=[R("gpm")])
            P.dma("sp", lambda e: e.dma_start(out=GVS, in_=bcast_rows(gv_t, l * 384, 384)), d_gv, writes=[AR("bcsb")])
            norm_transpose_n([(xg[:, tt, :], xres[tt]) for tt in range(4)], GCOL[l][:, 0, :], R(f"GCOL{l}"),
                             [(hT[:, :, tt * 128:(tt + 1) * 128], hres[tt]) for tt in range(4)], batched=(l == 0))
            if dbg == "norm":
                raise StopBuild()
            blk = [None] * 5

            def load_blk(bi):
                c0, ncol = WIN_BLOCKS[bi]
                si = load_slot(lambda s, ncol=ncol: s[:, 0:KC * ncol].rearrange("p (k n) -> p k n", n=ncol),
                               w_in_d[l][:, c0:c0 + ncol].rearrange("(k p) n -> p k n", p=128))
                blk[bi] = (si, slots[si][:, 0:KC * ncol].rearrange("p (k n) -> p k n", n=ncol))

            load_blk(0)
            load_blk(1)
            load_blk(2)
            load_blk(4)
            P.op("dve", lambda e: e.memset(QT[:, :, :], 0.0), writes=[AR("QTz")] + [AR(f"QT{c}") for c in range(3)])
            P.op("dve", lambda e: e.memset(attT[64:128, :, :], 0.0), writes=[AR("attTz")])
            zAb = [zA, arena[:, PT_OFF:PT_OFF + 1536].bitcast(F32)]
            zAr = [[AR("zA")], [AR("PT0"), AR("PT1"), AR("PT2")]]

            def a_mm(tt):
                zi = tt % 2
                ba, bb = next_bank(), next_bank()
                for kc in range(KC):
                    P.op("pe", lambda e, kc=kc: e.matmul(banks[ba][:, :], lhsT=hT[:, kc, tt * 128:(tt + 1) * 128], rhs=blk[0][1][:, kc, :],
                                                          start=(kc == 0), stop=(kc == KC - 1)),
                         reads=[hres[tt], R(f"slot{blk[0][0]}")], writes=[BK(ba)])
                for kc in range(KC):
                    P.op("pe", lambda e, kc=kc: e.matmul(banks[bb][:, 0:256], lhsT=hT[:, kc, tt * 128:(tt + 1) * 128], rhs=blk[1][1][:, kc, 0:256],
                                                          start=(kc == 0), stop=(kc == KC - 1)),
                         reads=[hres[tt], R(f"slot{blk[1][0]}")], writes=[BK(bb)])
                P.op("act", lambda e: e.activation(out=zAb[zi][:, 0:512], in_=banks[ba][:, :], func=AF.Gelu_apprx_tanh), reads=[BK(ba)], writes=zAr[zi])
                P.op("act", lambda e: e.activation(out=zAb[zi][:, 512:768], in_=banks[bb][:, 0:256], func=AF.Gelu_apprx_tanh), reads=[BK(bb)], writes=zAr[zi])

            def g_chain(tt):
                zi = tt % 2
                zz, zr = zAb[zi], zAr[zi]
                v3 = zz[:, 384:768].rearrange("p (g d) -> p g d", d=64)
                P.op("dve", lambda e: e.tensor_tensor(out=tmpv, in0=zz[:, 384:768], in1=zz[:, 384:768], op=ALU.mult), reads=zr, writes=[AR("rs")])
                c0 = 16 + tt * 8
                sr = R(f"statv{tt}")
                P.op("dve", lambda e: e.reduce_sum(out=stat[:, c0:c0 + 6], in_=tmpv.rearrange("p (g d) -> p g d", d=64), axis=AX.X),
                     reads=[AR("rs")], writes=[sr])
                rstd_from_ss(stat[:, c0:c0 + 6], stat[:, c0:c0 + 6], 64, [sr], [sr])
                P.op("dve", lambda e: e.tensor_tensor(out=tmpv.rearrange("p (g d) -> p g d", d=64), in0=v3,
                                                      in1=stat[:, c0:c0 + 6].unsqueeze(2).to_broadcast([128, 6, 64]), op=ALU.mult),
                     reads=zr + [sr], writes=[AR("rs")])
                P.op("dve", lambda e: e.tensor_tensor(out=vn, in0=tmpv, in1=GVS, op=ALU.mult), reads=[AR("rs"), AR("bcsb")], writes=[AR("vn")])
                bm = next_bank()
                for gg in range(6):
                    P.op("pe", lambda e, gg=gg: e.matmul(banks[bm][:, gg * 64:(gg + 1) * 64], lhsT=WST[l][:, gg, :], rhs=vn[:, gg * 64:(gg + 1) * 64],
                                                          start=True, stop=True),
                         reads=[AR("vn"), R(f"WST{l}")], writes=[BK(bm)])
                P.op("dve", lambda e: e.tensor_tensor(out=tmpv2.rearrange("p (g d) -> p g d", d=64), in0=banks[bm][:, 0:384].rearrange("p (g d) -> p g d", d=64),
                                                      in1=BS[l][:].unsqueeze(2).to_broadcast([128, 6, 64]), op=ALU.add),
                     reads=[BK(bm), R(f"BS{l}")], writes=[AR("tmpv2")])
                P.op("dve", lambda e: e.tensor_tensor(out=outa, in0=tmpv2, in1=zz[:, 0:384], op=ALU.mult), reads=[AR("tmpv2")] + zr, writes=[AR("outa")])
                tb = 6 + (tbank_ctr[0] % 2)
                tbank_ctr[0] += 1
                for c in range(3):
                    P.op("pe", lambda e, c=c: e.transpose(out=banks_bf[tb][:, c * 128:(c + 1) * 128], in_=outa[:, c * 128:(c + 1) * 128], identity=ident[:]),
                         reads=[AR("outa"), R("ident")], writes=[BK(tb)])
                P.op("act", lambda e: e.activation(out=gmT[:, :, tt * 128:(tt + 1) * 128], in_=banks_bf[tb][:, 0:384].rearrange("p (c t) -> p c t", t=128), func=AF.Copy),
                     reads=[BK(tb)], writes=[AR(f"gmT{tt}")])

            fm = [(1, 256, "q", 0), (1, 384, "q", 1), (2, 0, "q", 2),
                  (2, 128, "k", 0), (2, 256, "k", 1), (2, 384, "k", 2),
                  (4, 0, "m", 0), (4, 128, "m", 1)]

            def fm_chunk(bi, lc, kind, c):
                b = next_bank()
                for kc in range(KC):
                    P.op("pe", lambda e, kc=kc: e.matmul(banks[b][:, :], lhsT=blk[bi][1][:, kc, lc:lc + 128], rhs=hT[:, kc, :],
                                                          start=(kc == 0), stop=(kc == KC - 1)),
                         reads=hres + [R(f"slot{blk[bi][0]}")], writes=[BK(b)])
                if kind == "q":
                    P.op("act", lambda e: e.activation(out=QT[0:64, 2 * c, :], in_=banks[b][0:64, :], func=AF.Copy, scale=0.125), reads=[BK(b), AR("QTz")], writes=[AR(f"QT{c}")])
                    P.op("act", lambda e: e.activation(out=QT[64:128, 2 * c + 1, :], in_=banks[b][64:128, :], func=AF.Copy, scale=0.125), reads=[BK(b), AR("QTz")], writes=[AR(f"QT{c}")])
                elif kind == "m":
                    P.op("act", lambda e: e.activation(out=QMT[:, c, :], in_=banks[b][:, :], func=AF.Copy, scale=0.125), reads=[BK(b)], writes=[AR(f"QMT{c}")])
                else:
                    P.op("dve", lambda e: e.tensor_copy(out=KT[l][:, c, g * T:(g + 1) * T], in_=banks[b][:, :]), reads=[BK(b)], writes=[R(f"KT{l}_{c}")])

            def c_tile(tt):
                b = next_bank()
                for kc in range(KC):
                    P.op("pe", lambda e, kc=kc: e.matmul(banks[b][:, 0:390], lhsT=hT[:, kc, tt * 128:(tt + 1) * 128], rhs=blk[3][1][:, kc, :],
                                                          start=(kc == 0), stop=(kc == KC - 1)),
                         reads=[hres[tt], R(f"slot{blk[3][0]}")], writes=[BK(b)])
                P.op("act", lambda e: e.activation(out=VP[l][:, 4 * g + tt, :, 0:64], in_=banks[b][:, 0:384].rearrange("p (h d) -> p h d", d=64), func=AF.Copy),
                     reads=[BK(b)], writes=[R(f"VP{l}")])
                P.op("dve", lambda e: e.tensor_tensor(out=fg[:, tt, :], in0=banks[b][:, 384:390], in1=BFG[l][:], op=ALU.add),
                     reads=[BK(b), R(f"BFG{l}")], writes=[AR("fg")])

            a_mm(0)
            a_mm(1)
            for q in fm[0:4]:
                fm_chunk(*q)
            g_chain(0)
            a_mm(2)
            for q in fm[4:8]:
                fm_chunk(*q)
            g_chain(1)
            a_mm(3)
            load_blk(3)
            g_chain(2)
            for tt in range(4):
                c_tile(tt)
            g_chain(3)
            P.op("act", lambda e: e.activation(out=spb, in_=fg, func=AF.Exp, scale=-1.0), reads=[AR("fg")], writes=[AR("spb")])
            P.op("act", lambda e: e.activation(out=spb, in_=spb, func=AF.Ln, bias=1.0), reads=[AR("spb")], writes=[AR("spb")])
            for tt in range(4):
                ti = 4 * g + tt
                b = next_bank()
                P.op("pe", lambda e, b=b, tt=tt: e.matmul(banks[b][:, 0:6], lhsT=triuf[:], rhs=spb[:, tt, :], start=True, stop=True),
                     reads=[AR("spb"), R("triuf")], writes=[BK(b)])
                P.op("pe", lambda e, b=b, tt=tt: e.matmul(banks[b][:, 8:14], lhsT=onesf[:], rhs=spb[:, tt, :], start=True, stop=True),
                     reads=[AR("spb"), R("onesf")], writes=[BK(b)])
                cr = R(f"CE{l}")
                if ti == 0:
                    P.op("dve", lambda e, b=b, ti=ti: e.tensor_copy(out=CALL[l][:, ti, :], in_=banks[b][:, 0:6]), reads=[BK(b)], writes=[cr])
                    P.op("dve", lambda e, b=b, ti=ti: e.tensor_copy(out=EALL[l][:, ti, :], in_=banks[b][:, 8:14]), reads=[BK(b)], writes=[cr])
                else:
                    P.op("dve", lambda e, b=b, ti=ti: e.tensor_tensor(out=CALL[l][:, ti, :], in0=banks[b][:, 0:6], in1=EALL[l][:, ti - 1, :], op=ALU.add),
                         reads=[BK(b), cr], writes=[cr])
                    P.op("dve", lambda e, b=b, ti=ti: e.tensor_tensor(out=EALL[l][:, ti, :], in0=banks[b][:, 8:14], in1=EALL[l][:, ti - 1, :], op=ALU.add),
                         reads=[BK(b), cr], writes=[cr])

            if dbg in ("win", "gm", "gm2"):
                raise StopBuild()
            nj = 4 * g + 4
            sb_rr = [0]
            pt_rr = [0]
            ktres = [R(f"KT{l}_{c}") for c in range(3)]

            def normalize(ob, hidx):
                P.op("dve", lambda e: e.reciprocal(out=rs[64:65, :], in_=banks[ob][64:65, :]), reads=[BK(ob)], writes=[AR("rs")])
                P.op("pe", lambda e: e.matmul(banks[5][0:64, :], lhsT=onesf[64:65, 0:64], rhs=rs[64:65, :], start=True, stop=True),
                     reads=[AR("rs"), R("onesf")], writes=[BK(5)])
                P.op("act", lambda e: e.activation(out=bcsb[0:64, :], in_=banks[5][0:64, :], func=AF.Copy), reads=[BK(5)], writes=[AR("bcsb")])
                P.op("dve", lambda e: e.tensor_tensor(out=attT[0:64, hidx, :], in0=banks[ob][0:64, :], in1=bcsb[0:64, :], op=ALU.mult),
                     reads=[BK(ob), AR("bcsb")], writes=[AR(f"attT{hidx}")])

            LOOK = 2
            pend = []
            deferred = []

            def tick():
                for d in deferred:
                    d[0] -= 1
                while deferred and deferred[0][0] <= 0:
                    deferred.pop(0)[1]()

            def norm_part1(ob):
                P.op("dve", lambda e: e.reciprocal(out=rs[64:65, :], in_=banks[ob][64:65, :]), reads=[BK(ob)], writes=[AR("rs")])

            def norm_part2(ob, hidx):
                P.op("pe", lambda e: e.matmul(banks[5][0:64, :], lhsT=onesf[64:65, 0:64], rhs=rs[64:65, :], start=True, stop=True),
                     reads=[AR("rs"), R("onesf")], writes=[BK(5)])
                P.op("act", lambda e: e.activation(out=bcsb[0:64, :], in_=banks[5][0:64, :], func=AF.Copy), reads=[BK(5)], writes=[AR("bcsb")])
                P.op("dve", lambda e: e.tensor_tensor(out=attT[0:64, hidx, :], in0=banks[ob][0:64, :], in1=bcsb[0:64, :], op=ALU.mult),
                     reads=[BK(ob), AR("bcsb")], writes=[AR(f"attT{hidx}")])

            def emit_pv(blk_):
                (kind, hidx, j, col0, pk, ob, first, last) = blk_
                if kind == "f":
                    P.op("pe", lambda e: e.matmul(banks[ob][0:65, col0:T], lhsT=VP[l][:, j, hidx, :], rhs=PT[pk][:, col0:T], start=first, stop=last),
                         reads=[R(f"VP{l}"), R(f"VPones{l}"), AR(f"PT{pk}")], writes=[BK(ob)])
                else:
                    P.op("pe", lambda e: e.matmul(banks[ob][0:65, :], lhsT=VMP[l][:, j, hidx - 6, :], rhs=PT[pk][:, :], start=first, stop=last),
                         reads=[R(f"VMP{l}"), R(f"VMPones{l}"), AR(f"PT{pk}")], writes=[BK(ob)])
                if last:
                    while deferred:
                        deferred.pop(0)[1]()
                    norm_part1(ob)
                    deferred.append([4, lambda ob=ob, hidx=hidx: norm_part2(ob, hidx)])

            def push(blk_):
                pend.append(blk_)
                if len(pend) > LOOK:
                    emit_pv(pend.pop(0))
                tick()

            for h in range(6):
                p, r0 = h // 2, 64 * (h % 2)
                par = h % 2
                br = AR(f"beta{par}")
                P.op("dve", lambda e, par=par, h=h: e.tensor_scalar(out=beta[:, par, 0:nj], in0=CALL[l][:, 0:nj, h],
                                                                     scalar1=EALL[l][:, 4 * g + 3, h:h + 1], scalar2=None, op0=ALU.subtract),
                     reads=[R(f"CE{l}")], writes=[br])
                dr = AR(f"daug{par}")
                P.op("dve", lambda e, par=par, h=h: e.tensor_scalar(out=stat[:, 8 + 4 * par:12 + 4 * par], in0=EALL[l][:, 4 * g:4 * g + 4, h],
                                                                     scalar1=EALL[l][:, 4 * g + 3, h:h + 1], scalar2=-1.0 / 128, op0=ALU.subtract, op1=ALU.mult),
                     reads=[R(f"CE{l}")], writes=[R(f"dtmp{par}")])
                P.op("dve", lambda e, par=par: e.tensor_copy(out=daug[:, par, :].rearrange("p (a b) -> p a b", b=128),
                                                              in_=stat[:, 8 + 4 * par:12 + 4 * par].unsqueeze(2).to_broadcast([128, 4, 128])),
                     reads=[R(f"dtmp{par}")], writes=[dr])
                ob = 3 + (h % 2)
                for j in range(nj):
                    il0 = max(0, j - 4 * g)
                    col0 = il0 * 128
                    sbk = sb_rr[0] % 3
                    sb_rr[0] += 1
                    pk = pt_rr[0] % NPT
                    pt_rr[0] += 1
                    P.op("pe", lambda e, j=j, col0=col0, sbk=sbk, p=p, h=h: e.matmul(banks[sbk][:, col0:T], lhsT=KT[l][:, p, j * 128:(j + 1) * 128],
                                                                                        rhs=QT[:, h, col0:T], start=True, stop=False),
                         reads=[ktres[p], AR(f"QT{p}")], writes=[BK(sbk)])
                    P.op("pe", lambda e, col0=col0, sbk=sbk, par=par: e.matmul(banks[sbk][:, col0:T], lhsT=onesb[:], rhs=daug[:, par, col0:T], start=False, stop=True),
                         reads=[R("onesb"), dr], writes=[BK(sbk)])
                    P.op("act", lambda e, j=j, col0=col0, sbk=sbk, pk=pk, par=par: e.activation(out=PT[pk][:, col0:T], in_=banks[sbk][:, col0:T],
                                                                                                func=AF.Exp, bias=beta[:, par, j:j + 1]),
                         reads=[BK(sbk), br], writes=[AR(f"PT{pk}")])
                    if j >= 4 * g:
                        P.op("dve", lambda e, il0=il0, pk=pk: e.tensor_tensor(out=PT[pk][:, il0 * 128:(il0 + 1) * 128], in0=PT[pk][:, il0 * 128:(il0 + 1) * 128],
                                                                                in1=triu[:], op=ALU.mult),
                             reads=[AR(f"PT{pk}"), R("triu")], writes=[AR(f"PT{pk}")])
                    push(("f", h, j, col0, pk, ob, j == 0, j == nj - 1))
            for hm in range(4):
                p, r0 = hm // 2, 64 * (hm % 2)
                ob = 3 + (hm % 2)
                for jm in range(2):
                    sbk = sb_rr[0] % 3
                    sb_rr[0] += 1
                    pk = pt_rr[0] % NPT
                    pt_rr[0] += 1
                    P.op("pe", lambda e, jm=jm, sbk=sbk, p=p, r0=r0: e.matmul(banks[sbk][:, :], lhsT=KMT[l][r0:r0 + 64, p, jm * 128:(jm + 1) * 128],
                                                                               rhs=QMT[r0:r0 + 64, p, :], start=True, stop=True),
                         reads=[R(f"KMT{l}"), AR(f"QMT{p}")], writes=[BK(sbk)])
                    P.op("act", lambda e, sbk=sbk, pk=pk: e.activation(out=PT[pk][:, :], in_=banks[sbk][:, :], func=AF.Exp),
                         reads=[BK(sbk)], writes=[AR(f"PT{pk}")])
                    push(("m", 6 + hm, jm, 0, pk, ob, jm == 0, jm == 1))
            while pend:
                emit_pv(pend.pop(0))
                tick()
            while deferred:
                deferred.pop(0)[1]()
            if dbg == "att":
                raise StopBuild()
            sg = load_slot(lambda s: s[:, 0:3 * D].rearrange("p (c n) -> p c n", n=D), wout_d[l][0:384, :].rearrange("(c p) n -> p c n", p=128))
            wo_g = slots[sg][:, 0:3 * D].rearrange("p (c n) -> p c n", n=D)
            wo_h = []
            for (h0, nh) in ((0, 4), (4, 4), (8, 2)):
                si = load_slot(lambda s, nh=nh: s[0:64, 0:nh * D].rearrange("p (c n) -> p c n", n=D),
                               wout_d[l][384 + 64 * h0:384 + 64 * (h0 + nh), :].rearrange("(c p) n -> p c n", p=64))
                v = slots[si][:, 0:nh * D].rearrange("p (c n) -> p c n", n=D)
                for k in range(nh):
                    wo_h.append((si, v, k))
            attres = [AR(f"attT{i}") for i in range(10)]
            for tt in range(4):
                yb = [next_bank(0, 4), next_bank(0, 4)]
                for half in range(2):
                    b = yb[half]
                    for c in range(3):
                        P.op("pe", lambda e, c=c, tt=tt, half=half, b=b: e.matmul(banks[b][:, :], lhsT=gmT[:, c, tt * 128:(tt + 1) * 128],
                                                                                   rhs=wo_g[:, c, half * 512:(half + 1) * 512], start=(c == 0), stop=False),
                             reads=[AR(f"gmT{tt}"), R(f"slot{sg}")], writes=[BK(b)])
                    for hh in range(10):
                        si, v, k = wo_h[hh]
                        P.op("pe", lambda e, hh=hh, tt=tt, half=half, b=b, v=v, k=k: e.matmul(banks[b][:, :], lhsT=attT[:, hh, tt * 128:(tt + 1) * 128],
                                                                                               rhs=v[:, k, half * 512:(half + 1) * 512], start=False, stop=(hh == 9)),
                             reads=[attres[hh], AR("attTz"), R(f"slot{si}")], writes=[BK(b)])
                post_norm(yb, tt, gpm, R("gpm"), xres[tt])

        def post_norm(yb, tt, gbuf, gres, xr):
            c0 = 48 + tt * 4
            sr = R(f"statp{tt}")
            for half in range(2):
                P.op("act", lambda e, half=half: e.activation(out=ytmp_bf[:, half * 512:(half + 1) * 512], in_=banks[yb[half]][:, :], func=AF.Square,
                                                               accum_out=stat[:, c0 + half:c0 + half + 1]),
                     reads=[BK(yb[half])], writes=[sr, R("ytmp")])
            P.op("dve", lambda e: e.tensor_tensor(out=stat[:, c0 + 2:c0 + 3], in0=stat[:, c0:c0 + 1], in1=stat[:, c0 + 1:c0 + 2], op=ALU.add), reads=[sr], writes=[sr])
            rstd_from_ss(stat[:, c0 + 2:c0 + 3], stat[:, c0 + 3:c0 + 4], D, [sr], [sr])
            for half in range(2):
                P.op("dve", lambda e, half=half: e.scalar_tensor_tensor(out=ytmp, in0=banks[yb[half]][:, :], scalar=stat[:, c0 + 3:c0 + 4],
                                                                         in1=gbuf[:, half * 512:(half + 1) * 512], op0=ALU.mult, op1=ALU.mult),
                     reads=[BK(yb[half]), sr, gres], writes=[R("ytmp")])
                P.op("dve", lambda e, half=half: e.tensor_tensor(out=xg[:, tt, half * 512:(half + 1) * 512], in0=xg[:, tt, half * 512:(half + 1) * 512], in1=ytmp, op=ALU.add),
                     reads=[R("ytmp"), xr], writes=[xr])

        def ffn(g, l, last):
            xres = [R(f"xg{tt}") for tt in range(4)]
            hres = [R(f"hT{tt}") for tt in range(4)]
            P.dma("sp", lambda e: e.dma_start(out=gpf[:], in_=bcast_rows(g_postffn_t, l * D, D)), d_gpf, writes=[R("gpm")])
            norm_transpose_n([(xg[:, tt, :], xres[tt]) for tt in range(4)], GCOL[l][:, 1, :], R(f"GCOL{l}"),
                             [(hT[:, :, tt * 128:(tt + 1) * 128], hres[tt]) for tt in range(4)], batched=False)
            P.fence(AG)
            for blk_i in range(8):
                si = load_slot(lambda s: s[:, :].rearrange("p (k n) -> p k n", n=512),
                               w1_d[l][:, blk_i * 512:(blk_i + 1) * 512].rearrange("(k p) n -> p k n", p=128))
                wb = slots[si][:, :].rearrange("p (k n) -> p k n", n=512)
                for fcl in range(4):
                    fc = blk_i * 4 + fcl
                    b = next_bank()
                    for kc in range(KC):
                        P.op("pe", lambda e, kc=kc, fcl=fcl, b=b, wb=wb: e.matmul(banks[b][:, :], lhsT=wb[:, kc, fcl * 128:(fcl + 1) * 128], rhs=hT[:, kc, :],
                                                                                   start=(kc == 0), stop=(kc == KC - 1)),
                             reads=hres + [R(f"slot{si}")], writes=[BK(b)])
                    rk = 0
                    P.op("act", lambda e, b=b, rk=rk: e.activation(out=rtmp[rk], in_=banks[b][:, :], func=AF.Relu), reads=[BK(b)], writes=[AR(f"rtmp{rk}")])
                    P.op("dve", lambda e, fc=fc, rk=rk: e.tensor_tensor(out=hidT[:, fc, :], in0=rtmp[rk], in1=rtmp[rk], op=ALU.mult),
                         reads=[AR(f"rtmp{rk}")], writes=[AR(f"hidT{fc}")])
            if dbg == "ffn1":
                raise StopBuild()
            for blk_i in range(8):
                si = load_slot(lambda s: s[:, :].rearrange("p (c n) -> p c n", n=D),
                               w2_d[l][blk_i * 512:(blk_i + 1) * 512, :].rearrange("(c p) n -> p c n", p=128))
                wb = slots[si][:, :].rearrange("p (c n) -> p c n", n=D)
                for fcl in range(4):
                    fc = blk_i * 4 + fcl
                    for tt in range(4):
                        for half in range(2):
                            b = tt * 2 + half
                            P.op("pe", lambda e, fc=fc, fcl=fcl, tt=tt, half=half, b=b, wb=wb: e.matmul(banks[b][:, :], lhsT=hidT[:, fc, tt * 128:(tt + 1) * 128],
                                                                                                         rhs=wb[:, fcl, half * 512:(half + 1) * 512],
                                                                                                         start=(fc == 0), stop=(fc == 31)),
                                 reads=[AR(f"hidT{fc}"), R(f"slot{si}")], writes=[BK(b)])
            for tt in range(4):
                post_norm([tt * 2, tt * 2 + 1], tt, gpf, R("gpm"), xres[tt])
                if last:
                    r0 = (4 * g + tt) * 128
                    P.dma("sp", lambda e, tt=tt, r0=r0: e.dma_start(out=out_d[r0:r0 + 128, :], in_=xg[:, tt, :]), d_o[tt],
                          reads=[xres[tt]], writes=[R(f"outd{tt}")])
            P.fence(AG)

        try:
            for l in range(L):
                layer_setup(l)
            P.fence(AG)
            if dbg == "setup":
                raise StopBuild()
            for g in range(NG):
                for tt in range(4):
                    r0 = (4 * g + tt) * 128
                    P.dma("sp", lambda e, tt=tt, r0=r0: e.dma_start(out=xg[:, tt, :], in_=x_d[r0:r0 + 128, :]), d_x[tt], writes=[R(f"xg{tt}")])
                for l in range(L):
                    mixer(g, l)
                    if dbg == "mix" or dbg == f"mix:{g}:{l}":
                        raise StopBuild()
                    ffn(g, l, last=(l == L - 1))
                    if dbg == f"ffn:{g}:{l}":
                        raise StopBuild()
        except StopBuild:
            if dbg == "win":
                P.op("dve", lambda e: e.tensor_copy(out=xg[:, 0, :].rearrange("p (c t) -> p c t", t=T), in_=QMT[:, :, :]),
                     reads=[AR("QMT0"), AR("QMT1"), R("xg0")], writes=[R("xg0")])
                P.op("dve", lambda e: e.tensor_copy(out=xg[:, 1, 0:512].rearrange("p (c t) -> p c t", t=256), in_=KMT[0][:, :, :]),
                     reads=[R("KMT0"), R("xg1")], writes=[R("xg1")])
                P.op("dve", lambda e: e.tensor_copy(out=xg[:, 2, 0:520].rearrange("p (c t) -> p c t", t=260), in_=VMP[0][:, :, :, :].rearrange("p a b c -> p a (b c)")),
                     reads=[R("VMP0"), R("VMPones0"), R("xg2")], writes=[R("xg2")])
            if dbg == "gm2":
                ar = [AR("zA"), AR("vn"), AR("tmpv2"), AR("outa"), R("WST0")] + [R(f"xg{i}") for i in range(4)]
                wr = [R(f"xg{i}") for i in range(4)]
                P.op("dve", lambda e: e.tensor_copy(out=xg[:, 0, 0:768], in_=zA), reads=ar, writes=wr)
                P.op("dve", lambda e: e.tensor_copy(out=xg[:, 1, 0:384], in_=vn), reads=ar, writes=wr)
                P.op("dve", lambda e: e.tensor_copy(out=xg[:, 1, 384:768], in_=tmpv2), reads=ar, writes=wr)
                P.op("dve", lambda e: e.tensor_copy(out=xg[:, 2, 0:384], in_=outa), reads=ar, writes=wr)
                P.op("dve", lambda e: e.tensor_copy(out=xg[:, 3, 0:768], in_=WST[0][:, :, :].rearrange("p g t -> p (g t)")), reads=ar, writes=wr)
            if dbg == "gm":
                P.op("dve", lambda e: e.tensor_copy(out=xg[:, 0, :].rearrange("p (c t) -> p c t", t=T), in_=gmT[:, 0:2, :]),
                     reads=[AR(f"gmT{i}") for i in range(4)] + [R("xg0")], writes=[R("xg0")])
                P.op("dve", lambda e: e.tensor_copy(out=xg[:, 1, 0:512], in_=gmT[:, 2, :]),
                     reads=[AR(f"gmT{i}") for i in range(4)] + [R("xg1")], writes=[R("xg1")])
            if dbg in ("att3", "att", "att5", "att4"):
                for q in range(4):
                    P.op("dve", lambda e, q=q: e.tensor_copy(out=xg[:, q, :].rearrange("p (c t) -> p c t", t=T), in_=attT[:, 2 * q:2 * q + 2, :]),
                         reads=[AR(f"attT{i}") for i in range(10)] + [R(f"xg{q}")], writes=[R(f"xg{q}")])
            for tt in range(4):
                P.dma("sp", lambda e, tt=tt: e.dma_start(out=out_d[tt * 128:(tt + 1) * 128, :], in_=xg[:, tt, :]), d_o[tt],
                      reads=[R(f"xg{tt}")], writes=[R(f"outd{tt}")])
        P.final()
        if build.want_trace:
            P.trace = []
        P.emit(st)
        build.stats = P.stats
        build.trace = P.trace
    return nc


build.want_trace = False

WNAMES = ["norm_pre_mix", "norm_post_mix", "norm_pre_ffn", "norm_post_ffn", "norm_mem", "w_in", "b_forget",
          "gmlp_v_norm", "gmlp_w_s", "gmlp_b_s", "w_mem_kv", "w_out", "w_ff1", "w_ff2"]

_cache = {}


def _consts():
    iu = np.triu(np.ones((128, 128), np.float32))
    return {
        "c_ident": np.eye(128, dtype=np.float32).astype(ml_dtypes.bfloat16),
        "c_triu": iu.astype(ml_dtypes.bfloat16),
        "c_triuf": iu.copy(),
        "c_onesf": np.ones((128, 128), np.float32),
    }


DBG = None


def run_layers(x, mem, weights, L):
    B, S, _ = x.shape
    key = (S, L)
    if key not in _cache:
        _cache[key] = build(S, L, dbg=DBG)
    nc = _cache[key]
    consts = _consts()
    in_maps = []
    for b in range(B):
        m = {"x": np.ascontiguousarray(x[b]), "mem": np.ascontiguousarray(mem[b])}
        for k in WNAMES:
            m[k] = np.ascontiguousarray(weights[k])
        m.update(consts)
        in_maps.append(m)
    res = run_bass_kernel_spmd(nc, in_maps, core_ids=list(range(B)))
    return np.stack([res.results[b]["out"] for b in range(B)], axis=0)


FUSED = True


def kernel(**inputs):
    x = np.asarray(inputs["x"], dtype=np.float32)
    mem = np.asarray(inputs["mem"], dtype=np.float32)
    W = {k: np.asarray(inputs[k], dtype=np.float32) for k in WNAMES}
    depth = W["w_in"].shape[0]
    if FUSED:
        return run_layers(x, mem, W, depth)
    for l in range(depth):
        x = run_layers(x, mem, {k: v[l:l + 1] for k, v in W.items()}, 1)
    return x
```

```python
from contextlib import ExitStack
import numpy as np
import ml_dtypes
import concourse.bass as bass
import concourse.mybir as mybir
from concourse.bass_utils import run_bass_kernel_spmd

F32 = mybir.dt.float32
BF16 = mybir.dt.bfloat16
AF = mybir.ActivationFunctionType
ALU = mybir.AluOpType
AX = mybir.AxisListType

COMPUTE = ("pe", "act", "dve", "pool")
EPS = 1e-6


class Res:
    __slots__ = ("name", "lw", "rd", "group", "excl")

    def __init__(self, name, group=None):
        self.name = name
        self.lw = None
        self.rd = []
        self.group = group
        self.excl = name.startswith("bank")


class Group:
    def __init__(self):
        self.since = []
        self.fdeps = []


class DmaSem:
    __slots__ = ("name", "total", "sem")

    def __init__(self, name):
        self.name = name
        self.total = 0
        self.sem = None


class Op:
    __slots__ = ("eng", "fn", "deps", "idx", "dma", "dma_total", "needs_inc", "cnt", "tag")

    def __init__(self, eng, fn):
        self.eng = eng
        self.fn = fn
        self.deps = []
        self.dma = None
        self.dma_total = 0
        self.needs_inc = False
        self.cnt = 0


class Prog:
    def __init__(self, nc):
        self.nc = nc
        self.ops = []
        self.dmasems = []
        self.resd = {}
        self.trace = None

    def R(self, name, group=None):
        r = self.resd.get(name)
        if r is None:
            r = Res(name, group)
            self.resd[name] = r
        return r

    def dmasem(self, name):
        d = DmaSem(name)
        self.dmasems.append(d)
        return d

    def fence(self, group):
        last = {}
        keep = []
        for o in group.since + group.fdeps:
            if o.dma is not None:
                keep.append(o)
            else:
                if o.eng not in last or last[o.eng].idx < o.idx:
                    last[o.eng] = o
        group.fdeps = keep + list(last.values())
        group.since = []

    def _track(self, op, reads, writes):
        reads = list(reads)
        writes = list(writes)
        for r in list(reads):
            if r.excl:
                reads.remove(r)
                if r not in writes:
                    writes.append(r)
        op.tag = "R:" + ",".join(r.name for r in reads) + " W:" + ",".join(w.name for w in writes)
        deps = set()
        for r in reads:
            if r.lw is not None:
                deps.add(r.lw)
        for w in writes:
            if w.lw is not None:
                deps.add(w.lw)
            for o in w.rd:
                deps.add(o)
        groups = set()
        for r in list(reads) + list(writes):
            if r.group is not None:
                groups.add(r.group)
        for g in groups:
            for o in g.fdeps:
                deps.add(o)
            g.since.append(op)
        deps.discard(op)
        best = {}
        red = []
        for d in deps:
            if d.dma is not None:
                red.append(d)
            elif d.eng not in best or best[d.eng].idx < d.idx:
                best[d.eng] = d
        deps = red + list(best.values())
        for r in reads:
            r.rd.append(op)
        for w in writes:
            w.lw = op
            w.rd = []
        op.deps = list(deps)

    def op(self, eng, fn, reads=(), writes=()):
        o = Op(eng, fn)
        o.idx = len(self.ops)
        self.ops.append(o)
        self._track(o, reads, writes)
        return o

    def dma(self, queue, fn, sem, reads=(), writes=()):
        o = Op(queue, fn)
        o.idx = len(self.ops)
        o.dma = sem
        sem.total += 16
        o.dma_total = sem.total
        self.ops.append(o)
        self._track(o, reads, writes)
        return o

    def final(self):
        o = Op("sp", lambda e: e.nop())
        o.idx = len(self.ops)
        o.tag = "final"
        last = {}
        for p in self.ops:
            if p.dma is not None:
                last[("d", p.dma.name)] = p
            elif p.eng in COMPUTE:
                last[("e", p.eng)] = p
        o.deps = list(last.values())
        self.ops.append(o)

    def emit(self, stack):
        nc = self.nc
        for o in self.ops:
            for d in o.deps:
                if d.dma is None:
                    if d.eng == "pe" and o.eng == "pe" and o.dma is None:
                        continue
                    d.needs_inc = True
        sems = {}
        for e in COMPUTE:
            sems[e] = stack.enter_context(nc.semaphore("s_" + e))
        for d in self.dmasems:
            if d.total > 0:
                d.sem = stack.enter_context(nc.semaphore("d_" + d.name))
        cnt = {e: 0 for e in COMPUTE}
        for o in self.ops:
            if o.dma is None and o.needs_inc:
                cnt[o.eng] += 1
                o.cnt = cnt[o.eng]
        per_eng = {e: [] for e in ("pe", "act", "dve", "pool", "sp")}
        for o in self.ops:
            per_eng[o.eng].append(o)
        self.stats = {e: len(v) for e, v in per_eng.items()}
        self.stats["incs"] = dict(cnt)

        def run_engine(ename, eng):
            seen = {}
            nw = 0
            for o in per_eng[ename]:
                need = {}
                for d in o.deps:
                    if d.dma is not None:
                        key = ("d", d.dma.name)
                        val = d.dma_total
                        semh = d.dma.sem
                    else:
                        if d.eng == "pe" and ename == "pe" and o.dma is None:
                            continue
                        key = ("e", d.eng)
                        val = d.cnt
                        semh = sems[d.eng]
                    if seen.get(key, 0) >= val:
                        continue
                    if key not in need or need[key][1] < val:
                        need[key] = (semh, val)
                for key, (semh, val) in need.items():
                    eng.wait_ge(semh, val)
                    seen[key] = val
                    nw += 1
                if self.trace is not None:
                    self.trace.append((ename, o.idx, [(k, v[1]) for k, v in need.items()], o.cnt if o.needs_inc else None, o.dma_total if o.dma else None, o.tag))
                ins = o.fn(eng)
                if o.dma is not None:
                    ins.then_inc(o.dma.sem, 16)
                elif o.needs_inc:
                    ins.then_inc(sems[ename], 1)
            self.stats["waits_" + ename] = nw

        with nc.Block() as block:
            @block.tensor
            def _(e):
                run_engine("pe", e)

            @block.scalar
            def _(e):
                run_engine("act", e)

            @block.vector
            def _(e):
                run_engine("dve", e)

            @block.gpsimd
            def _(e):
                run_engine("pool", e)

            @block.sync
            def _(e):
                run_engine("sp", e)


D = 1024
KC = 8
DIN = 2182
DFF = 4096
NMEM = 256
T = 512
NSLOT = 4

WIN_BLOCKS = [(0, 512), (512, 512), (1024, 512), (1536, 390), (1926, 256)]


class StopBuild(Exception):
    pass


def build(S, L, dbg=None):
    NT = S // 128
    NG = S // T
    nc = bass.Bass("TRN2", target_bir_lowering=False)

    def din(name, shape, dt=F32):
        return nc.dram_tensor(name, shape, dt, kind="ExternalInput")

    x_t = din("x", [S, D])
    mem_t = din("mem", [NMEM, D])
    g_premix_t = din("norm_pre_mix", [L, D])
    g_postmix_t = din("norm_post_mix", [L, D])
    g_preffn_t = din("norm_pre_ffn", [L, D])
    g_postffn_t = din("norm_post_ffn", [L, D])
    g_mem_t = din("norm_mem", [L, D])
    w_in_t = din("w_in", [L, D, DIN])
    b_forget_t = din("b_forget", [L, 6])
    gv_t = din("gmlp_v_norm", [L, 384])
    ws_t = din("gmlp_w_s", [L, 6, 128, 128])
    bs_t = din("gmlp_b_s", [L, 6, 128])
    wkv_t = din("w_mem_kv", [L, D, 512])
    wout_t = din("w_out", [L, D, D])
    w1_t = din("w_ff1", [L, D, DFF])
    w2_t = din("w_ff2", [L, DFF, D])
    ident_t = din("c_ident", [128, 128], BF16)
    triu_t = din("c_triu", [128, 128], BF16)
    triuf_t = din("c_triuf", [128, 128], F32)
    onesf_t = din("c_onesf", [128, 128], F32)
    out_t = nc.dram_tensor("out", [S, D], F32, kind="ExternalOutput")

    x_d, mem_d, out_d = x_t.ap(), mem_t.ap(), out_t.ap()
    w_in_d, wkv_d, wout_d, w1_d, w2_d = w_in_t.ap(), wkv_t.ap(), wout_t.ap(), w1_t.ap(), w2_t.ap()
    ws_d = ws_t.ap()

    with ExitStack() as st:
        P = Prog(nc)
        R = P.R

        def sb(name, shape, dt):
            return st.enter_context(nc.sbuf_tensor(name, shape, dt))

        KT = [sb(f"KT{l}", [128, 3, S], BF16) for l in range(L)]
        VP = [sb(f"VP{l}", [128, NT, 6, 65], BF16) for l in range(L)]
        CALL = [sb(f"CALL{l}", [128, NT, 6], F32) for l in range(L)]
        EALL = [sb(f"EALL{l}", [128, NT, 6], F32) for l in range(L)]
        KMT = [sb(f"KMT{l}", [128, 2, NMEM], BF16) for l in range(L)]
        VMP = [sb(f"VMP{l}", [128, 2, 4, 65], BF16) for l in range(L)]
        WST = [sb(f"WST{l}", [128, 6, 128], BF16) for l in range(L)]
        GCOL = [sb(f"GCOL{l}", [128, 3, 8], F32) for l in range(L)]
        BS = [sb(f"BS{l}", [128, 6], F32) for l in range(L)]
        BFG = [sb(f"BFG{l}", [128, 6], F32) for l in range(L)]
        gpm = sb("gpm", [128, D], F32)
        gpf = gpm
        xg = sb("xg", [128, 4, D], F32)
        hT = sb("hT", [128, KC, T], BF16)
        onesb = sb("onesb", [128, 128], BF16)
        stat = sb("stat", [128, 64], F32)
        ident = sb("ident", [128, 128], BF16)
        triu = sb("triu", [128, 128], BF16)
        triuf = sb("triuf", [128, 128], F32)
        onesf = sb("onesf", [128, 128], F32)
        slots = [sb(f"slot{i}", [128, 4096], BF16) for i in range(NSLOT)]
        ARENA_BF = 19680
        arena = sb("arena", [128, ARENA_BF], BF16)
        AG = Group()

        class Carver:
            def __init__(self):
                self.off = 0

            def take(self, nbf):
                o = self.off
                self.off += nbf
                assert self.off <= ARENA_BF, self.off
                return o

        cv = Carver()

        def a_bf(n):
            o = cv.take(n)
            return arena[:, o:o + n]

        def a_f32(n):
            o = cv.take(2 * n)
            return arena[:, o:o + 2 * n].bitcast(F32)

        QT = a_bf(6 * T).rearrange("p (c t) -> p c t", t=T)
        QMT = a_bf(2 * T).rearrange("p (c t) -> p c t", t=T)
        gmT = a_bf(3 * T).rearrange("p (c t) -> p c t", t=T)
        attT = a_bf(10 * T).rearrange("p (c t) -> p c t", t=T)
        NPT = 3
        PT_OFF = cv.off
        PT = [a_bf(T) for _ in range(NPT)]
        xnb = [arena[:, PT_OFF:PT_OFF + D]]
        zA = a_f32(768)
        tmpv2 = a_f32(384)
        vn = a_bf(384)
        outa = a_bf(384)
        beta = a_f32(2 * 32).rearrange("p (a c) -> p a c", a=2)
        daug = a_bf(2 * T).rearrange("p (a t) -> p a t", a=2)
        rs = a_f32(T)
        tmpv = rs[:, 0:384]
        bcsb = a_f32(T)
        GVS = bcsb[:, 0:384]
        fg = a_f32(24).rearrange("p (a b) -> p a b", b=6)
        spb = a_f32(24).rearrange("p (a b) -> p a b", b=6)
        att_end = cv.off
        cv.off = 0
        hidT = a_bf(32 * T).rearrange("p (c t) -> p c t", t=T)
        rtmp = [a_f32(T)]
        cv.off = max(cv.off, att_end)
        _yo = cv.take(2 * T)
        ytmp_bf = arena[:, _yo:_yo + 2 * T]
        ytmp = ytmp_bf.bitcast(F32)

        def AR(name):
            return R(name, AG)

        banks = [st.enter_context(nc.psum_tensor(f"bank{i}", [128, 512], F32)) for i in range(8)]
        banks_bf = [b[:, :].bitcast(BF16) for b in banks]

        def BK(i):
            return R(f"bank{i}")

        d_setup = P.dmasem("setup")
        d_x = [P.dmasem(f"x{i}") for i in range(4)]
        d_o = [P.dmasem(f"o{i}") for i in range(4)]
        d_slot = [P.dmasem(f"sl{i}") for i in range(NSLOT)]
        d_ws = P.dmasem("ws")
        d_gpm = P.dmasem("gpm")
        d_gpf = d_gpm
        d_gv = P.dmasem("gv")

        slot_ctr = [0]

        def load_slot(dst_fn, src, reads=()):
            i = slot_ctr[0] % NSLOT
            slot_ctr[0] += 1
            dst = dst_fn(slots[i])
            P.dma("pool", lambda e, dst=dst, src=src: e.dma_start(out=dst, in_=src), d_slot[i],
                  reads=list(reads), writes=[R(f"slot{i}")])
            return i

        def bcast_rows(t, row_off, n):
            return bass.AP(t, row_off, [[0, 128], [1, n]])

        setup_res = []

        def setup_dma(dst, src, resname, **kw):
            P.dma("sp", lambda e, dst=dst, src=src, kw=kw: e.dma_start(out=dst, in_=src, **kw), d_setup, writes=[R(resname)])
            setup_res.append(R(resname))

        setup_dma(ident[:], ident_t.ap(), "ident")
        setup_dma(triu[:], triu_t.ap(), "triu")
        setup_dma(triuf[:], triuf_t.ap(), "triuf")
        setup_dma(onesf[:], onesf_t.ap(), "onesf")
        for l in range(L):
            for k, gt in enumerate((g_premix_t, g_preffn_t, g_mem_t)):
                setup_dma(GCOL[l][:, k, :], bass.AP(gt, l * D, [[1, 128], [128, 8]]), f"GCOL{l}", allow_slow_non_contiguous=True)
            setup_dma(BFG[l][:], bcast_rows(b_forget_t, l * 6, 6), f"BFG{l}")
            setup_dma(BS[l][:], bass.AP(bs_t, l * 768, [[1, 128], [128, 6]]), f"BS{l}", allow_slow_non_contiguous=True)
        last_setup = P.ops[-1]
        for r in setup_res:
            r.lw = last_setup
            r.rd = []
        P.op("dve", lambda e: e.memset(onesb[:], 1.0), writes=[R("onesb")])
        P.op("dve", lambda e: e.memset(stat[:], 1.0e6), writes=[R("statn"), R("dtmp0"), R("dtmp1")] + [R(f"statn{i}") for i in range(4)] + [R(f"statv{i}") for i in range(4)] + [R(f"statp{i}") for i in range(4)])
        for l in range(L):
            P.op("dve", lambda e, l=l: e.memset(VP[l][:, :, :, 64:65], 1.0), writes=[R(f"VPones{l}")])
            P.op("dve", lambda e, l=l: e.memset(VMP[l][:, :, :, 64:65], 1.0), writes=[R(f"VMPones{l}")])

        def rstd_from_ss(ss_ap, out_ap, n, rd, wr):
            P.op("act", lambda e: e.activation(out=out_ap, in_=ss_ap, func=AF.Ln, scale=1.0 / n, bias=EPS), reads=rd, writes=wr)
            P.op("act", lambda e: e.activation(out=out_ap, in_=out_ap, func=AF.Exp, scale=-0.5), reads=wr, writes=wr)

        tbank_ctr = [0]

        def norm_transpose_n(srcs, gcol_ap, gres, dsts, batched=True, base=0):
            if not batched and len(srcs) > 1:
                for i in range(len(srcs)):
                    norm_transpose_n(srcs[i:i + 1], gcol_ap, gres, dsts[i:i + 1], batched=True, base=i)
                return
            n = len(srcs)
            ssr = R("statn" if n > 1 else f"statn{base}")
            for i, (src_ap, src_res) in enumerate(srcs):
                i = i + base
                P.op("act", lambda e, i=i, src_ap=src_ap: e.activation(out=xnb[0], in_=src_ap, func=AF.Square, accum_out=stat[:, i:i + 1]),
                     reads=[src_res], writes=[ssr] + ([AR("PT0"), AR("PT1")] if i == base else []))
            rstd_from_ss(stat[:, base:base + n], stat[:, 4 + base:4 + base + n], D, [ssr], [ssr])
            for i, ((src_ap, src_res), (dst_ap, dst_res)) in enumerate(zip(srcs, dsts)):
                i = i + base
                P.op("dve", lambda e, i=i, src_ap=src_ap: e.tensor_scalar(out=xnb[0], in0=src_ap, scalar1=stat[:, 4 + i:5 + i], scalar2=None, op0=ALU.mult),
                     reads=[src_res, ssr], writes=[AR("PT0"), AR("PT1")])
                tb = 6 + (tbank_ctr[0] % 2)
                tbank_ctr[0] += 1
                for kc in range(KC):
                    P.op("pe", lambda e, kc=kc, tb=tb: e.transpose(out=banks_bf[tb][:, kc * 128:(kc + 1) * 128], in_=xnb[0][:, kc * 128:(kc + 1) * 128], identity=ident[:]),
                         reads=[AR("PT0"), AR("PT1"), R("ident")], writes=[BK(tb)])
                P.op("dve", lambda e, tb=tb, dst_ap=dst_ap: e.tensor_tensor(out=dst_ap, in0=banks_bf[tb][:, 0:1024].rearrange("p (k t) -> p k t", t=128),
                                                                             in1=gcol_ap.unsqueeze(2).to_broadcast([128, KC, 128]), op=ALU.mult),
                     reads=[BK(tb), gres], writes=[dst_res])

        bank_rr = [0]

        def next_bank(lo=0, hi=6):
            b = lo + (bank_rr[0] % (hi - lo))
            bank_rr[0] += 1
            return b

        def layer_setup(l):
            for mt in range(2):
                P.dma("sp", lambda e, mt=mt: e.dma_start(out=xg[:, mt, :], in_=mem_d[mt * 128:(mt + 1) * 128, :]), d_x[mt], writes=[R(f"xg{mt}")])
            norm_transpose_n([(xg[:, mt, :], R(f"xg{mt}")) for mt in range(2)], GCOL[l][:, 2, :], R(f"GCOL{l}"),
                             [(hT[:, :, mt * 128:(mt + 1) * 128], R(f"hT{mt}")) for mt in range(2)])
            si = load_slot(lambda s: s[:, :].rearrange("p (k n) -> p k n", n=512), wkv_d[l].rearrange("(k p) n -> p k n", p=128))
            wkv = slots[si][:, :].rearrange("p (k n) -> p k n", n=512)
            hres = [R("hT0"), R("hT1")]
            for pm in range(2):
                b = next_bank()
                for kc in range(KC):
                    P.op("pe", lambda e, kc=kc, pm=pm, b=b: e.matmul(banks[b][:, 0:NMEM], lhsT=wkv[:, kc, pm * 128:(pm + 1) * 128], rhs=hT[:, kc, 0:NMEM],
                                                                      start=(kc == 0), stop=(kc == KC - 1)),
                         reads=[R(f"slot{si}")] + hres, writes=[BK(b)])
                P.op("act", lambda e, pm=pm, b=b: e.activation(out=KMT[l][:, pm, :], in_=banks[b][:, 0:NMEM], func=AF.Copy),
                     reads=[BK(b)], writes=[R(f"KMT{l}")])
            for mt in range(2):
                b = next_bank()
                for kc in range(KC):
                    P.op("pe", lambda e, kc=kc, mt=mt, b=b: e.matmul(banks[b][:, 0:256], lhsT=hT[:, kc, mt * 128:(mt + 1) * 128], rhs=wkv[:, kc, 256:512],
                                                                      start=(kc == 0), stop=(kc == KC - 1)),
                         reads=[R(f"slot{si}"), hres[mt]], writes=[BK(b)])
                P.op("act", lambda e, mt=mt, b=b: e.activation(out=VMP[l][:, mt, :, 0:64], in_=banks[b][:, 0:256].rearrange("p (h d) -> p h d", d=64), func=AF.Copy),
                     reads=[BK(b)], writes=[R(f"VMP{l}")])
            P.dma("sp", lambda e: e.dma_start(out=zA.rearrange("p (g s) -> p g s", s=128), in_=ws_d[l].rearrange("g t s -> t g s")), d_ws,
                  writes=[AR("zA")])
            wsb = xnb[0][:, 0:768]
            P.op("dve", lambda e: e.tensor_copy(out=wsb, in_=zA), reads=[AR("zA")], writes=[AR("PT0"), AR("PT1")])
            tb = 6 + (tbank_ctr[0] % 2)
            tbank_ctr[0] += 1
            for gg in range(6):
                P.op("pe", lambda e, gg=gg: e.transpose(out=banks_bf[tb][:, gg * 128:(gg + 1) * 128], in_=wsb[:, gg * 128:(gg + 1) * 128], identity=ident[:]),
                     reads=[AR("PT0"), AR("PT1"), R("ident")], writes=[BK(tb)])
            P.op("dve", lambda e: e.tensor_tensor(out=WST[l][:], in0=banks_bf[tb][:, 0:768].rearrange("p (g t) -> p g t", t=128),
                                                  in1=triu[:].unsqueeze(1).to_broadcast([128, 6, 128]), op=ALU.mult),
                 reads=[BK(tb), R("triu")], writes=[R(f"WST{l}")])

        def mixer(g, l):
            xres = [R(f"xg{tt}") for tt in range(4)]
            hres = [R(f"hT{tt}") for tt in range(4)]
            P.dma("sp", lambda e: e.dma_start(out=gpm[:], in_=bcast_rows(g_postmix_t, l * D, D)), d_gpm, writes=[R("gpm")])
            P.dma("sp", lambda e: e.dma_start(out=GVS, in_=bcast_rows(gv_t, l * 384, 384)), d_gv, writes=[AR("bcsb")])
            norm_transpose_n([(xg[:, tt, :], xres[tt]) for tt in range(4)], GCOL[l][:, 0, :], R(f"GCOL{l}"),
                             [(hT[:, :, tt * 128:(tt + 1) * 128], hres[tt]) for tt in range(4)], batched=(l == 0))
            if dbg == "norm":
                raise StopBuild()
            blk = [None] * 5

            def load_blk(bi):
                c0, ncol = WIN_BLOCKS[bi]
                si = load_slot(lambda s, ncol=ncol: s[:, 0:KC * ncol].rearrange("p (k n) -> p k n", n=ncol),
                               w_in_d[l][:, c0:c0 + ncol].rearrange("(k p) n -> p k n", p=128))
                blk[bi] = (si, slots[si][:, 0:KC * ncol].rearrange("p (k n) -> p k n", n=ncol))

            load_blk(0)
            load_blk(1)
            load_blk(2)
            load_blk(4)
            P.op("dve", lambda e: e.memset(QT[:, :, :], 0.0), writes=[AR("QTz")] + [AR(f"QT{c}") for c in range(3)])
            P.op("dve", lambda e: e.memset(attT[64:128, :, :], 0.0), writes=[AR("attTz")])
            zAb = [zA, arena[:, PT_OFF:PT_OFF + 1536].bitcast(F32)]
            zAr = [[AR("zA")], [AR("PT0"), AR("PT1"), AR("PT2")]]

            def a_mm(tt):
                zi = tt % 2
                ba, bb = next_bank(), next_bank()
                for kc in range(KC):
                    P.op("pe", lambda e, kc=kc: e.matmul(banks[ba][:, :], lhsT=hT[:, kc, tt * 128:(tt + 1) * 128], rhs=blk[0][1][:, kc, :],
                                                          start=(kc == 0), stop=(kc == KC - 1)),
                         reads=[hres[tt], R(f"slot{blk[0][0]}")], writes=[BK(ba)])
                for kc in range(KC):
                    P.op("pe", lambda e, kc=kc: e.matmul(banks[bb][:, 0:256], lhsT=hT[:, kc, tt * 128:(tt + 1) * 128], rhs=blk[1][1][:, kc, 0:256],
                                                          start=(kc == 0), stop=(kc == KC - 1)),
                         reads=[hres[tt], R(f"slot{blk[1][0]}")], writes=[BK(bb)])
                P.op("act", lambda e: e.activation(out=zAb[zi][:, 0:512], in_=banks[ba][:, :], func=AF.Gelu_apprx_tanh), reads=[BK(ba)], writes=zAr[zi])
                P.op("act", lambda e: e.activation(out=zAb[zi][:, 512:768], in_=banks[bb][:, 0:256], func=AF.Gelu_apprx_tanh), reads=[BK(bb)], writes=zAr[zi])

            def g_chain(tt):
                zi = tt % 2
                zz, zr = zAb[zi], zAr[zi]
                v3 = zz[:, 384:768].rearrange("p (g d) -> p g d", d=64)
                P.op("dve", lambda e: e.tensor_tensor(out=tmpv, in0=zz[:, 384:768], in1=zz[:, 384:768], op=ALU.mult), reads=zr, writes=[AR("rs")])
                c0 = 16 + tt * 8
                sr = R(f"statv{tt}")
                P.op("dve", lambda e: e.reduce_sum(out=stat[:, c0:c0 + 6], in_=tmpv.rearrange("p (g d) -> p g d", d=64), axis=AX.X),
                     reads=[AR("rs")], writes=[sr])
                rstd_from_ss(stat[:, c0:c0 + 6], stat[:, c0:c0 + 6], 64, [sr], [sr])
                P.op("dve", lambda e: e.tensor_tensor(out=tmpv.rearrange("p (g d) -> p g d", d=64), in0=v3,
                                                      in1=stat[:, c0:c0 + 6].unsqueeze(2).to_broadcast([128, 6, 64]), op=ALU.mult),
                     reads=zr + [sr], writes=[AR("rs")])
                P.op("dve", lambda e: e.tensor_tensor(out=vn, in0=tmpv, in1=GVS, op=ALU.mult), reads=[AR("rs"), AR("bcsb")], writes=[AR("vn")])
                bm = next_bank()
                for gg in range(6):
                    P.op("pe", lambda e, gg=gg: e.matmul(banks[bm][:, gg * 64:(gg + 1) * 64], lhsT=WST[l][:, gg, :], rhs=vn[:, gg * 64:(gg + 1) * 64],
                                                          start=True, stop=True),
                         reads=[AR("vn"), R(f"WST{l}")], writes=[BK(bm)])
                P.op("dve", lambda e: e.tensor_tensor(out=tmpv2.rearrange("p (g d) -> p g d", d=64), in0=banks[bm][:, 0:384].rearrange("p (g d) -> p g d", d=64),
                                                      in1=BS[l][:].unsqueeze(2).to_broadcast([128, 6, 64]), op=ALU.add),
                     reads=[BK(bm), R(f"BS{l}")], writes=[AR("tmpv2")])
                P.op("dve", lambda e: e.tensor_tensor(out=outa, in0=tmpv2, in1=zz[:, 0:384], op=ALU.mult), reads=[AR("tmpv2")] + zr, writes=[AR("outa")])
                tb = 6 + (tbank_ctr[0] % 2)
                tbank_ctr[0] += 1
                for c in range(3):
                    P.op("pe", lambda e, c=c: e.transpose(out=banks_bf[tb][:, c * 128:(c + 1) * 128], in_=outa[:, c * 128:(c + 1) * 128], identity=ident[:]),
                         reads=[AR("outa"), R("ident")], writes=[BK(tb)])
                P.op("act", lambda e: e.activation(out=gmT[:, :, tt * 128:(tt + 1) * 128], in_=banks_bf[tb][:, 0:384].rearrange("p (c t) -> p c t", t=128), func=AF.Copy),
                     reads=[BK(tb)], writes=[AR(f"gmT{tt}")])

            fm = [(1, 256, "q", 0), (1, 384, "q", 1), (2, 0, "q", 2),
                  (2, 128, "k", 0), (2, 256, "k", 1), (2, 384, "k", 2),
                  (4, 0, "m", 0), (4, 128, "m", 1)]

            def fm_chunk(bi, lc, kind, c):
                b = next_bank()
                for kc in range(KC):
                    P.op("pe", lambda e, kc=kc: e.matmul(banks[b][:, :], lhsT=blk[bi][1][:, kc, lc:lc + 128], rhs=hT[:, kc, :],
                                                          start=(kc == 0), stop=(kc == KC - 1)),
                         reads=hres + [R(f"slot{blk[bi][0]}")], writes=[BK(b)])
                if kind == "q":
                    P.op("act", lambda e: e.activation(out=QT[0:64, 2 * c, :], in_=banks[b][0:64, :], func=AF.Copy, scale=0.125), reads=[BK(b), AR("QTz")], writes=[AR(f"QT{c}")])
                    P.op("act", lambda e: e.activation(out=QT[64:128, 2 * c + 1, :], in_=banks[b][64:128, :], func=AF.Copy, scale=0.125), reads=[BK(b), AR("QTz")], writes=[AR(f"QT{c}")])
                elif kind == "m":
                    P.op("act", lambda e: e.activation(out=QMT[:, c, :], in_=banks[b][:, :], func=AF.Copy, scale=0.125), reads=[BK(b)], writes=[AR(f"QMT{c}")])
                else:
                    P.op("dve", lambda e: e.tensor_copy(out=KT[l][:, c, g * T:(g + 1) * T], in_=banks[b][:, :]), reads=[BK(b)], writes=[R(f"KT{l}_{c}")])

            def c_tile(tt):
                b = next_bank()
                for kc in range(KC):
                    P.op("pe", lambda e, kc=kc: e.matmul(banks[b][:, 0:390], lhsT=hT[:, kc, tt * 128:(tt + 1) * 128], rhs=blk[3][1][:, kc, :],
                                                          start=(kc == 0), stop=(kc == KC - 1)),
                         reads=[hres[tt], R(f"slot{blk[3][0]}")], writes=[BK(b)])
                P.op("act", lambda e: e.activation(out=VP[l][:, 4 * g + tt, :, 0:64], in_=banks[b][:, 0:384].rearrange("p (h d) -> p h d", d=64), func=AF.Copy),
                     reads=[BK(b)], writes=[R(f"VP{l}")])
                P.op("dve", lambda e: e.tensor_tensor(out=fg[:, tt, :], in0=banks[b][:, 384:390], in1=BFG[l][:], op=ALU.add),
                     reads=[BK(b), R(f"BFG{l}")], writes=[AR("fg")])

            a_mm(0)
            a_mm(1)
            for q in fm[0:4]:
                fm_chunk(*q)
            g_chain(0)
            a_mm(2)
            for q in fm[4:8]:
                fm_chunk(*q)
            g_chain(1)
            a_mm(3)
            load_blk(3)
            g_chain(2)
            for tt in range(4):
                c_tile(tt)
            g_chain(3)
            P.op("act", lambda e: e.activation(out=spb, in_=fg, func=AF.Exp, scale=-1.0), reads=[AR("fg")], writes=[AR("spb")])
            P.op("act", lambda e: e.activation(out=spb, in_=spb, func=AF.Ln, bias=1.0), reads=[AR("spb")], writes=[AR("spb")])
            for tt in range(4):
                ti = 4 * g + tt
                b = next_bank()
                P.op("pe", lambda e, b=b, tt=tt: e.matmul(banks[b][:, 0:6], lhsT=triuf[:], rhs=spb[:, tt, :], start=True, stop=True),
                     reads=[AR("spb"), R("triuf")], writes=[BK(b)])
                P.op("pe", lambda e, b=b, tt=tt: e.matmul(banks[b][:, 8:14], lhsT=onesf[:], rhs=spb[:, tt, :], start=True, stop=True),
                     reads=[AR("spb"), R("onesf")], writes=[BK(b)])
                cr = R(f"CE{l}")
                if ti == 0:
                    P.op("dve", lambda e, b=b, ti=ti: e.tensor_copy(out=CALL[l][:, ti, :], in_=banks[b][:, 0:6]), reads=[BK(b)], writes=[cr])
                    P.op("dve", lambda e, b=b, ti=ti: e.tensor_copy(out=EALL[l][:, ti, :], in_=banks[b][:, 8:14]), reads=[BK(b)], writes=[cr])
                else:
                    P.op("dve", lambda e, b=b, ti=ti: e.tensor_tensor(out=CALL[l][:, ti, :], in0=banks[b][:, 0:6], in1=EALL[l][:, ti - 1, :], op=ALU.add),
                         reads=[BK(b), cr], writes=[cr])
                    P.op("dve", lambda e, b=b, ti=ti: e.tensor_tensor(out=EALL[l][:, ti, :], in0=banks[b][:, 8:14], in1=EALL[l][:, ti - 1, :], op=ALU.add),
                         reads=[BK(b), cr], writes=[cr])

            if dbg in ("win", "gm", "gm2"):
                raise StopBuild()
            nj = 4 * g + 4
            sb_rr = [0]
            pt_rr = [0]
            ktres = [R(f"KT{l}_{c}") for c in range(3)]

            def normalize(ob, hidx):
                P.op("dve", lambda e: e.reciprocal(out=rs[64:65, :], in_=banks[ob][64:65, :]), reads=[BK(ob)], writes=[AR("rs")])
                P.op("pe", lambda e: e.matmul(banks[5][0:64, :], lhsT=onesf[64:65, 0:64], rhs=rs[64:65, :], start=True, stop=True),
                     reads=[AR("rs"), R("onesf")], writes=[BK(5)])
                P.op("act", lambda e: e.activation(out=bcsb[0:64, :], in_=banks[5][0:64, :], func=AF.Copy), reads=[BK(5)], writes=[AR("bcsb")])
                P.op("dve", lambda e: e.tensor_tensor(out=attT[0:64, hidx, :], in0=banks[ob][0:64, :], in1=bcsb[0:64, :], op=ALU.mult),
                     reads=[BK(ob), AR("bcsb")], writes=[AR(f"attT{hidx}")])

            LOOK = 2
            pend = []
            deferred = []

            def tick():
                for d in deferred:
                    d[0] -= 1
                while deferred and deferred[0][0] <= 0:
                    deferred.pop(0)[1]()

            def norm_part1(ob):
                P.op("act", lambda e: e.activation(out=rs[64:65, :], in_=banks[ob][64:65, :], func=AF.Ln), reads=[BK(ob)], writes=[AR("rs")])
                P.op("act", lambda e: e.activation(out=rs[64:65, :], in_=rs[64:65, :], func=AF.Exp, scale=-1.0), reads=[AR("rs")], writes=[AR("rs")])

            def norm_part2(ob, hidx):
                P.op("pe", lambda e: e.matmul(banks[5][0:64, :], lhsT=onesf[64:65, 0:64], rhs=rs[64:65, :], start=True, stop=True),
                     reads=[AR("rs"), R("onesf")], writes=[BK(5)])
                P.op("act", lambda e: e.activation(out=bcsb[0:64, :], in_=banks[5][0:64, :], func=AF.Copy), reads=[BK(5)], writes=[AR("bcsb")])
                P.op("dve", lambda e: e.tensor_tensor(out=attT[0:64, hidx, :], in0=banks[ob][0:64, :], in1=bcsb[0:64, :], op=ALU.mult),
                     reads=[BK(ob), AR("bcsb")], writes=[AR(f"attT{hidx}")])

            def emit_pv(blk_):
                (kind, hidx, j, col0, pk, ob, first, last) = blk_
                if kind == "f":
                    P.op("pe", lambda e: e.matmul(banks[ob][0:65, col0:T], lhsT=VP[l][:, j, hidx, :], rhs=PT[pk][:, col0:T], start=first, stop=last),
                         reads=[R(f"VP{l}"), R(f"VPones{l}"), AR(f"PT{pk}")], writes=[BK(ob)])
                else:
                    P.op("pe", lambda e: e.matmul(banks[ob][0:65, :], lhsT=VMP[l][:, j, hidx - 6, :], rhs=PT[pk][:, :], start=first, stop=last),
                         reads=[R(f"VMP{l}"), R(f"VMPones{l}"), AR(f"PT{pk}")], writes=[BK(ob)])
                if last:
                    while deferred:
                        deferred.pop(0)[1]()
                    norm_part1(ob)
                    deferred.append([4, lambda ob=ob, hidx=hidx: norm_part2(ob, hidx)])

            def push(blk_):
                pend.append(blk_)
                if len(pend) > LOOK:
                    emit_pv(pend.pop(0))
                tick()

            for h in range(6):
                p, r0 = h // 2, 64 * (h % 2)
                par = h % 2
                br = AR(f"beta{par}")
                P.op("dve", lambda e, par=par, h=h: e.tensor_scalar(out=beta[:, par, 0:nj], in0=CALL[l][:, 0:nj, h],
                                                                     scalar1=EALL[l][:, 4 * g + 3, h:h + 1], scalar2=None, op0=ALU.subtract),
                     reads=[R(f"CE{l}")], writes=[br])
                dr = AR(f"daug{par}")
                P.op("dve", lambda e, par=par, h=h: e.tensor_scalar(out=stat[:, 8 + 4 * par:12 + 4 * par], in0=EALL[l][:, 4 * g:4 * g + 4, h],
                                                                     scalar1=EALL[l][:, 4 * g + 3, h:h + 1], scalar2=-1.0 / 128, op0=ALU.subtract, op1=ALU.mult),
                     reads=[R(f"CE{l}")], writes=[R(f"dtmp{par}")])
                P.op("dve", lambda e, par=par: e.tensor_copy(out=daug[:, par, :].rearrange("p (a b) -> p a b", b=128),
                                                              in_=stat[:, 8 + 4 * par:12 + 4 * par].unsqueeze(2).to_broadcast([128, 4, 128])),
                     reads=[R(f"dtmp{par}")], writes=[dr])
                ob = 3 + (h % 2)
                for j in range(nj):
                    il0 = max(0, j - 4 * g)
                    col0 = il0 * 128
                    sbk = sb_rr[0] % 3
                    sb_rr[0] += 1
                    pk = pt_rr[0] % NPT
                    pt_rr[0] += 1
                    P.op("pe", lambda e, j=j, col0=col0, sbk=sbk, p=p, h=h: e.matmul(banks[sbk][:, col0:T], lhsT=KT[l][:, p, j * 128:(j + 1) * 128],
                                                                                        rhs=QT[:, h, col0:T], start=True, stop=False),
                         reads=[ktres[p], AR(f"QT{p}")], writes=[BK(sbk)])
                    P.op("pe", lambda e, col0=col0, sbk=sbk, par=par: e.matmul(banks[sbk][:, col0:T], lhsT=onesb[:], rhs=daug[:, par, col0:T], start=False, stop=True),
                         reads=[R("onesb"), dr], writes=[BK(sbk)])
                    P.op("act", lambda e, j=j, col0=col0, sbk=sbk, pk=pk, par=par: e.activation(out=PT[pk][:, col0:T], in_=banks[sbk][:, col0:T],
                                                                                                func=AF.Exp, bias=beta[:, par, j:j + 1]),
                         reads=[BK(sbk), br], writes=[AR(f"PT{pk}")])
                    if j >= 4 * g:
                        P.op("dve", lambda e, il0=il0, pk=pk: e.tensor_tensor(out=PT[pk][:, il0 * 128:(il0 + 1) * 128], in0=PT[pk][:, il0 * 128:(il0 + 1) * 128],
                                                                                in1=triu[:], op=ALU.mult),
                             reads=[AR(f"PT{pk}"), R("triu")], writes=[AR(f"PT{pk}")])
                    push(("f", h, j, col0, pk, ob, j == 0, j == nj - 1))
            for hm in range(4):
                p, r0 = hm // 2, 64 * (hm % 2)
                ob = 3 + (hm % 2)
                for jm in range(2):
                    sbk = sb_rr[0] % 3
                    sb_rr[0] += 1
                    pk = pt_rr[0] % NPT
                    pt_rr[0] += 1
                    P.op("pe", lambda e, jm=jm, sbk=sbk, p=p, r0=r0: e.matmul(banks[sbk][:, :], lhsT=KMT[l][r0:r0 + 64, p, jm * 128:(jm + 1) * 128],
                                                                               rhs=QMT[r0:r0 + 64, p, :], start=True, stop=True),
                         reads=[R(f"KMT{l}"), AR(f"QMT{p}")], writes=[BK(sbk)])
                    P.op("act", lambda e, sbk=sbk, pk=pk: e.activation(out=PT[pk][:, :], in_=banks[sbk][:, :], func=AF.Exp),
                         reads=[BK(sbk)], writes=[AR(f"PT{pk}")])
                    push(("m", 6 + hm, jm, 0, pk, ob, jm == 0, jm == 1))
            while pend:
                emit_pv(pend.pop(0))
                tick()
            while deferred:
                deferred.pop(0)[1]()
            if dbg == "att":
                raise StopBuild()
            sg = load_slot(lambda s: s[:, 0:3 * D].rearrange("p (c n) -> p c n", n=D), wout_d[l][0:384, :].rearrange("(c p) n -> p c n", p=128))
            wo_g = slots[sg][:, 0:3 * D].rearrange("p (c n) -> p c n", n=D)
            wo_h = []
            for (h0, nh) in ((0, 4), (4, 4), (8, 2)):
                si = load_slot(lambda s, nh=nh: s[0:64, 0:nh * D].rearrange("p (c n) -> p c n", n=D),
                               wout_d[l][384 + 64 * h0:384 + 64 * (h0 + nh), :].rearrange("(c p) n -> p c n", p=64))
                v = slots[si][:, 0:nh * D].rearrange("p (c n) -> p c n", n=D)
                for k in range(nh):
                    wo_h.append((si, v, k))
            attres = [AR(f"attT{i}") for i in range(10)]
            for tt in range(4):
                yb = [next_bank(0, 4), next_bank(0, 4)]
                for half in range(2):
                    b = yb[half]
                    for c in range(3):
                        P.op("pe", lambda e, c=c, tt=tt, half=half, b=b: e.matmul(banks[b][:, :], lhsT=gmT[:, c, tt * 128:(tt + 1) * 128],
                                                                                   rhs=wo_g[:, c, half * 512:(half + 1) * 512], start=(c == 0), stop=False),
                             reads=[AR(f"gmT{tt}"), R(f"slot{sg}")], writes=[BK(b)])
                    for hh in range(10):
                        si, v, k = wo_h[hh]
                        P.op("pe", lambda e, hh=hh, tt=tt, half=half, b=b, v=v, k=k: e.matmul(banks[b][:, :], lhsT=attT[:, hh, tt * 128:(tt + 1) * 128],
                                                                                               rhs=v[:, k, half * 512:(half + 1) * 512], start=False, stop=(hh == 9)),
                             reads=[attres[hh], AR("attTz"), R(f"slot{si}")], writes=[BK(b)])
                post_norm(yb, tt, gpm, R("gpm"), xres[tt])

        def post_norm(yb, tt, gbuf, gres, xr):
            c0 = 48 + tt * 4
            sr = R(f"statp{tt}")
            for half in range(2):
                P.op("act", lambda e, half=half: e.activation(out=ytmp_bf[:, half * 512:(half + 1) * 512], in_=banks[yb[half]][:, :], func=AF.Square,
                                                               accum_out=stat[:, c0 + half:c0 + half + 1]),
                     reads=[BK(yb[half])], writes=[sr, R("ytmp")])
            P.op("dve", lambda e: e.tensor_tensor(out=stat[:, c0 + 2:c0 + 3], in0=stat[:, c0:c0 + 1], in1=stat[:, c0 + 1:c0 + 2], op=ALU.add), reads=[sr], writes=[sr])
            rstd_from_ss(stat[:, c0 + 2:c0 + 3], stat[:, c0 + 3:c0 + 4], D, [sr], [sr])
            for half in range(2):
                P.op("dve", lambda e, half=half: e.scalar_tensor_tensor(out=ytmp, in0=banks[yb[half]][:, :], scalar=stat[:, c0 + 3:c0 + 4],
                                                                         in1=gbuf[:, half * 512:(half + 1) * 512], op0=ALU.mult, op1=ALU.mult),
                     reads=[BK(yb[half]), sr, gres], writes=[R("ytmp")])
                P.op("dve", lambda e, half=half: e.tensor_tensor(out=xg[:, tt, half * 512:(half + 1) * 512], in0=xg[:, tt, half * 512:(half + 1) * 512], in1=ytmp, op=ALU.add),
                     reads=[R("ytmp"), xr], writes=[xr])

        def ffn(g, l, last):
            xres = [R(f"xg{tt}") for tt in range(4)]
            hres = [R(f"hT{tt}") for tt in range(4)]
            P.dma("sp", lambda e: e.dma_start(out=gpf[:], in_=bcast_rows(g_postffn_t, l * D, D)), d_gpf, writes=[R("gpm")])
            norm_transpose_n([(xg[:, tt, :], xres[tt]) for tt in range(4)], GCOL[l][:, 1, :], R(f"GCOL{l}"),
                             [(hT[:, :, tt * 128:(tt + 1) * 128], hres[tt]) for tt in range(4)], batched=False)
            P.fence(AG)
            for blk_i in range(8):
                si = load_slot(lambda s: s[:, :].rearrange("p (k n) -> p k n", n=512),
                               w1_d[l][:, blk_i * 512:(blk_i + 1) * 512].rearrange("(k p) n -> p k n", p=128))
                wb = slots[si][:, :].rearrange("p (k n) -> p k n", n=512)
                for fcl in range(4):
                    fc = blk_i * 4 + fcl
                    b = next_bank()
                    for kc in range(KC):
                        P.op("pe", lambda e, kc=kc, fcl=fcl, b=b, wb=wb: e.matmul(banks[b][:, :], lhsT=wb[:, kc, fcl * 128:(fcl + 1) * 128], rhs=hT[:, kc, :],
                                                                                   start=(kc == 0), stop=(kc == KC - 1)),
                             reads=hres + [R(f"slot{si}")], writes=[BK(b)])
                    rk = 0
                    P.op("act", lambda e, b=b, rk=rk: e.activation(out=rtmp[rk], in_=banks[b][:, :], func=AF.Relu), reads=[BK(b)], writes=[AR(f"rtmp{rk}")])
                    P.op("dve", lambda e, fc=fc, rk=rk: e.tensor_tensor(out=hidT[:, fc, :], in0=rtmp[rk], in1=rtmp[rk], op=ALU.mult),
                         reads=[AR(f"rtmp{rk}")], writes=[AR(f"hidT{fc}")])
            if dbg == "ffn1":
                raise StopBuild()
            for tp in range(2):
                for blk_i in range(8):
                    si = load_slot(lambda s: s[:, :].rearrange("p (c n) -> p c n", n=D),
                                   w2_d[l][blk_i * 512:(blk_i + 1) * 512, :].rearrange("(c p) n -> p c n", p=128))
                    wb = slots[si][:, :].rearrange("p (c n) -> p c n", n=D)
                    for fcl in range(4):
                        fc = blk_i * 4 + fcl
                        for tt in (2 * tp, 2 * tp + 1):
                            for half in range(2):
                                b = tt * 2 + half
                                P.op("pe", lambda e, fc=fc, fcl=fcl, tt=tt, half=half, b=b, wb=wb: e.matmul(banks[b][:, :], lhsT=hidT[:, fc, tt * 128:(tt + 1) * 128],
                                                                                                             rhs=wb[:, fcl, half * 512:(half + 1) * 512],
                                                                                                             start=(fc == 0), stop=(fc == 31)),
                                     reads=[AR(f"hidT{fc}"), R(f"slot{si}")], writes=[BK(b)])
                for tt in (2 * tp, 2 * tp + 1):
                    post_norm([tt * 2, tt * 2 + 1], tt, gpf, R("gpm"), xres[tt])
                    if last:
                        r0 = (4 * g + tt) * 128
                        P.dma("sp", lambda e, tt=tt, r0=r0: e.dma_start(out=out_d[r0:r0 + 128, :], in_=xg[:, tt, :]), d_o[tt],
                              reads=[xres[tt]], writes=[R(f"outd{tt}")])
            P.fence(AG)

        try:
            for l in range(L):
                layer_setup(l)
            P.fence(AG)
            if dbg == "setup":
                raise StopBuild()
            for g in range(NG):
                for tt in range(4):
                    r0 = (4 * g + tt) * 128
                    P.dma("sp", lambda e, tt=tt, r0=r0: e.dma_start(out=xg[:, tt, :], in_=x_d[r0:r0 + 128, :]), d_x[tt], writes=[R(f"xg{tt}")])
                for l in range(L):
                    mixer(g, l)
                    if dbg == "mix" or dbg == f"mix:{g}:{l}":
                        raise StopBuild()
                    ffn(g, l, last=(l == L - 1))
                    if dbg == f"ffn:{g}:{l}":
                        raise StopBuild()
        except StopBuild:
            if dbg == "win":
                P.op("dve", lambda e: e.tensor_copy(out=xg[:, 0, :].rearrange("p (c t) -> p c t", t=T), in_=QMT[:, :, :]),
                     reads=[AR("QMT0"), AR("QMT1"), R("xg0")], writes=[R("xg0")])
                P.op("dve", lambda e: e.tensor_copy(out=xg[:, 1, 0:512].rearrange("p (c t) -> p c t", t=256), in_=KMT[0][:, :, :]),
                     reads=[R("KMT0"), R("xg1")], writes=[R("xg1")])
                P.op("dve", lambda e: e.tensor_copy(out=xg[:, 2, 0:520].rearrange("p (c t) -> p c t", t=260), in_=VMP[0][:, :, :, :].rearrange("p a b c -> p a (b c)")),
                     reads=[R("VMP0"), R("VMPones0"), R("xg2")], writes=[R("xg2")])
            if dbg == "gm2":
                ar = [AR("zA"), AR("vn"), AR("tmpv2"), AR("outa"), R("WST0")] + [R(f"xg{i}") for i in range(4)]
                wr = [R(f"xg{i}") for i in range(4)]
                P.op("dve", lambda e: e.tensor_copy(out=xg[:, 0, 0:768], in_=zA), reads=ar, writes=wr)
                P.op("dve", lambda e: e.tensor_copy(out=xg[:, 1, 0:384], in_=vn), reads=ar, writes=wr)
                P.op("dve", lambda e: e.tensor_copy(out=xg[:, 1, 384:768], in_=tmpv2), reads=ar, writes=wr)
                P.op("dve", lambda e: e.tensor_copy(out=xg[:, 2, 0:384], in_=outa), reads=ar, writes=wr)
                P.op("dve", lambda e: e.tensor_copy(out=xg[:, 3, 0:768], in_=WST[0][:, :, :].rearrange("p g t -> p (g t)")), reads=ar, writes=wr)
            if dbg == "gm":
                P.op("dve", lambda e: e.tensor_copy(out=xg[:, 0, :].rearrange("p (c t) -> p c t", t=T), in_=gmT[:, 0:2, :]),
                     reads=[AR(f"gmT{i}") for i in range(4)] + [R("xg0")], writes=[R("xg0")])
                P.op("dve", lambda e: e.tensor_copy(out=xg[:, 1, 0:512], in_=gmT[:, 2, :]),
                     reads=[AR(f"gmT{i}") for i in range(4)] + [R("xg1")], writes=[R("xg1")])
            if dbg in ("att3", "att", "att5", "att4"):
                for q in range(4):
                    P.op("dve", lambda e, q=q: e.tensor_copy(out=xg[:, q, :].rearrange("p (c t) -> p c t", t=T), in_=attT[:, 2 * q:2 * q + 2, :]),
                         reads=[AR(f"attT{i}") for i in range(10)] + [R(f"xg{q}")], writes=[R(f"xg{q}")])
            for tt in range(4):
                P.dma("sp", lambda e, tt=tt: e.dma_start(out=out_d[tt * 128:(tt + 1) * 128, :], in_=xg[:, tt, :]), d_o[tt],
                      reads=[R(f"xg{tt}")], writes=[R(f"outd{tt}")])
        P.final()
        if build.want_trace:
            P.trace = []
        P.emit(st)
        build.stats = P.stats
        build.trace = P.trace
    return nc


build.want_trace = False

WNAMES = ["norm_pre_mix", "norm_post_mix", "norm_pre_ffn", "norm_post_ffn", "norm_mem", "w_in", "b_forget",
          "gmlp_v_norm", "gmlp_w_s", "gmlp_b_s", "w_mem_kv", "w_out", "w_ff1", "w_ff2"]

_cache = {}


def _consts():
    iu = np.triu(np.ones((128, 128), np.float32))
    return {
        "c_ident": np.eye(128, dtype=np.float32).astype(ml_dtypes.bfloat16),
        "c_triu": iu.astype(ml_dtypes.bfloat16),
        "c_triuf": iu.copy(),
        "c_onesf": np.ones((128, 128), np.float32),
    }


DBG = None


def run_layers(x, mem, weights, L):
    B, S, _ = x.shape
    key = (S, L)
    if key not in _cache:
        _cache[key] = build(S, L, dbg=DBG)
    nc = _cache[key]
    consts = _consts()
    in_maps = []
    for b in range(B):
        m = {"x": np.ascontiguousarray(x[b]), "mem": np.ascontiguousarray(mem[b])}
        for k in WNAMES:
            m[k] = np.ascontiguousarray(weights[k])
        m.update(consts)
        in_maps.append(m)
    res = run_bass_kernel_spmd(nc, in_maps, core_ids=list(range(B)))
    return np.stack([res.results[b]["out"] for b in range(B)], axis=0)


FUSED = True


def kernel(**inputs):
    x = np.asarray(inputs["x"], dtype=np.float32)
    mem = np.asarray(inputs["mem"], dtype=np.float32)
    W = {k: np.asarray(inputs[k], dtype=np.float32) for k in WNAMES}
    depth = W["w_in"].shape[0]
    if FUSED:
        return run_layers(x, mem, W, depth)
    for l in range(depth):
        x = run_layers(x, mem, {k: v[l:l + 1] for k, v in W.items()}, 1)
    return x
```

```python
from contextlib import ExitStack
import numpy as np
import ml_dtypes
import concourse.bass as bass
import concourse.mybir as mybir
from concourse.bass_utils import run_bass_kernel_spmd

F32 = mybir.dt.float32
BF16 = mybir.dt.bfloat16
AF = mybir.ActivationFunctionType
ALU = mybir.AluOpType
AX = mybir.AxisListType

COMPUTE = ("pe", "act", "dve", "pool")
EPS = 1e-6


class Res:
    __slots__ = ("name", "lw", "rd", "group", "excl")

    def __init__(self, name, group=None):
        self.name = name
        self.lw = None
        self.rd = []
        self.group = group
        self.excl = name.startswith("bank")


class Group:
    def __init__(self):
        self.since = []
        self.fdeps = []


class DmaSem:
    __slots__ = ("name", "total", "sem")

    def __init__(self, name):
        self.name = name
        self.total = 0
        self.sem = None


class Op:
    __slots__ = ("eng", "fn", "deps", "idx", "dma", "dma_total", "needs_inc", "cnt", "tag")

    def __init__(self, eng, fn):
        self.eng = eng
        self.fn = fn
        self.deps = []
        self.dma = None
        self.dma_total = 0
        self.needs_inc = False
        self.cnt = 0


class Prog:
    def __init__(self, nc):
        self.nc = nc
        self.ops = []
        self.dmasems = []
        self.resd = {}
        self.trace = None

    def R(self, name, group=None):
        r = self.resd.get(name)
        if r is None:
            r = Res(name, group)
            self.resd[name] = r
        return r

    def dmasem(self, name):
        d = DmaSem(name)
        self.dmasems.append(d)
        return d

    def fence(self, group):
        last = {}
        keep = []
        for o in group.since + group.fdeps:
            if o.dma is not None:
                keep.append(o)
            else:
                if o.eng not in last or last[o.eng].idx < o.idx:
                    last[o.eng] = o
        group.fdeps = keep + list(last.values())
        group.since = []

    def _track(self, op, reads, writes):
        reads = list(reads)
        writes = list(writes)
        for r in list(reads):
            if r.excl:
                reads.remove(r)
                if r not in writes:
                    writes.append(r)
        op.tag = "R:" + ",".join(r.name for r in reads) + " W:" + ",".join(w.name for w in writes)
        deps = set()
        for r in reads:
            if r.lw is not None:
                deps.add(r.lw)
        for w in writes:
            if w.lw is not None:
                deps.add(w.lw)
            for o in w.rd:
                deps.add(o)
        groups = set()
        for r in list(reads) + list(writes):
            if r.group is not None:
                groups.add(r.group)
        for g in groups:
            for o in g.fdeps:
                deps.add(o)
            g.since.append(op)
        deps.discard(op)
        best = {}
        red = []
        for d in deps:
            if d.dma is not None:
                red.append(d)
            elif d.eng not in best or best[d.eng].idx < d.idx:
                best[d.eng] = d
        deps = red + list(best.values())
        for r in reads:
            r.rd.append(op)
        for w in writes:
            w.lw = op
            w.rd = []
        op.deps = list(deps)

    def op(self, eng, fn, reads=(), writes=()):
        o = Op(eng, fn)
        o.idx = len(self.ops)
        self.ops.append(o)
        self._track(o, reads, writes)
        return o

    def dma(self, queue, fn, sem, reads=(), writes=()):
        o = Op(queue, fn)
        o.idx = len(self.ops)
        o.dma = sem
        sem.total += 16
        o.dma_total = sem.total
        self.ops.append(o)
        self._track(o, reads, writes)
        return o

    def final(self):
        o = Op("sp", lambda e: e.nop())
        o.idx = len(self.ops)
        o.tag = "final"
        last = {}
        for p in self.ops:
            if p.dma is not None:
                last[("d", p.dma.name)] = p
            elif p.eng in COMPUTE:
                last[("e", p.eng)] = p
        o.deps = list(last.values())
        self.ops.append(o)

    def emit(self, stack):
        nc = self.nc
        for o in self.ops:
            for d in o.deps:
                if d.dma is None:
                    if d.eng == "pe" and o.eng == "pe" and o.dma is None:
                        continue
                    d.needs_inc = True
        sems = {}
        for e in COMPUTE:
            sems[e] = stack.enter_context(nc.semaphore("s_" + e))
        for d in self.dmasems:
            if d.total > 0:
                d.sem = stack.enter_context(nc.semaphore("d_" + d.name))
        cnt = {e: 0 for e in COMPUTE}
        for o in self.ops:
            if o.dma is None and o.needs_inc:
                cnt[o.eng] += 1
                o.cnt = cnt[o.eng]
        per_eng = {e: [] for e in ("pe", "act", "dve", "pool", "sp")}
        for o in self.ops:
            per_eng[o.eng].append(o)
        self.stats = {e: len(v) for e, v in per_eng.items()}
        self.stats["incs"] = dict(cnt)

        def run_engine(ename, eng):
            seen = {}
            nw = 0
            for o in per_eng[ename]:
                need = {}
                for d in o.deps:
                    if d.dma is not None:
                        key = ("d", d.dma.name)
                        val = d.dma_total
                        semh = d.dma.sem
                    else:
                        if d.eng == "pe" and ename == "pe" and o.dma is None:
                            continue
                        key = ("e", d.eng)
                        val = d.cnt
                        semh = sems[d.eng]
                    if seen.get(key, 0) >= val:
                        continue
                    if key not in need or need[key][1] < val:
                        need[key] = (semh, val)
                for key, (semh, val) in need.items():
                    eng.wait_ge(semh, val)
                    seen[key] = val
                    nw += 1
                if self.trace is not None:
                    self.trace.append((ename, o.idx, [(k, v[1]) for k, v in need.items()], o.cnt if o.needs_inc else None, o.dma_total if o.dma else None, o.tag))
                ins = o.fn(eng)
                if o.dma is not None:
                    ins.then_inc(o.dma.sem, 16)
                elif o.needs_inc:
                    ins.then_inc(sems[ename], 1)
            self.stats["waits_" + ename] = nw

        with nc.Block() as block:
            @block.tensor
            def _(e):
                run_engine("pe", e)

            @block.scalar
            def _(e):
                run_engine("act", e)

            @block.vector
            def _(e):
                run_engine("dve", e)

            @block.gpsimd
            def _(e):
                run_engine("pool", e)

            @block.sync
            def _(e):
                run_engine("sp", e)


D = 1024
KC = 8
DIN = 2182
DFF = 4096
NMEM = 256
T = 512
NSLOT = 4

WIN_BLOCKS = [(0, 512), (512, 512), (1024, 512), (1536, 390), (1926, 256)]


class StopBuild(Exception):
    pass


def build(S, L, dbg=None):
    NT = S // 128
    NG = S // T
    nc = bass.Bass("TRN2", target_bir_lowering=False)

    def din(name, shape, dt=F32):
        return nc.dram_tensor(name, shape, dt, kind="ExternalInput")

    x_t = din("x", [S, D])
    mem_t = din("mem", [NMEM, D])
    g_premix_t = din("norm_pre_mix", [L, D])
    g_postmix_t = din("norm_post_mix", [L, D])
    g_preffn_t = din("norm_pre_ffn", [L, D])
    g_postffn_t = din("norm_post_ffn", [L, D])
    g_mem_t = din("norm_mem", [L, D])
    w_in_t = din("w_in", [L, D, DIN])
    b_forget_t = din("b_forget", [L, 6])
    gv_t = din("gmlp_v_norm", [L, 384])
    ws_t = din("gmlp_w_s", [L, 6, 128, 128])
    bs_t = din("gmlp_b_s", [L, 6, 128])
    wkv_t = din("w_mem_kv", [L, D, 512])
    wout_t = din("w_out", [L, D, D])
    w1_t = din("w_ff1", [L, D, DFF])
    w2_t = din("w_ff2", [L, DFF, D])
    ident_t = din("c_ident", [128, 128], BF16)
    triu_t = din("c_triu", [128, 128], BF16)
    triuf_t = din("c_triuf", [128, 128], F32)
    onesf_t = din("c_onesf", [128, 128], F32)
    out_t = nc.dram_tensor("out", [S, D], F32, kind="ExternalOutput")

    x_d, mem_d, out_d = x_t.ap(), mem_t.ap(), out_t.ap()
    w_in_d, wkv_d, wout_d, w1_d, w2_d = w_in_t.ap(), wkv_t.ap(), wout_t.ap(), w1_t.ap(), w2_t.ap()
    ws_d = ws_t.ap()

    with ExitStack() as st:
        P = Prog(nc)
        R = P.R

        def sb(name, shape, dt):
            return st.enter_context(nc.sbuf_tensor(name, shape, dt))

        KT = [sb(f"KT{l}", [128, 3, S], BF16) for l in range(L)]
        VP = [sb(f"VP{l}", [128, NT, 6, 65], BF16) for l in range(L)]
        CALL = [sb(f"CALL{l}", [128, NT, 6], F32) for l in range(L)]
        EALL = [sb(f"EALL{l}", [128, NT, 6], F32) for l in range(L)]
        KMT = [sb(f"KMT{l}", [128, 2, NMEM], BF16) for l in range(L)]
        VMP = [sb(f"VMP{l}", [128, 2, 4, 65], BF16) for l in range(L)]
        WST = [sb(f"WST{l}", [128, 6, 128], BF16) for l in range(L)]
        GCOL = [sb(f"GCOL{l}", [128, 3, 8], F32) for l in range(L)]
        BS = [sb(f"BS{l}", [128, 6], F32) for l in range(L)]
        BFG = [sb(f"BFG{l}", [128, 6], F32) for l in range(L)]
        gpm = sb("gpm", [128, D], F32)
        gpf = gpm
        xg = sb("xg", [128, 4, D], F32)
        hT = sb("hT", [128, KC, T], BF16)
        onesb = sb("onesb", [128, 128], BF16)
        stat = sb("stat", [128, 64], F32)
        ident = sb("ident", [128, 128], BF16)
        triu = sb("triu", [128, 128], BF16)
        triuf = sb("triuf", [128, 128], F32)
        onesf = sb("onesf", [128, 128], F32)
        slots = [sb(f"slot{i}", [128, 4096], BF16) for i in range(NSLOT)]
        ARENA_BF = 19680
        arena = sb("arena", [128, ARENA_BF], BF16)
        AG = Group()

        class Carver:
            def __init__(self):
                self.off = 0

            def take(self, nbf):
                o = self.off
                self.off += nbf
                assert self.off <= ARENA_BF, self.off
                return o

        cv = Carver()

        def a_bf(n):
            o = cv.take(n)
            return arena[:, o:o + n]

        def a_f32(n):
            o = cv.take(2 * n)
            return arena[:, o:o + 2 * n].bitcast(F32)

        QT = a_bf(6 * T).rearrange("p (c t) -> p c t", t=T)
        QMT = a_bf(2 * T).rearrange("p (c t) -> p c t", t=T)
        gmT = a_bf(3 * T).rearrange("p (c t) -> p c t", t=T)
        attT = a_bf(10 * T).rearrange("p (c t) -> p c t", t=T)
        NPT = 3
        PT_OFF = cv.off
        PT = [a_bf(T) for _ in range(NPT)]
        xnb = [arena[:, PT_OFF:PT_OFF + D], None]
        zA = a_f32(768)
        xnb[1] = arena[:, cv.off - 1536:cv.off - 1536 + D]
        tmpv2 = a_f32(384)
        vn = a_bf(384)
        outa = a_bf(384)
        beta = a_f32(2 * 32).rearrange("p (a c) -> p a c", a=2)
        daug = a_bf(2 * T).rearrange("p (a t) -> p a t", a=2)
        rs = a_f32(T)
        tmpv = rs[:, 0:384]
        bcsb = a_f32(T)
        GVS = bcsb[:, 0:384]
        fg = a_f32(24).rearrange("p (a b) -> p a b", b=6)
        spb = a_f32(24).rearrange("p (a b) -> p a b", b=6)
        att_end = cv.off
        cv.off = 0
        hidT = a_bf(32 * T).rearrange("p (c t) -> p c t", t=T)
        rtmp = [a_f32(T)]
        cv.off = max(cv.off, att_end)
        _yo = cv.take(2 * T)
        ytmp_bf = arena[:, _yo:_yo + 2 * T]
        ytmp = ytmp_bf.bitcast(F32)

        def AR(name):
            return R(name, AG)

        banks = [st.enter_context(nc.psum_tensor(f"bank{i}", [128, 512], F32)) for i in range(8)]
        banks_bf = [b[:, :].bitcast(BF16) for b in banks]

        def BK(i):
            return R(f"bank{i}")

        d_setup = P.dmasem("setup")
        d_x = [P.dmasem(f"x{i}") for i in range(4)]
        d_o = [P.dmasem(f"o{i}") for i in range(4)]
        d_slot = [P.dmasem(f"sl{i}") for i in range(NSLOT)]
        d_ws = P.dmasem("ws")
        d_gpm = P.dmasem("gpm")
        d_gpf = d_gpm
        d_gv = P.dmasem("gv")

        slot_ctr = [0]

        def load_slot(dst_fn, src, reads=()):
            i = slot_ctr[0] % NSLOT
            slot_ctr[0] += 1
            dst = dst_fn(slots[i])
            P.dma("pool", lambda e, dst=dst, src=src: e.dma_start(out=dst, in_=src), d_slot[i],
                  reads=list(reads), writes=[R(f"slot{i}")])
            return i

        def bcast_rows(t, row_off, n):
            return bass.AP(t, row_off, [[0, 128], [1, n]])

        setup_res = []

        def setup_dma(dst, src, resname, **kw):
            P.dma("sp", lambda e, dst=dst, src=src, kw=kw: e.dma_start(out=dst, in_=src, **kw), d_setup, writes=[R(resname)])
            setup_res.append(R(resname))

        setup_dma(ident[:], ident_t.ap(), "ident")
        setup_dma(triu[:], triu_t.ap(), "triu")
        setup_dma(triuf[:], triuf_t.ap(), "triuf")
        setup_dma(onesf[:], onesf_t.ap(), "onesf")
        for l in range(L):
            for k, gt in enumerate((g_premix_t, g_preffn_t, g_mem_t)):
                setup_dma(GCOL[l][:, k, :], bass.AP(gt, l * D, [[1, 128], [128, 8]]), f"GCOL{l}", allow_slow_non_contiguous=True)
            setup_dma(BFG[l][:], bcast_rows(b_forget_t, l * 6, 6), f"BFG{l}")
            setup_dma(BS[l][:], bass.AP(bs_t, l * 768, [[1, 128], [128, 6]]), f"BS{l}", allow_slow_non_contiguous=True)
        last_setup = P.ops[-1]
        for r in setup_res:
            r.lw = last_setup
            r.rd = []
        P.op("dve", lambda e: e.memset(onesb[:], 1.0), writes=[R("onesb")])
        P.op("dve", lambda e: e.memset(stat[:], 1.0e6), writes=[R("statn"), R("dtmp0"), R("dtmp1")] + [R(f"statn{i}") for i in range(4)] + [R(f"statv{i}") for i in range(4)] + [R(f"statp{i}") for i in range(4)])
        for l in range(L):
            P.op("dve", lambda e, l=l: e.memset(VP[l][:, :, :, 64:65], 1.0), writes=[R(f"VPones{l}")])
            P.op("dve", lambda e, l=l: e.memset(VMP[l][:, :, :, 64:65], 1.0), writes=[R(f"VMPones{l}")])

        def rstd_from_ss(ss_ap, out_ap, n, rd, wr):
            P.op("act", lambda e: e.activation(out=out_ap, in_=ss_ap, func=AF.Ln, scale=1.0 / n, bias=EPS), reads=rd, writes=wr)
            P.op("act", lambda e: e.activation(out=out_ap, in_=out_ap, func=AF.Exp, scale=-0.5), reads=wr, writes=wr)

        tbank_ctr = [0]

        def norm_transpose_n(srcs, gcol_ap, gres, dsts, batched=True, base=0):
            if not batched and len(srcs) > 1:
                for i in range(len(srcs)):
                    norm_transpose_n(srcs[i:i + 1], gcol_ap, gres, dsts[i:i + 1], batched=True, base=i)
                return
            n = len(srcs)
            ssr = R("statn" if n > 1 else f"statn{base}")
            xres_ = [[AR("PT0"), AR("PT1")], [AR("zA")]]
            for i, (src_ap, src_res) in enumerate(srcs):
                ci = i + base
                P.op("act", lambda e, ci=ci, src_ap=src_ap: e.activation(out=xnb[0], in_=src_ap, func=AF.Square, accum_out=stat[:, ci:ci + 1]),
                     reads=[src_res], writes=[ssr] + (xres_[0] if i == 0 else []))
            rstd_from_ss(stat[:, base:base + n], stat[:, 4 + base:4 + base + n], D, [ssr], [ssr])
            tbs = {}

            def stage_a(i):
                (src_ap, src_res) = srcs[i]
                ci = i + base
                xb = (i + base) % 2
                P.op("dve", lambda e: e.tensor_scalar(out=xnb[xb], in0=src_ap, scalar1=stat[:, 4 + ci:5 + ci], scalar2=None, op0=ALU.mult),
                     reads=[src_res, ssr], writes=xres_[xb])
                tb = 6 + (tbank_ctr[0] % 2)
                tbank_ctr[0] += 1
                tbs[i] = tb
                for kc in range(KC):
                    P.op("pe", lambda e, kc=kc: e.transpose(out=banks_bf[tb][:, kc * 128:(kc + 1) * 128], in_=xnb[xb][:, kc * 128:(kc + 1) * 128], identity=ident[:]),
                         reads=xres_[xb] + [R("ident")], writes=[BK(tb)])

            def stage_b(i):
                (dst_ap, dst_res) = dsts[i]
                tb = tbs[i]
                P.op("dve", lambda e: e.tensor_tensor(out=dst_ap, in0=banks_bf[tb][:, 0:1024].rearrange("p (k t) -> p k t", t=128),
                                                      in1=gcol_ap.unsqueeze(2).to_broadcast([128, KC, 128]), op=ALU.mult),
                     reads=[BK(tb), gres], writes=[dst_res])

            for i in range(n):
                stage_a(i)
                if i >= 1:
                    stage_b(i - 1)
            stage_b(n - 1)

        bank_rr = [0]

        def next_bank(lo=0, hi=6):
            b = lo + (bank_rr[0] % (hi - lo))
            bank_rr[0] += 1
            return b

        def layer_setup(l):
            for mt in range(2):
                P.dma("sp", lambda e, mt=mt: e.dma_start(out=xg[:, mt, :], in_=mem_d[mt * 128:(mt + 1) * 128, :]), d_x[mt], writes=[R(f"xg{mt}")])
            norm_transpose_n([(xg[:, mt, :], R(f"xg{mt}")) for mt in range(2)], GCOL[l][:, 2, :], R(f"GCOL{l}"),
                             [(hT[:, :, mt * 128:(mt + 1) * 128], R(f"hT{mt}")) for mt in range(2)])
            si = load_slot(lambda s: s[:, :].rearrange("p (k n) -> p k n", n=512), wkv_d[l].rearrange("(k p) n -> p k n", p=128))
            wkv = slots[si][:, :].rearrange("p (k n) -> p k n", n=512)
            hres = [R("hT0"), R("hT1")]
            for pm in range(2):
                b = next_bank()
                for kc in range(KC):
                    P.op("pe", lambda e, kc=kc, pm=pm, b=b: e.matmul(banks[b][:, 0:NMEM], lhsT=wkv[:, kc, pm * 128:(pm + 1) * 128], rhs=hT[:, kc, 0:NMEM],
                                                                      start=(kc == 0), stop=(kc == KC - 1)),
                         reads=[R(f"slot{si}")] + hres, writes=[BK(b)])
                P.op("act", lambda e, pm=pm, b=b: e.activation(out=KMT[l][:, pm, :], in_=banks[b][:, 0:NMEM], func=AF.Copy),
                     reads=[BK(b)], writes=[R(f"KMT{l}")])
            for mt in range(2):
                b = next_bank()
                for kc in range(KC):
                    P.op("pe", lambda e, kc=kc, mt=mt, b=b: e.matmul(banks[b][:, 0:256], lhsT=hT[:, kc, mt * 128:(mt + 1) * 128], rhs=wkv[:, kc, 256:512],
                                                                      start=(kc == 0), stop=(kc == KC - 1)),
                         reads=[R(f"slot{si}"), hres[mt]], writes=[BK(b)])
                P.op("act", lambda e, mt=mt, b=b: e.activation(out=VMP[l][:, mt, :, 0:64], in_=banks[b][:, 0:256].rearrange("p (h d) -> p h d", d=64), func=AF.Copy),
                     reads=[BK(b)], writes=[R(f"VMP{l}")])
            P.dma("sp", lambda e: e.dma_start(out=zA.rearrange("p (g s) -> p g s", s=128), in_=ws_d[l].rearrange("g t s -> t g s")), d_ws,
                  writes=[AR("zA")])
            wsb = xnb[0][:, 0:768]
            P.op("dve", lambda e: e.tensor_copy(out=wsb, in_=zA), reads=[AR("zA")], writes=[AR("PT0"), AR("PT1")])
            tb = 6 + (tbank_ctr[0] % 2)
            tbank_ctr[0] += 1
            for gg in range(6):
                P.op("pe", lambda e, gg=gg: e.transpose(out=banks_bf[tb][:, gg * 128:(gg + 1) * 128], in_=wsb[:, gg * 128:(gg + 1) * 128], identity=ident[:]),
                     reads=[AR("PT0"), AR("PT1"), R("ident")], writes=[BK(tb)])
            P.op("dve", lambda e: e.tensor_tensor(out=WST[l][:], in0=banks_bf[tb][:, 0:768].rearrange("p (g t) -> p g t", t=128),
                                                  in1=triu[:].unsqueeze(1).to_broadcast([128, 6, 128]), op=ALU.mult),
                 reads=[BK(tb), R("triu")], writes=[R(f"WST{l}")])

        def mixer(g, l):
            xres = [R(f"xg{tt}") for tt in range(4)]
            hres = [R(f"hT{tt}") for tt in range(4)]
            P.dma("sp", lambda e: e.dma_start(out=gpm[:], in_=bcast_rows(g_postmix_t, l * D, D)), d_gpm, writes=[R("gpm")])
            P.dma("sp", lambda e: e.dma_start(out=GVS, in_=bcast_rows(gv_t, l * 384, 384)), d_gv, writes=[AR("bcsb")])
            norm_transpose_n([(xg[:, tt, :], xres[tt]) for tt in range(4)], GCOL[l][:, 0, :], R(f"GCOL{l}"),
                             [(hT[:, :, tt * 128:(tt + 1) * 128], hres[tt]) for tt in range(4)], batched=(l == 0))
            if dbg == "norm":
                raise StopBuild()
            blk = [None] * 5

            def load_blk(bi):
                c0, ncol = WIN_BLOCKS[bi]
                si = load_slot(lambda s, ncol=ncol: s[:, 0:KC * ncol].rearrange("p (k n) -> p k n", n=ncol),
                               w_in_d[l][:, c0:c0 + ncol].rearrange("(k p) n -> p k n", p=128))
                blk[bi] = (si, slots[si][:, 0:KC * ncol].rearrange("p (k n) -> p k n", n=ncol))

            load_blk(0)
            load_blk(1)
            load_blk(2)
            load_blk(4)
            P.op("dve", lambda e: e.memset(QT[:, :, :], 0.0), writes=[AR("QTz")] + [AR(f"QT{c}") for c in range(3)])
            P.op("dve", lambda e: e.memset(attT[64:128, :, :], 0.0), writes=[AR("attTz")])
            zAb = [zA, arena[:, PT_OFF:PT_OFF + 1536].bitcast(F32)]
            zAr = [[AR("zA")], [AR("PT0"), AR("PT1"), AR("PT2")]]

            def a_mm(tt):
                zi = tt % 2
                ba, bb = next_bank(), next_bank()
                for kc in range(KC):
                    P.op("pe", lambda e, kc=kc: e.matmul(banks[ba][:, :], lhsT=hT[:, kc, tt * 128:(tt + 1) * 128], rhs=blk[0][1][:, kc, :],
                                                          start=(kc == 0), stop=(kc == KC - 1)),
                         reads=[hres[tt], R(f"slot{blk[0][0]}")], writes=[BK(ba)])
                for kc in range(KC):
                    P.op("pe", lambda e, kc=kc: e.matmul(banks[bb][:, 0:256], lhsT=hT[:, kc, tt * 128:(tt + 1) * 128], rhs=blk[1][1][:, kc, 0:256],
                                                          start=(kc == 0), stop=(kc == KC - 1)),
                         reads=[hres[tt], R(f"slot{blk[1][0]}")], writes=[BK(bb)])
                P.op("act", lambda e: e.activation(out=zAb[zi][:, 0:512], in_=banks[ba][:, :], func=AF.Gelu_apprx_tanh), reads=[BK(ba)], writes=zAr[zi])
                P.op("act", lambda e: e.activation(out=zAb[zi][:, 512:768], in_=banks[bb][:, 0:256], func=AF.Gelu_apprx_tanh), reads=[BK(bb)], writes=zAr[zi])

            gstate = {}

            def g_vnorm(tt):
                zi = tt % 2
                zz, zr = zAb[zi], zAr[zi]
                v3 = zz[:, 384:768].rearrange("p (g d) -> p g d", d=64)
                P.op("dve", lambda e: e.tensor_tensor(out=tmpv, in0=zz[:, 384:768], in1=zz[:, 384:768], op=ALU.mult), reads=zr, writes=[AR("rs")])
                c0 = 16 + tt * 8
                sr = R(f"statv{tt}")
                P.op("dve", lambda e: e.reduce_sum(out=stat[:, c0:c0 + 6], in_=tmpv.rearrange("p (g d) -> p g d", d=64), axis=AX.X),
                     reads=[AR("rs")], writes=[sr])
                rstd_from_ss(stat[:, c0:c0 + 6], stat[:, c0:c0 + 6], 64, [sr], [sr])
                P.op("dve", lambda e: e.tensor_tensor(out=tmpv.rearrange("p (g d) -> p g d", d=64), in0=v3,
                                                      in1=stat[:, c0:c0 + 6].unsqueeze(2).to_broadcast([128, 6, 64]), op=ALU.mult),
                     reads=zr + [sr], writes=[AR("rs")])
                P.op("dve", lambda e: e.tensor_tensor(out=vn, in0=tmpv, in1=GVS, op=ALU.mult), reads=[AR("rs"), AR("bcsb")], writes=[AR("vn")])

            def g_mix(tt):
                zi = tt % 2
                zz, zr = zAb[zi], zAr[zi]
                bm = next_bank()
                for gg in range(6):
                    P.op("pe", lambda e, gg=gg: e.matmul(banks[bm][:, gg * 64:(gg + 1) * 64], lhsT=WST[l][:, gg, :], rhs=vn[:, gg * 64:(gg + 1) * 64],
                                                          start=True, stop=True),
                         reads=[AR("vn"), R(f"WST{l}")], writes=[BK(bm)])
                P.op("dve", lambda e: e.tensor_tensor(out=tmpv2.rearrange("p (g d) -> p g d", d=64), in0=banks[bm][:, 0:384].rearrange("p (g d) -> p g d", d=64),
                                                      in1=BS[l][:].unsqueeze(2).to_broadcast([128, 6, 64]), op=ALU.add),
                     reads=[BK(bm), R(f"BS{l}")], writes=[AR("tmpv2")])
                P.op("dve", lambda e: e.tensor_tensor(out=outa, in0=tmpv2, in1=zz[:, 0:384], op=ALU.mult), reads=[AR("tmpv2")] + zr, writes=[AR("outa")])

            def g_tr(tt):
                tb = 6 + (tbank_ctr[0] % 2)
                tbank_ctr[0] += 1
                for c in range(3):
                    P.op("pe", lambda e, c=c: e.transpose(out=banks_bf[tb][:, c * 128:(c + 1) * 128], in_=outa[:, c * 128:(c + 1) * 128], identity=ident[:]),
                         reads=[AR("outa"), R("ident")], writes=[BK(tb)])
                P.op("act", lambda e: e.activation(out=gmT[:, :, tt * 128:(tt + 1) * 128], in_=banks_bf[tb][:, 0:384].rearrange("p (c t) -> p c t", t=128), func=AF.Copy),
                     reads=[BK(tb)], writes=[AR(f"gmT{tt}")])

            fm = [(1, 256, "q", 0), (1, 384, "q", 1), (2, 0, "q", 2),
                  (2, 128, "k", 0), (2, 256, "k", 1), (2, 384, "k", 2),
                  (4, 0, "m", 0), (4, 128, "m", 1)]

            def fm_chunk(bi, lc, kind, c):
                b = next_bank()
                for kc in range(KC):
                    P.op("pe", lambda e, kc=kc: e.matmul(banks[b][:, :], lhsT=blk[bi][1][:, kc, lc:lc + 128], rhs=hT[:, kc, :],
                                                          start=(kc == 0), stop=(kc == KC - 1)),
                         reads=hres + [R(f"slot{blk[bi][0]}")], writes=[BK(b)])
                if kind == "q":
                    P.op("act", lambda e: e.activation(out=QT[0:64, 2 * c, :], in_=banks[b][0:64, :], func=AF.Copy, scale=0.125), reads=[BK(b), AR("QTz")], writes=[AR(f"QT{c}")])
                    P.op("act", lambda e: e.activation(out=QT[64:128, 2 * c + 1, :], in_=banks[b][64:128, :], func=AF.Copy, scale=0.125), reads=[BK(b), AR("QTz")], writes=[AR(f"QT{c}")])
                elif kind == "m":
                    P.op("act", lambda e: e.activation(out=QMT[:, c, :], in_=banks[b][:, :], func=AF.Copy, scale=0.125), reads=[BK(b)], writes=[AR(f"QMT{c}")])
                else:
                    P.op("dve", lambda e: e.tensor_copy(out=KT[l][:, c, g * T:(g + 1) * T], in_=banks[b][:, :]), reads=[BK(b)], writes=[R(f"KT{l}_{c}")])

            def c_tile(tt):
                b = next_bank()
                for kc in range(KC):
                    P.op("pe", lambda e, kc=kc: e.matmul(banks[b][:, 0:390], lhsT=hT[:, kc, tt * 128:(tt + 1) * 128], rhs=blk[3][1][:, kc, :],
                                                          start=(kc == 0), stop=(kc == KC - 1)),
                         reads=[hres[tt], R(f"slot{blk[3][0]}")], writes=[BK(b)])
                P.op("act", lambda e: e.activation(out=VP[l][:, 4 * g + tt, :, 0:64], in_=banks[b][:, 0:384].rearrange("p (h d) -> p h d", d=64), func=AF.Copy),
                     reads=[BK(b)], writes=[R(f"VP{l}")])
                P.op("dve", lambda e: e.tensor_tensor(out=fg[:, tt, :], in0=banks[b][:, 384:390], in1=BFG[l][:], op=ALU.add),
                     reads=[BK(b), R(f"BFG{l}")], writes=[AR("fg")])

            a_mm(0)
            g_vnorm(0)
            a_mm(1)
            for q in fm[0:4]:
                fm_chunk(*q)
            g_mix(0)
            a_mm(2)
            for q in fm[4:6]:
                fm_chunk(*q)
            g_tr(0)
            g_vnorm(1)
            for q in fm[6:8]:
                fm_chunk(*q)
            g_mix(1)
            a_mm(3)
            load_blk(3)
            c_tile(0)
            g_tr(1)
            g_vnorm(2)
            c_tile(1)
            c_tile(2)
            g_mix(2)
            c_tile(3)
            g_tr(2)
            g_vnorm(3)
            P.op("act", lambda e: e.activation(out=spb, in_=fg, func=AF.Exp, scale=-1.0), reads=[AR("fg")], writes=[AR("spb")])
            P.op("act", lambda e: e.activation(out=spb, in_=spb, func=AF.Ln, bias=1.0), reads=[AR("spb")], writes=[AR("spb")])
            for tt in range(4):
                ti = 4 * g + tt
                b = next_bank()
                P.op("pe", lambda e, b=b, tt=tt: e.matmul(banks[b][:, 0:6], lhsT=triuf[:], rhs=spb[:, tt, :], start=True, stop=True),
                     reads=[AR("spb"), R("triuf")], writes=[BK(b)])
                P.op("pe", lambda e, b=b, tt=tt: e.matmul(banks[b][:, 8:14], lhsT=onesf[:], rhs=spb[:, tt, :], start=True, stop=True),
                     reads=[AR("spb"), R("onesf")], writes=[BK(b)])
                cr = R(f"CE{l}")
                if ti == 0:
                    P.op("dve", lambda e, b=b, ti=ti: e.tensor_copy(out=CALL[l][:, ti, :], in_=banks[b][:, 0:6]), reads=[BK(b)], writes=[cr])
                    P.op("dve", lambda e, b=b, ti=ti: e.tensor_copy(out=EALL[l][:, ti, :], in_=banks[b][:, 8:14]), reads=[BK(b)], writes=[cr])
                else:
                    P.op("dve", lambda e, b=b, ti=ti: e.tensor_tensor(out=CALL[l][:, ti, :], in0=banks[b][:, 0:6], in1=EALL[l][:, ti - 1, :], op=ALU.add),
                         reads=[BK(b), cr], writes=[cr])
                    P.op("dve", lambda e, b=b, ti=ti: e.tensor_tensor(out=EALL[l][:, ti, :], in0=banks[b][:, 8:14], in1=EALL[l][:, ti - 1, :], op=ALU.add),
                         reads=[BK(b), cr], writes=[cr])

            g_mix(3)
            g_tr(3)
            if dbg in ("win", "gm", "gm2"):
                raise StopBuild()
            nj = 4 * g + 4
            sb_rr = [0]
            pt_rr = [0]
            ktres = [R(f"KT{l}_{c}") for c in range(3)]

            def normalize(ob, hidx):
                P.op("dve", lambda e: e.reciprocal(out=rs[64:65, :], in_=banks[ob][64:65, :]), reads=[BK(ob)], writes=[AR("rs")])
                P.op("pe", lambda e: e.matmul(banks[5][0:64, :], lhsT=onesf[64:65, 0:64], rhs=rs[64:65, :], start=True, stop=True),
                     reads=[AR("rs"), R("onesf")], writes=[BK(5)])
                P.op("act", lambda e: e.activation(out=bcsb[0:64, :], in_=banks[5][0:64, :], func=AF.Copy), reads=[BK(5)], writes=[AR("bcsb")])
                P.op("dve", lambda e: e.tensor_tensor(out=attT[0:64, hidx, :], in0=banks[ob][0:64, :], in1=bcsb[0:64, :], op=ALU.mult),
                     reads=[BK(ob), AR("bcsb")], writes=[AR(f"attT{hidx}")])

            LOOK = 2
            pend = []
            deferred = []

            def tick():
                for d in deferred:
                    d[0] -= 1
                while deferred and deferred[0][0] <= 0:
                    deferred.pop(0)[1]()

            def norm_part1(ob):
                P.op("act", lambda e: e.activation(out=rs[64:65, :], in_=banks[ob][64:65, :], func=AF.Ln), reads=[BK(ob)], writes=[AR("rs")])
                P.op("act", lambda e: e.activation(out=rs[64:65, :], in_=rs[64:65, :], func=AF.Exp, scale=-1.0), reads=[AR("rs")], writes=[AR("rs")])

            def norm_part2(ob, hidx):
                P.op("pe", lambda e: e.matmul(banks[5][0:64, :], lhsT=onesf[64:65, 0:64], rhs=rs[64:65, :], start=True, stop=True),
                     reads=[AR("rs"), R("onesf")], writes=[BK(5)])
                P.op("act", lambda e: e.activation(out=bcsb[0:64, :], in_=banks[5][0:64, :], func=AF.Copy), reads=[BK(5)], writes=[AR("bcsb")])
                P.op("dve", lambda e: e.tensor_tensor(out=attT[0:64, hidx, :], in0=banks[ob][0:64, :], in1=bcsb[0:64, :], op=ALU.mult),
                     reads=[BK(ob), AR("bcsb")], writes=[AR(f"attT{hidx}")])

            def emit_pv(blk_):
                (kind, hidx, j, col0, pk, ob, first, last) = blk_
                if kind == "f":
                    P.op("pe", lambda e: e.matmul(banks[ob][0:65, col0:T], lhsT=VP[l][:, j, hidx, :], rhs=PT[pk][:, col0:T], start=first, stop=last),
                         reads=[R(f"VP{l}"), R(f"VPones{l}"), AR(f"PT{pk}")], writes=[BK(ob)])
                else:
                    P.op("pe", lambda e: e.matmul(banks[ob][0:65, :], lhsT=VMP[l][:, j, hidx - 6, :], rhs=PT[pk][:, :], start=first, stop=last),
                         reads=[R(f"VMP{l}"), R(f"VMPones{l}"), AR(f"PT{pk}")], writes=[BK(ob)])
                if last:
                    while deferred:
                        deferred.pop(0)[1]()
                    norm_part1(ob)
                    deferred.append([4, lambda ob=ob, hidx=hidx: norm_part2(ob, hidx)])

            def push(blk_):
                pend.append(blk_)
                if len(pend) > LOOK:
                    emit_pv(pend.pop(0))
                tick()

            for h in range(6):
                p, r0 = h // 2, 64 * (h % 2)
                par = h % 2
                br = AR(f"beta{par}")
                P.op("dve", lambda e, par=par, h=h: e.tensor_scalar(out=beta[:, par, 0:nj], in0=CALL[l][:, 0:nj, h],
                                                                     scalar1=EALL[l][:, 4 * g + 3, h:h + 1], scalar2=None, op0=ALU.subtract),
                     reads=[R(f"CE{l}")], writes=[br])
                dr = AR(f"daug{par}")
                P.op("dve", lambda e, par=par, h=h: e.tensor_scalar(out=stat[:, 8 + 4 * par:12 + 4 * par], in0=EALL[l][:, 4 * g:4 * g + 4, h],
                                                                     scalar1=EALL[l][:, 4 * g + 3, h:h + 1], scalar2=-1.0 / 128, op0=ALU.subtract, op1=ALU.mult),
                     reads=[R(f"CE{l}")], writes=[R(f"dtmp{par}")])
                P.op("dve", lambda e, par=par: e.tensor_copy(out=daug[:, par, :].rearrange("p (a b) -> p a b", b=128),
                                                              in_=stat[:, 8 + 4 * par:12 + 4 * par].unsqueeze(2).to_broadcast([128, 4, 128])),
                     reads=[R(f"dtmp{par}")], writes=[dr])
                ob = 3 + (h % 2)
                for j in range(nj):
                    il0 = max(0, j - 4 * g)
                    col0 = il0 * 128
                    sbk = sb_rr[0] % 3
                    sb_rr[0] += 1
                    pk = pt_rr[0] % NPT
                    pt_rr[0] += 1
                    P.op("pe", lambda e, j=j, col0=col0, sbk=sbk, p=p, h=h: e.matmul(banks[sbk][:, col0:T], lhsT=KT[l][:, p, j * 128:(j + 1) * 128],
                                                                                        rhs=QT[:, h, col0:T], start=True, stop=False),
                         reads=[ktres[p], AR(f"QT{p}")], writes=[BK(sbk)])
                    P.op("pe", lambda e, col0=col0, sbk=sbk, par=par: e.matmul(banks[sbk][:, col0:T], lhsT=onesb[:], rhs=daug[:, par, col0:T], start=False, stop=True),
                         reads=[R("onesb"), dr], writes=[BK(sbk)])
                    P.op("act", lambda e, j=j, col0=col0, sbk=sbk, pk=pk, par=par: e.activation(out=PT[pk][:, col0:T], in_=banks[sbk][:, col0:T],
                                                                                                func=AF.Exp, bias=beta[:, par, j:j + 1]),
                         reads=[BK(sbk), br], writes=[AR(f"PT{pk}")])
                    if j >= 4 * g:
                        P.op("dve", lambda e, il0=il0, pk=pk: e.tensor_tensor(out=PT[pk][:, il0 * 128:(il0 + 1) * 128], in0=PT[pk][:, il0 * 128:(il0 + 1) * 128],
                                                                                in1=triu[:], op=ALU.mult),
                             reads=[AR(f"PT{pk}"), R("triu")], writes=[AR(f"PT{pk}")])
                    push(("f", h, j, col0, pk, ob, j == 0, j == nj - 1))
            for hm in range(4):
                p, r0 = hm // 2, 64 * (hm % 2)
                ob = 3 + (hm % 2)
                for jm in range(2):
                    sbk = sb_rr[0] % 3
                    sb_rr[0] += 1
                    pk = pt_rr[0] % NPT
                    pt_rr[0] += 1
                    P.op("pe", lambda e, jm=jm, sbk=sbk, p=p, r0=r0: e.matmul(banks[sbk][:, :], lhsT=KMT[l][r0:r0 + 64, p, jm * 128:(jm + 1) * 128],
                                                                               rhs=QMT[r0:r0 + 64, p, :], start=True, stop=True),
                         reads=[R(f"KMT{l}"), AR(f"QMT{p}")], writes=[BK(sbk)])
                    P.op("act", lambda e, sbk=sbk, pk=pk: e.activation(out=PT[pk][:, :], in_=banks[sbk][:, :], func=AF.Exp),
                         reads=[BK(sbk)], writes=[AR(f"PT{pk}")])
                    push(("m", 6 + hm, jm, 0, pk, ob, jm == 0, jm == 1))
            while pend:
                emit_pv(pend.pop(0))
                tick()
            while deferred:
                deferred.pop(0)[1]()
            if dbg == "att":
                raise StopBuild()
            sg = load_slot(lambda s: s[:, 0:3 * D].rearrange("p (c n) -> p c n", n=D), wout_d[l][0:384, :].rearrange("(c p) n -> p c n", p=128))
            wo_g = slots[sg][:, 0:3 * D].rearrange("p (c n) -> p c n", n=D)
            wo_h = []
            for (h0, nh) in ((0, 4), (4, 4), (8, 2)):
                si = load_slot(lambda s, nh=nh: s[0:64, 0:nh * D].rearrange("p (c n) -> p c n", n=D),
                               wout_d[l][384 + 64 * h0:384 + 64 * (h0 + nh), :].rearrange("(c p) n -> p c n", p=64))
                v = slots[si][:, 0:nh * D].rearrange("p (c n) -> p c n", n=D)
                for k in range(nh):
                    wo_h.append((si, v, k))
            attres = [AR(f"attT{i}") for i in range(10)]
            for tt in range(4):
                yb = [next_bank(0, 4), next_bank(0, 4)]
                for half in range(2):
                    b = yb[half]
                    for c in range(3):
                        P.op("pe", lambda e, c=c, tt=tt, half=half, b=b: e.matmul(banks[b][:, :], lhsT=gmT[:, c, tt * 128:(tt + 1) * 128],
                                                                                   rhs=wo_g[:, c, half * 512:(half + 1) * 512], start=(c == 0), stop=False),
                             reads=[AR(f"gmT{tt}"), R(f"slot{sg}")], writes=[BK(b)])
                    for hh in range(10):
                        si, v, k = wo_h[hh]
                        P.op("pe", lambda e, hh=hh, tt=tt, half=half, b=b, v=v, k=k: e.matmul(banks[b][:, :], lhsT=attT[:, hh, tt * 128:(tt + 1) * 128],
                                                                                               rhs=v[:, k, half * 512:(half + 1) * 512], start=False, stop=(hh == 9)),
                             reads=[attres[hh], AR("attTz"), R(f"slot{si}")], writes=[BK(b)])
                post_norm(yb, tt, gpm, R("gpm"), xres[tt])

        def post_norm(yb, tt, gbuf, gres, xr):
            c0 = 48 + tt * 4
            sr = R(f"statp{tt}")
            for half in range(2):
                P.op("act", lambda e, half=half: e.activation(out=ytmp_bf[:, half * 512:(half + 1) * 512], in_=banks[yb[half]][:, :], func=AF.Square,
                                                               accum_out=stat[:, c0 + half:c0 + half + 1]),
                     reads=[BK(yb[half])], writes=[sr, R("ytmp")])
            P.op("dve", lambda e: e.tensor_tensor(out=stat[:, c0 + 2:c0 + 3], in0=stat[:, c0:c0 + 1], in1=stat[:, c0 + 1:c0 + 2], op=ALU.add), reads=[sr], writes=[sr])
            rstd_from_ss(stat[:, c0 + 2:c0 + 3], stat[:, c0 + 3:c0 + 4], D, [sr], [sr])
            for half in range(2):
                P.op("dve", lambda e, half=half: e.scalar_tensor_tensor(out=ytmp, in0=banks[yb[half]][:, :], scalar=stat[:, c0 + 3:c0 + 4],
                                                                         in1=gbuf[:, half * 512:(half + 1) * 512], op0=ALU.mult, op1=ALU.mult),
                     reads=[BK(yb[half]), sr, gres], writes=[R("ytmp")])
                P.op("dve", lambda e, half=half: e.tensor_tensor(out=xg[:, tt, half * 512:(half + 1) * 512], in0=xg[:, tt, half * 512:(half + 1) * 512], in1=ytmp, op=ALU.add),
                     reads=[R("ytmp"), xr], writes=[xr])

        def ffn(g, l, last):
            xres = [R(f"xg{tt}") for tt in range(4)]
            hres = [R(f"hT{tt}") for tt in range(4)]
            P.dma("sp", lambda e: e.dma_start(out=gpf[:], in_=bcast_rows(g_postffn_t, l * D, D)), d_gpf, writes=[R("gpm")])
            norm_transpose_n([(xg[:, tt, :], xres[tt]) for tt in range(4)], GCOL[l][:, 1, :], R(f"GCOL{l}"),
                             [(hT[:, :, tt * 128:(tt + 1) * 128], hres[tt]) for tt in range(4)], batched=False)
            P.fence(AG)
            for blk_i in range(8):
                si = load_slot(lambda s: s[:, :].rearrange("p (k n) -> p k n", n=512),
                               w1_d[l][:, blk_i * 512:(blk_i + 1) * 512].rearrange("(k p) n -> p k n", p=128))
                wb = slots[si][:, :].rearrange("p (k n) -> p k n", n=512)
                for fcl in range(4):
                    fc = blk_i * 4 + fcl
                    b = next_bank()
                    for kc in range(KC):
                        P.op("pe", lambda e, kc=kc, fcl=fcl, b=b, wb=wb: e.matmul(banks[b][:, :], lhsT=wb[:, kc, fcl * 128:(fcl + 1) * 128], rhs=hT[:, kc, :],
                                                                                   start=(kc == 0), stop=(kc == KC - 1)),
                             reads=hres + [R(f"slot{si}")], writes=[BK(b)])
                    rk = 0
                    P.op("act", lambda e, b=b, rk=rk: e.activation(out=rtmp[rk], in_=banks[b][:, :], func=AF.Relu), reads=[BK(b)], writes=[AR(f"rtmp{rk}")])
                    P.op("dve", lambda e, fc=fc, rk=rk: e.tensor_tensor(out=hidT[:, fc, :], in0=rtmp[rk], in1=rtmp[rk], op=ALU.mult),
                         reads=[AR(f"rtmp{rk}")], writes=[AR(f"hidT{fc}")])
            if dbg == "ffn1":
                raise StopBuild()
            for tp in range(1):
                for blk_i in range(8):
                    si = load_slot(lambda s: s[:, :].rearrange("p (c n) -> p c n", n=D),
                                   w2_d[l][blk_i * 512:(blk_i + 1) * 512, :].rearrange("(c p) n -> p c n", p=128))
                    wb = slots[si][:, :].rearrange("p (c n) -> p c n", n=D)
                    for fcl in range(4):
                        fc = blk_i * 4 + fcl
                        for tt in range(4):
                            for half in range(2):
                                b = tt * 2 + half
                                P.op("pe", lambda e, fc=fc, fcl=fcl, tt=tt, half=half, b=b, wb=wb: e.matmul(banks[b][:, :], lhsT=hidT[:, fc, tt * 128:(tt + 1) * 128],
                                                                                                             rhs=wb[:, fcl, half * 512:(half + 1) * 512],
                                                                                                             start=(fc == 0), stop=(fc == 31)),
                                     reads=[AR(f"hidT{fc}"), R(f"slot{si}")], writes=[BK(b)])
                for tt in range(4):
                    post_norm([tt * 2, tt * 2 + 1], tt, gpf, R("gpm"), xres[tt])
                    if last:
                        r0 = (4 * g + tt) * 128
                        P.dma("sp", lambda e, tt=tt, r0=r0: e.dma_start(out=out_d[r0:r0 + 128, :], in_=xg[:, tt, :]), d_o[tt],
                              reads=[xres[tt]], writes=[R(f"outd{tt}")])
            P.fence(AG)

        try:
            for l in range(L):
                layer_setup(l)
            P.fence(AG)
            if dbg == "setup":
                raise StopBuild()
            for g in range(NG):
                for tt in range(4):
                    r0 = (4 * g + tt) * 128
                    P.dma("sp", lambda e, tt=tt, r0=r0: e.dma_start(out=xg[:, tt, :], in_=x_d[r0:r0 + 128, :]), d_x[tt], writes=[R(f"xg{tt}")])
                for l in range(L):
                    mixer(g, l)
                    if dbg == "mix" or dbg == f"mix:{g}:{l}":
                        raise StopBuild()
                    ffn(g, l, last=(l == L - 1))
                    if dbg == f"ffn:{g}:{l}":
                        raise StopBuild()
        except StopBuild:
            if dbg == "win":
                P.op("dve", lambda e: e.tensor_copy(out=xg[:, 0, :].rearrange("p (c t) -> p c t", t=T), in_=QMT[:, :, :]),
                     reads=[AR("QMT0"), AR("QMT1"), R("xg0")], writes=[R("xg0")])
                P.op("dve", lambda e: e.tensor_copy(out=xg[:, 1, 0:512].rearrange("p (c t) -> p c t", t=256), in_=KMT[0][:, :, :]),
                     reads=[R("KMT0"), R("xg1")], writes=[R("xg1")])
                P.op("dve", lambda e: e.tensor_copy(out=xg[:, 2, 0:520].rearrange("p (c t) -> p c t", t=260), in_=VMP[0][:, :, :, :].rearrange("p a b c -> p a (b c)")),
                     reads=[R("VMP0"), R("VMPones0"), R("xg2")], writes=[R("xg2")])
            if dbg == "gm2":
                ar = [AR("zA"), AR("vn"), AR("tmpv2"), AR("outa"), R("WST0")] + [R(f"xg{i}") for i in range(4)]
                wr = [R(f"xg{i}") for i in range(4)]
                P.op("dve", lambda e: e.tensor_copy(out=xg[:, 0, 0:768], in_=zA), reads=ar, writes=wr)
                P.op("dve", lambda e: e.tensor_copy(out=xg[:, 1, 0:384], in_=vn), reads=ar, writes=wr)
                P.op("dve", lambda e: e.tensor_copy(out=xg[:, 1, 384:768], in_=tmpv2), reads=ar, writes=wr)
                P.op("dve", lambda e: e.tensor_copy(out=xg[:, 2, 0:384], in_=outa), reads=ar, writes=wr)
                P.op("dve", lambda e: e.tensor_copy(out=xg[:, 3, 0:768], in_=WST[0][:, :, :].rearrange("p g t -> p (g t)")), reads=ar, writes=wr)
            if dbg == "gm":
                P.op("dve", lambda e: e.tensor_copy(out=xg[:, 0, :].rearrange("p (c t) -> p c t", t=T), in_=gmT[:, 0:2, :]),
                     reads=[AR(f"gmT{i}") for i in range(4)] + [R("xg0")], writes=[R("xg0")])
                P.op("dve", lambda e: e.tensor_copy(out=xg[:, 1, 0:512], in_=gmT[:, 2, :]),
                     reads=[AR(f"gmT{i}") for i in range(4)] + [R("xg1")], writes=[R("xg1")])
            if dbg in ("att3", "att", "att5", "att4"):
                for q in range(4):
                    P.op("dve", lambda e, q=q: e.tensor_copy(out=xg[:, q, :].rearrange("p (c t) -> p c t", t=T), in_=attT[:, 2 * q:2 * q + 2, :]),
                         reads=[AR(f"attT{i}") for i in range(10)] + [R(f"xg{q}")], writes=[R(f"xg{q}")])
            for tt in range(4):
                P.dma("sp", lambda e, tt=tt: e.dma_start(out=out_d[tt * 128:(tt + 1) * 128, :], in_=xg[:, tt, :]), d_o[tt],
                      reads=[R(f"xg{tt}")], writes=[R(f"outd{tt}")])
        P.final()
        if build.want_trace:
            P.trace = []
        P.emit(st)
        build.stats = P.stats
        build.trace = P.trace
    return nc


build.want_trace = False

WNAMES = ["norm_pre_mix", "norm_post_mix", "norm_pre_ffn", "norm_post_ffn", "norm_mem", "w_in", "b_forget",
          "gmlp_v_norm", "gmlp_w_s", "gmlp_b_s", "w_mem_kv", "w_out", "w_ff1", "w_ff2"]

_cache = {}


def _consts():
    iu = np.triu(np.ones((128, 128), np.float32))
    return {
        "c_ident": np.eye(128, dtype=np.float32).astype(ml_dtypes.bfloat16),
        "c_triu": iu.astype(ml_dtypes.bfloat16),
        "c_triuf": iu.copy(),
        "c_onesf": np.ones((128, 128), np.float32),
    }


DBG = None


def run_layers(x, mem, weights, L):
    B, S, _ = x.shape
    key = (S, L)
    if key not in _cache:
        _cache[key] = build(S, L, dbg=DBG)
    nc = _cache[key]
    consts = _consts()
    in_maps = []
    for b in range(B):
        m = {"x": np.ascontiguousarray(x[b]), "mem": np.ascontiguousarray(mem[b])}
        for k in WNAMES:
            m[k] = np.ascontiguousarray(weights[k])
        m.update(consts)
        in_maps.append(m)
    res = run_bass_kernel_spmd(nc, in_maps, core_ids=list(range(B)))
    return np.stack([res.results[b]["out"] for b in range(B)], axis=0)


FUSED = True


def kernel(**inputs):
    x = np.asarray(inputs["x"], dtype=np.float32)
    mem = np.asarray(inputs["mem"], dtype=np.float32)
    W = {k: np.asarray(inputs[k], dtype=np.float32) for k in WNAMES}
    depth = W["w_in"].shape[0]
    if FUSED:
        return run_layers(x, mem, W, depth)
    for l in range(depth):
        x = run_layers(x, mem, {k: v[l:l + 1] for k, v in W.items()}, 1)
    return x
```

```python
from contextlib import ExitStack
import numpy as np
import ml_dtypes
import concourse.bass as bass
import concourse.mybir as mybir
from concourse.bass_utils import run_bass_kernel_spmd

F32 = mybir.dt.float32
BF16 = mybir.dt.bfloat16
AF = mybir.ActivationFunctionType
ALU = mybir.AluOpType
AX = mybir.AxisListType

COMPUTE = ("pe", "act", "dve", "pool")
EPS = 1e-6


class Res:
    __slots__ = ("name", "lw", "rd", "group", "excl")

    def __init__(self, name, group=None):
        self.name = name
        self.lw = None
        self.rd = []
        self.group = group
        self.excl = name.startswith("bank")


class Group:
    def __init__(self):
        self.since = []
        self.fdeps = []


class DmaSem:
    __slots__ = ("name", "total", "sem")

    def __init__(self, name):
        self.name = name
        self.total = 0
        self.sem = None


class Op:
    __slots__ = ("eng", "fn", "deps", "idx", "dma", "dma_total", "needs_inc", "cnt", "tag")

    def __init__(self, eng, fn):
        self.eng = eng
        self.fn = fn
        self.deps = []
        self.dma = None
        self.dma_total = 0
        self.needs_inc = False
        self.cnt = 0


class Prog:
    def __init__(self, nc):
        self.nc = nc
        self.ops = []
        self.dmasems = []
        self.resd = {}
        self.trace = None

    def R(self, name, group=None):
        r = self.resd.get(name)
        if r is None:
            r = Res(name, group)
            self.resd[name] = r
        return r

    def dmasem(self, name):
        d = DmaSem(name)
        self.dmasems.append(d)
        return d

    def fence(self, group):
        last = {}
        keep = []
        for o in group.since + group.fdeps:
            if o.dma is not None:
                keep.append(o)
            else:
                if o.eng not in last or last[o.eng].idx < o.idx:
                    last[o.eng] = o
        group.fdeps = keep + list(last.values())
        group.since = []

    def _track(self, op, reads, writes):
        reads = list(reads)
        writes = list(writes)
        for r in list(reads):
            if r.excl:
                reads.remove(r)
                if r not in writes:
                    writes.append(r)
        op.tag = "R:" + ",".join(r.name for r in reads) + " W:" + ",".join(w.name for w in writes)
        deps = set()
        for r in reads:
            if r.lw is not None:
                deps.add(r.lw)
        for w in writes:
            if w.lw is not None:
                deps.add(w.lw)
            for o in w.rd:
                deps.add(o)
        groups = set()
        for r in list(reads) + list(writes):
            if r.group is not None:
                groups.add(r.group)
        for g in groups:
            for o in g.fdeps:
                deps.add(o)
            g.since.append(op)
        deps.discard(op)
        best = {}
        red = []
        for d in deps:
            if d.dma is not None:
                red.append(d)
            elif d.eng not in best or best[d.eng].idx < d.idx:
                best[d.eng] = d
        deps = red + list(best.values())
        for r in reads:
            r.rd.append(op)
        for w in writes:
            w.lw = op
            w.rd = []
        op.deps = list(deps)

    def op(self, eng, fn, reads=(), writes=()):
        o = Op(eng, fn)
        o.idx = len(self.ops)
        self.ops.append(o)
        self._track(o, reads, writes)
        return o

    def dma(self, queue, fn, sem, reads=(), writes=()):
        o = Op(queue, fn)
        o.idx = len(self.ops)
        o.dma = sem
        sem.total += 16
        o.dma_total = sem.total
        self.ops.append(o)
        self._track(o, reads, writes)
        return o

    def final(self):
        o = Op("sp", lambda e: e.nop())
        o.idx = len(self.ops)
        o.tag = "final"
        last = {}
        for p in self.ops:
            if p.dma is not None:
                last[("d", p.dma.name)] = p
            elif p.eng in COMPUTE:
                last[("e", p.eng)] = p
        o.deps = list(last.values())
        self.ops.append(o)

    def emit(self, stack):
        nc = self.nc
        for o in self.ops:
            for d in o.deps:
                if d.dma is None:
                    if d.eng == "pe" and o.eng == "pe" and o.dma is None:
                        continue
                    d.needs_inc = True
        sems = {}
        for e in COMPUTE:
            sems[e] = stack.enter_context(nc.semaphore("s_" + e))
        for d in self.dmasems:
            if d.total > 0:
                d.sem = stack.enter_context(nc.semaphore("d_" + d.name))
        cnt = {e: 0 for e in COMPUTE}
        for o in self.ops:
            if o.dma is None and o.needs_inc:
                cnt[o.eng] += 1
                o.cnt = cnt[o.eng]
        per_eng = {e: [] for e in ("pe", "act", "dve", "pool", "sp")}
        for o in self.ops:
            per_eng[o.eng].append(o)
        self.stats = {e: len(v) for e, v in per_eng.items()}
        self.stats["incs"] = dict(cnt)

        def run_engine(ename, eng):
            seen = {}
            nw = 0
            for o in per_eng[ename]:
                need = {}
                for d in o.deps:
                    if d.dma is not None:
                        key = ("d", d.dma.name)
                        val = d.dma_total
                        semh = d.dma.sem
                    else:
                        if d.eng == "pe" and ename == "pe" and o.dma is None:
                            continue
                        key = ("e", d.eng)
                        val = d.cnt
                        semh = sems[d.eng]
                    if seen.get(key, 0) >= val:
                        continue
                    if key not in need or need[key][1] < val:
                        need[key] = (semh, val)
                for key, (semh, val) in need.items():
                    eng.wait_ge(semh, val)
                    seen[key] = val
                    nw += 1
                if self.trace is not None:
                    self.trace.append((ename, o.idx, [(k, v[1]) for k, v in need.items()], o.cnt if o.needs_inc else None, o.dma_total if o.dma else None, o.tag))
                ins = o.fn(eng)
                if o.dma is not None:
                    ins.then_inc(o.dma.sem, 16)
                elif o.needs_inc:
                    ins.then_inc(sems[ename], 1)
            self.stats["waits_" + ename] = nw

        with nc.Block() as block:
            @block.tensor
            def _(e):
                run_engine("pe", e)

            @block.scalar
            def _(e):
                run_engine("act", e)

            @block.vector
            def _(e):
                run_engine("dve", e)

            @block.gpsimd
            def _(e):
                run_engine("pool", e)

            @block.sync
            def _(e):
                run_engine("sp", e)


D = 1024
KC = 8
DIN = 2182
DFF = 4096
NMEM = 256
T = 512
NSLOT = 4

WIN_BLOCKS = [(0, 512), (512, 512), (1024, 512), (1536, 390), (1926, 256)]


class StopBuild(Exception):
    pass


def build(S, L, dbg=None):
    NT = S // 128
    NG = S // T
    nc = bass.Bass("TRN2", target_bir_lowering=False)

    def din(name, shape, dt=F32):
        return nc.dram_tensor(name, shape, dt, kind="ExternalInput")

    x_t = din("x", [S, D])
    mem_t = din("mem", [NMEM, D])
    g_premix_t = din("norm_pre_mix", [L, D])
    g_postmix_t = din("norm_post_mix", [L, D])
    g_preffn_t = din("norm_pre_ffn", [L, D])
    g_postffn_t = din("norm_post_ffn", [L, D])
    g_mem_t = din("norm_mem", [L, D])
    w_in_t = din("w_in", [L, D, DIN])
    b_forget_t = din("b_forget", [L, 6])
    gv_t = din("gmlp_v_norm", [L, 384])
    ws_t = din("gmlp_w_s", [L, 6, 128, 128])
    bs_t = din("gmlp_b_s", [L, 6, 128])
    wkv_t = din("w_mem_kv", [L, D, 512])
    wout_t = din("w_out", [L, D, D])
    w1_t = din("w_ff1", [L, D, DFF])
    w2_t = din("w_ff2", [L, DFF, D])
    ident_t = din("c_ident", [128, 128], BF16)
    triu_t = din("c_triu", [128, 128], BF16)
    triuf_t = din("c_triuf", [128, 128], F32)
    onesf_t = din("c_onesf", [128, 128], F32)
    out_t = nc.dram_tensor("out", [S, D], F32, kind="ExternalOutput")

    x_d, mem_d, out_d = x_t.ap(), mem_t.ap(), out_t.ap()
    w_in_d, wkv_d, wout_d, w1_d, w2_d = w_in_t.ap(), wkv_t.ap(), wout_t.ap(), w1_t.ap(), w2_t.ap()
    ws_d = ws_t.ap()

    with ExitStack() as st:
        P = Prog(nc)
        R = P.R

        def sb(name, shape, dt):
            return st.enter_context(nc.sbuf_tensor(name, shape, dt))

        KT = [sb(f"KT{l}", [128, 3, S], BF16) for l in range(L)]
        VP = [sb(f"VP{l}", [128, NT, 6, 65], BF16) for l in range(L)]
        CALL = [sb(f"CALL{l}", [128, NT, 6], F32) for l in range(L)]
        EALL = [sb(f"EALL{l}", [128, NT, 6], F32) for l in range(L)]
        KMT = [sb(f"KMT{l}", [128, 2, NMEM], BF16) for l in range(L)]
        VMP = [sb(f"VMP{l}", [128, 2, 4, 65], BF16) for l in range(L)]
        WST = [sb(f"WST{l}", [128, 6, 128], BF16) for l in range(L)]
        GCOL = [sb(f"GCOL{l}", [128, 3, 8], F32) for l in range(L)]
        BS = [sb(f"BS{l}", [128, 6], F32) for l in range(L)]
        BFG = [sb(f"BFG{l}", [128, 6], F32) for l in range(L)]
        gpm = sb("gpm", [128, D], F32)
        gpf = gpm
        xg = sb("xg", [128, 4, D], F32)
        hT = sb("hT", [128, KC, T], BF16)
        onesb = sb("onesb", [128, 128], BF16)
        stat = sb("stat", [128, 64], F32)
        ident = sb("ident", [128, 128], BF16)
        triu = sb("triu", [128, 128], BF16)
        triuf = sb("triuf", [128, 128], F32)
        onesf = sb("onesf", [128, 128], F32)
        slots = [sb(f"slot{i}", [128, 4096], BF16) for i in range(NSLOT)]
        ARENA_BF = 19680
        arena = sb("arena", [128, ARENA_BF], BF16)
        AG = Group()

        class Carver:
            def __init__(self):
                self.off = 0

            def take(self, nbf):
                o = self.off
                self.off += nbf
                assert self.off <= ARENA_BF, self.off
                return o

        cv = Carver()

        def a_bf(n):
            o = cv.take(n)
            return arena[:, o:o + n]

        def a_f32(n):
            o = cv.take(2 * n)
            return arena[:, o:o + 2 * n].bitcast(F32)

        QT = a_bf(6 * T).rearrange("p (c t) -> p c t", t=T)
        QMT = a_bf(2 * T).rearrange("p (c t) -> p c t", t=T)
        gmT = a_bf(3 * T).rearrange("p (c t) -> p c t", t=T)
        attT = a_bf(10 * T).rearrange("p (c t) -> p c t", t=T)
        NPT = 3
        PT_OFF = cv.off
        PT = [a_bf(T) for _ in range(NPT)]
        xnb = [arena[:, PT_OFF:PT_OFF + D], None]
        zA = a_f32(768)
        xnb[1] = arena[:, cv.off - 1536:cv.off - 1536 + D]
        tmpv2 = a_f32(384)
        vn = a_bf(384)
        outa = a_bf(384)
        beta = a_f32(2 * 32).rearrange("p (a c) -> p a c", a=2)
        daug = a_bf(2 * T).rearrange("p (a t) -> p a t", a=2)
        rs = a_f32(T)
        tmpv = rs[:, 0:384]
        bcsb = a_f32(T)
        GVS = bcsb[:, 0:384]
        fg = a_f32(24).rearrange("p (a b) -> p a b", b=6)
        spb = a_f32(24).rearrange("p (a b) -> p a b", b=6)
        att_end = cv.off
        cv.off = 0
        hidT = a_bf(32 * T).rearrange("p (c t) -> p c t", t=T)
        rtmp = [a_f32(T)]
        cv.off = max(cv.off, att_end)
        _yo = cv.take(2 * T)
        ytmp_bf = arena[:, _yo:_yo + 2 * T]
        ytmp = ytmp_bf.bitcast(F32)

        def AR(name):
            return R(name, AG)

        banks = [st.enter_context(nc.psum_tensor(f"bank{i}", [128, 512], F32)) for i in range(8)]
        banks_bf = [b[:, :].bitcast(BF16) for b in banks]

        def BK(i):
            return R(f"bank{i}")

        d_setup = P.dmasem("setup")
        d_x = [P.dmasem(f"x{i}") for i in range(4)]
        d_o = [P.dmasem(f"o{i}") for i in range(4)]
        d_slot = [P.dmasem(f"sl{i}") for i in range(NSLOT)]
        d_ws = P.dmasem("ws")
        d_gpm = P.dmasem("gpm")
        d_gpf = d_gpm
        d_gv = P.dmasem("gv")

        slot_ctr = [0]

        def load_slot(dst_fn, src, reads=()):
            i = slot_ctr[0] % NSLOT
            slot_ctr[0] += 1
            dst = dst_fn(slots[i])
            P.dma("pool", lambda e, dst=dst, src=src: e.dma_start(out=dst, in_=src), d_slot[i],
                  reads=list(reads), writes=[R(f"slot{i}")])
            return i

        def bcast_rows(t, row_off, n):
            return bass.AP(t, row_off, [[0, 128], [1, n]])

        setup_res = []

        def setup_dma(dst, src, resname, **kw):
            P.dma("sp", lambda e, dst=dst, src=src, kw=kw: e.dma_start(out=dst, in_=src, **kw), d_setup, writes=[R(resname)])
            setup_res.append(R(resname))

        setup_dma(ident[:], ident_t.ap(), "ident")
        setup_dma(triu[:], triu_t.ap(), "triu")
        setup_dma(triuf[:], triuf_t.ap(), "triuf")
        setup_dma(onesf[:], onesf_t.ap(), "onesf")
        for l in range(L):
            for k, gt in enumerate((g_premix_t, g_preffn_t, g_mem_t)):
                setup_dma(GCOL[l][:, k, :], bass.AP(gt, l * D, [[1, 128], [128, 8]]), f"GCOL{l}", allow_slow_non_contiguous=True)
            setup_dma(BFG[l][:], bcast_rows(b_forget_t, l * 6, 6), f"BFG{l}")
            setup_dma(BS[l][:], bass.AP(bs_t, l * 768, [[1, 128], [128, 6]]), f"BS{l}", allow_slow_non_contiguous=True)
        last_setup = P.ops[-1]
        for r in setup_res:
            r.lw = last_setup
            r.rd = []
        P.op("dve", lambda e: e.memset(onesb[:], 1.0), writes=[R("onesb")])
        P.op("dve", lambda e: e.memset(stat[:], 1.0e6), writes=[R("statn"), R("dtmp0"), R("dtmp1")] + [R(f"statn{i}") for i in range(4)] + [R(f"statv{i}") for i in range(4)] + [R(f"statp{i}") for i in range(4)])
        for l in range(L):
            P.op("dve", lambda e, l=l: e.memset(VP[l][:, :, :, 64:65], 1.0), writes=[R(f"VPones{l}")])
            P.op("dve", lambda e, l=l: e.memset(VMP[l][:, :, :, 64:65], 1.0), writes=[R(f"VMPones{l}")])

        def rstd_from_ss(ss_ap, out_ap, n, rd, wr):
            P.op("act", lambda e: e.activation(out=out_ap, in_=ss_ap, func=AF.Ln, scale=1.0 / n, bias=EPS), reads=rd, writes=wr)
            P.op("act", lambda e: e.activation(out=out_ap, in_=out_ap, func=AF.Exp, scale=-0.5), reads=wr, writes=wr)

        tbank_ctr = [0]

        def norm_transpose_n(srcs, gcol_ap, gres, dsts, batched=True, base=0):
            if not batched and len(srcs) > 1:
                for i in range(len(srcs)):
                    norm_transpose_n(srcs[i:i + 1], gcol_ap, gres, dsts[i:i + 1], batched=True, base=i)
                return
            n = len(srcs)
            ssr = R("statn" if n > 1 else f"statn{base}")
            xres_ = [[AR("PT0"), AR("PT1")], [AR("zA")]]
            for i, (src_ap, src_res) in enumerate(srcs):
                ci = i + base
                P.op("act", lambda e, ci=ci, src_ap=src_ap: e.activation(out=xnb[0], in_=src_ap, func=AF.Square, accum_out=stat[:, ci:ci + 1]),
                     reads=[src_res], writes=[ssr] + (xres_[0] if i == 0 else []))
            rstd_from_ss(stat[:, base:base + n], stat[:, 4 + base:4 + base + n], D, [ssr], [ssr])
            tbs = {}

            def stage_a(i):
                (src_ap, src_res) = srcs[i]
                ci = i + base
                xb = (i + base) % 2
                P.op("dve", lambda e: e.tensor_scalar(out=xnb[xb], in0=src_ap, scalar1=stat[:, 4 + ci:5 + ci], scalar2=None, op0=ALU.mult),
                     reads=[src_res, ssr], writes=xres_[xb])
                tb = 6 + (tbank_ctr[0] % 2)
                tbank_ctr[0] += 1
                tbs[i] = tb
                for kc in range(KC):
                    P.op("pe", lambda e, kc=kc: e.transpose(out=banks_bf[tb][:, kc * 128:(kc + 1) * 128], in_=xnb[xb][:, kc * 128:(kc + 1) * 128], identity=ident[:]),
                         reads=xres_[xb] + [R("ident")], writes=[BK(tb)])

            def stage_b(i):
                (dst_ap, dst_res) = dsts[i]
                tb = tbs[i]
                P.op("dve", lambda e: e.tensor_tensor(out=dst_ap, in0=banks_bf[tb][:, 0:1024].rearrange("p (k t) -> p k t", t=128),
                                                      in1=gcol_ap.unsqueeze(2).to_broadcast([128, KC, 128]), op=ALU.mult),
                     reads=[BK(tb), gres], writes=[dst_res])

            for i in range(n):
                stage_a(i)
                if i >= 1:
                    stage_b(i - 1)
            stage_b(n - 1)

        bank_rr = [0]

        def next_bank(lo=0, hi=6):
            b = lo + (bank_rr[0] % (hi - lo))
            bank_rr[0] += 1
            return b

        def layer_setup(l):
            for mt in range(2):
                P.dma("sp", lambda e, mt=mt: e.dma_start(out=xg[:, mt, :], in_=mem_d[mt * 128:(mt + 1) * 128, :]), d_x[mt], writes=[R(f"xg{mt}")])
            norm_transpose_n([(xg[:, mt, :], R(f"xg{mt}")) for mt in range(2)], GCOL[l][:, 2, :], R(f"GCOL{l}"),
                             [(hT[:, :, mt * 128:(mt + 1) * 128], R(f"hT{mt}")) for mt in range(2)])
            si = load_slot(lambda s: s[:, :].rearrange("p (k n) -> p k n", n=512), wkv_d[l].rearrange("(k p) n -> p k n", p=128))
            wkv = slots[si][:, :].rearrange("p (k n) -> p k n", n=512)
            hres = [R("hT0"), R("hT1")]
            for pm in range(2):
                b = next_bank()
                for kc in range(KC):
                    P.op("pe", lambda e, kc=kc, pm=pm, b=b: e.matmul(banks[b][:, 0:NMEM], lhsT=wkv[:, kc, pm * 128:(pm + 1) * 128], rhs=hT[:, kc, 0:NMEM],
                                                                      start=(kc == 0), stop=(kc == KC - 1)),
                         reads=[R(f"slot{si}")] + hres, writes=[BK(b)])
                P.op("act", lambda e, pm=pm, b=b: e.activation(out=KMT[l][:, pm, :], in_=banks[b][:, 0:NMEM], func=AF.Copy),
                     reads=[BK(b)], writes=[R(f"KMT{l}")])
            for mt in range(2):
                b = next_bank()
                for kc in range(KC):
                    P.op("pe", lambda e, kc=kc, mt=mt, b=b: e.matmul(banks[b][:, 0:256], lhsT=hT[:, kc, mt * 128:(mt + 1) * 128], rhs=wkv[:, kc, 256:512],
                                                                      start=(kc == 0), stop=(kc == KC - 1)),
                         reads=[R(f"slot{si}"), hres[mt]], writes=[BK(b)])
                P.op("act", lambda e, mt=mt, b=b: e.activation(out=VMP[l][:, mt, :, 0:64], in_=banks[b][:, 0:256].rearrange("p (h d) -> p h d", d=64), func=AF.Copy),
                     reads=[BK(b)], writes=[R(f"VMP{l}")])
            P.dma("sp", lambda e: e.dma_start(out=zA.rearrange("p (g s) -> p g s", s=128), in_=ws_d[l].rearrange("g t s -> t g s")), d_ws,
                  writes=[AR("zA")])
            wsb = xnb[0][:, 0:768]
            P.op("dve", lambda e: e.tensor_copy(out=wsb, in_=zA), reads=[AR("zA")], writes=[AR("PT0"), AR("PT1")])
            tb = 6 + (tbank_ctr[0] % 2)
            tbank_ctr[0] += 1
            for gg in range(6):
                P.op("pe", lambda e, gg=gg: e.transpose(out=banks_bf[tb][:, gg * 128:(gg + 1) * 128], in_=wsb[:, gg * 128:(gg + 1) * 128], identity=ident[:]),
                     reads=[AR("PT0"), AR("PT1"), R("ident")], writes=[BK(tb)])
            P.op("dve", lambda e: e.tensor_tensor(out=WST[l][:], in0=banks_bf[tb][:, 0:768].rearrange("p (g t) -> p g t", t=128),
                                                  in1=triu[:].unsqueeze(1).to_broadcast([128, 6, 128]), op=ALU.mult),
                 reads=[BK(tb), R("triu")], writes=[R(f"WST{l}")])

        def mixer(g, l):
            xres = [R(f"xg{tt}") for tt in range(4)]
            hres = [R(f"hT{tt}") for tt in range(4)]
            P.dma("sp", lambda e: e.dma_start(out=gpm[:], in_=bcast_rows(g_postmix_t, l * D, D)), d_gpm, writes=[R("gpm")])
            P.dma("sp", lambda e: e.dma_start(out=GVS, in_=bcast_rows(gv_t, l * 384, 384)), d_gv, writes=[AR("bcsb")])
            norm_transpose_n([(xg[:, tt, :], xres[tt]) for tt in range(4)], GCOL[l][:, 0, :], R(f"GCOL{l}"),
                             [(hT[:, :, tt * 128:(tt + 1) * 128], hres[tt]) for tt in range(4)], batched=(l == 0))
            if dbg == "norm":
                raise StopBuild()
            blk = [None] * 5

            def load_blk(bi):
                c0, ncol = WIN_BLOCKS[bi]
                si = load_slot(lambda s, ncol=ncol: s[:, 0:KC * ncol].rearrange("p (k n) -> p k n", n=ncol),
                               w_in_d[l][:, c0:c0 + ncol].rearrange("(k p) n -> p k n", p=128))
                blk[bi] = (si, slots[si][:, 0:KC * ncol].rearrange("p (k n) -> p k n", n=ncol))

            load_blk(0)
            load_blk(1)
            load_blk(2)
            load_blk(4)
            P.op("dve", lambda e: e.memset(QT[:, :, :], 0.0), writes=[AR("QTz")] + [AR(f"QT{c}") for c in range(3)])
            P.op("dve", lambda e: e.memset(attT[64:128, :, :], 0.0), writes=[AR("attTz")])
            zAb = [zA, arena[:, PT_OFF:PT_OFF + 1536].bitcast(F32)]
            zAr = [[AR("zA")], [AR("PT0"), AR("PT1"), AR("PT2")]]

            def a_mm(tt):
                zi = tt % 2
                ba, bb = next_bank(), next_bank()
                for kc in range(KC):
                    P.op("pe", lambda e, kc=kc: e.matmul(banks[ba][:, :], lhsT=hT[:, kc, tt * 128:(tt + 1) * 128], rhs=blk[0][1][:, kc, :],
                                                          start=(kc == 0), stop=(kc == KC - 1)),
                         reads=[hres[tt], R(f"slot{blk[0][0]}")], writes=[BK(ba)])
                for kc in range(KC):
                    P.op("pe", lambda e, kc=kc: e.matmul(banks[bb][:, 0:256], lhsT=hT[:, kc, tt * 128:(tt + 1) * 128], rhs=blk[1][1][:, kc, 0:256],
                                                          start=(kc == 0), stop=(kc == KC - 1)),
                         reads=[hres[tt], R(f"slot{blk[1][0]}")], writes=[BK(bb)])
                P.op("act", lambda e: e.activation(out=zAb[zi][:, 0:512], in_=banks[ba][:, :], func=AF.Gelu_apprx_tanh), reads=[BK(ba)], writes=zAr[zi])
                P.op("act", lambda e: e.activation(out=zAb[zi][:, 512:768], in_=banks[bb][:, 0:256], func=AF.Gelu_apprx_tanh), reads=[BK(bb)], writes=zAr[zi])

            gstate = {}

            def g_vnorm(tt):
                zi = tt % 2
                zz, zr = zAb[zi], zAr[zi]
                v3 = zz[:, 384:768].rearrange("p (g d) -> p g d", d=64)
                P.op("dve", lambda e: e.tensor_tensor(out=tmpv, in0=zz[:, 384:768], in1=zz[:, 384:768], op=ALU.mult), reads=zr, writes=[AR("rs")])
                c0 = 16 + tt * 8
                sr = R(f"statv{tt}")
                P.op("dve", lambda e: e.reduce_sum(out=stat[:, c0:c0 + 6], in_=tmpv.rearrange("p (g d) -> p g d", d=64), axis=AX.X),
                     reads=[AR("rs")], writes=[sr])
                rstd_from_ss(stat[:, c0:c0 + 6], stat[:, c0:c0 + 6], 64, [sr], [sr])
                P.op("dve", lambda e: e.tensor_tensor(out=tmpv.rearrange("p (g d) -> p g d", d=64), in0=v3,
                                                      in1=stat[:, c0:c0 + 6].unsqueeze(2).to_broadcast([128, 6, 64]), op=ALU.mult),
                     reads=zr + [sr], writes=[AR("rs")])
                P.op("dve", lambda e: e.tensor_tensor(out=vn, in0=tmpv, in1=GVS, op=ALU.mult), reads=[AR("rs"), AR("bcsb")], writes=[AR("vn")])

            def g_mix(tt):
                zi = tt % 2
                zz, zr = zAb[zi], zAr[zi]
                bm = next_bank()
                for gg in range(6):
                    P.op("pe", lambda e, gg=gg: e.matmul(banks[bm][:, gg * 64:(gg + 1) * 64], lhsT=WST[l][:, gg, :], rhs=vn[:, gg * 64:(gg + 1) * 64],
                                                          start=True, stop=True),
                         reads=[AR("vn"), R(f"WST{l}")], writes=[BK(bm)])
                P.op("dve", lambda e: e.tensor_tensor(out=tmpv2.rearrange("p (g d) -> p g d", d=64), in0=banks[bm][:, 0:384].rearrange("p (g d) -> p g d", d=64),
                                                      in1=BS[l][:].unsqueeze(2).to_broadcast([128, 6, 64]), op=ALU.add),
                     reads=[BK(bm), R(f"BS{l}")], writes=[AR("tmpv2")])
                P.op("dve", lambda e: e.tensor_tensor(out=outa, in0=tmpv2, in1=zz[:, 0:384], op=ALU.mult), reads=[AR("tmpv2")] + zr, writes=[AR("outa")])

            def g_tr(tt):
                tb = 6 + (tbank_ctr[0] % 2)
                tbank_ctr[0] += 1
                for c in range(3):
                    P.op("pe", lambda e, c=c: e.transpose(out=banks_bf[tb][:, c * 128:(c + 1) * 128], in_=outa[:, c * 128:(c + 1) * 128], identity=ident[:]),
                         reads=[AR("outa"), R("ident")], writes=[BK(tb)])
                P.op("act", lambda e: e.activation(out=gmT[:, :, tt * 128:(tt + 1) * 128], in_=banks_bf[tb][:, 0:384].rearrange("p (c t) -> p c t", t=128), func=AF.Copy),
                     reads=[BK(tb)], writes=[AR(f"gmT{tt}")])

            fm = [(1, 256, "q", 0), (1, 384, "q", 1), (2, 0, "q", 2),
                  (2, 128, "k", 0), (2, 256, "k", 1), (2, 384, "k", 2),
                  (4, 0, "m", 0), (4, 128, "m", 1)]

            def fm_chunk(bi, lc, kind, c):
                b = next_bank()
                for kc in range(KC):
                    P.op("pe", lambda e, kc=kc: e.matmul(banks[b][:, :], lhsT=blk[bi][1][:, kc, lc:lc + 128], rhs=hT[:, kc, :],
                                                          start=(kc == 0), stop=(kc == KC - 1)),
                         reads=hres + [R(f"slot{blk[bi][0]}")], writes=[BK(b)])
                if kind == "q":
                    P.op("act", lambda e: e.activation(out=QT[0:64, 2 * c, :], in_=banks[b][0:64, :], func=AF.Copy, scale=0.125), reads=[BK(b), AR("QTz")], writes=[AR(f"QT{c}")])
                    P.op("act", lambda e: e.activation(out=QT[64:128, 2 * c + 1, :], in_=banks[b][64:128, :], func=AF.Copy, scale=0.125), reads=[BK(b), AR("QTz")], writes=[AR(f"QT{c}")])
                elif kind == "m":
                    P.op("act", lambda e: e.activation(out=QMT[:, c, :], in_=banks[b][:, :], func=AF.Copy, scale=0.125), reads=[BK(b)], writes=[AR(f"QMT{c}")])
                else:
                    P.op("dve", lambda e: e.tensor_copy(out=KT[l][:, c, g * T:(g + 1) * T], in_=banks[b][:, :]), reads=[BK(b)], writes=[R(f"KT{l}_{c}")])

            def c_tile(tt):
                b = next_bank()
                for kc in range(KC):
                    P.op("pe", lambda e, kc=kc: e.matmul(banks[b][:, 0:390], lhsT=hT[:, kc, tt * 128:(tt + 1) * 128], rhs=blk[3][1][:, kc, :],
                                                          start=(kc == 0), stop=(kc == KC - 1)),
                         reads=[hres[tt], R(f"slot{blk[3][0]}")], writes=[BK(b)])
                P.op("act", lambda e: e.activation(out=VP[l][:, 4 * g + tt, :, 0:64], in_=banks[b][:, 0:384].rearrange("p (h d) -> p h d", d=64), func=AF.Copy),
                     reads=[BK(b)], writes=[R(f"VP{l}")])
                P.op("dve", lambda e: e.tensor_tensor(out=fg[:, tt, :], in0=banks[b][:, 384:390], in1=BFG[l][:], op=ALU.add),
                     reads=[BK(b), R(f"BFG{l}")], writes=[AR("fg")])

            a_mm(0)
            g_vnorm(0)
            a_mm(1)
            for q in fm[0:4]:
                fm_chunk(*q)
            g_mix(0)
            a_mm(2)
            for q in fm[4:6]:
                fm_chunk(*q)
            g_tr(0)
            g_vnorm(1)
            for q in fm[6:8]:
                fm_chunk(*q)
            g_mix(1)
            a_mm(3)
            load_blk(3)
            c_tile(0)
            g_tr(1)
            g_vnorm(2)
            c_tile(1)
            c_tile(2)
            g_mix(2)
            c_tile(3)
            g_tr(2)
            g_vnorm(3)
            P.op("act", lambda e: e.activation(out=spb, in_=fg, func=AF.Exp, scale=-1.0), reads=[AR("fg")], writes=[AR("spb")])
            P.op("act", lambda e: e.activation(out=spb, in_=spb, func=AF.Ln, bias=1.0), reads=[AR("spb")], writes=[AR("spb")])
            for tt in range(4):
                ti = 4 * g + tt
                b = next_bank()
                P.op("pe", lambda e, b=b, tt=tt: e.matmul(banks[b][:, 0:6], lhsT=triuf[:], rhs=spb[:, tt, :], start=True, stop=True),
                     reads=[AR("spb"), R("triuf")], writes=[BK(b)])
                P.op("pe", lambda e, b=b, tt=tt: e.matmul(banks[b][:, 8:14], lhsT=onesf[:], rhs=spb[:, tt, :], start=True, stop=True),
                     reads=[AR("spb"), R("onesf")], writes=[BK(b)])
                cr = R(f"CE{l}")
                if ti == 0:
                    P.op("dve", lambda e, b=b, ti=ti: e.tensor_copy(out=CALL[l][:, ti, :], in_=banks[b][:, 0:6]), reads=[BK(b)], writes=[cr])
                    P.op("dve", lambda e, b=b, ti=ti: e.tensor_copy(out=EALL[l][:, ti, :], in_=banks[b][:, 8:14]), reads=[BK(b)], writes=[cr])
                else:
                    P.op("dve", lambda e, b=b, ti=ti: e.tensor_tensor(out=CALL[l][:, ti, :], in0=banks[b][:, 0:6], in1=EALL[l][:, ti - 1, :], op=ALU.add),
                         reads=[BK(b), cr], writes=[cr])
                    P.op("dve", lambda e, b=b, ti=ti: e.tensor_tensor(out=EALL[l][:, ti, :], in0=banks[b][:, 8:14], in1=EALL[l][:, ti - 1, :], op=ALU.add),
                         reads=[BK(b), cr], writes=[cr])

            g_mix(3)
            g_tr(3)
            if dbg in ("win", "gm", "gm2"):
                raise StopBuild()
            nj = 4 * g + 4
            sb_rr = [0]
            pt_rr = [0]
            ktres = [R(f"KT{l}_{c}") for c in range(3)]

            def normalize(ob, hidx):
                P.op("dve", lambda e: e.reciprocal(out=rs[64:65, :], in_=banks[ob][64:65, :]), reads=[BK(ob)], writes=[AR("rs")])
                P.op("pe", lambda e: e.matmul(banks[5][0:64, :], lhsT=onesf[64:65, 0:64], rhs=rs[64:65, :], start=True, stop=True),
                     reads=[AR("rs"), R("onesf")], writes=[BK(5)])
                P.op("act", lambda e: e.activation(out=bcsb[0:64, :], in_=banks[5][0:64, :], func=AF.Copy), reads=[BK(5)], writes=[AR("bcsb")])
                P.op("dve", lambda e: e.tensor_tensor(out=attT[0:64, hidx, :], in0=banks[ob][0:64, :], in1=bcsb[0:64, :], op=ALU.mult),
                     reads=[BK(ob), AR("bcsb")], writes=[AR(f"attT{hidx}")])

            LOOK = 2
            pend = []
            deferred = []

            def tick():
                for d in deferred:
                    d[0] -= 1
                while deferred and deferred[0][0] <= 0:
                    deferred.pop(0)[1]()

            def norm_part1(ob):
                P.op("act", lambda e: e.activation(out=rs[64:65, :], in_=banks[ob][64:65, :], func=AF.Ln), reads=[BK(ob)], writes=[AR("rs")])
                P.op("act", lambda e: e.activation(out=rs[64:65, :], in_=rs[64:65, :], func=AF.Exp, scale=-1.0), reads=[AR("rs")], writes=[AR("rs")])

            def norm_part2(ob, hidx):
                P.op("pe", lambda e: e.matmul(banks[5][0:64, :], lhsT=onesf[64:65, 0:64], rhs=rs[64:65, :], start=True, stop=True),
                     reads=[AR("rs"), R("onesf")], writes=[BK(5)])
                P.op("act", lambda e: e.activation(out=bcsb[0:64, :], in_=banks[5][0:64, :], func=AF.Copy), reads=[BK(5)], writes=[AR("bcsb")])
                P.op("dve", lambda e: e.tensor_tensor(out=attT[0:64, hidx, :], in0=banks[ob][0:64, :], in1=bcsb[0:64, :], op=ALU.mult),
                     reads=[BK(ob), AR("bcsb")], writes=[AR(f"attT{hidx}")])

            def emit_pv(blk_):
                (kind, hidx, j, col0, pk, ob, first, last) = blk_
                if kind == "f":
                    P.op("pe", lambda e: e.matmul(banks[ob][0:65, col0:T], lhsT=VP[l][:, j, hidx, :], rhs=PT[pk][:, col0:T], start=first, stop=last),
                         reads=[R(f"VP{l}"), R(f"VPones{l}"), AR(f"PT{pk}")], writes=[BK(ob)])
                else:
                    P.op("pe", lambda e: e.matmul(banks[ob][0:65, :], lhsT=VMP[l][:, j, hidx - 6, :], rhs=PT[pk][:, :], start=first, stop=last),
                         reads=[R(f"VMP{l}"), R(f"VMPones{l}"), AR(f"PT{pk}")], writes=[BK(ob)])
                if last:
                    while deferred:
                        deferred.pop(0)[1]()
                    norm_part1(ob)
                    deferred.append([4, lambda ob=ob, hidx=hidx: norm_part2(ob, hidx)])

            def push(blk_):
                pend.append(blk_)
                if len(pend) > LOOK:
                    emit_pv(pend.pop(0))
                tick()

            for h in range(6):
                p, r0 = h // 2, 64 * (h % 2)
                par = h % 2
                br = AR(f"beta{par}")
                P.op("dve", lambda e, par=par, h=h: e.tensor_scalar(out=beta[:, par, 0:nj], in0=CALL[l][:, 0:nj, h],
                                                                     scalar1=EALL[l][:, 4 * g + 3, h:h + 1], scalar2=None, op0=ALU.subtract),
                     reads=[R(f"CE{l}")], writes=[br])
                dr = AR(f"daug{par}")
                P.op("dve", lambda e, par=par, h=h: e.tensor_scalar(out=stat[:, 8 + 4 * par:12 + 4 * par], in0=EALL[l][:, 4 * g:4 * g + 4, h],
                                                                     scalar1=EALL[l][:, 4 * g + 3, h:h + 1], scalar2=-1.0 / 128, op0=ALU.subtract, op1=ALU.mult),
                     reads=[R(f"CE{l}")], writes=[R(f"dtmp{par}")])
                P.op("dve", lambda e, par=par: e.tensor_copy(out=daug[:, par, :].rearrange("p (a b) -> p a b", b=128),
                                                              in_=stat[:, 8 + 4 * par:12 + 4 * par].unsqueeze(2).to_broadcast([128, 4, 128])),
                     reads=[R(f"dtmp{par}")], writes=[dr])
                ob = 3 + (h % 2)
                for j in range(nj):
                    il0 = max(0, j - 4 * g)
                    col0 = il0 * 128
                    sbk = sb_rr[0] % 3
                    sb_rr[0] += 1
                    pk = pt_rr[0] % NPT
                    pt_rr[0] += 1
                    P.op("pe", lambda e, j=j, col0=col0, sbk=sbk, p=p, h=h: e.matmul(banks[sbk][:, col0:T], lhsT=KT[l][:, p, j * 128:(j + 1) * 128],
                                                                                        rhs=QT[:, h, col0:T], start=True, stop=False),
                         reads=[ktres[p], AR(f"QT{p}")], writes=[BK(sbk)])
                    P.op("pe", lambda e, col0=col0, sbk=sbk, par=par: e.matmul(banks[sbk][:, col0:T], lhsT=onesb[:], rhs=daug[:, par, col0:T], start=False, stop=True),
                         reads=[R("onesb"), dr], writes=[BK(sbk)])
                    P.op("act", lambda e, j=j, col0=col0, sbk=sbk, pk=pk, par=par: e.activation(out=PT[pk][:, col0:T], in_=banks[sbk][:, col0:T],
                                                                                                func=AF.Exp, bias=beta[:, par, j:j + 1]),
                         reads=[BK(sbk), br], writes=[AR(f"PT{pk}")])
                    if j >= 4 * g:
                        P.op("dve", lambda e, il0=il0, pk=pk: e.tensor_tensor(out=PT[pk][:, il0 * 128:(il0 + 1) * 128], in0=PT[pk][:, il0 * 128:(il0 + 1) * 128],
                                                                                in1=triu[:], op=ALU.mult),
                             reads=[AR(f"PT{pk}"), R("triu")], writes=[AR(f"PT{pk}")])
                    push(("f", h, j, col0, pk, ob, j == 0, j == nj - 1))
            for hm in range(4):
                p, r0 = hm // 2, 64 * (hm % 2)
                ob = 3 + (hm % 2)
                for jm in range(2):
                    sbk = sb_rr[0] % 3
                    sb_rr[0] += 1
                    pk = pt_rr[0] % NPT
                    pt_rr[0] += 1
                    P.op("pe", lambda e, jm=jm, sbk=sbk, p=p, r0=r0: e.matmul(banks[sbk][:, :], lhsT=KMT[l][r0:r0 + 64, p, jm * 128:(jm + 1) * 128],
                                                                               rhs=QMT[r0:r0 + 64, p, :], start=True, stop=True),
                         reads=[R(f"KMT{l}"), AR(f"QMT{p}")], writes=[BK(sbk)])
                    P.op("act", lambda e, sbk=sbk, pk=pk: e.activation(out=PT[pk][:, :], in_=banks[sbk][:, :], func=AF.Exp),
                         reads=[BK(sbk)], writes=[AR(f"PT{pk}")])
                    push(("m", 6 + hm, jm, 0, pk, ob, jm == 0, jm == 1))
            while pend:
                emit_pv(pend.pop(0))
                tick()
            while deferred:
                deferred.pop(0)[1]()
            if dbg == "att":
                raise StopBuild()
            sg = load_slot(lambda s: s[:, 0:3 * D].rearrange("p (c n) -> p c n", n=D), wout_d[l][0:384, :].rearrange("(c p) n -> p c n", p=128))
            wo_g = slots[sg][:, 0:3 * D].rearrange("p (c n) -> p c n", n=D)
            wo_h = []
            for (h0, nh) in ((0, 4), (4, 4), (8, 2)):
                si = load_slot(lambda s, nh=nh: s[0:64, 0:nh * D].rearrange("p (c n) -> p c n", n=D),
                               wout_d[l][384 + 64 * h0:384 + 64 * (h0 + nh), :].rearrange("(c p) n -> p c n", p=64))
                v = slots[si][:, 0:nh * D].rearrange("p (c n) -> p c n", n=D)
                for k in range(nh):
                    wo_h.append((si, v, k))
            attres = [AR(f"attT{i}") for i in range(10)]
            for tt in range(4):
                yb = [next_bank(0, 4), next_bank(0, 4)]
                for half in range(2):
                    b = yb[half]
                    for c in range(3):
                        P.op("pe", lambda e, c=c, tt=tt, half=half, b=b: e.matmul(banks[b][:, :], lhsT=gmT[:, c, tt * 128:(tt + 1) * 128],
                                                                                   rhs=wo_g[:, c, half * 512:(half + 1) * 512], start=(c == 0), stop=False),
                             reads=[AR(f"gmT{tt}"), R(f"slot{sg}")], writes=[BK(b)])
                    for hh in range(10):
                        si, v, k = wo_h[hh]
                        P.op("pe", lambda e, hh=hh, tt=tt, half=half, b=b, v=v, k=k: e.matmul(banks[b][:, :], lhsT=attT[:, hh, tt * 128:(tt + 1) * 128],
                                                                                               rhs=v[:, k, half * 512:(half + 1) * 512], start=False, stop=(hh == 9)),
                             reads=[attres[hh], AR("attTz"), R(f"slot{si}")], writes=[BK(b)])
                post_norm(yb, tt, gpm, R("gpm"), xres[tt])

        def post_norm(yb, tt, gbuf, gres, xr):
            c0 = 48 + tt * 4
            sr = R(f"statp{tt}")
            for half in range(2):
                P.op("act", lambda e, half=half: e.activation(out=ytmp_bf[:, half * 512:(half + 1) * 512], in_=banks[yb[half]][:, :], func=AF.Square,
                                                               accum_out=stat[:, c0 + half:c0 + half + 1]),
                     reads=[BK(yb[half])], writes=[sr, R("ytmp")])
            P.op("dve", lambda e: e.tensor_tensor(out=stat[:, c0 + 2:c0 + 3], in0=stat[:, c0:c0 + 1], in1=stat[:, c0 + 1:c0 + 2], op=ALU.add), reads=[sr], writes=[sr])
            rstd_from_ss(stat[:, c0 + 2:c0 + 3], stat[:, c0 + 3:c0 + 4], D, [sr], [sr])
            for half in range(2):
                P.op("dve", lambda e, half=half: e.scalar_tensor_tensor(out=ytmp, in0=banks[yb[half]][:, :], scalar=stat[:, c0 + 3:c0 + 4],
                                                                         in1=gbuf[:, half * 512:(half + 1) * 512], op0=ALU.mult, op1=ALU.mult),
                     reads=[BK(yb[half]), sr, gres], writes=[R("ytmp")])
                P.op("dve", lambda e, half=half: e.tensor_tensor(out=xg[:, tt, half * 512:(half + 1) * 512], in0=xg[:, tt, half * 512:(half + 1) * 512], in1=ytmp, op=ALU.add),
                     reads=[R("ytmp"), xr], writes=[xr])

        def ffn(g, l, last):
            xres = [R(f"xg{tt}") for tt in range(4)]
            hres = [R(f"hT{tt}") for tt in range(4)]
            P.dma("sp", lambda e: e.dma_start(out=gpf[:], in_=bcast_rows(g_postffn_t, l * D, D)), d_gpf, writes=[R("gpm")])
            norm_transpose_n([(xg[:, tt, :], xres[tt]) for tt in range(4)], GCOL[l][:, 1, :], R(f"GCOL{l}"),
                             [(hT[:, :, tt * 128:(tt + 1) * 128], hres[tt]) for tt in range(4)], batched=False)
            P.fence(AG)
            for blk_i in range(8):
                si = load_slot(lambda s: s[:, :].rearrange("p (k n) -> p k n", n=512),
                               w1_d[l][:, blk_i * 512:(blk_i + 1) * 512].rearrange("(k p) n -> p k n", p=128))
                wb = slots[si][:, :].rearrange("p (k n) -> p k n", n=512)
                for fcl in range(4):
                    fc = blk_i * 4 + fcl
                    b = next_bank()
                    for kc in range(KC):
                        P.op("pe", lambda e, kc=kc, fcl=fcl, b=b, wb=wb: e.matmul(banks[b][:, :], lhsT=wb[:, kc, fcl * 128:(fcl + 1) * 128], rhs=hT[:, kc, :],
                                                                                   start=(kc == 0), stop=(kc == KC - 1)),
                             reads=hres + [R(f"slot{si}")], writes=[BK(b)])
                    rk = 0
                    P.op("act", lambda e, b=b, rk=rk: e.activation(out=rtmp[rk], in_=banks[b][:, :], func=AF.Relu), reads=[BK(b)], writes=[AR(f"rtmp{rk}")])
                    P.op("dve", lambda e, fc=fc, rk=rk: e.tensor_tensor(out=hidT[:, fc, :], in0=rtmp[rk], in1=rtmp[rk], op=ALU.mult),
                         reads=[AR(f"rtmp{rk}")], writes=[AR(f"hidT{fc}")])
            if dbg == "ffn1":
                raise StopBuild()
            for tp in range(1):
                def w2_load(blk_i):
                    si = load_slot(lambda s: s[:, :].rearrange("p (c n) -> p c n", n=D),
                                   w2_d[l][blk_i * 512:(blk_i + 1) * 512, :].rearrange("(c p) n -> p c n", p=128))
                    return si, slots[si][:, :].rearrange("p (c n) -> p c n", n=D)

                def w2_mm(si, wb, fc, fcl, tt, half):
                    b = tt * 2 + half
                    P.op("pe", lambda e: e.matmul(banks[b][:, :], lhsT=hidT[:, fc, tt * 128:(tt + 1) * 128], rhs=wb[:, fcl, half * 512:(half + 1) * 512],
                                                  start=(fc == 0), stop=(fc == 31)),
                         reads=[AR(f"hidT{fc}"), R(f"slot{si}")], writes=[BK(b)])

                for blk_i in range(6):
                    si, wb = w2_load(blk_i)
                    for fcl in range(4):
                        for tt in range(4):
                            for half in range(2):
                                w2_mm(si, wb, blk_i * 4 + fcl, fcl, tt, half)
                tail = [(blk_i,) + w2_load(blk_i) for blk_i in (6, 7)]
                for tt in range(4):
                    for (blk_i, si, wb) in tail:
                        for fcl in range(4):
                            for half in range(2):
                                w2_mm(si, wb, blk_i * 4 + fcl, fcl, tt, half)
                for tt in range(4):
                    post_norm([tt * 2, tt * 2 + 1], tt, gpf, R("gpm"), xres[tt])
                    if last:
                        r0 = (4 * g + tt) * 128
                        P.dma("sp", lambda e, tt=tt, r0=r0: e.dma_start(out=out_d[r0:r0 + 128, :], in_=xg[:, tt, :]), d_o[tt],
                              reads=[xres[tt]], writes=[R(f"outd{tt}")])
            P.fence(AG)

        try:
            for l in range(L):
                layer_setup(l)
            P.fence(AG)
            if dbg == "setup":
                raise StopBuild()
            for g in range(NG):
                for tt in range(4):
                    r0 = (4 * g + tt) * 128
                    P.dma("sp", lambda e, tt=tt, r0=r0: e.dma_start(out=xg[:, tt, :], in_=x_d[r0:r0 + 128, :]), d_x[tt], writes=[R(f"xg{tt}")])
                for l in range(L):
                    mixer(g, l)
                    if dbg == "mix" or dbg == f"mix:{g}:{l}":
                        raise StopBuild()
                    ffn(g, l, last=(l == L - 1))
                    if dbg == f"ffn:{g}:{l}":
                        raise StopBuild()
        except StopBuild:
            if dbg == "win":
                P.op("dve", lambda e: e.tensor_copy(out=xg[:, 0, :].rearrange("p (c t) -> p c t", t=T), in_=QMT[:, :, :]),
                     reads=[AR("QMT0"), AR("QMT1"), R("xg0")], writes=[R("xg0")])
                P.op("dve", lambda e: e.tensor_copy(out=xg[:, 1, 0:512].rearrange("p (c t) -> p c t", t=256), in_=KMT[0][:, :, :]),
                     reads=[R("KMT0"), R("xg1")], writes=[R("xg1")])
                P.op("dve", lambda e: e.tensor_copy(out=xg[:, 2, 0:520].rearrange("p (c t) -> p c t", t=260), in_=VMP[0][:, :, :, :].rearrange("p a b c -> p a (b c)")),
                     reads=[R("VMP0"), R("VMPones0"), R("xg2")], writes=[R("xg2")])
            if dbg == "gm2":
                ar = [AR("zA"), AR("vn"), AR("tmpv2"), AR("outa"), R("WST0")] + [R(f"xg{i}") for i in range(4)]
                wr = [R(f"xg{i}") for i in range(4)]
                P.op("dve", lambda e: e.tensor_copy(out=xg[:, 0, 0:768], in_=zA), reads=ar, writes=wr)
                P.op("dve", lambda e: e.tensor_copy(out=xg[:, 1, 0:384], in_=vn), reads=ar, writes=wr)
                P.op("dve", lambda e: e.tensor_copy(out=xg[:, 1, 384:768], in_=tmpv2), reads=ar, writes=wr)
                P.op("dve", lambda e: e.tensor_copy(out=xg[:, 2, 0:384], in_=outa), reads=ar, writes=wr)
                P.op("dve", lambda e: e.tensor_copy(out=xg[:, 3, 0:768], in_=WST[0][:, :, :].rearrange("p g t -> p (g t)")), reads=ar, writes=wr)
            if dbg == "gm":
                P.op("dve", lambda e: e.tensor_copy(out=xg[:, 0, :].rearrange("p (c t) -> p c t", t=T), in_=gmT[:, 0:2, :]),
                     reads=[AR(f"gmT{i}") for i in range(4)] + [R("xg0")], writes=[R("xg0")])
                P.op("dve", lambda e: e.tensor_copy(out=xg[:, 1, 0:512], in_=gmT[:, 2, :]),
                     reads=[AR(f"gmT{i}") for i in range(4)] + [R("xg1")], writes=[R("xg1")])
            if dbg in ("att3", "att", "att5", "att4"):
                for q in range(4):
                    P.op("dve", lambda e, q=q: e.tensor_copy(out=xg[:, q, :].rearrange("p (c t) -> p c t", t=T), in_=attT[:, 2 * q:2 * q + 2, :]),
                         reads=[AR(f"attT{i}") for i in range(10)] + [R(f"xg{q}")], writes=[R(f"xg{q}")])
            for tt in range(4):
                P.dma("sp", lambda e, tt=tt: e.dma_start(out=out_d[tt * 128:(tt + 1) * 128, :], in_=xg[:, tt, :]), d_o[tt],
                      reads=[R(f"xg{tt}")], writes=[R(f"outd{tt}")])
        P.final()
        if build.want_trace:
            P.trace = []
        P.emit(st)
        build.stats = P.stats
        build.trace = P.trace
    return nc


build.want_trace = False

WNAMES = ["norm_pre_mix", "norm_post_mix", "norm_pre_ffn", "norm_post_ffn", "norm_mem", "w_in", "b_forget",
          "gmlp_v_norm", "gmlp_w_s", "gmlp_b_s", "w_mem_kv", "w_out", "w_ff1", "w_ff2"]

_cache = {}


def _consts():
    iu = np.triu(np.ones((128, 128), np.float32))
    return {
        "c_ident": np.eye(128, dtype=np.float32).astype(ml_dtypes.bfloat16),
        "c_triu": iu.astype(ml_dtypes.bfloat16),
        "c_triuf": iu.copy(),
        "c_onesf": np.ones((128, 128), np.float32),
    }


DBG = None


def run_layers(x, mem, weights, L):
    B, S, _ = x.shape
    key = (S, L)
    if key not in _cache:
        _cache[key] = build(S, L, dbg=DBG)
    nc = _cache[key]
    consts = _consts()
    in_maps = []
    for b in range(B):
        m = {"x": np.ascontiguousarray(x[b]), "mem": np.ascontiguousarray(mem[b])}
        for k in WNAMES:
            m[k] = np.ascontiguousarray(weights[k])
        m.update(consts)
        in_maps.append(m)
    res = run_bass_kernel_spmd(nc, in_maps, core_ids=list(range(B)))
    return np.stack([res.results[b]["out"] for b in range(B)], axis=0)


FUSED = True


def kernel(**inputs):
    x = np.asarray(inputs["x"], dtype=np.float32)
    mem = np.asarray(inputs["mem"], dtype=np.float32)
    W = {k: np.asarray(inputs[k], dtype=np.float32) for k in WNAMES}
    depth = W["w_in"].shape[0]
    if FUSED:
        return run_layers(x, mem, W, depth)
    for l in range(depth):
        x = run_layers(x, mem, {k: v[l:l + 1] for k, v in W.items()}, 1)
    return x
```

```python
from contextlib import ExitStack
import numpy as np
import ml_dtypes
import concourse.bass as bass
import concourse.mybir as mybir
from concourse.bass_utils import run_bass_kernel_spmd

F32 = mybir.dt.float32
BF16 = mybir.dt.bfloat16
AF = mybir.ActivationFunctionType
ALU = mybir.AluOpType
AX = mybir.AxisListType

COMPUTE = ("pe", "act", "dve", "pool")
EPS = 1e-6


class Res:
    __slots__ = ("name", "lw", "rd", "group", "excl")

    def __init__(self, name, group=None):
        self.name = name
        self.lw = None
        self.rd = []
        self.group = group
        self.excl = name.startswith("bank")


class Group:
    def __init__(self):
        self.since = []
        self.fdeps = []


class DmaSem:
    __slots__ = ("name", "total", "sem")

    def __init__(self, name):
        self.name = name
        self.total = 0
        self.sem = None


class Op:
    __slots__ = ("eng", "fn", "deps", "idx", "dma", "dma_total", "needs_inc", "cnt", "tag")

    def __init__(self, eng, fn):
        self.eng = eng
        self.fn = fn
        self.deps = []
        self.dma = None
        self.dma_total = 0
        self.needs_inc = False
        self.cnt = 0


class Prog:
    def __init__(self, nc):
        self.nc = nc
        self.ops = []
        self.dmasems = []
        self.resd = {}
        self.trace = None

    def R(self, name, group=None):
        r = self.resd.get(name)
        if r is None:
            r = Res(name, group)
            self.resd[name] = r
        return r

    def dmasem(self, name):
        d = DmaSem(name)
        self.dmasems.append(d)
        return d

    def fence(self, group):
        last = {}
        keep = []
        for o in group.since + group.fdeps:
            if o.dma is not None:
                keep.append(o)
            else:
                if o.eng not in last or last[o.eng].idx < o.idx:
                    last[o.eng] = o
        group.fdeps = keep + list(last.values())
        group.since = []

    def _track(self, op, reads, writes):
        reads = list(reads)
        writes = list(writes)
        for r in list(reads):
            if r.excl:
                reads.remove(r)
                if r not in writes:
                    writes.append(r)
        op.tag = "R:" + ",".join(r.name for r in reads) + " W:" + ",".join(w.name for w in writes)
        deps = set()
        for r in reads:
            if r.lw is not None:
                deps.add(r.lw)
        for w in writes:
            if w.lw is not None:
                deps.add(w.lw)
            for o in w.rd:
                deps.add(o)
        groups = set()
        for r in list(reads) + list(writes):
            if r.group is not None:
                groups.add(r.group)
        for g in groups:
            for o in g.fdeps:
                deps.add(o)
            g.since.append(op)
        deps.discard(op)
        best = {}
        red = []
        for d in deps:
            if d.dma is not None:
                red.append(d)
            elif d.eng not in best or best[d.eng].idx < d.idx:
                best[d.eng] = d
        deps = red + list(best.values())
        for r in reads:
            r.rd.append(op)
        for w in writes:
            w.lw = op
            w.rd = []
        op.deps = list(deps)

    def op(self, eng, fn, reads=(), writes=()):
        o = Op(eng, fn)
        o.idx = len(self.ops)
        self.ops.append(o)
        self._track(o, reads, writes)
        return o

    def dma(self, queue, fn, sem, reads=(), writes=()):
        o = Op(queue, fn)
        o.idx = len(self.ops)
        o.dma = sem
        sem.total += 16
        o.dma_total = sem.total
        self.ops.append(o)
        self._track(o, reads, writes)
        return o

    def final(self):
        o = Op("sp", lambda e: e.nop())
        o.idx = len(self.ops)
        o.tag = "final"
        last = {}
        for p in self.ops:
            if p.dma is not None:
                last[("d", p.dma.name)] = p
            elif p.eng in COMPUTE:
                last[("e", p.eng)] = p
        o.deps = list(last.values())
        self.ops.append(o)

    def emit(self, stack):
        nc = self.nc
        for o in self.ops:
            for d in o.deps:
                if d.dma is None:
                    if d.eng == "pe" and o.eng == "pe" and o.dma is None:
                        continue
                    d.needs_inc = True
        sems = {}
        for e in COMPUTE:
            sems[e] = stack.enter_context(nc.semaphore("s_" + e))
        for d in self.dmasems:
            if d.total > 0:
                d.sem = stack.enter_context(nc.semaphore("d_" + d.name))
        cnt = {e: 0 for e in COMPUTE}
        for o in self.ops:
            if o.dma is None and o.needs_inc:
                cnt[o.eng] += 1
                o.cnt = cnt[o.eng]
        per_eng = {e: [] for e in ("pe", "act", "dve", "pool", "sp")}
        for o in self.ops:
            per_eng[o.eng].append(o)
        self.stats = {e: len(v) for e, v in per_eng.items()}
        self.stats["incs"] = dict(cnt)

        def run_engine(ename, eng):
            seen = {}
            nw = 0
            for o in per_eng[ename]:
                need = {}
                for d in o.deps:
                    if d.dma is not None:
                        key = ("d", d.dma.name)
                        val = d.dma_total
                        semh = d.dma.sem
                    else:
                        if d.eng == "pe" and ename == "pe" and o.dma is None:
                            continue
                        key = ("e", d.eng)
                        val = d.cnt
                        semh = sems[d.eng]
                    if seen.get(key, 0) >= val:
                        continue
                    if key not in need or need[key][1] < val:
                        need[key] = (semh, val)
                for key, (semh, val) in need.items():
                    eng.wait_ge(semh, val)
                    seen[key] = val
                    nw += 1
                if self.trace is not None:
                    self.trace.append((ename, o.idx, [(k, v[1]) for k, v in need.items()], o.cnt if o.needs_inc else None, o.dma_total if o.dma else None, o.tag))
                ins = o.fn(eng)
                if o.dma is not None:
                    ins.then_inc(o.dma.sem, 16)
                elif o.needs_inc:
                    ins.then_inc(sems[ename], 1)
            self.stats["waits_" + ename] = nw

        with nc.Block() as block:
            @block.tensor
            def _(e):
                run_engine("pe", e)

            @block.scalar
            def _(e):
                run_engine("act", e)

            @block.vector
            def _(e):
                run_engine("dve", e)

            @block.gpsimd
            def _(e):
                run_engine("pool", e)

            @block.sync
            def _(e):
                run_engine("sp", e)


D = 1024
KC = 8
DIN = 2182
DFF = 4096
NMEM = 256
T = 512
NSLOT = 4

WIN_BLOCKS = [(0, 512), (512, 512), (1024, 512), (1536, 390), (1926, 256)]


class StopBuild(Exception):
    pass


def build(S, L, dbg=None):
    NT = S // 128
    NG = S // T
    nc = bass.Bass("TRN2", target_bir_lowering=False)

    def din(name, shape, dt=F32):
        return nc.dram_tensor(name, shape, dt, kind="ExternalInput")

    x_t = din("x", [S, D])
    mem_t = din("mem", [NMEM, D])
    g_premix_t = din("norm_pre_mix", [L, D])
    g_postmix_t = din("norm_post_mix", [L, D])
    g_preffn_t = din("norm_pre_ffn", [L, D])
    g_postffn_t = din("norm_post_ffn", [L, D])
    g_mem_t = din("norm_mem", [L, D])
    w_in_t = din("w_in", [L, D, DIN])
    b_forget_t = din("b_forget", [L, 6])
    gv_t = din("gmlp_v_norm", [L, 384])
    ws_t = din("gmlp_w_s", [L, 6, 128, 128])
    bs_t = din("gmlp_b_s", [L, 6, 128])
    wkv_t = din("w_mem_kv", [L, D, 512])
    wout_t = din("w_out", [L, D, D])
    w1_t = din("w_ff1", [L, D, DFF])
    w2_t = din("w_ff2", [L, DFF, D])
    ident_t = din("c_ident", [128, 128], BF16)
    triu_t = din("c_triu", [128, 128], BF16)
    triuf_t = din("c_triuf", [128, 128], F32)
    onesf_t = din("c_onesf", [128, 128], F32)
    out_t = nc.dram_tensor("out", [S, D], F32, kind="ExternalOutput")

    x_d, mem_d, out_d = x_t.ap(), mem_t.ap(), out_t.ap()
    w_in_d, wkv_d, wout_d, w1_d, w2_d = w_in_t.ap(), wkv_t.ap(), wout_t.ap(), w1_t.ap(), w2_t.ap()
    ws_d = ws_t.ap()

    with ExitStack() as st:
        P = Prog(nc)
        R = P.R

        def sb(name, shape, dt):
            return st.enter_context(nc.sbuf_tensor(name, shape, dt))

        KT = [sb(f"KT{l}", [128, 3, S], BF16) for l in range(L)]
        VP = [sb(f"VP{l}", [128, NT, 6, 65], BF16) for l in range(L)]
        CALL = [sb(f"CALL{l}", [128, NT, 6], F32) for l in range(L)]
        EALL = [sb(f"EALL{l}", [128, NT, 6], F32) for l in range(L)]
        KMT = [sb(f"KMT{l}", [128, 2, NMEM], BF16) for l in range(L)]
        VMP = [sb(f"VMP{l}", [128, 2, 4, 65], BF16) for l in range(L)]
        WST = [sb(f"WST{l}", [128, 6, 128], BF16) for l in range(L)]
        GCOL = [sb(f"GCOL{l}", [128, 3, 8], F32) for l in range(L)]
        BS = [sb(f"BS{l}", [128, 6], F32) for l in range(L)]
        BFG = [sb(f"BFG{l}", [128, 6], F32) for l in range(L)]
        gpm = sb("gpm", [128, D], F32)
        gpf = gpm
        xg = sb("xg", [128, 4, D], F32)
        hT = sb("hT", [128, KC, T], BF16)
        onesb = sb("onesb", [128, 128], BF16)
        stat = sb("stat", [128, 64], F32)
        ident = sb("ident", [128, 128], BF16)
        triu = sb("triu", [128, 128], BF16)
        triuf = sb("triuf", [128, 128], F32)
        onesf = sb("onesf", [128, 128], F32)
        slots = [sb(f"slot{i}", [128, 4096], BF16) for i in range(NSLOT)]
        ARENA_BF = 19680
        arena = sb("arena", [128, ARENA_BF], BF16)
        AG = Group()

        class Carver:
            def __init__(self):
                self.off = 0

            def take(self, nbf):
                o = self.off
                self.off += nbf
                assert self.off <= ARENA_BF, self.off
                return o

        cv = Carver()

        def a_bf(n):
            o = cv.take(n)
            return arena[:, o:o + n]

        def a_f32(n):
            o = cv.take(2 * n)
            return arena[:, o:o + 2 * n].bitcast(F32)

        QT = a_bf(6 * T).rearrange("p (c t) -> p c t", t=T)
        QMT = a_bf(2 * T).rearrange("p (c t) -> p c t", t=T)
        gmT = a_bf(3 * T).rearrange("p (c t) -> p c t", t=T)
        attT = a_bf(10 * T).rearrange("p (c t) -> p c t", t=T)
        NPT = 3
        PT_OFF = cv.off
        PT = [a_bf(T) for _ in range(NPT)]
        xnb = [arena[:, PT_OFF:PT_OFF + D], None]
        zA = a_f32(768)
        xnb[1] = arena[:, cv.off - 1536:cv.off - 1536 + D]
        tmpv2 = a_f32(384)
        vn = a_bf(384)
        outa = a_bf(384)
        beta = a_f32(2 * 32).rearrange("p (a c) -> p a c", a=2)
        daug = a_bf(2 * T).rearrange("p (a t) -> p a t", a=2)
        rs = a_f32(T)
        tmpv = rs[:, 0:384]
        bcsb = a_f32(T)
        GVS = bcsb[:, 0:384]
        fg = a_f32(24).rearrange("p (a b) -> p a b", b=6)
        spb = a_f32(24).rearrange("p (a b) -> p a b", b=6)
        att_end = cv.off
        cv.off = 0
        hidT = a_bf(32 * T).rearrange("p (c t) -> p c t", t=T)
        rtmp = [a_f32(T)]
        cv.off = max(cv.off, att_end)
        _yo = cv.take(2 * T)
        ytmp_bf = arena[:, _yo:_yo + 2 * T]
        ytmp = ytmp_bf.bitcast(F32)

        def AR(name):
            return R(name, AG)

        banks = [st.enter_context(nc.psum_tensor(f"bank{i}", [128, 512], F32)) for i in range(8)]
        banks_bf = [b[:, :].bitcast(BF16) for b in banks]

        def BK(i):
            return R(f"bank{i}")

        d_setup = P.dmasem("setup")
        d_x = [P.dmasem(f"x{i}") for i in range(4)]
        d_o = [P.dmasem(f"o{i}") for i in range(4)]
        d_slot = [P.dmasem(f"sl{i}") for i in range(NSLOT)]
        d_ws = P.dmasem("ws")
        d_gpm = P.dmasem("gpm")
        d_gpf = d_gpm
        d_gv = P.dmasem("gv")

        slot_ctr = [0]

        def load_slot(dst_fn, src, reads=()):
            i = slot_ctr[0] % NSLOT
            slot_ctr[0] += 1
            dst = dst_fn(slots[i])
            P.dma("pool", lambda e, dst=dst, src=src: e.dma_start(out=dst, in_=src), d_slot[i],
                  reads=list(reads), writes=[R(f"slot{i}")])
            return i

        def bcast_rows(t, row_off, n):
            return bass.AP(t, row_off, [[0, 128], [1, n]])

        setup_res = []

        def setup_dma(dst, src, resname, **kw):
            P.dma("sp", lambda e, dst=dst, src=src, kw=kw: e.dma_start(out=dst, in_=src, **kw), d_setup, writes=[R(resname)])
            setup_res.append(R(resname))

        setup_dma(ident[:], ident_t.ap(), "ident")
        setup_dma(triu[:], triu_t.ap(), "triu")
        setup_dma(triuf[:], triuf_t.ap(), "triuf")
        setup_dma(onesf[:], onesf_t.ap(), "onesf")
        for l in range(L):
            for k, gt in enumerate((g_premix_t, g_preffn_t, g_mem_t)):
                setup_dma(GCOL[l][:, k, :], bass.AP(gt, l * D, [[1, 128], [128, 8]]), f"GCOL{l}", allow_slow_non_contiguous=True)
            setup_dma(BFG[l][:], bcast_rows(b_forget_t, l * 6, 6), f"BFG{l}")
            setup_dma(BS[l][:], bass.AP(bs_t, l * 768, [[1, 128], [128, 6]]), f"BS{l}", allow_slow_non_contiguous=True)
        last_setup = P.ops[-1]
        for r in setup_res:
            r.lw = last_setup
            r.rd = []
        P.op("dve", lambda e: e.memset(onesb[:], 1.0), writes=[R("onesb")])
        P.op("dve", lambda e: e.memset(stat[:], 1.0e6), writes=[R("statn"), R("dtmp0"), R("dtmp1")] + [R(f"statn{i}") for i in range(4)] + [R(f"statv{i}") for i in range(4)] + [R(f"statp{i}") for i in range(4)])
        for l in range(L):
            P.op("dve", lambda e, l=l: e.memset(VP[l][:, :, :, 64:65], 1.0), writes=[R(f"VPones{l}")])
            P.op("dve", lambda e, l=l: e.memset(VMP[l][:, :, :, 64:65], 1.0), writes=[R(f"VMPones{l}")])

        def rstd_from_ss(ss_ap, out_ap, n, rd, wr):
            P.op("act", lambda e: e.activation(out=out_ap, in_=ss_ap, func=AF.Ln, scale=1.0 / n, bias=EPS), reads=rd, writes=wr)
            P.op("act", lambda e: e.activation(out=out_ap, in_=out_ap, func=AF.Exp, scale=-0.5), reads=wr, writes=wr)

        tbank_ctr = [0]

        def norm_transpose_n(srcs, gcol_ap, gres, dsts, batched=True, base=0):
            if not batched and len(srcs) > 1:
                for i in range(len(srcs)):
                    norm_transpose_n(srcs[i:i + 1], gcol_ap, gres, dsts[i:i + 1], batched=True, base=i)
                return
            n = len(srcs)
            ssr = R("statn" if n > 1 else f"statn{base}")
            xres_ = [[AR("PT0"), AR("PT1")], [AR("zA")]]
            for i, (src_ap, src_res) in enumerate(srcs):
                ci = i + base
                P.op("act", lambda e, ci=ci, src_ap=src_ap: e.activation(out=xnb[0], in_=src_ap, func=AF.Square, accum_out=stat[:, ci:ci + 1]),
                     reads=[src_res], writes=[ssr] + (xres_[0] if i == 0 else []))
            rstd_from_ss(stat[:, base:base + n], stat[:, 4 + base:4 + base + n], D, [ssr], [ssr])
            tbs = {}

            def stage_a(i):
                (src_ap, src_res) = srcs[i]
                ci = i + base
                xb = (i + base) % 2
                P.op("dve", lambda e: e.tensor_scalar(out=xnb[xb], in0=src_ap, scalar1=stat[:, 4 + ci:5 + ci], scalar2=None, op0=ALU.mult),
                     reads=[src_res, ssr], writes=xres_[xb])
                tb = 6 + (tbank_ctr[0] % 2)
                tbank_ctr[0] += 1
                tbs[i] = tb
                for kc in range(KC):
                    P.op("pe", lambda e, kc=kc: e.transpose(out=banks_bf[tb][:, kc * 128:(kc + 1) * 128], in_=xnb[xb][:, kc * 128:(kc + 1) * 128], identity=ident[:]),
                         reads=xres_[xb] + [R("ident")], writes=[BK(tb)])

            def stage_b(i):
                (dst_ap, dst_res) = dsts[i]
                tb = tbs[i]
                P.op("dve", lambda e: e.tensor_tensor(out=dst_ap, in0=banks_bf[tb][:, 0:1024].rearrange("p (k t) -> p k t", t=128),
                                                      in1=gcol_ap.unsqueeze(2).to_broadcast([128, KC, 128]), op=ALU.mult),
                     reads=[BK(tb), gres], writes=[dst_res])

            for i in range(n):
                stage_a(i)
                if i >= 1:
                    stage_b(i - 1)
            stage_b(n - 1)

        bank_rr = [0]

        def next_bank(lo=0, hi=6):
            b = lo + (bank_rr[0] % (hi - lo))
            bank_rr[0] += 1
            return b

        def layer_setup(l):
            for mt in range(2):
                P.dma("sp", lambda e, mt=mt: e.dma_start(out=xg[:, mt, :], in_=mem_d[mt * 128:(mt + 1) * 128, :]), d_x[mt], writes=[R(f"xg{mt}")])
            norm_transpose_n([(xg[:, mt, :], R(f"xg{mt}")) for mt in range(2)], GCOL[l][:, 2, :], R(f"GCOL{l}"),
                             [(hT[:, :, mt * 128:(mt + 1) * 128], R(f"hT{mt}")) for mt in range(2)])
            si = load_slot(lambda s: s[:, :].rearrange("p (k n) -> p k n", n=512), wkv_d[l].rearrange("(k p) n -> p k n", p=128))
            wkv = slots[si][:, :].rearrange("p (k n) -> p k n", n=512)
            hres = [R("hT0"), R("hT1")]
            for pm in range(2):
                b = next_bank()
                for kc in range(KC):
                    P.op("pe", lambda e, kc=kc, pm=pm, b=b: e.matmul(banks[b][:, 0:NMEM], lhsT=wkv[:, kc, pm * 128:(pm + 1) * 128], rhs=hT[:, kc, 0:NMEM],
                                                                      start=(kc == 0), stop=(kc == KC - 1)),
                         reads=[R(f"slot{si}")] + hres, writes=[BK(b)])
                P.op("act", lambda e, pm=pm, b=b: e.activation(out=KMT[l][:, pm, :], in_=banks[b][:, 0:NMEM], func=AF.Copy),
                     reads=[BK(b)], writes=[R(f"KMT{l}")])
            for mt in range(2):
                b = next_bank()
                for kc in range(KC):
                    P.op("pe", lambda e, kc=kc, mt=mt, b=b: e.matmul(banks[b][:, 0:256], lhsT=hT[:, kc, mt * 128:(mt + 1) * 128], rhs=wkv[:, kc, 256:512],
                                                                      start=(kc == 0), stop=(kc == KC - 1)),
                         reads=[R(f"slot{si}"), hres[mt]], writes=[BK(b)])
                P.op("act", lambda e, mt=mt, b=b: e.activation(out=VMP[l][:, mt, :, 0:64], in_=banks[b][:, 0:256].rearrange("p (h d) -> p h d", d=64), func=AF.Copy),
                     reads=[BK(b)], writes=[R(f"VMP{l}")])
            P.dma("sp", lambda e: e.dma_start(out=zA.rearrange("p (g s) -> p g s", s=128), in_=ws_d[l].rearrange("g t s -> t g s")), d_ws,
                  writes=[AR("zA")])
            wsb = xnb[0][:, 0:768]
            P.op("dve", lambda e: e.tensor_copy(out=wsb, in_=zA), reads=[AR("zA")], writes=[AR("PT0"), AR("PT1")])
            tb = 6 + (tbank_ctr[0] % 2)
            tbank_ctr[0] += 1
            for gg in range(6):
                P.op("pe", lambda e, gg=gg: e.transpose(out=banks_bf[tb][:, gg * 128:(gg + 1) * 128], in_=wsb[:, gg * 128:(gg + 1) * 128], identity=ident[:]),
                     reads=[AR("PT0"), AR("PT1"), R("ident")], writes=[BK(tb)])
            P.op("dve", lambda e: e.tensor_tensor(out=WST[l][:], in0=banks_bf[tb][:, 0:768].rearrange("p (g t) -> p g t", t=128),
                                                  in1=triu[:].unsqueeze(1).to_broadcast([128, 6, 128]), op=ALU.mult),
                 reads=[BK(tb), R("triu")], writes=[R(f"WST{l}")])

        def mixer(g, l):
            xres = [R(f"xg{tt}") for tt in range(4)]
            hres = [R(f"hT{tt}") for tt in range(4)]
            P.dma("sp", lambda e: e.dma_start(out=gpm[:], in_=bcast_rows(g_postmix_t, l * D, D)), d_gpm, writes=[R("gpm")])
            P.dma("sp", lambda e: e.dma_start(out=GVS, in_=bcast_rows(gv_t, l * 384, 384)), d_gv, writes=[AR("bcsb")])
            norm_transpose_n([(xg[:, tt, :], xres[tt]) for tt in range(4)], GCOL[l][:, 0, :], R(f"GCOL{l}"),
                             [(hT[:, :, tt * 128:(tt + 1) * 128], hres[tt]) for tt in range(4)], batched=(l == 0))
            if dbg == "norm":
                raise StopBuild()
            blk = [None] * 5

            def load_blk(bi):
                c0, ncol = WIN_BLOCKS[bi]
                si = load_slot(lambda s, ncol=ncol: s[:, 0:KC * ncol].rearrange("p (k n) -> p k n", n=ncol),
                               w_in_d[l][:, c0:c0 + ncol].rearrange("(k p) n -> p k n", p=128))
                blk[bi] = (si, slots[si][:, 0:KC * ncol].rearrange("p (k n) -> p k n", n=ncol))

            load_blk(0)
            load_blk(1)
            load_blk(2)
            load_blk(4)
            P.op("dve", lambda e: e.memset(QT[:, :, :], 0.0), writes=[AR("QTz")] + [AR(f"QT{c}") for c in range(3)])
            P.op("dve", lambda e: e.memset(attT[64:128, :, :], 0.0), writes=[AR("attTz")])
            zAb = [zA, arena[:, PT_OFF:PT_OFF + 1536].bitcast(F32)]
            zAr = [[AR("zA")], [AR("PT0"), AR("PT1"), AR("PT2")]]

            def a_mm(tt):
                zi = tt % 2
                ba, bb = next_bank(), next_bank()
                for kc in range(KC):
                    P.op("pe", lambda e, kc=kc: e.matmul(banks[ba][:, :], lhsT=hT[:, kc, tt * 128:(tt + 1) * 128], rhs=blk[0][1][:, kc, :],
                                                          start=(kc == 0), stop=(kc == KC - 1)),
                         reads=[hres[tt], R(f"slot{blk[0][0]}")], writes=[BK(ba)])
                for kc in range(KC):
                    P.op("pe", lambda e, kc=kc: e.matmul(banks[bb][:, 0:256], lhsT=hT[:, kc, tt * 128:(tt + 1) * 128], rhs=blk[1][1][:, kc, 0:256],
                                                          start=(kc == 0), stop=(kc == KC - 1)),
                         reads=[hres[tt], R(f"slot{blk[1][0]}")], writes=[BK(bb)])
                P.op("act", lambda e: e.activation(out=zAb[zi][:, 0:512], in_=banks[ba][:, :], func=AF.Gelu_apprx_tanh), reads=[BK(ba)], writes=zAr[zi])
                P.op("act", lambda e: e.activation(out=zAb[zi][:, 512:768], in_=banks[bb][:, 0:256], func=AF.Gelu_apprx_tanh), reads=[BK(bb)], writes=zAr[zi])

            gstate = {}

            def g_vnorm(tt):
                zi = tt % 2
                zz, zr = zAb[zi], zAr[zi]
                v3 = zz[:, 384:768].rearrange("p (g d) -> p g d", d=64)
                P.op("dve", lambda e: e.tensor_tensor(out=tmpv, in0=zz[:, 384:768], in1=zz[:, 384:768], op=ALU.mult), reads=zr, writes=[AR("rs")])
                c0 = 16 + tt * 8
                sr = R(f"statv{tt}")
                P.op("dve", lambda e: e.reduce_sum(out=stat[:, c0:c0 + 6], in_=tmpv.rearrange("p (g d) -> p g d", d=64), axis=AX.X),
                     reads=[AR("rs")], writes=[sr])
                rstd_from_ss(stat[:, c0:c0 + 6], stat[:, c0:c0 + 6], 64, [sr], [sr])
                P.op("dve", lambda e: e.tensor_tensor(out=tmpv.rearrange("p (g d) -> p g d", d=64), in0=v3,
                                                      in1=stat[:, c0:c0 + 6].unsqueeze(2).to_broadcast([128, 6, 64]), op=ALU.mult),
                     reads=zr + [sr], writes=[AR("rs")])
                P.op("dve", lambda e: e.tensor_tensor(out=vn, in0=tmpv, in1=GVS, op=ALU.mult), reads=[AR("rs"), AR("bcsb")], writes=[AR("vn")])

            def g_mix(tt):
                zi = tt % 2
                zz, zr = zAb[zi], zAr[zi]
                bm = next_bank()
                for gg in range(6):
                    P.op("pe", lambda e, gg=gg: e.matmul(banks[bm][:, gg * 64:(gg + 1) * 64], lhsT=WST[l][:, gg, :], rhs=vn[:, gg * 64:(gg + 1) * 64],
                                                          start=True, stop=True),
                         reads=[AR("vn"), R(f"WST{l}")], writes=[BK(bm)])
                P.op("dve", lambda e: e.tensor_tensor(out=tmpv2.rearrange("p (g d) -> p g d", d=64), in0=banks[bm][:, 0:384].rearrange("p (g d) -> p g d", d=64),
                                                      in1=BS[l][:].unsqueeze(2).to_broadcast([128, 6, 64]), op=ALU.add),
                     reads=[BK(bm), R(f"BS{l}")], writes=[AR("tmpv2")])
                P.op("dve", lambda e: e.tensor_tensor(out=outa, in0=tmpv2, in1=zz[:, 0:384], op=ALU.mult), reads=[AR("tmpv2")] + zr, writes=[AR("outa")])

            def g_tr(tt):
                tb = 6 + (tbank_ctr[0] % 2)
                tbank_ctr[0] += 1
                for c in range(3):
                    P.op("pe", lambda e, c=c: e.transpose(out=banks_bf[tb][:, c * 128:(c + 1) * 128], in_=outa[:, c * 128:(c + 1) * 128], identity=ident[:]),
                         reads=[AR("outa"), R("ident")], writes=[BK(tb)])
                P.op("act", lambda e: e.activation(out=gmT[:, :, tt * 128:(tt + 1) * 128], in_=banks_bf[tb][:, 0:384].rearrange("p (c t) -> p c t", t=128), func=AF.Copy),
                     reads=[BK(tb)], writes=[AR(f"gmT{tt}")])

            fm = [(1, 256, "q", 0), (1, 384, "q", 1), (2, 0, "q", 2),
                  (2, 128, "k", 0), (2, 256, "k", 1), (2, 384, "k", 2),
                  (4, 0, "m", 0), (4, 128, "m", 1)]

            def fm_chunk(bi, lc, kind, c):
                b = next_bank()
                for kc in range(KC):
                    P.op("pe", lambda e, kc=kc: e.matmul(banks[b][:, :], lhsT=blk[bi][1][:, kc, lc:lc + 128], rhs=hT[:, kc, :],
                                                          start=(kc == 0), stop=(kc == KC - 1)),
                         reads=hres + [R(f"slot{blk[bi][0]}")], writes=[BK(b)])
                if kind == "q":
                    P.op("act", lambda e: e.activation(out=QT[0:64, 2 * c, :], in_=banks[b][0:64, :], func=AF.Copy, scale=0.125), reads=[BK(b), AR("QTz")], writes=[AR(f"QT{c}")])
                    P.op("act", lambda e: e.activation(out=QT[64:128, 2 * c + 1, :], in_=banks[b][64:128, :], func=AF.Copy, scale=0.125), reads=[BK(b), AR("QTz")], writes=[AR(f"QT{c}")])
                elif kind == "m":
                    P.op("act", lambda e: e.activation(out=QMT[:, c, :], in_=banks[b][:, :], func=AF.Copy, scale=0.125), reads=[BK(b)], writes=[AR(f"QMT{c}")])
                else:
                    P.op("dve", lambda e: e.tensor_copy(out=KT[l][:, c, g * T:(g + 1) * T], in_=banks[b][:, :]), reads=[BK(b)], writes=[R(f"KT{l}_{c}")])

            def c_tile(tt):
                b = next_bank()
                for kc in range(KC):
                    P.op("pe", lambda e, kc=kc: e.matmul(banks[b][:, 0:390], lhsT=hT[:, kc, tt * 128:(tt + 1) * 128], rhs=blk[3][1][:, kc, :],
                                                          start=(kc == 0), stop=(kc == KC - 1)),
                         reads=[hres[tt], R(f"slot{blk[3][0]}")], writes=[BK(b)])
                P.op("act", lambda e: e.activation(out=VP[l][:, 4 * g + tt, :, 0:64], in_=banks[b][:, 0:384].rearrange("p (h d) -> p h d", d=64), func=AF.Copy),
                     reads=[BK(b)], writes=[R(f"VP{l}")])
                P.op("dve", lambda e: e.tensor_tensor(out=fg[:, tt, :], in0=banks[b][:, 384:390], in1=BFG[l][:], op=ALU.add),
                     reads=[BK(b), R(f"BFG{l}")], writes=[AR("fg")])

            a_mm(0)
            g_vnorm(0)
            a_mm(1)
            for q in fm[0:4]:
                fm_chunk(*q)
            g_mix(0)
            a_mm(2)
            for q in fm[4:6]:
                fm_chunk(*q)
            g_tr(0)
            g_vnorm(1)
            for q in fm[6:8]:
                fm_chunk(*q)
            g_mix(1)
            a_mm(3)
            load_blk(3)
            c_tile(0)
            g_tr(1)
            g_vnorm(2)
            c_tile(1)
            c_tile(2)
            g_mix(2)
            c_tile(3)
            g_tr(2)
            g_vnorm(3)
            P.op("act", lambda e: e.activation(out=spb, in_=fg, func=AF.Exp, scale=-1.0), reads=[AR("fg")], writes=[AR("spb")])
            P.op("act", lambda e: e.activation(out=spb, in_=spb, func=AF.Ln, bias=1.0), reads=[AR("spb")], writes=[AR("spb")])
            for tt in range(4):
                ti = 4 * g + tt
                b = next_bank()
                P.op("pe", lambda e, b=b, tt=tt: e.matmul(banks[b][:, 0:6], lhsT=triuf[:], rhs=spb[:, tt, :], start=True, stop=True),
                     reads=[AR("spb"), R("triuf")], writes=[BK(b)])
                P.op("pe", lambda e, b=b, tt=tt: e.matmul(banks[b][:, 8:14], lhsT=onesf[:], rhs=spb[:, tt, :], start=True, stop=True),
                     reads=[AR("spb"), R("onesf")], writes=[BK(b)])
                cr = R(f"CE{l}")
                if ti == 0:
                    P.op("dve", lambda e, b=b, ti=ti: e.tensor_copy(out=CALL[l][:, ti, :], in_=banks[b][:, 0:6]), reads=[BK(b)], writes=[cr])
                    P.op("dve", lambda e, b=b, ti=ti: e.tensor_copy(out=EALL[l][:, ti, :], in_=banks[b][:, 8:14]), reads=[BK(b)], writes=[cr])
                else:
                    P.op("dve", lambda e, b=b, ti=ti: e.tensor_tensor(out=CALL[l][:, ti, :], in0=banks[b][:, 0:6], in1=EALL[l][:, ti - 1, :], op=ALU.add),
                         reads=[BK(b), cr], writes=[cr])
                    P.op("dve", lambda e, b=b, ti=ti: e.tensor_tensor(out=EALL[l][:, ti, :], in0=banks[b][:, 8:14], in1=EALL[l][:, ti - 1, :], op=ALU.add),
                         reads=[BK(b), cr], writes=[cr])

            g_mix(3)
            g_tr(3)
            if dbg in ("win", "gm", "gm2"):
                raise StopBuild()
            nj = 4 * g + 4
            sb_rr = [0]
            pt_rr = [0]
            ktres = [R(f"KT{l}_{c}") for c in range(3)]

            def normalize(ob, hidx):
                P.op("dve", lambda e: e.reciprocal(out=rs[64:65, :], in_=banks[ob][64:65, :]), reads=[BK(ob)], writes=[AR("rs")])
                P.op("pe", lambda e: e.matmul(banks[5][0:64, :], lhsT=onesf[64:65, 0:64], rhs=rs[64:65, :], start=True, stop=True),
                     reads=[AR("rs"), R("onesf")], writes=[BK(5)])
                P.op("act", lambda e: e.activation(out=bcsb[0:64, :], in_=banks[5][0:64, :], func=AF.Copy), reads=[BK(5)], writes=[AR("bcsb")])
                P.op("dve", lambda e: e.tensor_tensor(out=attT[0:64, hidx, :], in0=banks[ob][0:64, :], in1=bcsb[0:64, :], op=ALU.mult),
                     reads=[BK(ob), AR("bcsb")], writes=[AR(f"attT{hidx}")])

            LOOK = 2
            pend = []
            deferred = []

            def tick():
                for d in deferred:
                    d[0] -= 1
                while deferred and deferred[0][0] <= 0:
                    deferred.pop(0)[1]()

            def norm_part1(ob):
                P.op("act", lambda e: e.activation(out=rs[64:65, :], in_=banks[ob][64:65, :], func=AF.Ln), reads=[BK(ob)], writes=[AR("rs")])
                P.op("act", lambda e: e.activation(out=rs[64:65, :], in_=rs[64:65, :], func=AF.Exp, scale=-1.0), reads=[AR("rs")], writes=[AR("rs")])

            def norm_part2(ob, hidx):
                P.op("pe", lambda e: e.matmul(banks[5][0:64, :], lhsT=onesf[64:65, 0:64], rhs=rs[64:65, :], start=True, stop=True),
                     reads=[AR("rs"), R("onesf")], writes=[BK(5)])
                P.op("act", lambda e: e.activation(out=bcsb[0:64, :], in_=banks[5][0:64, :], func=AF.Copy), reads=[BK(5)], writes=[AR("bcsb")])
                P.op("dve", lambda e: e.tensor_tensor(out=attT[0:64, hidx, :], in0=banks[ob][0:64, :], in1=bcsb[0:64, :], op=ALU.mult),
                     reads=[BK(ob), AR("bcsb")], writes=[AR(f"attT{hidx}")])

            def emit_pv(blk_):
                (kind, hidx, j, col0, pk, ob, first, last) = blk_
                if kind == "f":
                    P.op("pe", lambda e: e.matmul(banks[ob][0:65, col0:T], lhsT=VP[l][:, j, hidx, :], rhs=PT[pk][:, col0:T], start=first, stop=last),
                         reads=[R(f"VP{l}"), R(f"VPones{l}"), AR(f"PT{pk}")], writes=[BK(ob)])
                else:
                    P.op("pe", lambda e: e.matmul(banks[ob][0:65, :], lhsT=VMP[l][:, j, hidx - 6, :], rhs=PT[pk][:, :], start=first, stop=last),
                         reads=[R(f"VMP{l}"), R(f"VMPones{l}"), AR(f"PT{pk}")], writes=[BK(ob)])
                if last:
                    while deferred:
                        deferred.pop(0)[1]()
                    norm_part1(ob)
                    deferred.append([4, lambda ob=ob, hidx=hidx: norm_part2(ob, hidx)])

            def push(blk_):
                pend.append(blk_)
                if len(pend) > LOOK:
                    emit_pv(pend.pop(0))
                tick()

            for h in range(6):
                p, r0 = h // 2, 64 * (h % 2)
                par = h % 2
                br = AR(f"beta{par}")
                P.op("dve", lambda e, par=par, h=h: e.tensor_scalar(out=beta[:, par, 0:nj], in0=CALL[l][:, 0:nj, h],
                                                                     scalar1=EALL[l][:, 4 * g + 3, h:h + 1], scalar2=None, op0=ALU.subtract),
                     reads=[R(f"CE{l}")], writes=[br])
                dr = AR(f"daug{par}")
                P.op("dve", lambda e, par=par, h=h: e.tensor_scalar(out=stat[:, 8 + 4 * par:12 + 4 * par], in0=EALL[l][:, 4 * g:4 * g + 4, h],
                                                                     scalar1=EALL[l][:, 4 * g + 3, h:h + 1], scalar2=-1.0 / 128, op0=ALU.subtract, op1=ALU.mult),
                     reads=[R(f"CE{l}")], writes=[R(f"dtmp{par}")])
                P.op("dve", lambda e, par=par: e.tensor_copy(out=daug[:, par, :].rearrange("p (a b) -> p a b", b=128),
                                                              in_=stat[:, 8 + 4 * par:12 + 4 * par].unsqueeze(2).to_broadcast([128, 4, 128])),
                     reads=[R(f"dtmp{par}")], writes=[dr])
                ob = 3 + (h % 2)
                for j in range(nj):
                    il0 = max(0, j - 4 * g)
                    col0 = il0 * 128
                    sbk = sb_rr[0] % 3
                    sb_rr[0] += 1
                    pk = pt_rr[0] % NPT
                    pt_rr[0] += 1
                    P.op("pe", lambda e, j=j, col0=col0, sbk=sbk, p=p, h=h: e.matmul(banks[sbk][:, col0:T], lhsT=KT[l][:, p, j * 128:(j + 1) * 128],
                                                                                        rhs=QT[:, h, col0:T], start=True, stop=False),
                         reads=[ktres[p], AR(f"QT{p}")], writes=[BK(sbk)])
                    P.op("pe", lambda e, col0=col0, sbk=sbk, par=par: e.matmul(banks[sbk][:, col0:T], lhsT=onesb[:], rhs=daug[:, par, col0:T], start=False, stop=True),
                         reads=[R("onesb"), dr], writes=[BK(sbk)])
                    P.op("act", lambda e, j=j, col0=col0, sbk=sbk, pk=pk, par=par: e.activation(out=PT[pk][:, col0:T], in_=banks[sbk][:, col0:T],
                                                                                                func=AF.Exp, bias=beta[:, par, j:j + 1]),
                         reads=[BK(sbk), br], writes=[AR(f"PT{pk}")])
                    if j >= 4 * g:
                        P.op("dve", lambda e, il0=il0, pk=pk: e.tensor_tensor(out=PT[pk][:, il0 * 128:(il0 + 1) * 128], in0=PT[pk][:, il0 * 128:(il0 + 1) * 128],
                                                                                in1=triu[:], op=ALU.mult),
                             reads=[AR(f"PT{pk}"), R("triu")], writes=[AR(f"PT{pk}")])
                    push(("f", h, j, col0, pk, ob, j == 0, j == nj - 1))
            for hm in range(4):
                p, r0 = hm // 2, 64 * (hm % 2)
                ob = 3 + (hm % 2)
                for jm in range(2):
                    sbk = sb_rr[0] % 3
                    sb_rr[0] += 1
                    pk = pt_rr[0] % NPT
                    pt_rr[0] += 1
                    P.op("pe", lambda e, jm=jm, sbk=sbk, p=p, r0=r0: e.matmul(banks[sbk][:, :], lhsT=KMT[l][r0:r0 + 64, p, jm * 128:(jm + 1) * 128],
                                                                               rhs=QMT[r0:r0 + 64, p, :], start=True, stop=True),
                         reads=[R(f"KMT{l}"), AR(f"QMT{p}")], writes=[BK(sbk)])
                    P.op("act", lambda e, sbk=sbk, pk=pk: e.activation(out=PT[pk][:, :], in_=banks[sbk][:, :], func=AF.Exp),
                         reads=[BK(sbk)], writes=[AR(f"PT{pk}")])
                    push(("m", 6 + hm, jm, 0, pk, ob, jm == 0, jm == 1))
            while pend:
                emit_pv(pend.pop(0))
                tick()
            while deferred:
                deferred.pop(0)[1]()
            if dbg == "att":
                raise StopBuild()
            sg = load_slot(lambda s: s[:, 0:3 * D].rearrange("p (c n) -> p c n", n=D), wout_d[l][0:384, :].rearrange("(c p) n -> p c n", p=128))
            wo_g = slots[sg][:, 0:3 * D].rearrange("p (c n) -> p c n", n=D)
            wo_h = []
            for (h0, nh) in ((0, 4), (4, 4), (8, 2)):
                si = load_slot(lambda s, nh=nh: s[0:64, 0:nh * D].rearrange("p (c n) -> p c n", n=D),
                               wout_d[l][384 + 64 * h0:384 + 64 * (h0 + nh), :].rearrange("(c p) n -> p c n", p=64))
                v = slots[si][:, 0:nh * D].rearrange("p (c n) -> p c n", n=D)
                for k in range(nh):
                    wo_h.append((si, v, k))
            attres = [AR(f"attT{i}") for i in range(10)]
            for tt in range(4):
                yb = [next_bank(0, 4), next_bank(0, 4)]
                for half in range(2):
                    b = yb[half]
                    for c in range(3):
                        P.op("pe", lambda e, c=c, tt=tt, half=half, b=b: e.matmul(banks[b][:, :], lhsT=gmT[:, c, tt * 128:(tt + 1) * 128],
                                                                                   rhs=wo_g[:, c, half * 512:(half + 1) * 512], start=(c == 0), stop=False),
                             reads=[AR(f"gmT{tt}"), R(f"slot{sg}")], writes=[BK(b)])
                    for hh in range(10):
                        si, v, k = wo_h[hh]
                        P.op("pe", lambda e, hh=hh, tt=tt, half=half, b=b, v=v, k=k: e.matmul(banks[b][:, :], lhsT=attT[:, hh, tt * 128:(tt + 1) * 128],
                                                                                               rhs=v[:, k, half * 512:(half + 1) * 512], start=False, stop=(hh == 9)),
                             reads=[attres[hh], AR("attTz"), R(f"slot{si}")], writes=[BK(b)])
                post_norm(yb, tt, gpm, R("gpm"), xres[tt])

        def post_norm(yb, tt, gbuf, gres, xr):
            c0 = 48 + tt * 4
            sr = R(f"statp{tt}")
            for half in range(2):
                P.op("act", lambda e, half=half: e.activation(out=ytmp_bf[:, half * 512:(half + 1) * 512], in_=banks[yb[half]][:, :], func=AF.Square,
                                                               accum_out=stat[:, c0 + half:c0 + half + 1]),
                     reads=[BK(yb[half])], writes=[sr, R("ytmp")])
            P.op("dve", lambda e: e.tensor_tensor(out=stat[:, c0 + 2:c0 + 3], in0=stat[:, c0:c0 + 1], in1=stat[:, c0 + 1:c0 + 2], op=ALU.add), reads=[sr], writes=[sr])
            rstd_from_ss(stat[:, c0 + 2:c0 + 3], stat[:, c0 + 3:c0 + 4], D, [sr], [sr])
            for half in range(2):
                P.op("dve", lambda e, half=half: e.scalar_tensor_tensor(out=ytmp, in0=banks[yb[half]][:, :], scalar=stat[:, c0 + 3:c0 + 4],
                                                                         in1=gbuf[:, half * 512:(half + 1) * 512], op0=ALU.mult, op1=ALU.mult),
                     reads=[BK(yb[half]), sr, gres], writes=[R("ytmp")])
                P.op("dve", lambda e, half=half: e.tensor_tensor(out=xg[:, tt, half * 512:(half + 1) * 512], in0=xg[:, tt, half * 512:(half + 1) * 512], in1=ytmp, op=ALU.add),
                     reads=[R("ytmp"), xr], writes=[xr])

        def ffn(g, l, last):
            xres = [R(f"xg{tt}") for tt in range(4)]
            hres = [R(f"hT{tt}") for tt in range(4)]
            P.dma("sp", lambda e: e.dma_start(out=gpf[:], in_=bcast_rows(g_postffn_t, l * D, D)), d_gpf, writes=[R("gpm")])
            norm_transpose_n([(xg[:, tt, :], xres[tt]) for tt in range(4)], GCOL[l][:, 1, :], R(f"GCOL{l}"),
                             [(hT[:, :, tt * 128:(tt + 1) * 128], hres[tt]) for tt in range(4)], batched=False)
            P.fence(AG)
            for blk_i in range(8):
                si = load_slot(lambda s: s[:, :].rearrange("p (k n) -> p k n", n=512),
                               w1_d[l][:, blk_i * 512:(blk_i + 1) * 512].rearrange("(k p) n -> p k n", p=128))
                wb = slots[si][:, :].rearrange("p (k n) -> p k n", n=512)
                for fcl in range(4):
                    fc = blk_i * 4 + fcl
                    b = next_bank()
                    for kc in range(KC):
                        P.op("pe", lambda e, kc=kc, fcl=fcl, b=b, wb=wb: e.matmul(banks[b][:, :], lhsT=wb[:, kc, fcl * 128:(fcl + 1) * 128], rhs=hT[:, kc, :],
                                                                                   start=(kc == 0), stop=(kc == KC - 1)),
                             reads=hres + [R(f"slot{si}")], writes=[BK(b)])
                    rk = 0
                    P.op("act", lambda e, b=b, rk=rk: e.activation(out=rtmp[rk], in_=banks[b][:, :], func=AF.Relu), reads=[BK(b)], writes=[AR(f"rtmp{rk}")])
                    P.op("dve", lambda e, fc=fc, rk=rk: e.tensor_tensor(out=hidT[:, fc, :], in0=rtmp[rk], in1=rtmp[rk], op=ALU.mult),
                         reads=[AR(f"rtmp{rk}")], writes=[AR(f"hidT{fc}")])
            if dbg == "ffn1":
                raise StopBuild()
            for tp in range(1):
                def w2_load(blk_i):
                    si = load_slot(lambda s: s[:, :].rearrange("p (c n) -> p c n", n=D),
                                   w2_d[l][blk_i * 512:(blk_i + 1) * 512, :].rearrange("(c p) n -> p c n", p=128))
                    return si, slots[si][:, :].rearrange("p (c n) -> p c n", n=D)

                def w2_mm(si, wb, fc, fcl, tt, half):
                    b = tt * 2 + half
                    P.op("pe", lambda e: e.matmul(banks[b][:, :], lhsT=hidT[:, fc, tt * 128:(tt + 1) * 128], rhs=wb[:, fcl, half * 512:(half + 1) * 512],
                                                  start=(fc == 0), stop=(fc == 31)),
                         reads=[AR(f"hidT{fc}"), R(f"slot{si}")], writes=[BK(b)])

                for blk_i in range(5):
                    si, wb = w2_load(blk_i)
                    for fcl in range(4):
                        for tt in range(4):
                            for half in range(2):
                                w2_mm(si, wb, blk_i * 4 + fcl, fcl, tt, half)
                tail = [(blk_i,) + w2_load(blk_i) for blk_i in (5, 6, 7)]
                for tt in range(4):
                    for (blk_i, si, wb) in tail:
                        for fcl in range(4):
                            for half in range(2):
                                w2_mm(si, wb, blk_i * 4 + fcl, fcl, tt, half)
                for tt in range(4):
                    post_norm([tt * 2, tt * 2 + 1], tt, gpf, R("gpm"), xres[tt])
                    if last:
                        r0 = (4 * g + tt) * 128
                        P.dma("sp", lambda e, tt=tt, r0=r0: e.dma_start(out=out_d[r0:r0 + 128, :], in_=xg[:, tt, :]), d_o[tt],
                              reads=[xres[tt]], writes=[R(f"outd{tt}")])
            P.fence(AG)

        try:
            for l in range(L):
                layer_setup(l)
            P.fence(AG)
            if dbg == "setup":
                raise StopBuild()
            for g in range(NG):
                for tt in range(4):
                    r0 = (4 * g + tt) * 128
                    P.dma("sp", lambda e, tt=tt, r0=r0: e.dma_start(out=xg[:, tt, :], in_=x_d[r0:r0 + 128, :]), d_x[tt], writes=[R(f"xg{tt}")])
                for l in range(L):
                    mixer(g, l)
                    if dbg == "mix" or dbg == f"mix:{g}:{l}":
                        raise StopBuild()
                    ffn(g, l, last=(l == L - 1))
                    if dbg == f"ffn:{g}:{l}":
                        raise StopBuild()
        except StopBuild:
            if dbg == "win":
                P.op("dve", lambda e: e.tensor_copy(out=xg[:, 0, :].rearrange("p (c t) -> p c t", t=T), in_=QMT[:, :, :]),
                     reads=[AR("QMT0"), AR("QMT1"), R("xg0")], writes=[R("xg0")])
                P.op("dve", lambda e: e.tensor_copy(out=xg[:, 1, 0:512].rearrange("p (c t) -> p c t", t=256), in_=KMT[0][:, :, :]),
                     reads=[R("KMT0"), R("xg1")], writes=[R("xg1")])
                P.op("dve", lambda e: e.tensor_copy(out=xg[:, 2, 0:520].rearrange("p (c t) -> p c t", t=260), in_=VMP[0][:, :, :, :].rearrange("p a b c -> p a (b c)")),
                     reads=[R("VMP0"), R("VMPones0"), R("xg2")], writes=[R("xg2")])
            if dbg == "gm2":
                ar = [AR("zA"), AR("vn"), AR("tmpv2"), AR("outa"), R("WST0")] + [R(f"xg{i}") for i in range(4)]
                wr = [R(f"xg{i}") for i in range(4)]
                P.op("dve", lambda e: e.tensor_copy(out=xg[:, 0, 0:768], in_=zA), reads=ar, writes=wr)
                P.op("dve", lambda e: e.tensor_copy(out=xg[:, 1, 0:384], in_=vn), reads=ar, writes=wr)
                P.op("dve", lambda e: e.tensor_copy(out=xg[:, 1, 384:768], in_=tmpv2), reads=ar, writes=wr)
                P.op("dve", lambda e: e.tensor_copy(out=xg[:, 2, 0:384], in_=outa), reads=ar, writes=wr)
                P.op("dve", lambda e: e.tensor_copy(out=xg[:, 3, 0:768], in_=WST[0][:, :, :].rearrange("p g t -> p (g t)")), reads=ar, writes=wr)
            if dbg == "gm":
                P.op("dve", lambda e: e.tensor_copy(out=xg[:, 0, :].rearrange("p (c t) -> p c t", t=T), in_=gmT[:, 0:2, :]),
                     reads=[AR(f"gmT{i}") for i in range(4)] + [R("xg0")], writes=[R("xg0")])
                P.op("dve", lambda e: e.tensor_copy(out=xg[:, 1, 0:512], in_=gmT[:, 2, :]),
                     reads=[AR(f"gmT{i}") for i in range(4)] + [R("xg1")], writes=[R("xg1")])
            if dbg in ("att3", "att", "att5", "att4"):
                for q in range(4):
                    P.op("dve", lambda e, q=q: e.tensor_copy(out=xg[:, q, :].rearrange("p (c t) -> p c t", t=T), in_=attT[:, 2 * q:2 * q + 2, :]),
                         reads=[AR(f"attT{i}") for i in range(10)] + [R(f"xg{q}")], writes=[R(f"xg{q}")])
            for tt in range(4):
                P.dma("sp", lambda e, tt=tt: e.dma_start(out=out_d[tt * 128:(tt + 1) * 128, :], in_=xg[:, tt, :]), d_o[tt],
                      reads=[R(f"xg{tt}")], writes=[R(f"outd{tt}")])
        P.final()
        if build.want_trace:
            P.trace = []
        P.emit(st)
        build.stats = P.stats
        build.trace = P.trace
    return nc


build.want_trace = False

WNAMES = ["norm_pre_mix", "norm_post_mix", "norm_pre_ffn", "norm_post_ffn", "norm_mem", "w_in", "b_forget",
          "gmlp_v_norm", "gmlp_w_s", "gmlp_b_s", "w_mem_kv", "w_out", "w_ff1", "w_ff2"]

_cache = {}


def _consts():
    iu = np.triu(np.ones((128, 128), np.float32))
    return {
        "c_ident": np.eye(128, dtype=np.float32).astype(ml_dtypes.bfloat16),
        "c_triu": iu.astype(ml_dtypes.bfloat16),
        "c_triuf": iu.copy(),
        "c_onesf": np.ones((128, 128), np.float32),
    }


DBG = None


def run_layers(x, mem, weights, L):
    B, S, _ = x.shape
    key = (S, L)
    if key not in _cache:
        _cache[key] = build(S, L, dbg=DBG)
    nc = _cache[key]
    consts = _consts()
    in_maps = []
    for b in range(B):
        m = {"x": np.ascontiguousarray(x[b]), "mem": np.ascontiguousarray(mem[b])}
        for k in WNAMES:
            m[k] = np.ascontiguousarray(weights[k])
        m.update(consts)
        in_maps.append(m)
    res = run_bass_kernel_spmd(nc, in_maps, core_ids=list(range(B)))
    return np.stack([res.results[b]["out"] for b in range(B)], axis=0)


FUSED = True


def kernel(**inputs):
    x = np.asarray(inputs["x"], dtype=np.float32)
    mem = np.asarray(inputs["mem"], dtype=np.float32)
    W = {k: np.asarray(inputs[k], dtype=np.float32) for k in WNAMES}
    depth = W["w_in"].shape[0]
    if FUSED:
        return run_layers(x, mem, W, depth)
    for l in range(depth):
        x = run_layers(x, mem, {k: v[l:l + 1] for k, v in W.items()}, 1)
    return x
```
